# Optimizing a Trainium2 kernel written in Bass

```python
import jax, jax.numpy as jnp
from jax import lax
import numpy as np

D_MODEL = 1024
BATCH = 32
SEQ = 2048
DEPTH = 1
DEC_BATCH = 2
DEC_SEQ = 16384
PAST_LEN = 128

POOL_GROUPS = 4
POOL_GROUP_DIM = 128
POOL_WIDTH = POOL_GROUPS * POOL_GROUP_DIM
POOL_WINDOWS = (2, 4, 8, 16)
HEAD_DIM = 64
N_HEADS = D_MODEL // HEAD_DIM
RWKV_WIDTH = N_HEADS * HEAD_DIM
DECAY_LORA = 64
AAA_LORA = 64
GATE_LORA = 128
D_FF = 4 * D_MODEL
N_BRANCHES = 2
RMS_EPS = 1e-6
GN_EPS = 64e-5
L2_EPS = 1e-12

COL_POOL = 0
COL_R = COL_POOL + POOL_WIDTH
COL_K = COL_R + RWKV_WIDTH
COL_V = COL_K + RWKV_WIDTH
COL_W = COL_V + RWKV_WIDTH
COL_A = COL_W + DECAY_LORA
COL_G = COL_A + AAA_LORA
COL_GATE = COL_G + GATE_LORA
IN_COLS = COL_GATE + N_BRANCHES * D_MODEL
SHIFT_WIDTH = COL_GATE - COL_R

kernel_name = "pool_rwkv7_bidir_hybrid_encoder"


def rms_norm(x, g):
    xf = x.astype(jnp.float32)
    y = xf * lax.rsqrt(jnp.mean(xf * xf, axis=-1, keepdims=True) + RMS_EPS)
    return (y * g.astype(jnp.float32)).astype(x.dtype)


def centred_shift_mix(z, mu_prev, mu_next):
    z_prev = jnp.pad(z, ((0, 0), (1, 0), (0, 0)))[:, :-1]
    z_next = jnp.pad(z, ((0, 0), (0, 1), (0, 0)))[:, 1:]
    return z + mu_prev * (z_prev - z) + mu_next * (z_next - z)


def multiscale_pool(u):
    B, S, _ = u.shape
    ug = u.reshape(B, S, POOL_GROUPS, POOL_GROUP_DIM).astype(jnp.float32)
    cs = jnp.concatenate([jnp.zeros((B, 1, POOL_GROUPS, POOL_GROUP_DIM), jnp.float32),
                          jnp.cumsum(ug, axis=1)], axis=1)
    t = np.arange(S)
    outs = []
    for gi, w in enumerate(POOL_WINDOWS):
        lo = np.maximum(t - w // 2, 0)
        hi = np.minimum(t + w // 2 - 1, S - 1)
        cnt = (hi - lo + 1).astype(np.float32)[None, :, None]
        csg = cs[:, :, gi]
        outs.append((csg[:, hi + 1] - csg[:, lo]) / cnt - ug[:, :, gi])
    return jnp.stack(outs, axis=2)


def wkv7_scan(r, k, v, w, kk, a, reverse):
    B, _, H, N = r.shape
    xs = tuple(jnp.moveaxis(t_, 1, 0) for t_ in (r, k, v, w, -kk, kk * a))

    def step(state, inp):
        r_t, k_t, v_t, w_t, a_t, b_t = inp
        sa = jnp.einsum('bhvk,bhk->bhv', state, a_t)
        state = (state * w_t[:, :, None, :] + sa[..., None] * b_t[:, :, None, :]
                 + v_t[..., None] * k_t[:, :, None, :])
        return state, jnp.einsum('bhvk,bhk->bhv', state, r_t)

    s0 = jnp.zeros((B, H, N, N), jnp.float32)
    _, ys = lax.scan(step, s0, xs, reverse=reverse)
    return jnp.moveaxis(ys, 0, 1)


def head_group_norm(y, w, b):
    mean = jnp.mean(y, axis=-1, keepdims=True)
    var = jnp.mean(jnp.square(y - mean), axis=-1, keepdims=True)
    return ((y - mean) * lax.rsqrt(var + GN_EPS) * w.reshape(N_HEADS, HEAD_DIM).astype(jnp.float32)
            + b.reshape(N_HEADS, HEAD_DIM).astype(jnp.float32))


def rwkv_branch(zr, zk, zv, zw, za, zg, k_k, k_a, r_k, w0_f, w_up_f, a0_f, a_up_f,
                w0_b, w_up_b, a0_b, a_up_b, g_up, ln_w, ln_b):
    B, S, _ = zr.shape
    heads = lambda t_: t_.astype(jnp.float32).reshape(B, S, N_HEADS, HEAD_DIM)
    r, k, v = heads(zr), heads(zk), heads(zv)
    kk = heads(zk * k_k)
    kk = kk / jnp.maximum(jnp.linalg.norm(kk, axis=-1, keepdims=True), L2_EPS)
    k_a_h = k_a.reshape(N_HEADS, HEAD_DIM).astype(jnp.float32)
    tw = jnp.tanh(zw)
    y_sum = jnp.zeros_like(r)
    k_sum = jnp.zeros_like(r)
    for w0, w_up, a0, a_up, rev in ((w0_f, w_up_f, a0_f, a_up_f, False),
                                    (w0_b, w_up_b, a0_b, a_up_b, True)):
        w_raw = (w0 + tw @ w_up).astype(jnp.float32)
        w = jnp.exp(-jnp.exp(-jax.nn.softplus(-w_raw) - 0.5))
        a = heads(jax.nn.sigmoid(a0 + za @ a_up))
        k_d = k * (1.0 + (a - 1.0) * k_a_h)
        y_sum = y_sum + wkv7_scan(r, k_d, v, heads(w), kk, a, rev)
        k_sum = k_sum + k_d
    y = head_group_norm(y_sum, ln_w, ln_b)
    y = y + jnp.sum(r * k_sum * r_k.astype(jnp.float32), axis=-1, keepdims=True) * v
    g = jax.nn.sigmoid(zg) @ g_up
    return (y.reshape(B, S, RWKV_WIDTH) * g.astype(jnp.float32)).astype(zr.dtype)


def encoder_layer(x, g_mix, w_in, b_gate, mu_prev, mu_next, pool_w, pool_scale, w_pool_br,
                  k_k, k_a, r_k, w0_f, w_up_f, a0_f, a_up_f, w0_b, w_up_b, a0_b, a_up_b,
                  g_up, ln_w, ln_b, w_rwkv_br, w_out, g_ffn, w_ff1, w_ff2):
    B, S, _ = x.shape
    xn = rms_norm(x, g_mix)
    z = xn @ w_in
    pooled = multiscale_pool(z[..., COL_POOL:COL_R])
    pooled = jnp.einsum('bsgc,gcd->bsgd', pooled, pool_w.astype(jnp.float32))
    pooled = (pooled.reshape(B, S, POOL_WIDTH) * pool_scale.astype(jnp.float32)).astype(x.dtype)
    pool_out = pooled @ w_pool_br
    zs = centred_shift_mix(z[..., COL_R:COL_GATE], mu_prev, mu_next)
    o = lambda c0, c1: zs[..., c0 - COL_R:c1 - COL_R]
    rw = rwkv_branch(o(COL_R, COL_K), o(COL_K, COL_V), o(COL_V, COL_W), o(COL_W, COL_A),
                     o(COL_A, COL_G), o(COL_G, COL_GATE), k_k, k_a, r_k, w0_f, w_up_f, a0_f, a_up_f,
                     w0_b, w_up_b, a0_b, a_up_b, g_up, ln_w, ln_b)
    rwkv_out = rw @ w_rwkv_br
    gates = jax.nn.sigmoid(z[..., COL_GATE:] + b_gate)
    merged = gates[..., :D_MODEL] * pool_out + gates[..., D_MODEL:] * rwkv_out
    x = x + merged @ w_out
    hn = rms_norm(x, g_ffn)
    return x + jnp.square(jax.nn.relu(hn @ w_ff1)) @ w_ff2


def setup_inputs(seed: int = 0) -> dict:
    key = jax.random.key(seed)
    ks = iter(jax.random.split(key, 40))
    nrm = lambda shape, scale: scale * jax.random.normal(next(ks), shape, jnp.float32)
    uni = lambda shape, lo, hi: jax.random.uniform(next(ks), shape, jnp.float32, lo, hi)
    L = DEPTH
    return {
        "x_prompt": nrm((BATCH, SEQ, D_MODEL), 1.0),
        "x_sample": nrm((DEC_BATCH, DEC_SEQ, D_MODEL), 1.0),
        "g_mix": 1.0 + nrm((L, D_MODEL), 0.05),
        "w_in": nrm((L, D_MODEL, IN_COLS), D_MODEL ** -0.5),
        "b_gate": nrm((L, N_BRANCHES * D_MODEL), 0.1),
        "mu_prev": uni((L, SHIFT_WIDTH), 0.1, 0.5),
        "mu_next": uni((L, SHIFT_WIDTH), 0.1, 0.5),
        "pool_w": nrm((L, POOL_GROUPS, POOL_GROUP_DIM, POOL_GROUP_DIM), POOL_GROUP_DIM ** -0.5),
        "pool_scale": 1.0 + nrm((L, POOL_WIDTH), 0.1),
        "w_pool_br": nrm((L, POOL_WIDTH, D_MODEL), POOL_WIDTH ** -0.5),
        "k_k": 0.85 + nrm((L, RWKV_WIDTH), 0.05),
        "k_a": 1.0 + nrm((L, RWKV_WIDTH), 0.05),
        "r_k": nrm((L, N_HEADS, HEAD_DIM), 0.1),
        "w0_f": uni((L, RWKV_WIDTH), -5.0, 1.0),
        "w_up_f": nrm((L, DECAY_LORA, RWKV_WIDTH), 0.5 * DECAY_LORA ** -0.5),
        "a0_f": nrm((L, RWKV_WIDTH), 0.5),
        "a_up_f": nrm((L, AAA_LORA, RWKV_WIDTH), 0.5 * AAA_LORA ** -0.5),
        "w0_b": uni((L, RWKV_WIDTH), -5.0, 1.0),
        "w_up_b": nrm((L, DECAY_LORA, RWKV_WIDTH), 0.5 * DECAY_LORA ** -0.5),
        "a0_b": nrm((L, RWKV_WIDTH), 0.5),
        "a_up_b": nrm((L, AAA_LORA, RWKV_WIDTH), 0.5 * AAA_LORA ** -0.5),
        "g_up": nrm((L, GATE_LORA, RWKV_WIDTH), GATE_LORA ** -0.5),
        "ln_w": 1.0 + nrm((L, RWKV_WIDTH), 0.05),
        "ln_b": nrm((L, RWKV_WIDTH), 0.02),
        "w_rwkv_br": nrm((L, RWKV_WIDTH, D_MODEL), RWKV_WIDTH ** -0.5),
        "w_out": nrm((L, D_MODEL, D_MODEL), D_MODEL ** -0.5),
        "g_ffn": 1.0 + nrm((L, D_MODEL), 0.05),
        "w_ff1": nrm((L, D_MODEL, D_FF), D_MODEL ** -0.5),
        "w_ff2": nrm((L, D_FF, D_MODEL), D_FF ** -0.5),
        "g_final": 1.0 + nrm((D_MODEL,), 0.05),
    }


def reference(x_prompt, x_sample, g_mix, w_in, b_gate, mu_prev, mu_next, pool_w, pool_scale,
              w_pool_br, k_k, k_a, r_k, w0_f, w_up_f, a0_f, a_up_f, w0_b, w_up_b, a0_b, a_up_b,
              g_up, ln_w, ln_b, w_rwkv_br, w_out, g_ffn, w_ff1, w_ff2, g_final):
    def trunk(x):
        for l in range(DEPTH):
            x = encoder_layer(x, g_mix[l], w_in[l], b_gate[l], mu_prev[l], mu_next[l], pool_w[l],
                              pool_scale[l], w_pool_br[l], k_k[l], k_a[l], r_k[l], w0_f[l], w_up_f[l],
                              a0_f[l], a_up_f[l], w0_b[l], w_up_b[l], a0_b[l], a_up_b[l], g_up[l],
                              ln_w[l], ln_b[l], w_rwkv_br[l], w_out[l], g_ffn[l], w_ff1[l], w_ff2[l])
        return rms_norm(x, g_final)

    y_prompt = trunk(x_prompt)
    y_sample = trunk(x_sample)
    return (y_prompt, y_sample)
```

```python
from contextlib import ExitStack
import numpy as np
import concourse.bass as bass
import concourse.mybir as mybir
from concourse.bass_utils import run_bass_kernel_spmd

F32 = mybir.dt.float32
AF = mybir.ActivationFunctionType
ALU = mybir.AluOpType
AX = mybir.AxisListType

D = 1024
NJ = 8
TT = 512
C = 64
IN_COLS = 5888
NCC = IN_COLS // 128
RMS_EPS = 1e-6
GN_EPS = 64e-5
CENGS = ("pe", "act", "dve", "pool")
ENGS = ("pe", "act", "dve", "pool", "sp")
NSLOT = 8
EPOCH = 30000


class T:
    def __init__(self, name, ap):
        self.name = name
        self.ap = ap

    def __getitem__(self, k):
        return self.ap[k]


class Prog:
    def __init__(self):
        self.streams = {e: [] for e in ENGS}
        self.cnt = {e: 0 for e in CENGS}
        self.seen = {e: {} for e in ENGS}
        self.lastw = {}
        self.readers = {}
        self.semkeys = {}
        self.dma_slot = {e: 0 for e in ENGS}
        self.dma_val = {}
        self.nops = 0

    @staticmethod
    def _k(x):
        if isinstance(x, T):
            return x.name
        if isinstance(x, tuple) and isinstance(x[0], T):
            return (x[0].name,) + tuple(x[1:])
        return x

    def _need(self, eng, ev):
        if ev is None:
            return
        k, v = ev
        if k[0] == eng and eng == "pe":
            return
        if self.seen[eng].get(k, 0) >= v:
            return
        self.seen[eng][k] = v
        self.streams[eng].append(("wait", k, v))

    def _deps(self, eng, r, w):
        for key in r:
            self._need(eng, self.lastw.get(key))
        for key in w:
            self._need(eng, self.lastw.get(key))
            for ev in list(self.readers.get(key, {}).items()):
                self._need(eng, ev)

    def _commit(self, ev, r, w):
        for key in r:
            d = self.readers.setdefault(key, {})
            d[ev[0]] = max(d.get(ev[0], 0), ev[1])
        for key in w:
            self.lastw[key] = ev
            self.readers[key] = {}

    def op(self, eng, fn, r=(), w=()):
        r = [self._k(x) for x in r]
        w = [self._k(x) for x in w]
        self._deps(eng, r, w)
        self.cnt[eng] += 1
        n = self.cnt[eng]
        ep = (n - 1) // EPOCH
        k = (eng, ep)
        self.semkeys[k] = 1
        self.streams[eng].append(("op", fn, k, 1))
        self._commit((k, n - ep * EPOCH), r, w)
        self.nops += 1

    def dma(self, q, out, in_, r=(), w=()):
        r = [self._k(x) for x in r]
        w = [self._k(x) for x in w]
        self._deps(q, r, w)
        slot = self.dma_slot[q]
        self.dma_slot[q] = (slot + 1) % NSLOT
        k = ("dma", q, slot)
        self.semkeys[k] = 1
        prev = self.dma_val.get(k, 0)
        if prev:
            self._need(q, (k, prev))
        v = prev + 16
        self.dma_val[k] = v
        self.streams[q].append(("op", lambda e, o=out, i=in_: e.dma_start(out=o, in_=i), k, 16))
        self._commit((k, v), r, w)
        self.nops += 1

    def barrier(self):
        evs = []
        for e in CENGS:
            n = self.cnt[e]
            if n:
                ep = (n - 1) // EPOCH
                evs.append(((e, ep), n - ep * EPOCH))
        for k, v in self.dma_val.items():
            evs.append((k, v))
        for e in ENGS:
            for ev in evs:
                self._need(e, ev)

    def emit(self, nc):
        blockname = {"pe": "tensor", "act": "scalar", "dve": "vector", "pool": "gpsimd", "sp": "sync"}
        with ExitStack() as st:
            sems = {}
            for i, k in enumerate(self.semkeys):
                sems[k] = st.enter_context(nc.semaphore("s%d" % i))
            with nc.Block() as block:
                for eng in ENGS:
                    items = self.streams[eng]

                    def body(e, items=items):
                        for it in items:
                            if it[0] == "wait":
                                e.wait_ge(sems[it[1]], it[2])
                            else:
                                it[1](e).then_inc(sems[it[2]], it[3])

                    getattr(block, blockname[eng])(body)


class Arena:
    def __init__(self, nc, st, name, ncols):
        self.t = st.enter_context(nc.sbuf_tensor(name, [128, ncols], F32))
        self.ncols = ncols
        self.off = 0
        self.uid = 0
        self.name = name

    def reset(self):
        self.off = 0

    def alloc(self, name, shape):
        n = int(np.prod(shape[1:]))
        assert self.off + n <= self.ncols, ("SBUF arena overflow", name, self.off + n, self.ncols)
        ap = self.t[0:shape[0], self.off:self.off + n]
        self.off += n
        if len(shape) == 3:
            ap = ap.rearrange("p (a b) -> p a b", a=shape[1])
        elif len(shape) == 4:
            ap = ap.rearrange("p (a b c) -> p a b c", a=shape[1], b=shape[2])
        self.uid += 1
        return T("%s.%s.%d" % (self.name, name, self.uid), ap)


def build(NS, SL, dbg=False):
    N = NS * SL
    NT = N // TT
    NCK = N // C
    CPS = SL // C
    nc = bass.Bass("TRN2", target_bir_lowering=False)
    P = Prog()

    def din(name, shape):
        return nc.dram_tensor(name, list(shape), F32, kind="ExternalInput").ap()

    def dscr(name, shape):
        kind = "ExternalOutput" if dbg else "Internal"
        return nc.dram_tensor(name, list(shape), F32, kind=kind).ap()

    xpad = din("xpad", [N + 16, D])
    hm_d = din("hm", [16, NT])
    rcnt_d = din("rcnt", [4, N])
    cm_d = din("cm", [128, 2 * NS])
    cols_d = din("cols", [128, NCOLS])
    consts_d = din("consts", [128, 3328])
    rows_d = din("rows", [3, D])
    w_in_d = din("w_in", [NCC, 128, NJ, 128])
    pool_w_d = din("pool_w", [128, 4, 128])
    lora_d = din("lora", [128, 2, D])
    g_up_d = din("g_up", [128, D])
    w_pb_d = din("w_pb", [128, 4, D])
    w_rb_d = din("w_rb", [128, NJ, D])
    w_out_d = din("w_out", [NJ, 128, D])
    w_ff1_d = din("w_ff1", [32, 128, NJ, 128])
    w_ff2_d = din("w_ff2", [32, 128, D])
    out_d = nc.dram_tensor("out", [N, D], F32, kind="ExternalOutput").ap()

    FM = {}
    for d_ in "fb":
        for nm in ("A", "B", "K", "R"):
            FM[nm + d_] = dscr("FM_%s%s" % (nm, d_), [NCK, 128, NJ * C])
    TMn = ["V", "BV", "BHf", "KHf", "BHb", "KHb", "G", "Yf", "Yb"]
    TM = {nm: dscr("TM_" + nm, [N, D]) for nm in TMn}
    GATE = dscr("GATE", [16, 128, N])
    P2 = dscr("P2", [4, 128, N])

    with ExitStack() as st:
        ar = Arena(nc, st, "ar", 43000)
        cst = Arena(nc, st, "cst", 2 * 512 + 128 + 128 + NCOLS + 2 * NJ * NCK + 2 * NS + NT + 64)
        banks = [T("bank%d" % i, st.enter_context(nc.psum_tensor("bank%d" % i, [128, 1024], F32))) for i in range(4)]

        def bk(i):
            return banks[i // 2].ap[:, (i % 2) * 512:(i % 2 + 1) * 512], ("ps", i)

        colt = cst.alloc("cols", [128, NCOLS])
        ident = cst.alloc("ident", [128, 128])
        blk1 = cst.alloc("blk1", [128, 128])
        rmask = cst.alloc("rmask", [128, 512])
        gcs = {"f": cst.alloc("gcf", [128, NJ, NCK]), "b": cst.alloc("gcb", [128, NJ, NCK])}
        cmt = cst.alloc("cm", [128, 2 * NS])
        hmt = cst.alloc("hm", [16, NT])
        P.dma("sp", colt.ap, cols_d, w=[colt])
        P.dma("sp", ident.ap, consts_d[:, 2560:2688], w=[ident])
        P.dma("sp", blk1.ap, consts_d[:, 2688:2816], w=[blk1])
        P.dma("sp", rmask.ap, consts_d[:, 2048:2560], w=[rmask])
        P.dma("sp", cmt.ap, cm_d, w=[cmt])
        P.dma("sp", hmt.ap, hm_d, w=[hmt])

        def col(name, j):
            o = COLOFF[name] + j
            return colt.ap[:, o:o + 1]

        for i in range(26):
            P.op("dve", lambda e, i=i: e.tensor_tensor(out=col("c0", i), in0=col("mu_prev", i), in1=col("mu_next", i), op=ALU.add), r=[colt], w=[colt])
        P.op("dve", lambda e: e.tensor_scalar(out=colt.ap[:, COLOFF["c0"]:COLOFF["c0"] + 26], in0=colt.ap[:, COLOFF["c0"]:COLOFF["c0"] + 26], scalar1=-1.0, scalar2=1.0, op0=ALU.mult, op1=ALU.add), r=[colt], w=[colt])
        P.op("dve", lambda e: e.tensor_scalar(out=colt.ap[:, COLOFF["omka"]:COLOFF["omka"] + 8], in0=colt.ap[:, COLOFF["k_a"]:COLOFF["k_a"] + 8], scalar1=-1.0, scalar2=1.0, op0=ALU.mult, op1=ALU.add), r=[colt], w=[colt])

        phase1(nc, P, ar, locals())
        P.barrier()
        ar.reset()
        phase2(nc, P, ar, locals())
        P.barrier()
        ar.reset()
        phase3(nc, P, ar, locals())
        P.barrier()
        P.emit(nc)
    return nc


COLSPEC = [("g_mix", 8), ("b_gate", 16), ("mu_prev", 26), ("mu_next", 26), ("pool_scale", 4), ("k_k", 8), ("k_a", 8),
           ("r_k", 8), ("w0_f", 8), ("a0_f", 8), ("w0_b", 8), ("a0_b", 8), ("g_ffn", 8), ("c0", 26), ("omka", 8)]
COLOFF = {}
_o = 0
for _n, _c in COLSPEC:
    COLOFF[_n] = _o
    _o += _c
NCOLS = _o


def phase1(nc, P, ar, env):
    NT, N, NCK = env["NT"], env["N"], env["NCK"]
    xpad, rcnt_d, w_in_d = env["xpad"], env["rcnt_d"], env["w_in_d"]
    FM, TM, GATE, P2 = env["FM"], env["TM"], env["GATE"], env["P2"]
    colt, ident, blk1, rmask, gcs, hmt = env["colt"], env["ident"], env["blk1"], env["rmask"], env["gcs"], env["hmt"]
    col, bk = env["col"], env["bk"]

    poolw = ar.alloc("poolw", [128, 4, 128])
    lora = ar.alloc("lora", [128, 2, D])
    gup = ar.alloc("gup", [128, D])
    P.dma("sp", poolw.ap, env["pool_w_d"], w=[poolw])
    P.dma("sp", lora.ap, env["lora_d"], w=[lora])
    P.dma("sp", gup.ap, env["g_up_d"], w=[gup])

    xt = [ar.alloc("xt%d" % b, [128, D]) for b in range(4)]
    xh = ar.alloc("xh", [16, D])
    ss = ar.alloc("ss", [128, 8])
    xnT = ar.alloc("xnT", [128, NJ, TT])
    xnTh = ar.alloc("xnTh", [128, NJ, 16])
    wring = [ar.alloc("w%d" % i, [128, NJ, 128]) for i in range(3)]
    zext = [ar.alloc("zext%d" % i, [128, 528]) for i in range(2)]
    zr = ar.alloc("zr", [128, NJ, TT])
    zk = ar.alloc("zk", [128, NJ, TT])
    zv = ar.alloc("zv", [128, NJ, TT])
    twza = ar.alloc("twza", [128, TT])
    sg = ar.alloc("sg", [128, TT])
    rc = ar.alloc("rc", [128, TT])
    pt = [ar.alloc("pt%d" % i, [128, 528]) for i in range(2)]
    p2T = ar.alloc("p2T", [128, TT])
    gst = [ar.alloc("gst%d" % i, [128, TT]) for i in range(2)]
    tmst = [ar.alloc("tmst%d" % i, [128, 4, 128]) for i in range(3)]
    gtm = [ar.alloc("gtm%d" % i, [128, D]) for i in range(1)]
    junk = gtm[0]
    ded = [ar.alloc("ded%d" % i, [128, TT]) for i in range(13)]
    xsl = [T((xnT.name, j), xnT.ap[:, j, :]) for j in range(NJ)]
    xth = [T((xt[b].name, h), xt[b].ap[:, h * 512:(h + 1) * 512]) for b in range(4) for h in range(2)]
    xtk = lambda b: [(xt[b], 0), (xt[b], 1)]
    print("phase1 arena cols", ar.off)
    PT = dict(kk=ded[0], kkn=ded[1], ksum=ded[2], sq=ded[3], rn=ded[4], bv=ded[5])
    DT1 = dict(sgm=xsl[0], a_=xsl[1], lw=xsl[2], PI=xsl[3], X1=xsl[4], X2=xsl[5], X3=xsl[6], Ea=xsl[7], Eb=xth[0], kd=xth[1], b_=xth[2])
    outs = [xth[3], xth[4], xth[5], xth[6], xth[7]] + ded[6:13]
    DT2 = [dict(At=outs[0], Bt=outs[1], Kt=outs[2], Rt=outs[3], BH=outs[4], KH=outs[5]),
           dict(At=outs[6], Bt=outs[7], Kt=outs[8], Rt=outs[9], BH=outs[10], KH=outs[11])]
    ptmp = [zext[0], zext[1]]
    tctr = [0]
    mctr = [0]
    wctr = [0]
    dctr = [0]

    def transposes_to_tm(src_fn, src_keys, name, j, t0):
        bi = 4 + (mctr[0] % 2)
        bap, bkey = bk(bi)
        for b in range(4):
            P.op("pe", lambda e, b=b: e.transpose(out=bap[:, b * 128:(b + 1) * 128], in_=src_fn(b), identity=ident.ap), r=list(src_keys) + [ident], w=[bkey])
        stg = tmst[mctr[0] % 3]
        mctr[0] += 1
        P.op("act", lambda e: e.activation(out=stg.ap, in_=bap.rearrange("p (b c) -> p b c", b=4), func=AF.Copy), r=[bkey], w=[stg])
        dst = TM[name][t0:t0 + TT, j * 128:(j + 1) * 128].rearrange("(b t) c -> t b c", b=4)
        P.dma("pool", dst, stg.ap, r=[stg])

    def do_tile(ti):
        t0 = ti * TT
        for b in range(4):
            P.dma("sp", xt[b].ap, xpad[8 + t0 + b * 128: 8 + t0 + (b + 1) * 128, :], w=xtk(b))
        P.dma("sp", xh.ap[0:8, :], xpad[t0:t0 + 8, :], w=[xh])
        P.dma("sp", xh.ap[8:16, :], xpad[t0 + 520:t0 + 528, :], w=[xh])
        for b in range(4):
            P.op("act", lambda e, b=b: e.activation(out=junk.ap, in_=xt[b].ap, func=AF.Square, accum_out=ss.ap[:, b:b + 1]), r=xtk(b), w=[junk, ss])
        P.op("act", lambda e: e.activation(out=junk.ap[0:16, :], in_=xh.ap, func=AF.Square, accum_out=ss.ap[0:16, 4:5]), r=[xh], w=[junk, ss])
        P.op("act", lambda e: e.activation(out=ss.ap[:, 0:5], in_=ss.ap[:, 0:5], func=AF.Sqrt, scale=1.0 / D, bias=col_eps(env)), r=[ss], w=[ss])
        P.op("dve", lambda e: e.reciprocal(out=ss.ap[:, 0:5], in_=ss.ap[:, 0:5]), r=[ss], w=[ss])
        P.op("dve", lambda e, ti=ti: e.tensor_tensor(out=ss.ap[0:16, 4:5], in0=ss.ap[0:16, 4:5], in1=hmt.ap[0:16, ti:ti + 1], op=ALU.mult), r=[ss, hmt], w=[ss])
        for b in range(4):
            P.op("dve", lambda e, b=b: e.tensor_scalar(out=xt[b].ap, in0=xt[b].ap, scalar1=ss.ap[:, b:b + 1], scalar2=None, op0=ALU.mult), r=xtk(b) + [ss], w=xtk(b))
        P.op("dve", lambda e: e.tensor_scalar(out=xh.ap, in0=xh.ap, scalar1=ss.ap[0:16, 4:5], scalar2=None, op0=ALU.mult), r=[xh, ss], w=[xh])
        for j in range(NJ):
            bap, bkey = bk(j % 2)
            for b in range(4):
                P.op("pe", lambda e, b=b, j=j, bap=bap: e.transpose(out=bap[:, b * 128:(b + 1) * 128], in_=xt[b].ap[:, j * 128:(j + 1) * 128], identity=ident.ap), r=xtk(b) + [ident], w=[bkey])
            P.op("act", lambda e, j=j, bap=bap: e.activation(out=xnT.ap[:, j, :], in_=bap, func=AF.Copy, scale=col("g_mix", j)), r=[bkey, colt], w=[(xnT, j)])
            hap, hkey = bk(2 + j % 2)
            P.op("pe", lambda e, j=j, hap=hap: e.transpose(out=hap[:, 0:16], in_=xh.ap[0:16, j * 128:(j + 1) * 128], identity=ident.ap[0:16, 0:16]), r=[xh, ident], w=[hkey])
            P.op("act", lambda e, j=j, hap=hap: e.activation(out=xnTh.ap[:, j, :], in_=hap[:, 0:16], func=AF.Copy, scale=col("g_mix", j)), r=[hkey, colt], w=[xnTh])
        xkeys = [(xnT, j) for j in range(NJ)]

        def zchunk(cc, halo):
            wt = wring[wctr[0] % 3]
            wctr[0] += 1
            P.dma("sp", wt.ap, w_in_d[cc], w=[wt])
            bi = wctr[0] % 2
            bap, bkey = bk(bi)
            for j in range(NJ):
                P.op("pe", lambda e, j=j: e.matmul(bap, lhsT=wt.ap[:, j, :], rhs=xnT.ap[:, j, :], start=(j == 0), stop=(j == NJ - 1)), r=[wt, (xnT, j)], w=[bkey])
            hap = hkey = None
            if halo:
                hap, hkey = bk(2 + bi)
                for j in range(NJ):
                    P.op("pe", lambda e, j=j: e.matmul(hap[:, 0:16], lhsT=wt.ap[:, j, :], rhs=xnTh.ap[:, j, :], start=(j == 0), stop=(j == NJ - 1)), r=[wt, xnTh], w=[hkey])
            return bap, bkey, hap, hkey

        def to_zext(cc):
            bap, bkey, hap, hkey = zchunk(cc, True)
            ze = zext[cc % 2]
            P.op("act", lambda e: e.activation(out=ze.ap[:, 8:520], in_=bap, func=AF.Copy), r=[bkey], w=[ze])
            P.op("dve", lambda e: e.tensor_copy(out=ze.ap[:, 0:8], in_=hap[:, 0:8]), r=[hkey], w=[ze])
            P.op("dve", lambda e: e.tensor_copy(out=ze.ap[:, 520:528], in_=hap[:, 8:16]), r=[hkey], w=[ze])
            return ze

        for g in range(4):
            ze = to_zext(g)
            P.dma("sp", rc.ap, rcnt_d[g:g + 1, t0:t0 + TT].partition_broadcast(128), w=[rc])
            cur, L = ze, 528
            sh = 1
            for lev in range(g + 1):
                nxt = pt[lev % 2]
                L2 = L - sh
                P.op("pool", lambda e, cur=cur, nxt=nxt, L2=L2, sh=sh: e.tensor_tensor(out=nxt.ap[:, 0:L2], in0=cur.ap[:, 0:L2], in1=cur.ap[:, sh:sh + L2], op=ALU.add), r=[cur], w=[nxt])
                cur, L, sh = nxt, L2, sh * 2
            w2 = 1 << g
            pl = ded[g % 2]
            P.op("pool", lambda e, cur=cur, w2=w2, pl=pl: e.tensor_tensor(out=pl.ap, in0=cur.ap[:, 8 - w2:8 - w2 + TT], in1=rc.ap, op=ALU.mult), r=[cur, rc], w=[pl])
            P.op("pool", lambda e, pl=pl, ze=ze: e.tensor_tensor(out=pl.ap, in0=pl.ap, in1=ze.ap[:, 8:520], op=ALU.subtract), r=[pl, ze], w=[pl])
            bap, bkey = bk(4)
            P.op("pe", lambda e, g=g, pl=pl: e.matmul(bap, lhsT=poolw.ap[:, g, :], rhs=pl.ap, start=True, stop=True), r=[poolw, pl], w=[bkey])
            P.op("act", lambda e, g=g: e.activation(out=p2T.ap, in_=bap, func=AF.Copy, scale=col("pool_scale", g)), r=[bkey, colt], w=[p2T])
            P.dma("pool", P2[g, :, t0:t0 + TT], p2T.ap, r=[p2T])

        def mix(ze, i, dst_ap, dkeys):
            t1 = ded[2 + (i % 2)]
            P.op("dve", lambda e: e.tensor_scalar(out=t1.ap, in0=ze.ap[:, 8:520], scalar1=col("c0", i), scalar2=None, op0=ALU.mult), r=[ze, colt], w=[t1])
            P.op("dve", lambda e: e.scalar_tensor_tensor(out=t1.ap, in0=ze.ap[:, 7:519], scalar=col("mu_prev", i), in1=t1.ap, op0=ALU.mult, op1=ALU.add), r=[ze, colt, t1], w=[t1])
            P.op("dve", lambda e: e.scalar_tensor_tensor(out=dst_ap, in0=ze.ap[:, 9:521], scalar=col("mu_next", i), in1=t1.ap, op0=ALU.mult, op1=ALU.add), r=[ze, colt, t1], w=dkeys)

        for i in range(26):
            ze = to_zext(4 + i)
            if i < 8:
                mix(ze, i, zr.ap[:, i, :], [(zr, i)])
            elif i < 16:
                mix(ze, i, zk.ap[:, i - 8, :], [(zk, i - 8)])
            elif i < 24:
                mix(ze, i, zv.ap[:, i - 16, :], [(zv, i - 16)])
            elif i == 24:
                mix(ze, i, twza.ap, [twza])
                P.op("act", lambda e: e.activation(out=twza.ap[0:64, :], in_=twza.ap[0:64, :], func=AF.Tanh), r=[twza], w=[twza])
            else:
                mix(ze, i, sg.ap, [sg])
                P.op("act", lambda e: e.activation(out=sg.ap, in_=sg.ap, func=AF.Sigmoid), r=[sg], w=[sg])

        for i in range(16):
            bap, bkey, _, _ = zchunk(30 + i, False)
            gt = gst[i % 2]
            P.op("act", lambda e, i=i, bap=bap, gt=gt: e.activation(out=gt.ap, in_=bap, func=AF.Sigmoid, bias=col("b_gate", i)), r=[bkey, colt], w=[gt])
            P.dma("pool", GATE[i, :, t0:t0 + TT], gt.ap, r=[gt])

        for b in range(4):
            gt_ = gtm[0]
            for hh in range(2):
                bap, bkey = bk(6 + hh)
                P.op("pe", lambda e, b=b, hh=hh, bap=bap: e.matmul(bap, lhsT=sg.ap[:, b * 128:(b + 1) * 128], rhs=gup.ap[:, hh * 512:(hh + 1) * 512], start=True, stop=True), r=[sg, gup], w=[bkey])
                P.op("act", lambda e, hh=hh, bap=bap, gt_=gt_: e.activation(out=gt_.ap[:, hh * 512:(hh + 1) * 512], in_=bap, func=AF.Copy), r=[bkey], w=[gt_])
            P.dma("pool", TM["G"][t0 + b * 128:t0 + (b + 1) * 128, :], gt_.ap, r=[gt_])

        ck0 = ti * (TT // C)
        for j in range(NJ):
            do_pair(j, t0, ck0)

    def do_pair(j, t0, ck0):
        kk, sq, rn, kkn, ksum, bv = PT["kk"], PT["sq"], PT["rn"], PT["kkn"], PT["ksum"], PT["bv"]
        P.op("dve", lambda e: e.tensor_scalar(out=kk.ap, in0=zk.ap[:, j, :], scalar1=col("k_k", j), scalar2=None, op0=ALU.mult), r=[(zk, j), colt], w=[kk])
        P.op("pool", lambda e: e.tensor_tensor(out=sq.ap, in0=kk.ap, in1=kk.ap, op=ALU.mult), r=[kk], w=[sq])
        bap, bkey = bk(6)
        P.op("pe", lambda e: e.matmul(bap, lhsT=blk1.ap, rhs=sq.ap, start=True, stop=True), r=[blk1, sq], w=[bkey])
        P.op("act", lambda e: e.activation(out=rn.ap, in_=bap, func=AF.Sqrt), r=[bkey], w=[rn])
        P.op("dve", lambda e: e.tensor_scalar(out=rn.ap, in0=rn.ap, scalar1=1e-12, scalar2=None, op0=ALU.max), r=[rn], w=[rn])
        P.op("dve", lambda e: e.reciprocal(out=rn.ap, in_=rn.ap), r=[rn], w=[rn])
        P.op("dve", lambda e: e.tensor_tensor(out=kkn.ap, in0=kk.ap, in1=rn.ap, op=ALU.mult), r=[kk, rn], w=[kkn])
        for di, d_ in enumerate("fb"):
            do_dir(j, t0, ck0, di, d_, kkn, ksum)
        t1 = sq
        P.op("dve", lambda e: e.scalar_tensor_tensor(out=t1.ap, in0=ksum.ap, scalar=col("r_k", j), in1=zr.ap[:, j, :], op0=ALU.mult, op1=ALU.mult), r=[ksum, colt, (zr, j)], w=[t1])
        b7ap, b7key = bk(7)
        P.op("pe", lambda e: e.matmul(b7ap, lhsT=blk1.ap, rhs=t1.ap, start=True, stop=True), r=[blk1, t1], w=[b7key])
        P.op("dve", lambda e: e.tensor_tensor(out=bv.ap, in0=b7ap, in1=zv.ap[:, j, :], op=ALU.mult), r=[b7key, (zv, j)], w=[bv])
        transposes_to_tm(lambda b: bv.ap[:, b * 128:(b + 1) * 128], [bv], "BV", j, t0)
        transposes_to_tm(lambda b: zv.ap[:, j, b * 128:(b + 1) * 128], [(zv, j)], "V", j, t0)

    def do_dir(j, t0, ck0, di, d_, kkn, ksum):
        g_ = DT1
        sgm, a_, lw, PI, X1, X2, X3, Ea, Eb, kd, b_ = (g_[k] for k in ("sgm", "a_", "lw", "PI", "X1", "X2", "X3", "Ea", "Eb", "kd", "b_"))
        o_ = DT2[dctr[0] % 2]
        dctr[0] += 1
        At, Bt, Kt, Rt, BH, KH = (o_[k] for k in ("At", "Bt", "Kt", "Rt", "BH", "KH"))
        b1ap, b1key = bk(7)
        P.op("pe", lambda e: e.matmul(b1ap, lhsT=lora.ap[0:64, di, j * 128:(j + 1) * 128], rhs=twza.ap[0:64, :], start=True, stop=True), r=[lora, twza], w=[b1key])
        P.op("act", lambda e: e.activation(out=sgm.ap, in_=b1ap, func=AF.Sigmoid, bias=col("w0_" + d_, j)), r=[b1key, colt], w=[sgm])
        b2ap, b2key = bk(6)
        P.op("pe", lambda e: e.matmul(b2ap, lhsT=lora.ap[64:128, di, j * 128:(j + 1) * 128], rhs=twza.ap[64:128, :], start=True, stop=True), r=[lora, twza], w=[b2key])
        P.op("act", lambda e: e.activation(out=a_.ap, in_=b2ap, func=AF.Sigmoid, bias=col("a0_" + d_, j)), r=[b2key, colt], w=[a_])
        P.op("pool", lambda e: e.tensor_scalar(out=lw.ap, in0=sgm.ap, scalar1=-0.6065306597126334, scalar2=None, op0=ALU.mult), r=[sgm], w=[lw])
        P.op("dve", lambda e: e.tensor_tensor_scan(out=PI.ap, data0=rmask.ap, data1=lw.ap, initial=0.0, op0=ALU.mult, op1=ALU.add), r=[rmask, lw], w=[PI])
        P.op("pool", lambda e: e.tensor_tensor(out=X1.ap, in0=PI.ap, in1=lw.ap, op=ALU.subtract), r=[PI, lw], w=[X1])
        PI3 = PI.ap.rearrange("p (c t) -> p c t", c=8)
        P.op("dve", lambda e: e.tensor_tensor(out=X2.ap.rearrange("p (c t) -> p c t", c=8), in0=PI3[:, :, 63:64].to_broadcast([128, 8, 64]), in1=PI3, op=ALU.subtract), r=[PI], w=[X2])
        if d_ == "f":
            srcs = [(X1, 1.0), (PI, -1.0), (PI, 1.0), (X2, 1.0)]
        else:
            P.op("pool", lambda e: e.tensor_tensor(out=X3.ap, in0=X2.ap, in1=lw.ap, op=ALU.add), r=[X2, lw], w=[X3])
            srcs = [(X2, 1.0), (X3, -1.0), (X3, 1.0), (X1, 1.0)]
        P.op("dve", lambda e: e.tensor_scalar(out=kd.ap, in0=a_.ap, scalar1=col("k_a", j), scalar2=col("omka", j), op0=ALU.mult, op1=ALU.add), r=[a_, colt], w=[kd])
        P.op("dve", lambda e: e.tensor_tensor(out=kd.ap, in0=kd.ap, in1=zk.ap[:, j, :], op=ALU.mult), r=[kd, (zk, j)], w=[kd])
        P.op("pool", lambda e: e.tensor_tensor(out=b_.ap, in0=kkn.ap, in1=a_.ap, op=ALU.mult), r=[kkn, a_], w=[b_])
        if di == 0:
            P.op("pool", lambda e: e.tensor_copy(out=ksum.ap, in_=kd.ap), r=[kd], w=[ksum])
        else:
            P.op("pool", lambda e: e.tensor_tensor(out=ksum.ap, in0=ksum.ap, in1=kd.ap, op=ALU.add), r=[kd, ksum], w=[ksum])

        def ex(Et, k):
            src_, sc = srcs[k]
            P.op("act", lambda e: e.activation(out=Et.ap, in_=src_.ap, func=AF.Exp, scale=sc), r=[src_], w=[Et])

        ex(Ea, 0)
        P.op("dve", lambda e: e.scalar_tensor_tensor(out=At.ap, in0=kkn.ap, scalar=-1.0, in1=Ea.ap, op0=ALU.mult, op1=ALU.mult), r=[kkn, Ea], w=[At])
        ex(Eb, 1)
        P.op("dve", lambda e: e.tensor_tensor(out=Bt.ap, in0=b_.ap, in1=Eb.ap, op=ALU.mult), r=[b_, Eb], w=[Bt])
        P.op("dve", lambda e: e.tensor_tensor(out=Kt.ap, in0=kd.ap, in1=Eb.ap, op=ALU.mult), r=[kd, Eb], w=[Kt])
        ex(Ea, 2)
        P.op("dve", lambda e: e.tensor_tensor(out=Rt.ap, in0=zr.ap[:, j, :], in1=Ea.ap, op=ALU.mult), r=[(zr, j), Ea], w=[Rt])
        E33 = Ea.ap.rearrange("p (c t) -> p c t", c=8)
        cc_ = 63 if d_ == "f" else 0
        P.op("pool", lambda e: e.tensor_copy(out=gcs[d_].ap[:, j, ck0:ck0 + 8], in_=E33[:, :, cc_]), r=[Ea], w=[gcs[d_]])
        ex(Eb, 3)
        P.op("pool", lambda e: e.tensor_tensor(out=BH.ap, in0=b_.ap, in1=Eb.ap, op=ALU.mult), r=[b_, Eb], w=[BH])
        P.op("pool", lambda e: e.tensor_tensor(out=KH.ap, in0=kd.ap, in1=Eb.ap, op=ALU.mult), r=[kd, Eb], w=[KH])
        for nm, tl in (("A", At), ("B", Bt), ("K", Kt), ("R", Rt)):
            dst = FM[nm + d_][ck0:ck0 + 8].rearrange("c p (j t) -> p c j t", j=NJ)[:, :, j, :]
            P.dma("sp", dst, tl.ap.rearrange("p (c t) -> p c t", c=8), r=[tl])
        transposes_to_tm(lambda b: BH.ap[:, b * 128:(b + 1) * 128], [BH], "BH" + d_, j, t0)
        transposes_to_tm(lambda b: KH.ap[:, b * 128:(b + 1) * 128], [KH], "KH" + d_, j, t0)

    for ti in range(NT):
        do_tile(ti)


def col_eps(env):
    return RMS_EPS


def phase2(nc, P, ar, env):
    NCK, NS, CPS = env["NCK"], env["NS"], env["CPS"]
    FM, TM = env["FM"], env["TM"]
    gcs, cmt, bk, banks, consts_d = env["gcs"], env["cmt"], env["bk"], env["banks"], env["consts_d"]
    mk = ar.alloc("masks", [64, 4, 512])
    P.dma("sp", mk.ap, consts_d[0:64, 0:2048].rearrange("p (m c) -> p m c", m=4), w=[mk])
    irep = ar.alloc("irep", [64, 8, 64])
    P.dma("sp", irep.ap, consts_d[0:64, 2816:3328].rearrange("p (a b) -> p a b", a=8), w=[irep])
    S = {}
    for d_ in "fb":
        s = {}
        s["fm"] = [{nm: ar.alloc("fm" + nm + d_ + str(q), [128, NJ, C]) for nm in "ABKR"} for q in range(2)]
        for nm in ("V", "BH", "KH"):
            s[nm] = ar.alloc(nm + d_, [64, D])
        s["ARbd"] = ar.alloc("ARbd" + d_, [128, NJ, 256])
        s["Bbd"] = ar.alloc("Bbd" + d_, [128, NJ, 128])
        s["A_sb"] = ar.alloc("A_sb" + d_, [64, 16, 64])
        s["BP"] = ar.alloc("BP" + d_, [64, 16, 128])
        s["ArbT"] = ar.alloc("ArbT" + d_, [64, 16, 64])
        s["AakT"] = ar.alloc("AakT" + d_, [64, 16, 64])
        s["ArkT"] = ar.alloc("ArkT" + d_, [64, 16, 64])
        s["W"] = ar.alloc("W" + d_, [64, D])
        s["U"] = ar.alloc("U" + d_, [64, D])
        s["H"] = ar.alloc("H" + d_, [128, NJ, 128])
        for nm in ("ARbd", "Bbd", "H"):
            t_ = s[nm]
            P.op("pool", lambda e, t_=t_: e.memset(t_.ap, 0.0), w=[t_])
        S[d_] = s
    print("phase2 arena cols", ar.off)
    Q = [banks[i].ap for i in range(4)]
    QK = [[("ps", 2 * i), ("ps", 2 * i + 1)] for i in range(4)]
    MIDX = {"f": dict(A=2, B=0, R=1), "b": dict(A=0, B=2, R=3)}

    def load(d_, c, q):
        s = S[d_]
        for nm in "ABKR":
            t_ = s["fm"][q][nm]
            P.dma("sp", t_.ap.rearrange("p j t -> p (j t)"), FM[nm + d_][c], w=[t_])
        for nm, src_ in (("V", "V"), ("BH", "BH" + d_), ("KH", "KH" + d_)):
            P.dma("sp", s[nm].ap, TM[src_][c * C:(c + 1) * C, :], w=[s[nm]])

    def scan(d_, c, q):
        s = S[d_]
        fm = s["fm"][q]
        A, B, K, R = fm["A"], fm["B"], fm["K"], fm["R"]
        V, BH, KH, ARbd, Bbd, A_sb, BP, ArbT, AakT, ArkT, W, U, H = (s[k] for k in ("V", "BH", "KH", "ARbd", "Bbd", "A_sb", "BP", "ArbT", "AakT", "ArkT", "W", "U", "H"))
        mi = MIDX[d_]
        gc = gcs[d_]
        if d_ == "f" and c % CPS == 0 and c > 0:
            idx = c // CPS
            P.op("pool", lambda e: e.tensor_scalar(out=H.ap, in0=H.ap, scalar1=cmt.ap[:, idx:idx + 1], scalar2=None, op0=ALU.mult), r=[H, cmt], w=[H])
        if d_ == "b" and c % CPS == CPS - 1 and c < NCK - 1:
            idx = NS + c // CPS
            P.op("pool", lambda e: e.tensor_scalar(out=H.ap, in0=H.ap, scalar1=cmt.ap[:, idx:idx + 1], scalar2=None, op0=ALU.mult), r=[H, cmt], w=[H])
        for (dst, src_, o0) in ((ARbd, A, 0), (ARbd, R, 128), (Bbd, B, 0)):
            P.op("pool", lambda e, dst=dst, src_=src_, o0=o0: e.tensor_copy(out=dst.ap[0:64, :, o0:o0 + 64], in_=src_.ap[0:64, :, :]), r=[src_], w=[dst])
            P.op("pool", lambda e, dst=dst, src_=src_, o0=o0: e.tensor_copy(out=dst.ap[64:128, :, o0 + 64:o0 + 128], in_=src_.ap[64:128, :, :]), r=[src_], w=[dst])
        for hf in range(2):
            pA, pAk = bk(4)
            for jl in range(4):
                jj = 4 * hf + jl
                P.op("pe", lambda e, jl=jl, jj=jj: e.matmul(pA[0:64, jl * 128:(jl + 1) * 128], lhsT=A.ap[:, jj, :], rhs=Bbd.ap[:, jj, :], start=True, stop=True), r=[A, Bbd], w=[pAk])
                P.op("pe", lambda e, jl=jl, jj=jj: e.matmul(Q[0][0:64, jl * 256:(jl + 1) * 256], lhsT=B.ap[:, jj, :], rhs=ARbd.ap[:, jj, :], start=True, stop=True), r=[B, ARbd], w=QK[0])
                P.op("pe", lambda e, jl=jl, jj=jj: e.matmul(Q[1][0:64, jl * 256:(jl + 1) * 256], lhsT=K.ap[:, jj, :], rhs=ARbd.ap[:, jj, :], start=True, stop=True), r=[K, ARbd], w=QK[1])
            hs = slice(8 * hf, 8 * hf + 8)
            P.op("dve", lambda e, hs=hs: e.tensor_tensor(out=A_sb.ap[:, hs, :], in0=pA[0:64, :].rearrange("p (h s) -> p h s", h=8), in1=mk.ap[:, mi["A"], :].rearrange("p (h s) -> p h s", h=8), op=ALU.mult), r=[pAk, mk], w=[A_sb])
            q0v = Q[0][0:64, :].rearrange("p (j q s) -> p j q s", j=4, q=4)
            q1v = Q[1][0:64, :].rearrange("p (j q s) -> p j q s", j=4, q=4)
            mB = mk.ap[:, mi["B"], :].rearrange("p (j q s) -> p j q s", j=4, q=2)
            mR = mk.ap[:, mi["R"], :].rearrange("p (j q s) -> p j q s", j=4, q=2)
            v4 = lambda ap: ap.rearrange("p (j q) s -> p j q s", j=4)
            P.op("dve", lambda e, hs=hs, q0v=q0v, mB=mB: e.tensor_tensor(out=v4(BP.ap[:, hs, 0:64]), in0=q0v[:, :, 0:2, :], in1=mB, op=ALU.mult), r=QK[0] + [mk], w=[BP])
            P.op("dve", lambda e, hs=hs, q0v=q0v, mR=mR: e.tensor_tensor(out=v4(ArbT.ap[:, hs, :]), in0=q0v[:, :, 2:4, :], in1=mR, op=ALU.mult), r=QK[0] + [mk], w=[ArbT])
            P.op("dve", lambda e, hs=hs, q1v=q1v, mB=mB: e.tensor_tensor(out=v4(AakT.ap[:, hs, :]), in0=q1v[:, :, 0:2, :], in1=mB, op=ALU.mult), r=QK[1] + [mk], w=[AakT])
            P.op("dve", lambda e, hs=hs, q1v=q1v, mR=mR: e.tensor_tensor(out=v4(ArkT.ap[:, hs, :]), in0=q1v[:, :, 2:4, :], in1=mR, op=ALU.mult), r=QK[1] + [mk], w=[ArkT])
            P.op("pool", lambda e, hs=hs: e.tensor_tensor(out=BP.ap[:, hs, 64:128], in0=BP.ap[:, hs, 0:64], in1=irep.ap, op=ALU.add), r=[BP, irep], w=[BP])
        pset = [(Q[0], QK[0], bk(4)), (Q[1], QK[1], bk(5))]
        for st_ in range(6):
            for hf in range(2):
                pX, pXk, (pY, pYk) = pset[hf]
                hs = slice(8 * hf, 8 * hf + 8)
                for hl in range(8):
                    h = 8 * hf + hl
                    if st_ == 0:
                        P.op("pe", lambda e, hl=hl, h=h, pX=pX: e.matmul(pX[0:64, hl * 128:hl * 128 + 64], lhsT=A_sb.ap[:, h, :], rhs=BP.ap[:, h, 0:64], start=True, stop=True), r=[A_sb, BP], w=pXk)
                    elif st_ < 5:
                        P.op("pe", lambda e, hl=hl, h=h, pX=pX: e.matmul(pX[0:64, hl * 128:(hl + 1) * 128], lhsT=A_sb.ap[:, h, :], rhs=BP.ap[:, h, :], start=True, stop=True), r=[A_sb, BP], w=pXk)
                    else:
                        P.op("pe", lambda e, hl=hl, h=h, pX=pX: e.matmul(pX[0:64, hl * 128 + 64:(hl + 1) * 128], lhsT=A_sb.ap[:, h, :], rhs=BP.ap[:, h, 64:128], start=True, stop=True), r=[A_sb, BP], w=pXk)
                    if st_ < 5:
                        P.op("pe", lambda e, hl=hl, h=h, pY=pY: e.matmul(pY[0:64, hl * 64:(hl + 1) * 64], lhsT=BP.ap[:, h, 0:64], rhs=A_sb.ap[:, h, :], start=True, stop=True), r=[A_sb, BP], w=[pYk])
                pXv = pX[0:64, :].rearrange("p (h c) -> p h c", h=8)
                if st_ < 5:
                    P.op("act", lambda e, hs=hs, pXv=pXv: e.activation(out=BP.ap[:, hs, 0:64], in_=pXv[:, :, 0:64], func=AF.Copy), r=pXk, w=[BP])
                    P.op("act", lambda e, hs=hs, pY=pY: e.activation(out=A_sb.ap[:, hs, :], in_=pY[0:64, :].rearrange("p (h s) -> p h s", h=8), func=AF.Copy), r=[pYk], w=[A_sb])
                if st_ > 0:
                    P.op("dve", lambda e, hs=hs, pXv=pXv: e.tensor_tensor(out=BP.ap[:, hs, 64:128], in0=BP.ap[:, hs, 64:128], in1=pXv[:, :, 64:128], op=ALU.add), r=pXk + [BP], w=[BP])
        for jj in range(NJ):
            P.op("pe", lambda e, jj=jj: e.matmul(Q[3][0:64, jj * 128:(jj + 1) * 128], lhsT=A.ap[:, jj, :], rhs=H.ap[:, jj, :], start=True, stop=False, skip_group_check=True), r=[A, H], w=QK[3])
            for par in range(2):
                h = 2 * jj + par
                P.op("pe", lambda e, h=h: e.matmul(Q[3][0:64, h * 64:(h + 1) * 64], lhsT=AakT.ap[:, h, :], rhs=V.ap[:, h * 64:(h + 1) * 64], start=False, stop=True, skip_group_check=True), r=[AakT, V], w=QK[3])
        P.op("act", lambda e: e.activation(out=W.ap, in_=Q[3][0:64, :], func=AF.Copy), r=QK[3], w=[W])
        for h in range(16):
            P.op("pe", lambda e, h=h: e.matmul(Q[3][0:64, h * 64:(h + 1) * 64], lhsT=BP.ap[:, h, 64:128], rhs=W.ap[:, h * 64:(h + 1) * 64], start=True, stop=True), r=[BP, W], w=QK[3])
        P.op("dve", lambda e: e.tensor_copy(out=U.ap, in_=Q[3][0:64, :]), r=QK[3], w=[U])
        for jj in range(NJ):
            P.op("pe", lambda e, jj=jj: e.matmul(Q[3][0:64, jj * 128:(jj + 1) * 128], lhsT=R.ap[:, jj, :], rhs=H.ap[:, jj, :], start=True, stop=False, skip_group_check=True), r=[R, H], w=QK[3])
            for par in range(2):
                h = 2 * jj + par
                P.op("pe", lambda e, h=h: e.matmul(Q[3][0:64, h * 64:(h + 1) * 64], lhsT=ArbT.ap[:, h, :], rhs=U.ap[:, h * 64:(h + 1) * 64], start=False, stop=False, skip_group_check=True), r=[ArbT, U], w=QK[3])
                P.op("pe", lambda e, h=h: e.matmul(Q[3][0:64, h * 64:(h + 1) * 64], lhsT=ArkT.ap[:, h, :], rhs=V.ap[:, h * 64:(h + 1) * 64], start=False, stop=True, skip_group_check=True), r=[ArkT, V], w=QK[3])
        P.op("act", lambda e: e.activation(out=W.ap, in_=Q[3][0:64, :], func=AF.Copy), r=QK[3], w=[W])
        P.dma("pool", TM["Y" + d_][c * C:(c + 1) * C, :], W.ap, r=[W])
        for jj in range(NJ):
            P.op("pe", lambda e, jj=jj: e.matmul(Q[2][:, jj * 128:(jj + 1) * 128], lhsT=BH.ap[:, jj * 128:(jj + 1) * 128], rhs=U.ap[:, jj * 128:(jj + 1) * 128], start=True, stop=False), r=[BH, U], w=QK[2])
            P.op("pe", lambda e, jj=jj: e.matmul(Q[2][:, jj * 128:(jj + 1) * 128], lhsT=KH.ap[:, jj * 128:(jj + 1) * 128], rhs=V.ap[:, jj * 128:(jj + 1) * 128], start=False, stop=True), r=[KH, V], w=QK[2])
        q2v = Q[2].rearrange("p (j c) -> p j c", j=NJ)
        for (p0, c0_) in ((0, 0), (64, 64)):
            hb = H.ap[p0:p0 + 64, :, c0_:c0_ + 64]
            P.op("pool", lambda e, hb=hb, p0=p0: e.tensor_tensor(out=hb, in0=hb, in1=gc.ap[p0:p0 + 64, :, c:c + 1].to_broadcast([64, NJ, 64]), op=ALU.mult), r=[H, gc], w=[H])
            P.op("dve", lambda e, hb=hb, p0=p0, c0_=c0_: e.tensor_tensor(out=hb, in0=hb, in1=q2v[p0:p0 + 64, :, c0_:c0_ + 64], op=ALU.add), r=[H] + QK[2], w=[H])

    load("f", 0, 0)
    load("b", NCK - 1, 0)
    for i in range(NCK):
        q = i % 2
        scan_f = lambda: scan("f", i, q)
        scan("f", i, q)
        if i + 1 < NCK:
            load("f", i + 1, 1 - q)
        scan("b", NCK - 1 - i, q)
        if i + 1 < NCK:
            load("b", NCK - 2 - i, 1 - q)


def phase3(nc, P, ar, env):
    NT = env["NT"]
    TM, GATE, P2, xpad, out_d, rows_d = env["TM"], env["GATE"], env["P2"], env["xpad"], env["out_d"], env["rows_d"]
    colt, ident, col, bk = env["colt"], env["ident"], env["col"], env["bk"]
    w_pb_d, w_rb_d, w_out_d, w_ff1_d, w_ff2_d = env["w_pb_d"], env["w_rb_d"], env["w_out_d"], env["w_ff1_d"], env["w_ff2_d"]
    rowt = [ar.alloc("row%d" % i, [128, D]) for i in range(3)]
    for i in range(3):
        P.dma("sp", rowt[i].ap, rows_d[i:i + 1, :].partition_broadcast(128), w=[rowt[i]])
    lnw, lnb, gfin = rowt
    L = [ar.alloc("L%d" % i, [128, D]) for i in range(4)]
    W1 = ar.alloc("W1", [128, D])
    W2 = ar.alloc("W2", [128, D])
    rw = [ar.alloc("rw%d" % i, [128, D]) for i in range(4)]
    rwT = ar.alloc("rwT", [128, NJ, TT])
    mT = ar.alloc("mT", [128, NJ, TT])
    gtr = [ar.alloc("gtr%d" % i, [128, TT]) for i in range(2)]
    gtp = [ar.alloc("gtp%d" % i, [128, TT]) for i in range(2)]
    x1 = [ar.alloc("x1%d" % i, [128, D]) for i in range(4)]
    xr = [ar.alloc("xr%d" % i, [128, D]) for i in range(2)]
    h2 = [ar.alloc("h2%d" % i, [128, 256]) for i in range(3)]
    tmpf = [ar.alloc("tmpf%d" % i, [128, 256]) for i in range(2)]
    wrb = [ar.alloc("wrb%d" % i, [128, NJ, 128]) for i in range(2)]
    wpb = [ar.alloc("wpb%d" % i, [128, 4, 128]) for i in range(2)]
    wo = [ar.alloc("wo%d" % i, [128, D]) for i in range(2)]
    wf1 = [ar.alloc("wf1%d" % i, [128, NJ, 128]) for i in range(3)]
    wf2 = [ar.alloc("wf2%d" % i, [128, D]) for i in range(3)]
    stt = ar.alloc("stt", [128, 64])
    print("phase3 arena cols", ar.off)
    h3 = lambda ap: ap.rearrange("p (h c) -> p h c", h=16)
    bc3 = lambda ap: ap.rearrange("p (h o) -> p h o", o=1).to_broadcast([128, 16, 64])

    def do_block(ti, b):
        t0 = ti * TT + b * 128
        for i, nm in enumerate(("Yf", "Yb", "BV", "G")):
            P.dma("sp", L[i].ap, TM[nm][t0:t0 + 128, :], w=[L[i]])
        P.op("dve", lambda e: e.tensor_tensor(out=L[0].ap, in0=L[0].ap, in1=L[1].ap, op=ALU.add), r=[L[0], L[1]], w=[L[0]])
        P.op("dve", lambda e: e.tensor_reduce(out=stt.ap[:, 0:16], in_=h3(L[0].ap), axis=AX.X, op=ALU.add), r=[L[0]], w=[stt])
        P.op("dve", lambda e: e.tensor_scalar(out=stt.ap[:, 16:32], in0=stt.ap[:, 0:16], scalar1=1.0 / 64, scalar2=None, op0=ALU.mult), r=[stt], w=[stt])
        P.op("dve", lambda e: e.tensor_tensor(out=h3(W1.ap), in0=h3(L[0].ap), in1=bc3(stt.ap[:, 16:32]), op=ALU.subtract), r=[L[0], stt], w=[W1])
        P.op("pool", lambda e: e.tensor_tensor(out=W2.ap, in0=W1.ap, in1=W1.ap, op=ALU.mult), r=[W1], w=[W2])
        P.op("dve", lambda e: e.tensor_reduce(out=stt.ap[:, 32:48], in_=h3(W2.ap), axis=AX.X, op=ALU.add), r=[W2], w=[stt])
        P.op("act", lambda e: e.activation(out=stt.ap[:, 48:64], in_=stt.ap[:, 32:48], func=AF.Sqrt, scale=1.0 / 64, bias=gn_eps_ap(env)), r=[stt], w=[stt])
        P.op("dve", lambda e: e.reciprocal(out=stt.ap[:, 48:64], in_=stt.ap[:, 48:64]), r=[stt], w=[stt])
        P.op("dve", lambda e: e.tensor_tensor(out=h3(W1.ap), in0=h3(W1.ap), in1=bc3(stt.ap[:, 48:64]), op=ALU.mult), r=[W1, stt], w=[W1])
        P.op("pool", lambda e: e.tensor_tensor(out=W1.ap, in0=W1.ap, in1=lnw.ap, op=ALU.mult), r=[W1, lnw], w=[W1])
        P.op("pool", lambda e: e.tensor_tensor(out=W1.ap, in0=W1.ap, in1=lnb.ap, op=ALU.add), r=[W1, lnb], w=[W1])
        P.op("dve", lambda e: e.tensor_tensor(out=W1.ap, in0=W1.ap, in1=L[2].ap, op=ALU.add), r=[W1, L[2]], w=[W1])
        P.op("dve", lambda e: e.tensor_tensor(out=rw[b].ap, in0=W1.ap, in1=L[3].ap, op=ALU.mult), r=[W1, L[3]], w=[rw[b]])

    def rstd_of(src, colidx, junk):
        P.op("act", lambda e: e.activation(out=junk.ap, in_=src.ap, func=AF.Square, accum_out=stt.ap[:, colidx:colidx + 1]), r=[src], w=[junk, stt])
        P.op("act", lambda e: e.activation(out=stt.ap[:, colidx:colidx + 1], in_=stt.ap[:, colidx:colidx + 1], func=AF.Sqrt, scale=1.0 / D, bias=RMS_EPS), r=[stt], w=[stt])
        P.op("dve", lambda e: e.reciprocal(out=stt.ap[:, colidx:colidx + 1], in_=stt.ap[:, colidx:colidx + 1]), r=[stt], w=[stt])

    def do_tile(ti):
        t0 = ti * TT
        for b in range(4):
            do_block(ti, b)
        for j in range(NJ):
            bap, bkey = bk(j % 2)
            for b in range(4):
                P.op("pe", lambda e, b=b, j=j, bap=bap: e.transpose(out=bap[:, b * 128:(b + 1) * 128], in_=rw[b].ap[:, j * 128:(j + 1) * 128], identity=ident.ap), r=[rw[b], ident], w=[bkey])
            P.op("act", lambda e, j=j, bap=bap: e.activation(out=rwT.ap[:, j, :], in_=bap, func=AF.Copy), r=[bkey], w=[rwT])
        for g in range(4):
            P.dma("sp", L[g].ap[:, 0:TT], P2[g, :, t0:t0 + TT], w=[L[g]])
        for ec in range(NJ):
            wr, wp, gr, gp = wrb[ec % 2], wpb[ec % 2], gtr[ec % 2], gtp[ec % 2]
            P.dma("sp", wr.ap, w_rb_d[:, :, ec * 128:(ec + 1) * 128], w=[wr])
            P.dma("sp", wp.ap, w_pb_d[:, :, ec * 128:(ec + 1) * 128], w=[wp])
            P.dma("sp", gr.ap, GATE[8 + ec, :, t0:t0 + TT], w=[gr])
            P.dma("sp", gp.ap, GATE[ec, :, t0:t0 + TT], w=[gp])
            bA, kA = bk(ec % 2)
            bB, kB = bk(2 + ec % 2)
            for j in range(NJ):
                P.op("pe", lambda e, j=j, wr=wr, bA=bA: e.matmul(bA, lhsT=wr.ap[:, j, :], rhs=rwT.ap[:, j, :], start=(j == 0), stop=(j == NJ - 1)), r=[wr, rwT], w=[kA])
            for g in range(4):
                P.op("pe", lambda e, g=g, wp=wp, bB=bB: e.matmul(bB, lhsT=wp.ap[:, g, :], rhs=L[g].ap[:, 0:TT], start=(g == 0), stop=(g == 3)), r=[wp, L[g]], w=[kB])
            P.op("dve", lambda e, ec=ec, bA=bA, gr=gr: e.tensor_tensor(out=mT.ap[:, ec, :], in0=bA, in1=gr.ap, op=ALU.mult), r=[kA, gr], w=[(mT, ec)])
            P.op("dve", lambda e, bB=bB, gp=gp: e.tensor_tensor(out=W2.ap[:, 0:TT], in0=bB, in1=gp.ap, op=ALU.mult), r=[kB, gp], w=[W2])
            P.op("pool", lambda e, ec=ec: e.tensor_tensor(out=mT.ap[:, ec, :], in0=mT.ap[:, ec, :], in1=W2.ap[:, 0:TT], op=ALU.add), r=[(mT, ec), W2], w=[(mT, ec)])
        for ec in range(NJ):
            wt = wo[ec % 2]
            P.dma("sp", wt.ap, w_out_d[ec], w=[wt])
            for b in range(4):
                for hh in range(2):
                    bap, bkey = bk(b * 2 + hh)
                    P.op("pe", lambda e, ec=ec, b=b, hh=hh, wt=wt, bap=bap: e.matmul(bap[:, :], lhsT=mT.ap[:, ec, b * 128:(b + 1) * 128], rhs=wt.ap[:, hh * 512:(hh + 1) * 512], start=(ec == 0), stop=(ec == NJ - 1)), r=[wt, (mT, ec)], w=[bkey])
        for b in range(4):
            xt_ = xr[b % 2]
            P.dma("sp", xt_.ap, xpad[8 + t0 + b * 128:8 + t0 + (b + 1) * 128, :], w=[xt_])
            for hh in range(2):
                bap, bkey = bk(b * 2 + hh)
                P.op("dve", lambda e, b=b, hh=hh, bap=bap, xt_=xt_: e.tensor_tensor(out=x1[b].ap[:, hh * 512:(hh + 1) * 512], in0=bap, in1=xt_.ap[:, hh * 512:(hh + 1) * 512], op=ALU.add), r=[bkey, xt_], w=[x1[b]])
        hsb = [W1, W2]
        for sb in range(2):
            for bl in range(2):
                b = 2 * sb + bl
                rstd_of(x1[b], bl, L[0])
                P.op("dve", lambda e, b=b, bl=bl: e.tensor_scalar(out=hsb[bl].ap, in0=x1[b].ap, scalar1=stt.ap[:, bl:bl + 1], scalar2=None, op0=ALU.mult), r=[x1[b], stt], w=[hsb[bl]])
            for j in range(NJ):
                bap, bkey = bk(4 + j % 2)
                for bl in range(2):
                    P.op("pe", lambda e, j=j, bl=bl, bap=bap: e.transpose(out=bap[:, bl * 128:(bl + 1) * 128], in_=hsb[bl].ap[:, j * 128:(j + 1) * 128], identity=ident.ap), r=[hsb[bl], ident], w=[bkey])
                P.op("act", lambda e, j=j, bap=bap: e.activation(out=rwT.ap[:, j, 0:256], in_=bap[:, 0:256], func=AF.Copy, scale=col("g_ffn", j)), r=[bkey, colt], w=[rwT])
            for fc in range(32):
                w1_, w2_ = wf1[fc % 3], wf2[fc % 3]
                P.dma("sp", w1_.ap, w_ff1_d[fc], w=[w1_])
                P.dma("sp", w2_.ap, w_ff2_d[fc], w=[w2_])
                bap, bkey = bk(4 + fc % 2)
                for j in range(NJ):
                    P.op("pe", lambda e, j=j, w1_=w1_, bap=bap: e.matmul(bap[:, 0:256], lhsT=w1_.ap[:, j, :], rhs=rwT.ap[:, j, 0:256], start=(j == 0), stop=(j == NJ - 1)), r=[w1_, rwT], w=[bkey])
                tf, hh_ = tmpf[fc % 2], h2[fc % 3]
                P.op("act", lambda e, bap=bap, tf=tf: e.activation(out=tf.ap, in_=bap[:, 0:256], func=AF.Copy), r=[bkey], w=[tf])
                P.op("dve", lambda e, tf=tf, hh_=hh_: e.scalar_tensor_tensor(out=hh_.ap, in0=tf.ap, scalar=0.0, in1=tf.ap, op0=ALU.max, op1=ALU.mult), r=[tf], w=[hh_])
                for bl in range(2):
                    for hh in range(2):
                        oap, okey = bk(bl * 2 + hh)
                        P.op("pe", lambda e, fc=fc, bl=bl, hh=hh, hh_=hh_, w2_=w2_, oap=oap: e.matmul(oap, lhsT=hh_.ap[:, bl * 128:(bl + 1) * 128], rhs=w2_.ap[:, hh * 512:(hh + 1) * 512], start=(fc == 0), stop=(fc == 31)), r=[hh_, w2_], w=[okey])
            for bl in range(2):
                b = 2 * sb + bl
                for hh in range(2):
                    oap, okey = bk(bl * 2 + hh)
                    P.op("dve", lambda e, b=b, hh=hh, oap=oap: e.tensor_tensor(out=x1[b].ap[:, hh * 512:(hh + 1) * 512], in0=oap, in1=x1[b].ap[:, hh * 512:(hh + 1) * 512], op=ALU.add), r=[okey, x1[b]], w=[x1[b]])
                rstd_of(x1[b], 2 + bl, L[0])
                P.op("dve", lambda e, b=b, bl=bl: e.scalar_tensor_tensor(out=x1[b].ap, in0=x1[b].ap, scalar=stt.ap[:, 2 + bl:3 + bl], in1=gfin.ap, op0=ALU.mult, op1=ALU.mult), r=[x1[b], stt, gfin], w=[x1[b]])
                P.dma("pool", out_d[t0 + b * 128:t0 + (b + 1) * 128, :], x1[b].ap, r=[x1[b]])

    for ti in range(NT):
        do_tile(ti)


def gn_eps_ap(env):
    return GN_EPS


def _colpack(v, n):
    return np.ascontiguousarray(np.asarray(v, np.float32).reshape(n, 128).T)


def prep_shared(inp):
    f = lambda a: np.asarray(a, np.float32)
    cols = np.zeros((128, NCOLS), np.float32)
    src = {"g_mix": inp["g_mix"][0], "b_gate": inp["b_gate"][0], "mu_prev": inp["mu_prev"][0], "mu_next": inp["mu_next"][0],
           "pool_scale": inp["pool_scale"][0], "k_k": inp["k_k"][0], "k_a": inp["k_a"][0], "r_k": f(inp["r_k"][0]).reshape(-1),
           "w0_f": inp["w0_f"][0], "a0_f": inp["a0_f"][0], "w0_b": inp["w0_b"][0], "a0_b": inp["a0_b"][0], "g_ffn": inp["g_ffn"][0]}
    for n, c in COLSPEC:
        if n in src:
            cols[:, COLOFF[n]:COLOFF[n] + c] = _colpack(src[n], c)
    consts = np.zeros((128, 3328), np.float32)
    s = np.arange(64)[:, None]
    t = np.arange(64)[None, :]
    masks = [(s < t), (s <= t), (s > t), (s >= t)]
    for i, m in enumerate(masks):
        consts[0:64, i * 512:(i + 1) * 512] = np.tile(m.astype(np.float32), (1, 8))
    rm = np.ones(512, np.float32)
    rm[::64] = 0.0
    consts[:, 2048:2560] = rm[None, :]
    consts[:, 2560:2688] = np.eye(128, dtype=np.float32)
    b1 = np.zeros((128, 128), np.float32)
    b1[0:64, 0:64] = 1.0
    b1[64:128, 64:128] = 1.0
    consts[:, 2688:2816] = b1
    consts[0:64, 2816:3328] = np.tile(np.eye(64, dtype=np.float32), (1, 8))
    rows = np.stack([f(inp["ln_w"][0]), f(inp["ln_b"][0]), f(inp["g_final"])], 0)
    lora = np.zeros((128, 2, D), np.float32)
    lora[0:64, 0] = inp["w_up_f"][0]
    lora[64:128, 0] = inp["a_up_f"][0]
    lora[0:64, 1] = inp["w_up_b"][0]
    lora[64:128, 1] = inp["a_up_b"][0]
    sh = {
        "cols": cols, "consts": consts, "rows": rows,
        "w_in": np.ascontiguousarray(f(inp["w_in"][0]).reshape(NJ, 128, NCC, 128).transpose(2, 1, 0, 3)),
        "pool_w": np.ascontiguousarray(f(inp["pool_w"][0]).transpose(1, 0, 2)),
        "lora": lora, "g_up": np.ascontiguousarray(f(inp["g_up"][0])),
        "w_pb": np.ascontiguousarray(f(inp["w_pool_br"][0]).reshape(4, 128, D).transpose(1, 0, 2)),
        "w_rb": np.ascontiguousarray(f(inp["w_rwkv_br"][0]).reshape(NJ, 128, D).transpose(1, 0, 2)),
        "w_out": np.ascontiguousarray(f(inp["w_out"][0]).reshape(NJ, 128, D)),
        "w_ff1": np.ascontiguousarray(f(inp["w_ff1"][0]).reshape(NJ, 128, 32, 128).transpose(2, 1, 0, 3)),
        "w_ff2": np.ascontiguousarray(f(inp["w_ff2"][0]).reshape(32, 128, D)),
    }
    return sh


def prep_core(segs, NS, SL):
    N = NS * SL
    NT = N // TT
    xpad = np.zeros((N + 16, D), np.float32)
    segid = np.full(NS, -1, np.int64)
    rcnt = np.ones((4, N), np.float32)
    cover = np.zeros(NS, bool)
    allsegs = list(segs)
    for s0, ns, arr in segs:
        cover[s0:s0 + ns] = True
    for s in range(NS):
        if not cover[s]:
            allsegs.append((s, 1, None))
    for i, (s0, ns, arr) in enumerate(allsegs):
        segid[s0:s0 + ns] = i
        S = ns * SL
        if arr is not None:
            xpad[8 + s0 * SL: 8 + s0 * SL + S] = arr
        pos = np.arange(S)
        for g, w in enumerate((2, 4, 8, 16)):
            lo = np.maximum(pos - w // 2, 0)
            hi = np.minimum(pos + w // 2 - 1, S - 1)
            rcnt[g, s0 * SL:s0 * SL + S] = 1.0 / (hi - lo + 1).astype(np.float32)
    hm = np.ones((16, NT), np.float32)
    for ti in range(NT):
        t0 = ti * TT
        sl = t0 // SL
        if t0 % SL == 0 and (sl == 0 or segid[sl - 1] != segid[sl]):
            hm[0:8, ti] = 0.0
        t1 = t0 + TT
        sl1 = (t1 - 1) // SL
        if t1 % SL == 0 and (sl1 == NS - 1 or segid[sl1 + 1] != segid[sl1]):
            hm[8:16, ti] = 0.0
    cm = np.zeros((128, 2 * NS), np.float32)
    for s in range(NS):
        if s > 0 and segid[s - 1] == segid[s]:
            cm[:, s] = 1.0
        if s < NS - 1 and segid[s + 1] == segid[s]:
            cm[:, NS + s] = 1.0
    return {"xpad": xpad, "hm": hm, "rcnt": rcnt, "cm": cm}


_NC_CACHE = {}


def kernel(**inputs):
    NS, SL = 8, 2048
    xp = np.asarray(inputs["x_prompt"], np.float32)
    xs = np.asarray(inputs["x_sample"], np.float32)
    sh = prep_shared(inputs)
    plan = []
    plan.append([(0, 8, xs[0])])
    plan.append([(0, 8, xs[1])])
    counts = [6, 6, 5, 5, 5, 5]
    nxt = 0
    owners = []
    for c in counts:
        segs = []
        own = []
        for i in range(c):
            segs.append((i, 1, xp[nxt]))
            own.append(nxt)
            nxt += 1
        plan.append(segs)
        owners.append(own)
    in_maps = []
    for segs in plan:
        m = dict(sh)
        m.update(prep_core(segs, NS, SL))
        in_maps.append(m)
    key = (NS, SL)
    if key not in _NC_CACHE:
        _NC_CACHE[key] = build(NS, SL)
    nc = _NC_CACHE[key]
    res = run_bass_kernel_spmd(nc, in_maps, core_ids=list(range(8)))
    y_prompt = np.empty_like(xp)
    y_sample = np.empty_like(xs)
    for c in range(2):
        y_sample[c] = np.asarray(res.results[c]["out"]).reshape(NS * SL, D)
    for ci, own in enumerate(owners):
        o = np.asarray(res.results[2 + ci]["out"]).reshape(NS, SL, D)
        for i, b in enumerate(own):
            y_prompt[b] = o[i]
    return (y_prompt, y_sample)
```

```python
from contextlib import ExitStack
import numpy as np
import concourse.bass as bass
import concourse.mybir as mybir
from concourse.bass_utils import run_bass_kernel_spmd

F32 = mybir.dt.float32
AF = mybir.ActivationFunctionType
ALU = mybir.AluOpType
AX = mybir.AxisListType

D = 1024
NJ = 8
TT = 512
C = 64
IN_COLS = 5888
NCC = IN_COLS // 128
RMS_EPS = 1e-6
GN_EPS = 64e-5
CENGS = ("pe", "act", "dve", "pool")
ENGS = ("pe", "act", "dve", "pool", "sp")
NSLOT = 8
PHASES = 3
INLINE_WAIT = True
EPOCH = 30000


class T:
    def __init__(self, name, ap):
        self.name = name
        self.ap = ap

    def __getitem__(self, k):
        return self.ap[k]


class Prog:
    def __init__(self):
        self.streams = {e: [] for e in ENGS}
        self.cnt = {e: 0 for e in CENGS}
        self.seen = {e: {} for e in ENGS}
        self.lastw = {}
        self.readers = {}
        self.semkeys = {}
        self.dma_slot = {e: 0 for e in ENGS}
        self.dma_val = {}
        self.nops = 0

    @staticmethod
    def _k(x):
        if isinstance(x, T):
            return x.name
        if isinstance(x, tuple) and isinstance(x[0], T):
            return (x[0].name,) + tuple(x[1:])
        return x

    def _need(self, eng, ev):
        if ev is None:
            return
        k, v = ev
        if k[0] == eng and eng == "pe":
            return
        if self.seen[eng].get(k, 0) >= v:
            return
        self.seen[eng][k] = v
        self.streams[eng].append(("wait", k, v))

    def _deps(self, eng, r, w):
        for key in r:
            self._need(eng, self.lastw.get(key))
        for key in w:
            self._need(eng, self.lastw.get(key))
            for ev in list(self.readers.get(key, {}).items()):
                self._need(eng, ev)

    def _commit(self, ev, r, w):
        for key in r:
            d = self.readers.setdefault(key, {})
            d[ev[0]] = max(d.get(ev[0], 0), ev[1])
        for key in w:
            self.lastw[key] = ev
            self.readers[key] = {}

    def op(self, eng, fn, r=(), w=()):
        r = [self._k(x) for x in r]
        w = [self._k(x) for x in w]
        self._deps(eng, r, w)
        self.cnt[eng] += 1
        n = self.cnt[eng]
        ep = (n - 1) // EPOCH
        k = (eng, ep)
        self.semkeys[k] = 1
        self.streams[eng].append(("op", fn, k, 1))
        self._commit((k, n - ep * EPOCH), r, w)
        self.nops += 1

    def dma(self, q, out, in_, r=(), w=()):
        r = [self._k(x) for x in r]
        w = [self._k(x) for x in w]
        self._deps(q, r, w)
        slot = self.dma_slot[q]
        self.dma_slot[q] = (slot + 1) % NSLOT
        k = ("dma", q, slot)
        self.semkeys[k] = 1
        prev = self.dma_val.get(k, 0)
        if prev:
            self._need(q, (k, prev))
        v = prev + 16
        self.dma_val[k] = v
        self.streams[q].append(("op", lambda e, o=out, i=in_: e.dma_start(out=o, in_=i), k, 16))
        self._commit((k, v), r, w)
        self.nops += 1

    def barrier(self):
        evs = []
        for e in CENGS:
            n = self.cnt[e]
            if n:
                ep = (n - 1) // EPOCH
                evs.append(((e, ep), n - ep * EPOCH))
        for k, v in self.dma_val.items():
            evs.append((k, v))
        for e in ENGS:
            for ev in evs:
                self._need(e, ev)

    def emit(self, nc):
        blockname = {"pe": "tensor", "act": "scalar", "dve": "vector", "pool": "gpsimd", "sp": "sync"}
        with ExitStack() as st:
            sems = {}
            for i, k in enumerate(self.semkeys):
                sems[k] = st.enter_context(nc.semaphore("s%d" % i))
            with nc.Block() as block:
                for eng in ENGS:
                    items = self.streams[eng]

                    def body(e, items=items):
                        pend = []
                        for it in items:
                            if it[0] == "wait":
                                pend.append(it)
                            else:
                                if INLINE_WAIT and pend:
                                    for w_ in pend[:-1]:
                                        e.wait_ge(sems[w_[1]], w_[2])
                                    ins = it[1](e)
                                    ins._wait_ge(sems[pend[-1][1]], pend[-1][2])
                                else:
                                    for w_ in pend:
                                        e.wait_ge(sems[w_[1]], w_[2])
                                    ins = it[1](e)
                                pend = []
                                ins.then_inc(sems[it[2]], it[3])
                        for w_ in pend:
                            e.wait_ge(sems[w_[1]], w_[2])

                    getattr(block, blockname[eng])(body)


class Arena:
    def __init__(self, nc, st, name, ncols):
        self.t = st.enter_context(nc.sbuf_tensor(name, [128, ncols], F32))
        self.ncols = ncols
        self.off = 0
        self.uid = 0
        self.name = name

    def reset(self):
        self.off = 0

    def alloc(self, name, shape):
        n = int(np.prod(shape[1:]))
        assert self.off + n <= self.ncols, ("SBUF arena overflow", name, self.off + n, self.ncols)
        ap = self.t[0:shape[0], self.off:self.off + n]
        self.off += n
        if len(shape) == 3:
            ap = ap.rearrange("p (a b) -> p a b", a=shape[1])
        elif len(shape) == 4:
            ap = ap.rearrange("p (a b c) -> p a b c", a=shape[1], b=shape[2])
        self.uid += 1
        return T("%s.%s.%d" % (self.name, name, self.uid), ap)


def build(NS, SL, dbg=False):
    N = NS * SL
    NT = N // TT
    NCK = N // C
    CPS = SL // C
    nc = bass.Bass("TRN2", target_bir_lowering=False)
    P = Prog()

    def din(name, shape):
        return nc.dram_tensor(name, list(shape), F32, kind="ExternalInput").ap()

    def dscr(name, shape):
        kind = "ExternalOutput" if dbg else "Internal"
        return nc.dram_tensor(name, list(shape), F32, kind=kind).ap()

    xpad = din("xpad", [N + 16, D])
    hm_d = din("hm", [16, NT])
    rcnt_d = din("rcnt", [4, N])
    cm_d = din("cm", [128, 2 * NS])
    cols_d = din("cols", [128, NCOLS])
    consts_d = din("consts", [128, 3328])
    rows_d = din("rows", [3, D])
    w_in_d = din("w_in", [NCC, 128, NJ, 128])
    pool_w_d = din("pool_w", [128, 4, 128])
    lora_d = din("lora", [128, 2, D])
    g_up_d = din("g_up", [128, D])
    w_pb_d = din("w_pb", [128, 4, D])
    w_rb_d = din("w_rb", [128, NJ, D])
    w_out_d = din("w_out", [NJ, 128, D])
    w_ff1_d = din("w_ff1", [32, 128, NJ, 128])
    w_ff2_d = din("w_ff2", [32, 128, D])
    out_d = nc.dram_tensor("out", [N, D], F32, kind="ExternalOutput").ap()

    FM = {}
    for d_ in "fb":
        for nm in ("A", "B", "K", "R"):
            FM[nm + d_] = dscr("FM_%s%s" % (nm, d_), [NCK, 128, NJ * C])
    TMn = ["V", "BV", "BHf", "KHf", "BHb", "KHb", "G", "Yf", "Yb"]
    TM = {nm: dscr("TM_" + nm, [N, D]) for nm in TMn}
    GATE = dscr("GATE", [16, 128, N])
    P2 = dscr("P2", [4, 128, N])

    with ExitStack() as st:
        ar = Arena(nc, st, "ar", 43000)
        cst = Arena(nc, st, "cst", 2 * 512 + 128 + 128 + NCOLS + 2 * NJ * NCK + 2 * NS + NT + 64)
        banks = [T("bank%d" % i, st.enter_context(nc.psum_tensor("bank%d" % i, [128, 1024], F32))) for i in range(4)]

        def bk(i):
            return banks[i // 2].ap[:, (i % 2) * 512:(i % 2 + 1) * 512], ("ps", i)

        colt = cst.alloc("cols", [128, NCOLS])
        ident = cst.alloc("ident", [128, 128])
        blk1 = cst.alloc("blk1", [128, 128])
        rmask = cst.alloc("rmask", [128, 512])
        gcs = {"f": cst.alloc("gcf", [128, NJ, NCK]), "b": cst.alloc("gcb", [128, NJ, NCK])}
        cmt = cst.alloc("cm", [128, 2 * NS])
        hmt = cst.alloc("hm", [16, NT])
        P.dma("sp", colt.ap, cols_d, w=[colt])
        P.dma("sp", ident.ap, consts_d[:, 2560:2688], w=[ident])
        P.dma("sp", blk1.ap, consts_d[:, 2688:2816], w=[blk1])
        P.dma("sp", rmask.ap, consts_d[:, 2048:2560], w=[rmask])
        P.dma("sp", cmt.ap, cm_d, w=[cmt])
        P.dma("sp", hmt.ap, hm_d, w=[hmt])

        def col(name, j):
            o = COLOFF[name] + j
            return colt.ap[:, o:o + 1]

        for i in range(26):
            P.op("dve", lambda e, i=i: e.tensor_tensor(out=col("c0", i), in0=col("mu_prev", i), in1=col("mu_next", i), op=ALU.add), r=[colt], w=[colt])
        P.op("dve", lambda e: e.tensor_scalar(out=colt.ap[:, COLOFF["c0"]:COLOFF["c0"] + 26], in0=colt.ap[:, COLOFF["c0"]:COLOFF["c0"] + 26], scalar1=-1.0, scalar2=1.0, op0=ALU.mult, op1=ALU.add), r=[colt], w=[colt])
        P.op("dve", lambda e: e.tensor_scalar(out=colt.ap[:, COLOFF["omka"]:COLOFF["omka"] + 8], in0=colt.ap[:, COLOFF["k_a"]:COLOFF["k_a"] + 8], scalar1=-1.0, scalar2=1.0, op0=ALU.mult, op1=ALU.add), r=[colt], w=[colt])

        phase1(nc, P, ar, locals())
        P.barrier()
        ar.reset()
        if PHASES >= 2:
            phase2(nc, P, ar, locals())
            P.barrier()
            ar.reset()
        if PHASES >= 3:
            phase3(nc, P, ar, locals())
            P.barrier()
        P.emit(nc)
    return nc


COLSPEC = [("g_mix", 8), ("b_gate", 16), ("mu_prev", 26), ("mu_next", 26), ("pool_scale", 4), ("k_k", 8), ("k_a", 8),
           ("r_k", 8), ("w0_f", 8), ("a0_f", 8), ("w0_b", 8), ("a0_b", 8), ("g_ffn", 8), ("c0", 26), ("omka", 8)]
COLOFF = {}
_o = 0
for _n, _c in COLSPEC:
    COLOFF[_n] = _o
    _o += _c
NCOLS = _o


def phase1(nc, P, ar, env):
    NT, N, NCK = env["NT"], env["N"], env["NCK"]
    xpad, rcnt_d, w_in_d = env["xpad"], env["rcnt_d"], env["w_in_d"]
    FM, TM, GATE, P2 = env["FM"], env["TM"], env["GATE"], env["P2"]
    colt, ident, blk1, rmask, gcs, hmt = env["colt"], env["ident"], env["blk1"], env["rmask"], env["gcs"], env["hmt"]
    col, bk = env["col"], env["bk"]

    poolw = ar.alloc("poolw", [128, 4, 128])
    lora = ar.alloc("lora", [128, 2, D])
    gup = ar.alloc("gup", [128, D])
    P.dma("sp", poolw.ap, env["pool_w_d"], w=[poolw])
    P.dma("sp", lora.ap, env["lora_d"], w=[lora])
    P.dma("sp", gup.ap, env["g_up_d"], w=[gup])

    xt = [ar.alloc("xt%d" % b, [128, D]) for b in range(4)]
    xh = ar.alloc("xh", [16, D])
    ss = ar.alloc("ss", [128, 8])
    xnT = ar.alloc("xnT", [128, NJ, TT])
    xnTh = ar.alloc("xnTh", [128, NJ, 16])
    wring = [ar.alloc("w%d" % i, [128, NJ, 128]) for i in range(3)]
    zext = [ar.alloc("zext%d" % i, [128, 528]) for i in range(2)]
    zr = ar.alloc("zr", [128, NJ, TT])
    zk = ar.alloc("zk", [128, NJ, TT])
    zv = ar.alloc("zv", [128, NJ, TT])
    twza = ar.alloc("twza", [128, TT])
    sg = ar.alloc("sg", [128, TT])
    rc = ar.alloc("rc", [128, TT])
    pt = [ar.alloc("pt%d" % i, [128, 528]) for i in range(2)]
    p2T = ar.alloc("p2T", [128, TT])
    gst = [ar.alloc("gst%d" % i, [128, TT]) for i in range(2)]
    tmst = [ar.alloc("tmst%d" % i, [128, 4, 128]) for i in range(3)]
    gtm = [ar.alloc("gtm%d" % i, [128, D]) for i in range(1)]
    junk = gtm[0]
    ded = [ar.alloc("ded%d" % i, [128, TT]) for i in range(13)]
    xsl = [T((xnT.name, j), xnT.ap[:, j, :]) for j in range(NJ)]
    xth = [T((xt[b].name, h), xt[b].ap[:, h * 512:(h + 1) * 512]) for b in range(4) for h in range(2)]
    xtk = lambda b: [(xt[b], 0), (xt[b], 1)]
    print("phase1 arena cols", ar.off)
    PT = dict(kk=ded[0], kkn=ded[1], ksum=ded[2], sq=ded[3], rn=ded[4], bv=ded[5])
    DT1 = dict(sgm=xsl[0], a_=xsl[1], lw=xsl[2], PI=xsl[3], X1=xsl[4], X2=xsl[5], X3=xsl[6], Ea=xsl[7], Eb=xth[0], kd=xth[1], b_=xth[2])
    outs = [xth[3], xth[4], xth[5], xth[6], xth[7]] + ded[6:13]
    DT2 = [dict(At=outs[0], Bt=outs[1], Kt=outs[2], Rt=outs[3], BH=outs[4], KH=outs[5]),
           dict(At=outs[6], Bt=outs[7], Kt=outs[8], Rt=outs[9], BH=outs[10], KH=outs[11])]
    ptmp = [zext[0], zext[1]]
    tctr = [0]
    mctr = [0]
    wctr = [0]
    dctr = [0]

    def transposes_to_tm(src_fn, src_keys, name, j, t0):
        bi = 4 + (mctr[0] % 2)
        bap, bkey = bk(bi)
        for b in range(4):
            P.op("pe", lambda e, b=b: e.transpose(out=bap[:, b * 128:(b + 1) * 128], in_=src_fn(b), identity=ident.ap), r=list(src_keys) + [ident], w=[bkey])
        stg = tmst[mctr[0] % 3]
        mctr[0] += 1
        P.op("act", lambda e: e.activation(out=stg.ap, in_=bap.rearrange("p (b c) -> p b c", b=4), func=AF.Copy), r=[bkey], w=[stg])
        dst = TM[name][t0:t0 + TT, j * 128:(j + 1) * 128].rearrange("(b t) c -> t b c", b=4)
        P.dma("pool", dst, stg.ap, r=[stg])

    def do_tile(ti):
        t0 = ti * TT
        for b in range(4):
            P.dma("sp", xt[b].ap, xpad[8 + t0 + b * 128: 8 + t0 + (b + 1) * 128, :], w=xtk(b))
        P.dma("sp", xh.ap[0:8, :], xpad[t0:t0 + 8, :], w=[xh])
        P.dma("sp", xh.ap[8:16, :], xpad[t0 + 520:t0 + 528, :], w=[xh])
        for b in range(4):
            P.op("act", lambda e, b=b: e.activation(out=junk.ap, in_=xt[b].ap, func=AF.Square, accum_out=ss.ap[:, b:b + 1]), r=xtk(b), w=[junk, ss])
        P.op("act", lambda e: e.activation(out=junk.ap[0:16, :], in_=xh.ap, func=AF.Square, accum_out=ss.ap[0:16, 4:5]), r=[xh], w=[junk, ss])
        P.op("act", lambda e: e.activation(out=ss.ap[:, 0:5], in_=ss.ap[:, 0:5], func=AF.Sqrt, scale=1.0 / D, bias=col_eps(env)), r=[ss], w=[ss])
        P.op("dve", lambda e: e.reciprocal(out=ss.ap[:, 0:5], in_=ss.ap[:, 0:5]), r=[ss], w=[ss])
        P.op("dve", lambda e, ti=ti: e.tensor_tensor(out=ss.ap[0:16, 4:5], in0=ss.ap[0:16, 4:5], in1=hmt.ap[0:16, ti:ti + 1], op=ALU.mult), r=[ss, hmt], w=[ss])
        for b in range(4):
            P.op("dve", lambda e, b=b: e.tensor_scalar(out=xt[b].ap, in0=xt[b].ap, scalar1=ss.ap[:, b:b + 1], scalar2=None, op0=ALU.mult), r=xtk(b) + [ss], w=xtk(b))
        P.op("dve", lambda e: e.tensor_scalar(out=xh.ap, in0=xh.ap, scalar1=ss.ap[0:16, 4:5], scalar2=None, op0=ALU.mult), r=[xh, ss], w=[xh])
        for j in range(NJ):
            bap, bkey = bk(j % 2)
            for b in range(4):
                P.op("pe", lambda e, b=b, j=j, bap=bap: e.transpose(out=bap[:, b * 128:(b + 1) * 128], in_=xt[b].ap[:, j * 128:(j + 1) * 128], identity=ident.ap), r=xtk(b) + [ident], w=[bkey])
            P.op("act", lambda e, j=j, bap=bap: e.activation(out=xnT.ap[:, j, :], in_=bap, func=AF.Copy, scale=col("g_mix", j)), r=[bkey, colt], w=[(xnT, j)])
            hap, hkey = bk(2 + j % 2)
            P.op("pe", lambda e, j=j, hap=hap: e.transpose(out=hap[:, 0:16], in_=xh.ap[0:16, j * 128:(j + 1) * 128], identity=ident.ap[0:16, 0:16]), r=[xh, ident], w=[hkey])
            P.op("act", lambda e, j=j, hap=hap: e.activation(out=xnTh.ap[:, j, :], in_=hap[:, 0:16], func=AF.Copy, scale=col("g_mix", j)), r=[hkey, colt], w=[xnTh])
        xkeys = [(xnT, j) for j in range(NJ)]

        def zchunk(cc, halo):
            wt = wring[wctr[0] % 3]
            wctr[0] += 1
            P.dma("sp", wt.ap, w_in_d[cc], w=[wt])
            bi = wctr[0] % 2
            bap, bkey = bk(bi)
            for j in range(NJ):
                P.op("pe", lambda e, j=j: e.matmul(bap, lhsT=wt.ap[:, j, :], rhs=xnT.ap[:, j, :], start=(j == 0), stop=(j == NJ - 1)), r=[wt, (xnT, j)], w=[bkey])
            hap = hkey = None
            if halo:
                hap, hkey = bk(2 + bi)
                for j in range(NJ):
                    P.op("pe", lambda e, j=j: e.matmul(hap[:, 0:16], lhsT=wt.ap[:, j, :], rhs=xnTh.ap[:, j, :], start=(j == 0), stop=(j == NJ - 1)), r=[wt, xnTh], w=[hkey])
            return bap, bkey, hap, hkey

        def to_zext(cc):
            bap, bkey, hap, hkey = zchunk(cc, True)
            ze = zext[cc % 2]
            P.op("act", lambda e: e.activation(out=ze.ap[:, 8:520], in_=bap, func=AF.Copy), r=[bkey], w=[ze])
            P.op("dve", lambda e: e.tensor_copy(out=ze.ap[:, 0:8], in_=hap[:, 0:8]), r=[hkey], w=[ze])
            P.op("dve", lambda e: e.tensor_copy(out=ze.ap[:, 520:528], in_=hap[:, 8:16]), r=[hkey], w=[ze])
            return ze

        for g in range(4):
            ze = to_zext(g)
            P.dma("sp", rc.ap, rcnt_d[g:g + 1, t0:t0 + TT].partition_broadcast(128), w=[rc])
            cur, L = ze, 528
            sh = 1
            for lev in range(g + 1):
                nxt = pt[lev % 2]
                L2 = L - sh
                P.op("pool", lambda e, cur=cur, nxt=nxt, L2=L2, sh=sh: e.tensor_tensor(out=nxt.ap[:, 0:L2], in0=cur.ap[:, 0:L2], in1=cur.ap[:, sh:sh + L2], op=ALU.add), r=[cur], w=[nxt])
                cur, L, sh = nxt, L2, sh * 2
            w2 = 1 << g
            pl = ded[g % 2]
            P.op("pool", lambda e, cur=cur, w2=w2, pl=pl: e.tensor_tensor(out=pl.ap, in0=cur.ap[:, 8 - w2:8 - w2 + TT], in1=rc.ap, op=ALU.mult), r=[cur, rc], w=[pl])
            P.op("pool", lambda e, pl=pl, ze=ze: e.tensor_tensor(out=pl.ap, in0=pl.ap, in1=ze.ap[:, 8:520], op=ALU.subtract), r=[pl, ze], w=[pl])
            bap, bkey = bk(4)
            P.op("pe", lambda e, g=g, pl=pl: e.matmul(bap, lhsT=poolw.ap[:, g, :], rhs=pl.ap, start=True, stop=True), r=[poolw, pl], w=[bkey])
            P.op("act", lambda e, g=g: e.activation(out=p2T.ap, in_=bap, func=AF.Copy, scale=col("pool_scale", g)), r=[bkey, colt], w=[p2T])
            P.dma("pool", P2[g, :, t0:t0 + TT], p2T.ap, r=[p2T])

        def mix(ze, i, dst_ap, dkeys):
            t1 = ded[2 + (i % 2)]
            P.op("dve", lambda e: e.tensor_scalar(out=t1.ap, in0=ze.ap[:, 8:520], scalar1=col("c0", i), scalar2=None, op0=ALU.mult), r=[ze, colt], w=[t1])
            P.op("dve", lambda e: e.scalar_tensor_tensor(out=t1.ap, in0=ze.ap[:, 7:519], scalar=col("mu_prev", i), in1=t1.ap, op0=ALU.mult, op1=ALU.add), r=[ze, colt, t1], w=[t1])
            P.op("dve", lambda e: e.scalar_tensor_tensor(out=dst_ap, in0=ze.ap[:, 9:521], scalar=col("mu_next", i), in1=t1.ap, op0=ALU.mult, op1=ALU.add), r=[ze, colt, t1], w=dkeys)

        for i in range(26):
            ze = to_zext(4 + i)
            if i < 8:
                mix(ze, i, zr.ap[:, i, :], [(zr, i)])
            elif i < 16:
                mix(ze, i, zk.ap[:, i - 8, :], [(zk, i - 8)])
            elif i < 24:
                mix(ze, i, zv.ap[:, i - 16, :], [(zv, i - 16)])
            elif i == 24:
                mix(ze, i, twza.ap, [twza])
                P.op("act", lambda e: e.activation(out=twza.ap[0:64, :], in_=twza.ap[0:64, :], func=AF.Tanh), r=[twza], w=[twza])
            else:
                mix(ze, i, sg.ap, [sg])
                P.op("act", lambda e: e.activation(out=sg.ap, in_=sg.ap, func=AF.Sigmoid), r=[sg], w=[sg])

        for i in range(16):
            bap, bkey, _, _ = zchunk(30 + i, False)
            gt = gst[i % 2]
            P.op("act", lambda e, i=i, bap=bap, gt=gt: e.activation(out=gt.ap, in_=bap, func=AF.Sigmoid, bias=col("b_gate", i)), r=[bkey, colt], w=[gt])
            P.dma("pool", GATE[i, :, t0:t0 + TT], gt.ap, r=[gt])

        for b in range(4):
            gt_ = gtm[0]
            for hh in range(2):
                bap, bkey = bk(6 + hh)
                P.op("pe", lambda e, b=b, hh=hh, bap=bap: e.matmul(bap, lhsT=sg.ap[:, b * 128:(b + 1) * 128], rhs=gup.ap[:, hh * 512:(hh + 1) * 512], start=True, stop=True), r=[sg, gup], w=[bkey])
                P.op("act", lambda e, hh=hh, bap=bap, gt_=gt_: e.activation(out=gt_.ap[:, hh * 512:(hh + 1) * 512], in_=bap, func=AF.Copy), r=[bkey], w=[gt_])
            P.dma("pool", TM["G"][t0 + b * 128:t0 + (b + 1) * 128, :], gt_.ap, r=[gt_])

        ck0 = ti * (TT // C)
        for j in range(NJ):
            do_pair(j, t0, ck0)

    def do_pair(j, t0, ck0):
        kk, sq, rn, kkn, ksum, bv = PT["kk"], PT["sq"], PT["rn"], PT["kkn"], PT["ksum"], PT["bv"]
        P.op("dve", lambda e: e.tensor_scalar(out=kk.ap, in0=zk.ap[:, j, :], scalar1=col("k_k", j), scalar2=None, op0=ALU.mult), r=[(zk, j), colt], w=[kk])
        P.op("pool", lambda e: e.tensor_tensor(out=sq.ap, in0=kk.ap, in1=kk.ap, op=ALU.mult), r=[kk], w=[sq])
        bap, bkey = bk(6)
        P.op("pe", lambda e: e.matmul(bap, lhsT=blk1.ap, rhs=sq.ap, start=True, stop=True), r=[blk1, sq], w=[bkey])
        P.op("act", lambda e: e.activation(out=rn.ap, in_=bap, func=AF.Sqrt), r=[bkey], w=[rn])
        P.op("dve", lambda e: e.tensor_scalar(out=rn.ap, in0=rn.ap, scalar1=1e-12, scalar2=None, op0=ALU.max), r=[rn], w=[rn])
        P.op("dve", lambda e: e.reciprocal(out=rn.ap, in_=rn.ap), r=[rn], w=[rn])
        P.op("dve", lambda e: e.tensor_tensor(out=kkn.ap, in0=kk.ap, in1=rn.ap, op=ALU.mult), r=[kk, rn], w=[kkn])
        for di, d_ in enumerate("fb"):
            do_dir(j, t0, ck0, di, d_, kkn, ksum)
        t1 = sq
        P.op("dve", lambda e: e.scalar_tensor_tensor(out=t1.ap, in0=ksum.ap, scalar=col("r_k", j), in1=zr.ap[:, j, :], op0=ALU.mult, op1=ALU.mult), r=[ksum, colt, (zr, j)], w=[t1])
        b7ap, b7key = bk(7)
        P.op("pe", lambda e: e.matmul(b7ap, lhsT=blk1.ap, rhs=t1.ap, start=True, stop=True), r=[blk1, t1], w=[b7key])
        P.op("dve", lambda e: e.tensor_tensor(out=bv.ap, in0=b7ap, in1=zv.ap[:, j, :], op=ALU.mult), r=[b7key, (zv, j)], w=[bv])
        transposes_to_tm(lambda b: bv.ap[:, b * 128:(b + 1) * 128], [bv], "BV", j, t0)
        transposes_to_tm(lambda b: zv.ap[:, j, b * 128:(b + 1) * 128], [(zv, j)], "V", j, t0)

    def do_dir(j, t0, ck0, di, d_, kkn, ksum):
        g_ = DT1
        sgm, a_, lw, PI, X1, X2, X3, Ea, Eb, kd, b_ = (g_[k] for k in ("sgm", "a_", "lw", "PI", "X1", "X2", "X3", "Ea", "Eb", "kd", "b_"))
        o_ = DT2[dctr[0] % 2]
        dctr[0] += 1
        At, Bt, Kt, Rt, BH, KH = (o_[k] for k in ("At", "Bt", "Kt", "Rt", "BH", "KH"))
        b1ap, b1key = bk(7)
        P.op("pe", lambda e: e.matmul(b1ap, lhsT=lora.ap[0:64, di, j * 128:(j + 1) * 128], rhs=twza.ap[0:64, :], start=True, stop=True), r=[lora, twza], w=[b1key])
        P.op("act", lambda e: e.activation(out=sgm.ap, in_=b1ap, func=AF.Sigmoid, bias=col("w0_" + d_, j)), r=[b1key, colt], w=[sgm])
        b2ap, b2key = bk(6)
        P.op("pe", lambda e: e.matmul(b2ap, lhsT=lora.ap[64:128, di, j * 128:(j + 1) * 128], rhs=twza.ap[64:128, :], start=True, stop=True), r=[lora, twza], w=[b2key])
        P.op("act", lambda e: e.activation(out=a_.ap, in_=b2ap, func=AF.Sigmoid, bias=col("a0_" + d_, j)), r=[b2key, colt], w=[a_])
        P.op("pool", lambda e: e.tensor_scalar(out=lw.ap, in0=sgm.ap, scalar1=-0.6065306597126334, scalar2=None, op0=ALU.mult), r=[sgm], w=[lw])
        P.op("dve", lambda e: e.tensor_tensor_scan(out=PI.ap, data0=rmask.ap, data1=lw.ap, initial=0.0, op0=ALU.mult, op1=ALU.add), r=[rmask, lw], w=[PI])
        P.op("pool", lambda e: e.tensor_tensor(out=X1.ap, in0=PI.ap, in1=lw.ap, op=ALU.subtract), r=[PI, lw], w=[X1])
        PI3 = PI.ap.rearrange("p (c t) -> p c t", c=8)
        P.op("dve", lambda e: e.tensor_tensor(out=X2.ap.rearrange("p (c t) -> p c t", c=8), in0=PI3[:, :, 63:64].to_broadcast([128, 8, 64]), in1=PI3, op=ALU.subtract), r=[PI], w=[X2])
        if d_ == "f":
            srcs = [(X1, 1.0), (PI, -1.0), (PI, 1.0), (X2, 1.0)]
        else:
            P.op("pool", lambda e: e.tensor_tensor(out=X3.ap, in0=X2.ap, in1=lw.ap, op=ALU.add), r=[X2, lw], w=[X3])
            srcs = [(X2, 1.0), (X3, -1.0), (X3, 1.0), (X1, 1.0)]
        P.op("dve", lambda e: e.tensor_scalar(out=kd.ap, in0=a_.ap, scalar1=col("k_a", j), scalar2=col("omka", j), op0=ALU.mult, op1=ALU.add), r=[a_, colt], w=[kd])
        P.op("dve", lambda e: e.tensor_tensor(out=kd.ap, in0=kd.ap, in1=zk.ap[:, j, :], op=ALU.mult), r=[kd, (zk, j)], w=[kd])
        P.op("pool", lambda e: e.tensor_tensor(out=b_.ap, in0=kkn.ap, in1=a_.ap, op=ALU.mult), r=[kkn, a_], w=[b_])
        if di == 0:
            P.op("pool", lambda e: e.tensor_copy(out=ksum.ap, in_=kd.ap), r=[kd], w=[ksum])
        else:
            P.op("pool", lambda e: e.tensor_tensor(out=ksum.ap, in0=ksum.ap, in1=kd.ap, op=ALU.add), r=[kd, ksum], w=[ksum])

        def ex(Et, k):
            src_, sc = srcs[k]
            P.op("act", lambda e: e.activation(out=Et.ap, in_=src_.ap, func=AF.Exp, scale=sc), r=[src_], w=[Et])

        ex(Ea, 0)
        P.op("dve", lambda e: e.scalar_tensor_tensor(out=At.ap, in0=kkn.ap, scalar=-1.0, in1=Ea.ap, op0=ALU.mult, op1=ALU.mult), r=[kkn, Ea], w=[At])
        ex(Eb, 1)
        P.op("dve", lambda e: e.tensor_tensor(out=Bt.ap, in0=b_.ap, in1=Eb.ap, op=ALU.mult), r=[b_, Eb], w=[Bt])
        P.op("dve", lambda e: e.tensor_tensor(out=Kt.ap, in0=kd.ap, in1=Eb.ap, op=ALU.mult), r=[kd, Eb], w=[Kt])
        ex(Ea, 2)
        P.op("dve", lambda e: e.tensor_tensor(out=Rt.ap, in0=zr.ap[:, j, :], in1=Ea.ap, op=ALU.mult), r=[(zr, j), Ea], w=[Rt])
        E33 = Ea.ap.rearrange("p (c t) -> p c t", c=8)
        cc_ = 63 if d_ == "f" else 0
        P.op("pool", lambda e: e.tensor_copy(out=gcs[d_].ap[:, j, ck0:ck0 + 8], in_=E33[:, :, cc_]), r=[Ea], w=[gcs[d_]])
        ex(Eb, 3)
        P.op("pool", lambda e: e.tensor_tensor(out=BH.ap, in0=b_.ap, in1=Eb.ap, op=ALU.mult), r=[b_, Eb], w=[BH])
        P.op("pool", lambda e: e.tensor_tensor(out=KH.ap, in0=kd.ap, in1=Eb.ap, op=ALU.mult), r=[kd, Eb], w=[KH])
        for nm, tl in (("A", At), ("B", Bt), ("K", Kt), ("R", Rt)):
            dst = FM[nm + d_][ck0:ck0 + 8].rearrange("c p (j t) -> p c j t", j=NJ)[:, :, j, :]
            P.dma("sp", dst, tl.ap.rearrange("p (c t) -> p c t", c=8), r=[tl])
        transposes_to_tm(lambda b: BH.ap[:, b * 128:(b + 1) * 128], [BH], "BH" + d_, j, t0)
        transposes_to_tm(lambda b: KH.ap[:, b * 128:(b + 1) * 128], [KH], "KH" + d_, j, t0)

    for ti in range(NT):
        do_tile(ti)


def col_eps(env):
    return RMS_EPS


def phase2(nc, P, ar, env):
    NCK, NS, CPS = env["NCK"], env["NS"], env["CPS"]
    FM, TM = env["FM"], env["TM"]
    gcs, cmt, bk, banks, consts_d = env["gcs"], env["cmt"], env["bk"], env["banks"], env["consts_d"]
    mk = ar.alloc("masks", [64, 4, 512])
    P.dma("sp", mk.ap, consts_d[0:64, 0:2048].rearrange("p (m c) -> p m c", m=4), w=[mk])
    irep = ar.alloc("irep", [64, 8, 64])
    P.dma("sp", irep.ap, consts_d[0:64, 2816:3328].rearrange("p (a b) -> p a b", a=8), w=[irep])
    S = {}
    for d_ in "fb":
        s = {}
        s["fm"] = [{nm: ar.alloc("fm" + nm + d_ + str(q), [128, NJ, C]) for nm in "ABKR"} for q in range(2)]
        for nm in ("V", "BH", "KH"):
            s[nm] = ar.alloc(nm + d_, [64, D])
        s["ARbd"] = ar.alloc("ARbd" + d_, [128, NJ, 256])
        s["Bbd"] = ar.alloc("Bbd" + d_, [128, NJ, 128])
        s["A_sb"] = ar.alloc("A_sb" + d_, [64, 16, 64])
        s["BP"] = ar.alloc("BP" + d_, [64, 16, 128])
        s["ArbT"] = ar.alloc("ArbT" + d_, [64, 16, 64])
        s["AakT"] = ar.alloc("AakT" + d_, [64, 16, 64])
        s["ArkT"] = ar.alloc("ArkT" + d_, [64, 16, 64])
        s["W"] = ar.alloc("W" + d_, [64, D])
        s["U"] = ar.alloc("U" + d_, [64, D])
        s["H"] = ar.alloc("H" + d_, [128, NJ, 128])
        for nm in ("ARbd", "Bbd", "H"):
            t_ = s[nm]
            P.op("pool", lambda e, t_=t_: e.memset(t_.ap, 0.0), w=[t_])
        S[d_] = s
    print("phase2 arena cols", ar.off)
    Q = [banks[i].ap for i in range(4)]
    QK = [[("ps", 2 * i), ("ps", 2 * i + 1)] for i in range(4)]
    MIDX = {"f": dict(A=2, B=0, R=1), "b": dict(A=0, B=2, R=3)}

    def load(d_, c, q):
        s = S[d_]
        for nm in "ABKR":
            t_ = s["fm"][q][nm]
            P.dma("sp", t_.ap.rearrange("p j t -> p (j t)"), FM[nm + d_][c], w=[t_])
        for nm, src_ in (("V", "V"), ("BH", "BH" + d_), ("KH", "KH" + d_)):
            P.dma("sp", s[nm].ap, TM[src_][c * C:(c + 1) * C, :], w=[s[nm]])

    def scan(d_, c, q):
        s = S[d_]
        fm = s["fm"][q]
        A, B, K, R = fm["A"], fm["B"], fm["K"], fm["R"]
        V, BH, KH, ARbd, Bbd, A_sb, BP, ArbT, AakT, ArkT, W, U, H = (s[k] for k in ("V", "BH", "KH", "ARbd", "Bbd", "A_sb", "BP", "ArbT", "AakT", "ArkT", "W", "U", "H"))
        mi = MIDX[d_]
        gc = gcs[d_]
        if d_ == "f" and c % CPS == 0 and c > 0:
            idx = c // CPS
            P.op("pool", lambda e: e.tensor_scalar(out=H.ap, in0=H.ap, scalar1=cmt.ap[:, idx:idx + 1], scalar2=None, op0=ALU.mult), r=[H, cmt], w=[H])
        if d_ == "b" and c % CPS == CPS - 1 and c < NCK - 1:
            idx = NS + c // CPS
            P.op("pool", lambda e: e.tensor_scalar(out=H.ap, in0=H.ap, scalar1=cmt.ap[:, idx:idx + 1], scalar2=None, op0=ALU.mult), r=[H, cmt], w=[H])
        for (dst, src_, o0) in ((ARbd, A, 0), (ARbd, R, 128), (Bbd, B, 0)):
            P.op("pool", lambda e, dst=dst, src_=src_, o0=o0: e.tensor_copy(out=dst.ap[0:64, :, o0:o0 + 64], in_=src_.ap[0:64, :, :]), r=[src_], w=[dst])
            P.op("pool", lambda e, dst=dst, src_=src_, o0=o0: e.tensor_copy(out=dst.ap[64:128, :, o0 + 64:o0 + 128], in_=src_.ap[64:128, :, :]), r=[src_], w=[dst])
        for hf in range(2):
            pA, pAk = bk(4)
            for jl in range(4):
                jj = 4 * hf + jl
                P.op("pe", lambda e, jl=jl, jj=jj: e.matmul(pA[0:64, jl * 128:(jl + 1) * 128], lhsT=A.ap[:, jj, :], rhs=Bbd.ap[:, jj, :], start=True, stop=True), r=[A, Bbd], w=[pAk])
                P.op("pe", lambda e, jl=jl, jj=jj: e.matmul(Q[0][0:64, jl * 256:(jl + 1) * 256], lhsT=B.ap[:, jj, :], rhs=ARbd.ap[:, jj, :], start=True, stop=True), r=[B, ARbd], w=QK[0])
                P.op("pe", lambda e, jl=jl, jj=jj: e.matmul(Q[1][0:64, jl * 256:(jl + 1) * 256], lhsT=K.ap[:, jj, :], rhs=ARbd.ap[:, jj, :], start=True, stop=True), r=[K, ARbd], w=QK[1])
            hs = slice(8 * hf, 8 * hf + 8)
            P.op("dve", lambda e, hs=hs: e.tensor_tensor(out=A_sb.ap[:, hs, :], in0=pA[0:64, :].rearrange("p (h s) -> p h s", h=8), in1=mk.ap[:, mi["A"], :].rearrange("p (h s) -> p h s", h=8), op=ALU.mult), r=[pAk, mk], w=[A_sb])
            q0v = Q[0][0:64, :].rearrange("p (j q s) -> p j q s", j=4, q=4)
            q1v = Q[1][0:64, :].rearrange("p (j q s) -> p j q s", j=4, q=4)
            mB = mk.ap[:, mi["B"], :].rearrange("p (j q s) -> p j q s", j=4, q=2)
            mR = mk.ap[:, mi["R"], :].rearrange("p (j q s) -> p j q s", j=4, q=2)
            v4 = lambda ap: ap.rearrange("p (j q) s -> p j q s", j=4)
            P.op("dve", lambda e, hs=hs, q0v=q0v, mB=mB: e.tensor_tensor(out=v4(BP.ap[:, hs, 0:64]), in0=q0v[:, :, 0:2, :], in1=mB, op=ALU.mult), r=QK[0] + [mk], w=[BP])
            P.op("dve", lambda e, hs=hs, q0v=q0v, mR=mR: e.tensor_tensor(out=v4(ArbT.ap[:, hs, :]), in0=q0v[:, :, 2:4, :], in1=mR, op=ALU.mult), r=QK[0] + [mk], w=[ArbT])
            P.op("dve", lambda e, hs=hs, q1v=q1v, mB=mB: e.tensor_tensor(out=v4(AakT.ap[:, hs, :]), in0=q1v[:, :, 0:2, :], in1=mB, op=ALU.mult), r=QK[1] + [mk], w=[AakT])
            P.op("dve", lambda e, hs=hs, q1v=q1v, mR=mR: e.tensor_tensor(out=v4(ArkT.ap[:, hs, :]), in0=q1v[:, :, 2:4, :], in1=mR, op=ALU.mult), r=QK[1] + [mk], w=[ArkT])
            P.op("pool", lambda e, hs=hs: e.tensor_tensor(out=BP.ap[:, hs, 64:128], in0=BP.ap[:, hs, 0:64], in1=irep.ap, op=ALU.add), r=[BP, irep], w=[BP])
        pset = [(Q[0], QK[0], bk(4)), (Q[1], QK[1], bk(5))]
        for st_ in range(6):
            for hf in range(2):
                pX, pXk, (pY, pYk) = pset[hf]
                hs = slice(8 * hf, 8 * hf + 8)
                for hl in range(8):
                    h = 8 * hf + hl
                    if st_ == 0:
                        P.op("pe", lambda e, hl=hl, h=h, pX=pX: e.matmul(pX[0:64, hl * 128:hl * 128 + 64], lhsT=A_sb.ap[:, h, :], rhs=BP.ap[:, h, 0:64], start=True, stop=True), r=[A_sb, BP], w=pXk)
                    elif st_ < 5:
                        P.op("pe", lambda e, hl=hl, h=h, pX=pX: e.matmul(pX[0:64, hl * 128:(hl + 1) * 128], lhsT=A_sb.ap[:, h, :], rhs=BP.ap[:, h, :], start=True, stop=True), r=[A_sb, BP], w=pXk)
                    else:
                        P.op("pe", lambda e, hl=hl, h=h, pX=pX: e.matmul(pX[0:64, hl * 128 + 64:(hl + 1) * 128], lhsT=A_sb.ap[:, h, :], rhs=BP.ap[:, h, 64:128], start=True, stop=True), r=[A_sb, BP], w=pXk)
                    if st_ < 5:
                        P.op("pe", lambda e, hl=hl, h=h, pY=pY: e.matmul(pY[0:64, hl * 64:(hl + 1) * 64], lhsT=BP.ap[:, h, 0:64], rhs=A_sb.ap[:, h, :], start=True, stop=True), r=[A_sb, BP], w=[pYk])
                pXv = pX[0:64, :].rearrange("p (h c) -> p h c", h=8)
                if st_ < 5:
                    P.op("act", lambda e, hs=hs, pXv=pXv: e.activation(out=BP.ap[:, hs, 0:64], in_=pXv[:, :, 0:64], func=AF.Copy), r=pXk, w=[BP])
                    P.op("act", lambda e, hs=hs, pY=pY: e.activation(out=A_sb.ap[:, hs, :], in_=pY[0:64, :].rearrange("p (h s) -> p h s", h=8), func=AF.Copy), r=[pYk], w=[A_sb])
                if st_ > 0:
                    P.op("dve", lambda e, hs=hs, pXv=pXv: e.tensor_tensor(out=BP.ap[:, hs, 64:128], in0=BP.ap[:, hs, 64:128], in1=pXv[:, :, 64:128], op=ALU.add), r=pXk + [BP], w=[BP])
        for jj in range(NJ):
            P.op("pe", lambda e, jj=jj: e.matmul(Q[3][0:64, jj * 128:(jj + 1) * 128], lhsT=A.ap[:, jj, :], rhs=H.ap[:, jj, :], start=True, stop=False, skip_group_check=True), r=[A, H], w=QK[3])
            for par in range(2):
                h = 2 * jj + par
                P.op("pe", lambda e, h=h: e.matmul(Q[3][0:64, h * 64:(h + 1) * 64], lhsT=AakT.ap[:, h, :], rhs=V.ap[:, h * 64:(h + 1) * 64], start=False, stop=True, skip_group_check=True), r=[AakT, V], w=QK[3])
        P.op("act", lambda e: e.activation(out=W.ap, in_=Q[3][0:64, :], func=AF.Copy), r=QK[3], w=[W])
        for h in range(16):
            P.op("pe", lambda e, h=h: e.matmul(Q[3][0:64, h * 64:(h + 1) * 64], lhsT=BP.ap[:, h, 64:128], rhs=W.ap[:, h * 64:(h + 1) * 64], start=True, stop=True), r=[BP, W], w=QK[3])
        P.op("dve", lambda e: e.tensor_copy(out=U.ap, in_=Q[3][0:64, :]), r=QK[3], w=[U])
        for jj in range(NJ):
            P.op("pe", lambda e, jj=jj: e.matmul(Q[3][0:64, jj * 128:(jj + 1) * 128], lhsT=R.ap[:, jj, :], rhs=H.ap[:, jj, :], start=True, stop=False, skip_group_check=True), r=[R, H], w=QK[3])
            for par in range(2):
                h = 2 * jj + par
                P.op("pe", lambda e, h=h: e.matmul(Q[3][0:64, h * 64:(h + 1) * 64], lhsT=ArbT.ap[:, h, :], rhs=U.ap[:, h * 64:(h + 1) * 64], start=False, stop=False, skip_group_check=True), r=[ArbT, U], w=QK[3])
                P.op("pe", lambda e, h=h: e.matmul(Q[3][0:64, h * 64:(h + 1) * 64], lhsT=ArkT.ap[:, h, :], rhs=V.ap[:, h * 64:(h + 1) * 64], start=False, stop=True, skip_group_check=True), r=[ArkT, V], w=QK[3])
        P.op("act", lambda e: e.activation(out=W.ap, in_=Q[3][0:64, :], func=AF.Copy), r=QK[3], w=[W])
        P.dma("pool", TM["Y" + d_][c * C:(c + 1) * C, :], W.ap, r=[W])
        for jj in range(NJ):
            P.op("pe", lambda e, jj=jj: e.matmul(Q[2][:, jj * 128:(jj + 1) * 128], lhsT=BH.ap[:, jj * 128:(jj + 1) * 128], rhs=U.ap[:, jj * 128:(jj + 1) * 128], start=True, stop=False), r=[BH, U], w=QK[2])
            P.op("pe", lambda e, jj=jj: e.matmul(Q[2][:, jj * 128:(jj + 1) * 128], lhsT=KH.ap[:, jj * 128:(jj + 1) * 128], rhs=V.ap[:, jj * 128:(jj + 1) * 128], start=False, stop=True), r=[KH, V], w=QK[2])
        q2v = Q[2].rearrange("p (j c) -> p j c", j=NJ)
        for (p0, c0_) in ((0, 0), (64, 64)):
            hb = H.ap[p0:p0 + 64, :, c0_:c0_ + 64]
            P.op("pool", lambda e, hb=hb, p0=p0: e.tensor_tensor(out=hb, in0=hb, in1=gc.ap[p0:p0 + 64, :, c:c + 1].to_broadcast([64, NJ, 64]), op=ALU.mult), r=[H, gc], w=[H])
            P.op("dve", lambda e, hb=hb, p0=p0, c0_=c0_: e.tensor_tensor(out=hb, in0=hb, in1=q2v[p0:p0 + 64, :, c0_:c0_ + 64], op=ALU.add), r=[H] + QK[2], w=[H])

    load("f", 0, 0)
    load("b", NCK - 1, 0)
    for i in range(NCK):
        q = i % 2
        scan_f = lambda: scan("f", i, q)
        scan("f", i, q)
        if i + 1 < NCK:
            load("f", i + 1, 1 - q)
        scan("b", NCK - 1 - i, q)
        if i + 1 < NCK:
            load("b", NCK - 2 - i, 1 - q)


def phase3(nc, P, ar, env):
    NT = env["NT"]
    TM, GATE, P2, xpad, out_d, rows_d = env["TM"], env["GATE"], env["P2"], env["xpad"], env["out_d"], env["rows_d"]
    colt, ident, col, bk = env["colt"], env["ident"], env["col"], env["bk"]
    w_pb_d, w_rb_d, w_out_d, w_ff1_d, w_ff2_d = env["w_pb_d"], env["w_rb_d"], env["w_out_d"], env["w_ff1_d"], env["w_ff2_d"]
    rowt = [ar.alloc("row%d" % i, [128, D]) for i in range(3)]
    for i in range(3):
        P.dma("sp", rowt[i].ap, rows_d[i:i + 1, :].partition_broadcast(128), w=[rowt[i]])
    lnw, lnb, gfin = rowt
    L = [ar.alloc("L%d" % i, [128, D]) for i in range(4)]
    W1 = ar.alloc("W1", [128, D])
    W2 = ar.alloc("W2", [128, D])
    rw = [ar.alloc("rw%d" % i, [128, D]) for i in range(4)]
    rwT = ar.alloc("rwT", [128, NJ, TT])
    mT = ar.alloc("mT", [128, NJ, TT])
    gtr = [ar.alloc("gtr%d" % i, [128, TT]) for i in range(2)]
    gtp = [ar.alloc("gtp%d" % i, [128, TT]) for i in range(2)]
    x1 = [ar.alloc("x1%d" % i, [128, D]) for i in range(4)]
    xr = [ar.alloc("xr%d" % i, [128, D]) for i in range(2)]
    h2 = [ar.alloc("h2%d" % i, [128, 256]) for i in range(3)]
    tmpf = [ar.alloc("tmpf%d" % i, [128, 256]) for i in range(2)]
    wrb = [ar.alloc("wrb%d" % i, [128, NJ, 128]) for i in range(2)]
    wpb = [ar.alloc("wpb%d" % i, [128, 4, 128]) for i in range(2)]
    wo = [ar.alloc("wo%d" % i, [128, D]) for i in range(2)]
    wf1 = [ar.alloc("wf1%d" % i, [128, NJ, 128]) for i in range(3)]
    wf2 = [ar.alloc("wf2%d" % i, [128, D]) for i in range(3)]
    stt = ar.alloc("stt", [128, 64])
    print("phase3 arena cols", ar.off)
    h3 = lambda ap: ap.rearrange("p (h c) -> p h c", h=16)
    bc3 = lambda ap: ap.rearrange("p (h o) -> p h o", o=1).to_broadcast([128, 16, 64])

    def do_block(ti, b):
        t0 = ti * TT + b * 128
        for i, nm in enumerate(("Yf", "Yb", "BV", "G")):
            P.dma("sp", L[i].ap, TM[nm][t0:t0 + 128, :], w=[L[i]])
        P.op("dve", lambda e: e.tensor_tensor(out=L[0].ap, in0=L[0].ap, in1=L[1].ap, op=ALU.add), r=[L[0], L[1]], w=[L[0]])
        P.op("dve", lambda e: e.tensor_reduce(out=stt.ap[:, 0:16], in_=h3(L[0].ap), axis=AX.X, op=ALU.add), r=[L[0]], w=[stt])
        P.op("dve", lambda e: e.tensor_scalar(out=stt.ap[:, 16:32], in0=stt.ap[:, 0:16], scalar1=1.0 / 64, scalar2=None, op0=ALU.mult), r=[stt], w=[stt])
        P.op("dve", lambda e: e.tensor_tensor(out=h3(W1.ap), in0=h3(L[0].ap), in1=bc3(stt.ap[:, 16:32]), op=ALU.subtract), r=[L[0], stt], w=[W1])
        P.op("pool", lambda e: e.tensor_tensor(out=W2.ap, in0=W1.ap, in1=W1.ap, op=ALU.mult), r=[W1], w=[W2])
        P.op("dve", lambda e: e.tensor_reduce(out=stt.ap[:, 32:48], in_=h3(W2.ap), axis=AX.X, op=ALU.add), r=[W2], w=[stt])
        P.op("act", lambda e: e.activation(out=stt.ap[:, 48:64], in_=stt.ap[:, 32:48], func=AF.Sqrt, scale=1.0 / 64, bias=gn_eps_ap(env)), r=[stt], w=[stt])
        P.op("dve", lambda e: e.reciprocal(out=stt.ap[:, 48:64], in_=stt.ap[:, 48:64]), r=[stt], w=[stt])
        P.op("dve", lambda e: e.tensor_tensor(out=h3(W1.ap), in0=h3(W1.ap), in1=bc3(stt.ap[:, 48:64]), op=ALU.mult), r=[W1, stt], w=[W1])
        P.op("pool", lambda e: e.tensor_tensor(out=W1.ap, in0=W1.ap, in1=lnw.ap, op=ALU.mult), r=[W1, lnw], w=[W1])
        P.op("pool", lambda e: e.tensor_tensor(out=W1.ap, in0=W1.ap, in1=lnb.ap, op=ALU.add), r=[W1, lnb], w=[W1])
        P.op("dve", lambda e: e.tensor_tensor(out=W1.ap, in0=W1.ap, in1=L[2].ap, op=ALU.add), r=[W1, L[2]], w=[W1])
        P.op("dve", lambda e: e.tensor_tensor(out=rw[b].ap, in0=W1.ap, in1=L[3].ap, op=ALU.mult), r=[W1, L[3]], w=[rw[b]])

    def rstd_of(src, colidx, junk):
        P.op("act", lambda e: e.activation(out=junk.ap, in_=src.ap, func=AF.Square, accum_out=stt.ap[:, colidx:colidx + 1]), r=[src], w=[junk, stt])
        P.op("act", lambda e: e.activation(out=stt.ap[:, colidx:colidx + 1], in_=stt.ap[:, colidx:colidx + 1], func=AF.Sqrt, scale=1.0 / D, bias=RMS_EPS), r=[stt], w=[stt])
        P.op("dve", lambda e: e.reciprocal(out=stt.ap[:, colidx:colidx + 1], in_=stt.ap[:, colidx:colidx + 1]), r=[stt], w=[stt])

    def do_tile(ti):
        t0 = ti * TT
        for b in range(4):
            do_block(ti, b)
        for j in range(NJ):
            bap, bkey = bk(j % 2)
            for b in range(4):
                P.op("pe", lambda e, b=b, j=j, bap=bap: e.transpose(out=bap[:, b * 128:(b + 1) * 128], in_=rw[b].ap[:, j * 128:(j + 1) * 128], identity=ident.ap), r=[rw[b], ident], w=[bkey])
            P.op("act", lambda e, j=j, bap=bap: e.activation(out=rwT.ap[:, j, :], in_=bap, func=AF.Copy), r=[bkey], w=[rwT])
        for g in range(4):
            P.dma("sp", L[g].ap[:, 0:TT], P2[g, :, t0:t0 + TT], w=[L[g]])
        for ec in range(NJ):
            wr, wp, gr, gp = wrb[ec % 2], wpb[ec % 2], gtr[ec % 2], gtp[ec % 2]
            P.dma("sp", wr.ap, w_rb_d[:, :, ec * 128:(ec + 1) * 128], w=[wr])
            P.dma("sp", wp.ap, w_pb_d[:, :, ec * 128:(ec + 1) * 128], w=[wp])
            P.dma("sp", gr.ap, GATE[8 + ec, :, t0:t0 + TT], w=[gr])
            P.dma("sp", gp.ap, GATE[ec, :, t0:t0 + TT], w=[gp])
            bA, kA = bk(ec % 2)
            bB, kB = bk(2 + ec % 2)
            for j in range(NJ):
                P.op("pe", lambda e, j=j, wr=wr, bA=bA: e.matmul(bA, lhsT=wr.ap[:, j, :], rhs=rwT.ap[:, j, :], start=(j == 0), stop=(j == NJ - 1)), r=[wr, rwT], w=[kA])
            for g in range(4):
                P.op("pe", lambda e, g=g, wp=wp, bB=bB: e.matmul(bB, lhsT=wp.ap[:, g, :], rhs=L[g].ap[:, 0:TT], start=(g == 0), stop=(g == 3)), r=[wp, L[g]], w=[kB])
            P.op("dve", lambda e, ec=ec, bA=bA, gr=gr: e.tensor_tensor(out=mT.ap[:, ec, :], in0=bA, in1=gr.ap, op=ALU.mult), r=[kA, gr], w=[(mT, ec)])
            P.op("dve", lambda e, bB=bB, gp=gp: e.tensor_tensor(out=W2.ap[:, 0:TT], in0=bB, in1=gp.ap, op=ALU.mult), r=[kB, gp], w=[W2])
            P.op("pool", lambda e, ec=ec: e.tensor_tensor(out=mT.ap[:, ec, :], in0=mT.ap[:, ec, :], in1=W2.ap[:, 0:TT], op=ALU.add), r=[(mT, ec), W2], w=[(mT, ec)])
        for ec in range(NJ):
            wt = wo[ec % 2]
            P.dma("sp", wt.ap, w_out_d[ec], w=[wt])
            for b in range(4):
                for hh in range(2):
                    bap, bkey = bk(b * 2 + hh)
                    P.op("pe", lambda e, ec=ec, b=b, hh=hh, wt=wt, bap=bap: e.matmul(bap[:, :], lhsT=mT.ap[:, ec, b * 128:(b + 1) * 128], rhs=wt.ap[:, hh * 512:(hh + 1) * 512], start=(ec == 0), stop=(ec == NJ - 1)), r=[wt, (mT, ec)], w=[bkey])
        for b in range(4):
            xt_ = xr[b % 2]
            P.dma("sp", xt_.ap, xpad[8 + t0 + b * 128:8 + t0 + (b + 1) * 128, :], w=[xt_])
            for hh in range(2):
                bap, bkey = bk(b * 2 + hh)
                P.op("dve", lambda e, b=b, hh=hh, bap=bap, xt_=xt_: e.tensor_tensor(out=x1[b].ap[:, hh * 512:(hh + 1) * 512], in0=bap, in1=xt_.ap[:, hh * 512:(hh + 1) * 512], op=ALU.add), r=[bkey, xt_], w=[x1[b]])
        hsb = [W1, W2]
        for sb in range(2):
            for bl in range(2):
                b = 2 * sb + bl
                rstd_of(x1[b], bl, L[0])
                P.op("dve", lambda e, b=b, bl=bl: e.tensor_scalar(out=hsb[bl].ap, in0=x1[b].ap, scalar1=stt.ap[:, bl:bl + 1], scalar2=None, op0=ALU.mult), r=[x1[b], stt], w=[hsb[bl]])
            for j in range(NJ):
                bap, bkey = bk(4 + j % 2)
                for bl in range(2):
                    P.op("pe", lambda e, j=j, bl=bl, bap=bap: e.transpose(out=bap[:, bl * 128:(bl + 1) * 128], in_=hsb[bl].ap[:, j * 128:(j + 1) * 128], identity=ident.ap), r=[hsb[bl], ident], w=[bkey])
                P.op("act", lambda e, j=j, bap=bap: e.activation(out=rwT.ap[:, j, 0:256], in_=bap[:, 0:256], func=AF.Copy, scale=col("g_ffn", j)), r=[bkey, colt], w=[rwT])
            for fc in range(32):
                w1_, w2_ = wf1[fc % 3], wf2[fc % 3]
                P.dma("sp", w1_.ap, w_ff1_d[fc], w=[w1_])
                P.dma("sp", w2_.ap, w_ff2_d[fc], w=[w2_])
                bap, bkey = bk(4 + fc % 2)
                for j in range(NJ):
                    P.op("pe", lambda e, j=j, w1_=w1_, bap=bap: e.matmul(bap[:, 0:256], lhsT=w1_.ap[:, j, :], rhs=rwT.ap[:, j, 0:256], start=(j == 0), stop=(j == NJ - 1)), r=[w1_, rwT], w=[bkey])
                tf, hh_ = tmpf[fc % 2], h2[fc % 3]
                P.op("act", lambda e, bap=bap, tf=tf: e.activation(out=tf.ap, in_=bap[:, 0:256], func=AF.Copy), r=[bkey], w=[tf])
                P.op("dve", lambda e, tf=tf, hh_=hh_: e.scalar_tensor_tensor(out=hh_.ap, in0=tf.ap, scalar=0.0, in1=tf.ap, op0=ALU.max, op1=ALU.mult), r=[tf], w=[hh_])
                for bl in range(2):
                    for hh in range(2):
                        oap, okey = bk(bl * 2 + hh)
                        P.op("pe", lambda e, fc=fc, bl=bl, hh=hh, hh_=hh_, w2_=w2_, oap=oap: e.matmul(oap, lhsT=hh_.ap[:, bl * 128:(bl + 1) * 128], rhs=w2_.ap[:, hh * 512:(hh + 1) * 512], start=(fc == 0), stop=(fc == 31)), r=[hh_, w2_], w=[okey])
            for bl in range(2):
                b = 2 * sb + bl
                for hh in range(2):
                    oap, okey = bk(bl * 2 + hh)
                    P.op("dve", lambda e, b=b, hh=hh, oap=oap: e.tensor_tensor(out=x1[b].ap[:, hh * 512:(hh + 1) * 512], in0=oap, in1=x1[b].ap[:, hh * 512:(hh + 1) * 512], op=ALU.add), r=[okey, x1[b]], w=[x1[b]])
                rstd_of(x1[b], 2 + bl, L[0])
                P.op("dve", lambda e, b=b, bl=bl: e.scalar_tensor_tensor(out=x1[b].ap, in0=x1[b].ap, scalar=stt.ap[:, 2 + bl:3 + bl], in1=gfin.ap, op0=ALU.mult, op1=ALU.mult), r=[x1[b], stt, gfin], w=[x1[b]])
                P.dma("pool", out_d[t0 + b * 128:t0 + (b + 1) * 128, :], x1[b].ap, r=[x1[b]])

    for ti in range(NT):
        do_tile(ti)


def gn_eps_ap(env):
    return GN_EPS


def _colpack(v, n):
    return np.ascontiguousarray(np.asarray(v, np.float32).reshape(n, 128).T)


def prep_shared(inp):
    f = lambda a: np.asarray(a, np.float32)
    cols = np.zeros((128, NCOLS), np.float32)
    src = {"g_mix": inp["g_mix"][0], "b_gate": inp["b_gate"][0], "mu_prev": inp["mu_prev"][0], "mu_next": inp["mu_next"][0],
           "pool_scale": inp["pool_scale"][0], "k_k": inp["k_k"][0], "k_a": inp["k_a"][0], "r_k": f(inp["r_k"][0]).reshape(-1),
           "w0_f": inp["w0_f"][0], "a0_f": inp["a0_f"][0], "w0_b": inp["w0_b"][0], "a0_b": inp["a0_b"][0], "g_ffn": inp["g_ffn"][0]}
    for n, c in COLSPEC:
        if n in src:
            cols[:, COLOFF[n]:COLOFF[n] + c] = _colpack(src[n], c)
    consts = np.zeros((128, 3328), np.float32)
    s = np.arange(64)[:, None]
    t = np.arange(64)[None, :]
    masks = [(s < t), (s <= t), (s > t), (s >= t)]
    for i, m in enumerate(masks):
        consts[0:64, i * 512:(i + 1) * 512] = np.tile(m.astype(np.float32), (1, 8))
    rm = np.ones(512, np.float32)
    rm[::64] = 0.0
    consts[:, 2048:2560] = rm[None, :]
    consts[:, 2560:2688] = np.eye(128, dtype=np.float32)
    b1 = np.zeros((128, 128), np.float32)
    b1[0:64, 0:64] = 1.0
    b1[64:128, 64:128] = 1.0
    consts[:, 2688:2816] = b1
    consts[0:64, 2816:3328] = np.tile(np.eye(64, dtype=np.float32), (1, 8))
    rows = np.stack([f(inp["ln_w"][0]), f(inp["ln_b"][0]), f(inp["g_final"])], 0)
    lora = np.zeros((128, 2, D), np.float32)
    lora[0:64, 0] = inp["w_up_f"][0]
    lora[64:128, 0] = inp["a_up_f"][0]
    lora[0:64, 1] = inp["w_up_b"][0]
    lora[64:128, 1] = inp["a_up_b"][0]
    sh = {
        "cols": cols, "consts": consts, "rows": rows,
        "w_in": np.ascontiguousarray(f(inp["w_in"][0]).reshape(NJ, 128, NCC, 128).transpose(2, 1, 0, 3)),
        "pool_w": np.ascontiguousarray(f(inp["pool_w"][0]).transpose(1, 0, 2)),
        "lora": lora, "g_up": np.ascontiguousarray(f(inp["g_up"][0])),
        "w_pb": np.ascontiguousarray(f(inp["w_pool_br"][0]).reshape(4, 128, D).transpose(1, 0, 2)),
        "w_rb": np.ascontiguousarray(f(inp["w_rwkv_br"][0]).reshape(NJ, 128, D).transpose(1, 0, 2)),
        "w_out": np.ascontiguousarray(f(inp["w_out"][0]).reshape(NJ, 128, D)),
        "w_ff1": np.ascontiguousarray(f(inp["w_ff1"][0]).reshape(NJ, 128, 32, 128).transpose(2, 1, 0, 3)),
        "w_ff2": np.ascontiguousarray(f(inp["w_ff2"][0]).reshape(32, 128, D)),
    }
    return sh


def prep_core(segs, NS, SL):
    N = NS * SL
    NT = N // TT
    xpad = np.zeros((N + 16, D), np.float32)
    segid = np.full(NS, -1, np.int64)
    rcnt = np.ones((4, N), np.float32)
    cover = np.zeros(NS, bool)
    allsegs = list(segs)
    for s0, ns, arr in segs:
        cover[s0:s0 + ns] = True
    for s in range(NS):
        if not cover[s]:
            allsegs.append((s, 1, None))
    for i, (s0, ns, arr) in enumerate(allsegs):
        segid[s0:s0 + ns] = i
        S = ns * SL
        if arr is not None:
            xpad[8 + s0 * SL: 8 + s0 * SL + S] = arr
        pos = np.arange(S)
        for g, w in enumerate((2, 4, 8, 16)):
            lo = np.maximum(pos - w // 2, 0)
            hi = np.minimum(pos + w // 2 - 1, S - 1)
            rcnt[g, s0 * SL:s0 * SL + S] = 1.0 / (hi - lo + 1).astype(np.float32)
    hm = np.ones((16, NT), np.float32)
    for ti in range(NT):
        t0 = ti * TT
        sl = t0 // SL
        if t0 % SL == 0 and (sl == 0 or segid[sl - 1] != segid[sl]):
            hm[0:8, ti] = 0.0
        t1 = t0 + TT
        sl1 = (t1 - 1) // SL
        if t1 % SL == 0 and (sl1 == NS - 1 or segid[sl1 + 1] != segid[sl1]):
            hm[8:16, ti] = 0.0
    cm = np.zeros((128, 2 * NS), np.float32)
    for s in range(NS):
        if s > 0 and segid[s - 1] == segid[s]:
            cm[:, s] = 1.0
        if s < NS - 1 and segid[s + 1] == segid[s]:
            cm[:, NS + s] = 1.0
    return {"xpad": xpad, "hm": hm, "rcnt": rcnt, "cm": cm}


_NC_CACHE = {}


def kernel(**inputs):
    NS, SL = 8, 2048
    xp = np.asarray(inputs["x_prompt"], np.float32)
    xs = np.asarray(inputs["x_sample"], np.float32)
    sh = prep_shared(inputs)
    plan = []
    plan.append([(0, 8, xs[0])])
    plan.append([(0, 8, xs[1])])
    counts = [6, 6, 5, 5, 5, 5]
    nxt = 0
    owners = []
    for c in counts:
        segs = []
        own = []
        for i in range(c):
            segs.append((i, 1, xp[nxt]))
            own.append(nxt)
            nxt += 1
        plan.append(segs)
        owners.append(own)
    in_maps = []
    for segs in plan:
        m = dict(sh)
        m.update(prep_core(segs, NS, SL))
        in_maps.append(m)
    key = (NS, SL)
    if key not in _NC_CACHE:
        _NC_CACHE[key] = build(NS, SL)
    nc = _NC_CACHE[key]
    res = run_bass_kernel_spmd(nc, in_maps, core_ids=list(range(8)))
    y_prompt = np.empty_like(xp)
    y_sample = np.empty_like(xs)
    for c in range(2):
        y_sample[c] = np.asarray(res.results[c]["out"]).reshape(NS * SL, D)
    for ci, own in enumerate(owners):
        o = np.asarray(res.results[2 + ci]["out"]).reshape(NS, SL, D)
        for i, b in enumerate(own):
            y_prompt[b] = o[i]
    return (y_prompt, y_sample)
```

```python
from contextlib import ExitStack
import numpy as np
import concourse.bass as bass
import concourse.mybir as mybir
from concourse.bass_utils import run_bass_kernel_spmd

F32 = mybir.dt.float32
AF = mybir.ActivationFunctionType
ALU = mybir.AluOpType
AX = mybir.AxisListType

D = 1024
NJ = 8
TT = 512
C = 64
IN_COLS = 5888
NCC = IN_COLS // 128
RMS_EPS = 1e-6
GN_EPS = 64e-5
CENGS = ("pe", "act", "dve", "pool")
ENGS = ("pe", "act", "dve", "pool", "sp")
NSLOT = 8
PHASES = 3
INLINE_WAIT = True
EPOCH = 30000


class T:
    def __init__(self, name, ap):
        self.name = name
        self.ap = ap

    def __getitem__(self, k):
        return self.ap[k]


class Prog:
    def __init__(self):
        self.streams = {e: [] for e in ENGS}
        self.cnt = {e: 0 for e in CENGS}
        self.seen = {e: {} for e in ENGS}
        self.lastw = {}
        self.readers = {}
        self.semkeys = {}
        self.dma_slot = {e: 0 for e in ENGS}
        self.dma_val = {}
        self.nops = 0

    @staticmethod
    def _k(x):
        if isinstance(x, T):
            return x.name
        if isinstance(x, tuple) and isinstance(x[0], T):
            return (x[0].name,) + tuple(x[1:])
        return x

    def _need(self, eng, ev):
        if ev is None:
            return
        k, v = ev
        if k[0] == eng and eng == "pe":
            return
        if self.seen[eng].get(k, 0) >= v:
            return
        self.seen[eng][k] = v
        self.streams[eng].append(("wait", k, v))

    def _deps(self, eng, r, w):
        for key in r:
            self._need(eng, self.lastw.get(key))
        for key in w:
            self._need(eng, self.lastw.get(key))
            for ev in list(self.readers.get(key, {}).items()):
                self._need(eng, ev)

    def _commit(self, ev, r, w):
        for key in r:
            d = self.readers.setdefault(key, {})
            d[ev[0]] = max(d.get(ev[0], 0), ev[1])
        for key in w:
            self.lastw[key] = ev
            self.readers[key] = {}

    def op(self, eng, fn, r=(), w=()):
        r = [self._k(x) for x in r]
        w = [self._k(x) for x in w]
        self._deps(eng, r, w)
        self.cnt[eng] += 1
        n = self.cnt[eng]
        ep = (n - 1) // EPOCH
        k = (eng, ep)
        self.semkeys[k] = 1
        self.streams[eng].append(("op", fn, k, 1))
        self._commit((k, n - ep * EPOCH), r, w)
        self.nops += 1

    def dma(self, q, out, in_, r=(), w=()):
        r = [self._k(x) for x in r]
        w = [self._k(x) for x in w]
        self._deps(q, r, w)
        slot = self.dma_slot[q]
        self.dma_slot[q] = (slot + 1) % NSLOT
        k = ("dma", q, slot)
        self.semkeys[k] = 1
        prev = self.dma_val.get(k, 0)
        if prev:
            self._need(q, (k, prev))
        v = prev + 16
        self.dma_val[k] = v
        self.streams[q].append(("op", lambda e, o=out, i=in_: e.dma_start(out=o, in_=i), k, 16))
        self._commit((k, v), r, w)
        self.nops += 1

    def barrier(self):
        evs = []
        for e in CENGS:
            n = self.cnt[e]
            if n:
                ep = (n - 1) // EPOCH
                evs.append(((e, ep), n - ep * EPOCH))
        for k, v in self.dma_val.items():
            evs.append((k, v))
        for e in ENGS:
            for ev in evs:
                self._need(e, ev)

    def emit(self, nc):
        blockname = {"pe": "tensor", "act": "scalar", "dve": "vector", "pool": "gpsimd", "sp": "sync"}
        waited = {}
        for eng in ENGS:
            for it in self.streams[eng]:
                if it[0] == "wait" and it[1][0] != "dma":
                    waited.setdefault(it[1], set()).add(it[2])
        rank = {k: {v: i + 1 for i, v in enumerate(sorted(vs))} for k, vs in waited.items()}
        with ExitStack() as st:
            sems = {}
            for i, k in enumerate(self.semkeys):
                sems[k] = st.enter_context(nc.semaphore("s%d" % i))

            def do_wait(e, w_, ins=None):
                k, v = w_[1], w_[2]
                if k[0] != "dma":
                    v = rank[k][v]
                if ins is None:
                    e.wait_ge(sems[k], v)
                else:
                    ins._wait_ge(sems[k], v)

            with nc.Block() as block:
                for eng in ENGS:
                    items = self.streams[eng]

                    def body(e, items=items):
                        pend = []
                        cnt = {}
                        for it in items:
                            if it[0] == "wait":
                                pend.append(it)
                            else:
                                if INLINE_WAIT and pend:
                                    for w_ in pend[:-1]:
                                        do_wait(e, w_)
                                    ins = it[1](e)
                                    do_wait(e, pend[-1], ins)
                                else:
                                    for w_ in pend:
                                        do_wait(e, w_)
                                    ins = it[1](e)
                                pend = []
                                k = it[2]
                                if k[0] == "dma":
                                    ins.then_inc(sems[k], it[3])
                                else:
                                    n = cnt.get(k, 0) + 1
                                    cnt[k] = n
                                    if n in waited.get(k, ()):
                                        ins.then_inc(sems[k], 1)
                        for w_ in pend:
                            do_wait(e, w_)

                    getattr(block, blockname[eng])(body)


class Arena:
    def __init__(self, nc, st, name, ncols):
        self.t = st.enter_context(nc.sbuf_tensor(name, [128, ncols], F32))
        self.ncols = ncols
        self.off = 0
        self.uid = 0
        self.name = name

    def reset(self):
        self.off = 0

    def alloc(self, name, shape):
        n = int(np.prod(shape[1:]))
        assert self.off + n <= self.ncols, ("SBUF arena overflow", name, self.off + n, self.ncols)
        ap = self.t[0:shape[0], self.off:self.off + n]
        self.off += n
        if len(shape) == 3:
            ap = ap.rearrange("p (a b) -> p a b", a=shape[1])
        elif len(shape) == 4:
            ap = ap.rearrange("p (a b c) -> p a b c", a=shape[1], b=shape[2])
        self.uid += 1
        return T("%s.%s.%d" % (self.name, name, self.uid), ap)


def build(NS, SL, dbg=False):
    N = NS * SL
    NT = N // TT
    NCK = N // C
    CPS = SL // C
    nc = bass.Bass("TRN2", target_bir_lowering=False)
    P = Prog()

    def din(name, shape):
        return nc.dram_tensor(name, list(shape), F32, kind="ExternalInput").ap()

    def dscr(name, shape):
        kind = "ExternalOutput" if dbg else "Internal"
        return nc.dram_tensor(name, list(shape), F32, kind=kind).ap()

    xpad = din("xpad", [N + 16, D])
    hm_d = din("hm", [16, NT])
    rcnt_d = din("rcnt", [4, N])
    cm_d = din("cm", [128, 2 * NS])
    cols_d = din("cols", [128, NCOLS])
    consts_d = din("consts", [128, 3328])
    rows_d = din("rows", [3, D])
    w_in_d = din("w_in", [NCC, 128, NJ, 128])
    pool_w_d = din("pool_w", [128, 4, 128])
    lora_d = din("lora", [128, 2, D])
    g_up_d = din("g_up", [128, D])
    w_pb_d = din("w_pb", [128, 4, D])
    w_rb_d = din("w_rb", [128, NJ, D])
    w_out_d = din("w_out", [NJ, 128, D])
    w_ff1_d = din("w_ff1", [32, 128, NJ, 128])
    w_ff2_d = din("w_ff2", [32, 128, D])
    out_d = nc.dram_tensor("out", [N, D], F32, kind="ExternalOutput").ap()

    FM = {}
    for d_ in "fb":
        for nm in ("A", "B", "K", "R"):
            FM[nm + d_] = dscr("FM_%s%s" % (nm, d_), [NCK, 128, NJ * C])
    TMn = ["V", "BV", "BHf", "KHf", "BHb", "KHb", "G", "Yf", "Yb"]
    TM = {nm: dscr("TM_" + nm, [N, D]) for nm in TMn}
    GATE = dscr("GATE", [16, 128, N])
    P2 = dscr("P2", [4, 128, N])

    with ExitStack() as st:
        ar = Arena(nc, st, "ar", 43000)
        cst = Arena(nc, st, "cst", 2 * 512 + 128 + 128 + NCOLS + 2 * NJ * NCK + 2 * NS + NT + 64)
        banks = [T("bank%d" % i, st.enter_context(nc.psum_tensor("bank%d" % i, [128, 1024], F32))) for i in range(4)]

        def bk(i):
            return banks[i // 2].ap[:, (i % 2) * 512:(i % 2 + 1) * 512], ("ps", i)

        colt = cst.alloc("cols", [128, NCOLS])
        ident = cst.alloc("ident", [128, 128])
        blk1 = cst.alloc("blk1", [128, 128])
        rmask = cst.alloc("rmask", [128, 512])
        gcs = {"f": cst.alloc("gcf", [128, NJ, NCK]), "b": cst.alloc("gcb", [128, NJ, NCK])}
        cmt = cst.alloc("cm", [128, 2 * NS])
        hmt = cst.alloc("hm", [16, NT])
        P.dma("sp", colt.ap, cols_d, w=[colt])
        P.dma("sp", ident.ap, consts_d[:, 2560:2688], w=[ident])
        P.dma("sp", blk1.ap, consts_d[:, 2688:2816], w=[blk1])
        P.dma("sp", rmask.ap, consts_d[:, 2048:2560], w=[rmask])
        P.dma("sp", cmt.ap, cm_d, w=[cmt])
        P.dma("sp", hmt.ap, hm_d, w=[hmt])

        def col(name, j):
            o = COLOFF[name] + j
            return colt.ap[:, o:o + 1]

        for i in range(26):
            P.op("dve", lambda e, i=i: e.tensor_tensor(out=col("c0", i), in0=col("mu_prev", i), in1=col("mu_next", i), op=ALU.add), r=[colt], w=[colt])
        P.op("dve", lambda e: e.tensor_scalar(out=colt.ap[:, COLOFF["c0"]:COLOFF["c0"] + 26], in0=colt.ap[:, COLOFF["c0"]:COLOFF["c0"] + 26], scalar1=-1.0, scalar2=1.0, op0=ALU.mult, op1=ALU.add), r=[colt], w=[colt])
        P.op("dve", lambda e: e.tensor_scalar(out=colt.ap[:, COLOFF["omka"]:COLOFF["omka"] + 8], in0=colt.ap[:, COLOFF["k_a"]:COLOFF["k_a"] + 8], scalar1=-1.0, scalar2=1.0, op0=ALU.mult, op1=ALU.add), r=[colt], w=[colt])

        phase1(nc, P, ar, locals())
        P.barrier()
        ar.reset()
        if PHASES >= 2:
            phase2(nc, P, ar, locals())
            P.barrier()
            ar.reset()
        if PHASES >= 3:
            phase3(nc, P, ar, locals())
            P.barrier()
        P.emit(nc)
    return nc


COLSPEC = [("g_mix", 8), ("b_gate", 16), ("mu_prev", 26), ("mu_next", 26), ("pool_scale", 4), ("k_k", 8), ("k_a", 8),
           ("r_k", 8), ("w0_f", 8), ("a0_f", 8), ("w0_b", 8), ("a0_b", 8), ("g_ffn", 8), ("c0", 26), ("omka", 8)]
COLOFF = {}
_o = 0
for _n, _c in COLSPEC:
    COLOFF[_n] = _o
    _o += _c
NCOLS = _o


def phase1(nc, P, ar, env):
    NT, N, NCK = env["NT"], env["N"], env["NCK"]
    xpad, rcnt_d, w_in_d = env["xpad"], env["rcnt_d"], env["w_in_d"]
    FM, TM, GATE, P2 = env["FM"], env["TM"], env["GATE"], env["P2"]
    colt, ident, blk1, rmask, gcs, hmt = env["colt"], env["ident"], env["blk1"], env["rmask"], env["gcs"], env["hmt"]
    col, bk = env["col"], env["bk"]

    poolw = ar.alloc("poolw", [128, 4, 128])
    lora = ar.alloc("lora", [128, 2, D])
    gup = ar.alloc("gup", [128, D])
    P.dma("sp", poolw.ap, env["pool_w_d"], w=[poolw])
    P.dma("sp", lora.ap, env["lora_d"], w=[lora])
    P.dma("sp", gup.ap, env["g_up_d"], w=[gup])

    xt = [ar.alloc("xt%d" % b, [128, D]) for b in range(4)]
    xh = ar.alloc("xh", [16, D])
    ss = ar.alloc("ss", [128, 8])
    xnT = ar.alloc("xnT", [128, NJ, TT])
    xnTh = ar.alloc("xnTh", [128, NJ, 16])
    wring = [ar.alloc("w%d" % i, [128, NJ, 128]) for i in range(3)]
    zext = [ar.alloc("zext%d" % i, [128, 528]) for i in range(2)]
    zr = ar.alloc("zr", [128, NJ, TT])
    zk = ar.alloc("zk", [128, NJ, TT])
    zv = ar.alloc("zv", [128, NJ, TT])
    twza = ar.alloc("twza", [128, TT])
    sg = ar.alloc("sg", [128, TT])
    rc = ar.alloc("rc", [128, TT])
    pt = [ar.alloc("pt%d" % i, [128, 528]) for i in range(2)]
    p2T = ar.alloc("p2T", [128, TT])
    gst = [ar.alloc("gst%d" % i, [128, TT]) for i in range(2)]
    tmst = [ar.alloc("tmst%d" % i, [128, 4, 128]) for i in range(3)]
    gtm = [ar.alloc("gtm%d" % i, [128, D]) for i in range(1)]
    junk = gtm[0]
    ded = [ar.alloc("ded%d" % i, [128, TT]) for i in range(13)]
    xsl = [T((xnT.name, j), xnT.ap[:, j, :]) for j in range(NJ)]
    xth = [T((xt[b].name, h), xt[b].ap[:, h * 512:(h + 1) * 512]) for b in range(4) for h in range(2)]
    xtk = lambda b: [(xt[b], 0), (xt[b], 1)]
    print("phase1 arena cols", ar.off)
    PT = dict(kk=ded[0], kkn=ded[1], ksum=ded[2], sq=ded[3], rn=ded[4], bv=ded[5])
    DT1 = dict(sgm=xsl[0], a_=xsl[1], lw=xsl[2], PI=xsl[3], X1=xsl[4], X2=xsl[5], X3=xsl[6], Ea=xsl[7], Eb=xth[0], kd=xth[1], b_=xth[2])
    outs = [xth[3], xth[4], xth[5], xth[6], xth[7]] + ded[6:13]
    DT2 = [dict(At=outs[0], Bt=outs[1], Kt=outs[2], Rt=outs[3], BH=outs[4], KH=outs[5]),
           dict(At=outs[6], Bt=outs[7], Kt=outs[8], Rt=outs[9], BH=outs[10], KH=outs[11])]
    ptmp = [zext[0], zext[1]]
    tctr = [0]
    mctr = [0]
    wctr = [0]
    dctr = [0]

    def transposes_to_tm(src_fn, src_keys, name, j, t0):
        bi = 4 + (mctr[0] % 2)
        bap, bkey = bk(bi)
        for b in range(4):
            P.op("pe", lambda e, b=b: e.transpose(out=bap[:, b * 128:(b + 1) * 128], in_=src_fn(b), identity=ident.ap), r=list(src_keys) + [ident], w=[bkey])
        stg = tmst[mctr[0] % 3]
        mctr[0] += 1
        P.op("act", lambda e: e.activation(out=stg.ap, in_=bap.rearrange("p (b c) -> p b c", b=4), func=AF.Copy), r=[bkey], w=[stg])
        dst = TM[name][t0:t0 + TT, j * 128:(j + 1) * 128].rearrange("(b t) c -> t b c", b=4)
        P.dma("pool", dst, stg.ap, r=[stg])

    def do_tile(ti):
        t0 = ti * TT
        for b in range(4):
            P.dma("sp", xt[b].ap, xpad[8 + t0 + b * 128: 8 + t0 + (b + 1) * 128, :], w=xtk(b))
        P.dma("sp", xh.ap[0:8, :], xpad[t0:t0 + 8, :], w=[xh])
        P.dma("sp", xh.ap[8:16, :], xpad[t0 + 520:t0 + 528, :], w=[xh])
        for b in range(4):
            P.op("act", lambda e, b=b: e.activation(out=junk.ap, in_=xt[b].ap, func=AF.Square, accum_out=ss.ap[:, b:b + 1]), r=xtk(b), w=[junk, ss])
        P.op("act", lambda e: e.activation(out=junk.ap[0:16, :], in_=xh.ap, func=AF.Square, accum_out=ss.ap[0:16, 4:5]), r=[xh], w=[junk, ss])
        P.op("act", lambda e: e.activation(out=ss.ap[:, 0:5], in_=ss.ap[:, 0:5], func=AF.Sqrt, scale=1.0 / D, bias=col_eps(env)), r=[ss], w=[ss])
        P.op("dve", lambda e: e.reciprocal(out=ss.ap[:, 0:5], in_=ss.ap[:, 0:5]), r=[ss], w=[ss])
        P.op("dve", lambda e, ti=ti: e.tensor_tensor(out=ss.ap[0:16, 4:5], in0=ss.ap[0:16, 4:5], in1=hmt.ap[0:16, ti:ti + 1], op=ALU.mult), r=[ss, hmt], w=[ss])
        for b in range(4):
            P.op("dve", lambda e, b=b: e.tensor_scalar(out=xt[b].ap, in0=xt[b].ap, scalar1=ss.ap[:, b:b + 1], scalar2=None, op0=ALU.mult), r=xtk(b) + [ss], w=xtk(b))
        P.op("dve", lambda e: e.tensor_scalar(out=xh.ap, in0=xh.ap, scalar1=ss.ap[0:16, 4:5], scalar2=None, op0=ALU.mult), r=[xh, ss], w=[xh])
        for j in range(NJ):
            bap, bkey = bk(j % 2)
            for b in range(4):
                P.op("pe", lambda e, b=b, j=j, bap=bap: e.transpose(out=bap[:, b * 128:(b + 1) * 128], in_=xt[b].ap[:, j * 128:(j + 1) * 128], identity=ident.ap), r=xtk(b) + [ident], w=[bkey])
            P.op("act", lambda e, j=j, bap=bap: e.activation(out=xnT.ap[:, j, :], in_=bap, func=AF.Copy, scale=col("g_mix", j)), r=[bkey, colt], w=[(xnT, j)])
            hap, hkey = bk(2 + j % 2)
            P.op("pe", lambda e, j=j, hap=hap: e.transpose(out=hap[:, 0:16], in_=xh.ap[0:16, j * 128:(j + 1) * 128], identity=ident.ap[0:16, 0:16]), r=[xh, ident], w=[hkey])
            P.op("act", lambda e, j=j, hap=hap: e.activation(out=xnTh.ap[:, j, :], in_=hap[:, 0:16], func=AF.Copy, scale=col("g_mix", j)), r=[hkey, colt], w=[xnTh])
        xkeys = [(xnT, j) for j in range(NJ)]

        def zchunk(cc, halo):
            wt = wring[wctr[0] % 3]
            wctr[0] += 1
            P.dma("sp", wt.ap, w_in_d[cc], w=[wt])
            bi = wctr[0] % 2
            bap, bkey = bk(bi)
            for j in range(NJ):
                P.op("pe", lambda e, j=j: e.matmul(bap, lhsT=wt.ap[:, j, :], rhs=xnT.ap[:, j, :], start=(j == 0), stop=(j == NJ - 1)), r=[wt, (xnT, j)], w=[bkey])
            hap = hkey = None
            if halo:
                hap, hkey = bk(2 + bi)
                for j in range(NJ):
                    P.op("pe", lambda e, j=j: e.matmul(hap[:, 0:16], lhsT=wt.ap[:, j, :], rhs=xnTh.ap[:, j, :], start=(j == 0), stop=(j == NJ - 1)), r=[wt, xnTh], w=[hkey])
            return bap, bkey, hap, hkey

        def to_zext(cc):
            bap, bkey, hap, hkey = zchunk(cc, True)
            ze = zext[cc % 2]
            P.op("act", lambda e: e.activation(out=ze.ap[:, 8:520], in_=bap, func=AF.Copy), r=[bkey], w=[ze])
            P.op("dve", lambda e: e.tensor_copy(out=ze.ap[:, 0:8], in_=hap[:, 0:8]), r=[hkey], w=[ze])
            P.op("dve", lambda e: e.tensor_copy(out=ze.ap[:, 520:528], in_=hap[:, 8:16]), r=[hkey], w=[ze])
            return ze

        for g in range(4):
            ze = to_zext(g)
            P.dma("sp", rc.ap, rcnt_d[g:g + 1, t0:t0 + TT].partition_broadcast(128), w=[rc])
            cur, L = ze, 528
            sh = 1
            for lev in range(g + 1):
                nxt = pt[lev % 2]
                L2 = L - sh
                P.op("pool", lambda e, cur=cur, nxt=nxt, L2=L2, sh=sh: e.tensor_tensor(out=nxt.ap[:, 0:L2], in0=cur.ap[:, 0:L2], in1=cur.ap[:, sh:sh + L2], op=ALU.add), r=[cur], w=[nxt])
                cur, L, sh = nxt, L2, sh * 2
            w2 = 1 << g
            pl = ded[g % 2]
            P.op("pool", lambda e, cur=cur, w2=w2, pl=pl: e.tensor_tensor(out=pl.ap, in0=cur.ap[:, 8 - w2:8 - w2 + TT], in1=rc.ap, op=ALU.mult), r=[cur, rc], w=[pl])
            P.op("pool", lambda e, pl=pl, ze=ze: e.tensor_tensor(out=pl.ap, in0=pl.ap, in1=ze.ap[:, 8:520], op=ALU.subtract), r=[pl, ze], w=[pl])
            bap, bkey = bk(4)
            P.op("pe", lambda e, g=g, pl=pl: e.matmul(bap, lhsT=poolw.ap[:, g, :], rhs=pl.ap, start=True, stop=True), r=[poolw, pl], w=[bkey])
            P.op("act", lambda e, g=g: e.activation(out=p2T.ap, in_=bap, func=AF.Copy, scale=col("pool_scale", g)), r=[bkey, colt], w=[p2T])
            P.dma("pool", P2[g, :, t0:t0 + TT], p2T.ap, r=[p2T])

        def mix(ze, i, dst_ap, dkeys):
            t1 = ded[2 + (i % 2)]
            P.op("dve", lambda e: e.tensor_scalar(out=t1.ap, in0=ze.ap[:, 8:520], scalar1=col("c0", i), scalar2=None, op0=ALU.mult), r=[ze, colt], w=[t1])
            P.op("dve", lambda e: e.scalar_tensor_tensor(out=t1.ap, in0=ze.ap[:, 7:519], scalar=col("mu_prev", i), in1=t1.ap, op0=ALU.mult, op1=ALU.add), r=[ze, colt, t1], w=[t1])
            P.op("dve", lambda e: e.scalar_tensor_tensor(out=dst_ap, in0=ze.ap[:, 9:521], scalar=col("mu_next", i), in1=t1.ap, op0=ALU.mult, op1=ALU.add), r=[ze, colt, t1], w=dkeys)

        for i in range(26):
            ze = to_zext(4 + i)
            if i < 8:
                mix(ze, i, zr.ap[:, i, :], [(zr, i)])
            elif i < 16:
                mix(ze, i, zk.ap[:, i - 8, :], [(zk, i - 8)])
            elif i < 24:
                mix(ze, i, zv.ap[:, i - 16, :], [(zv, i - 16)])
            elif i == 24:
                mix(ze, i, twza.ap, [twza])
                P.op("act", lambda e: e.activation(out=twza.ap[0:64, :], in_=twza.ap[0:64, :], func=AF.Tanh), r=[twza], w=[twza])
            else:
                mix(ze, i, sg.ap, [sg])
                P.op("act", lambda e: e.activation(out=sg.ap, in_=sg.ap, func=AF.Sigmoid), r=[sg], w=[sg])

        for i in range(16):
            bap, bkey, _, _ = zchunk(30 + i, False)
            gt = gst[i % 2]
            P.op("act", lambda e, i=i, bap=bap, gt=gt: e.activation(out=gt.ap, in_=bap, func=AF.Sigmoid, bias=col("b_gate", i)), r=[bkey, colt], w=[gt])
            P.dma("pool", GATE[i, :, t0:t0 + TT], gt.ap, r=[gt])

        for b in range(4):
            gt_ = gtm[0]
            for hh in range(2):
                bap, bkey = bk(6 + hh)
                P.op("pe", lambda e, b=b, hh=hh, bap=bap: e.matmul(bap, lhsT=sg.ap[:, b * 128:(b + 1) * 128], rhs=gup.ap[:, hh * 512:(hh + 1) * 512], start=True, stop=True), r=[sg, gup], w=[bkey])
                P.op("act", lambda e, hh=hh, bap=bap, gt_=gt_: e.activation(out=gt_.ap[:, hh * 512:(hh + 1) * 512], in_=bap, func=AF.Copy), r=[bkey], w=[gt_])
            P.dma("pool", TM["G"][t0 + b * 128:t0 + (b + 1) * 128, :], gt_.ap, r=[gt_])

        ck0 = ti * (TT // C)
        for j in range(NJ):
            do_pair(j, t0, ck0)

    def do_pair(j, t0, ck0):
        kk, sq, rn, kkn, ksum, bv = PT["kk"], PT["sq"], PT["rn"], PT["kkn"], PT["ksum"], PT["bv"]
        P.op("dve", lambda e: e.tensor_scalar(out=kk.ap, in0=zk.ap[:, j, :], scalar1=col("k_k", j), scalar2=None, op0=ALU.mult), r=[(zk, j), colt], w=[kk])
        P.op("pool", lambda e: e.tensor_tensor(out=sq.ap, in0=kk.ap, in1=kk.ap, op=ALU.mult), r=[kk], w=[sq])
        bap, bkey = bk(6)
        P.op("pe", lambda e: e.matmul(bap, lhsT=blk1.ap, rhs=sq.ap, start=True, stop=True), r=[blk1, sq], w=[bkey])
        P.op("act", lambda e: e.activation(out=rn.ap, in_=bap, func=AF.Sqrt), r=[bkey], w=[rn])
        P.op("dve", lambda e: e.tensor_scalar(out=rn.ap, in0=rn.ap, scalar1=1e-12, scalar2=None, op0=ALU.max), r=[rn], w=[rn])
        P.op("dve", lambda e: e.reciprocal(out=rn.ap, in_=rn.ap), r=[rn], w=[rn])
        P.op("dve", lambda e: e.tensor_tensor(out=kkn.ap, in0=kk.ap, in1=rn.ap, op=ALU.mult), r=[kk, rn], w=[kkn])
        for di, d_ in enumerate("fb"):
            do_dir(j, t0, ck0, di, d_, kkn, ksum)
        t1 = sq
        P.op("dve", lambda e: e.scalar_tensor_tensor(out=t1.ap, in0=ksum.ap, scalar=col("r_k", j), in1=zr.ap[:, j, :], op0=ALU.mult, op1=ALU.mult), r=[ksum, colt, (zr, j)], w=[t1])
        b7ap, b7key = bk(7)
        P.op("pe", lambda e: e.matmul(b7ap, lhsT=blk1.ap, rhs=t1.ap, start=True, stop=True), r=[blk1, t1], w=[b7key])
        P.op("dve", lambda e: e.tensor_tensor(out=bv.ap, in0=b7ap, in1=zv.ap[:, j, :], op=ALU.mult), r=[b7key, (zv, j)], w=[bv])
        transposes_to_tm(lambda b: bv.ap[:, b * 128:(b + 1) * 128], [bv], "BV", j, t0)
        transposes_to_tm(lambda b: zv.ap[:, j, b * 128:(b + 1) * 128], [(zv, j)], "V", j, t0)

    def do_dir(j, t0, ck0, di, d_, kkn, ksum):
        g_ = DT1
        sgm, a_, lw, PI, X1, X2, X3, Ea, Eb, kd, b_ = (g_[k] for k in ("sgm", "a_", "lw", "PI", "X1", "X2", "X3", "Ea", "Eb", "kd", "b_"))
        o_ = DT2[dctr[0] % 2]
        dctr[0] += 1
        At, Bt, Kt, Rt, BH, KH = (o_[k] for k in ("At", "Bt", "Kt", "Rt", "BH", "KH"))
        b1ap, b1key = bk(7)
        P.op("pe", lambda e: e.matmul(b1ap, lhsT=lora.ap[0:64, di, j * 128:(j + 1) * 128], rhs=twza.ap[0:64, :], start=True, stop=True), r=[lora, twza], w=[b1key])
        P.op("act", lambda e: e.activation(out=sgm.ap, in_=b1ap, func=AF.Sigmoid, bias=col("w0_" + d_, j)), r=[b1key, colt], w=[sgm])
        b2ap, b2key = bk(6)
        P.op("pe", lambda e: e.matmul(b2ap, lhsT=lora.ap[64:128, di, j * 128:(j + 1) * 128], rhs=twza.ap[64:128, :], start=True, stop=True), r=[lora, twza], w=[b2key])
        P.op("act", lambda e: e.activation(out=a_.ap, in_=b2ap, func=AF.Sigmoid, bias=col("a0_" + d_, j)), r=[b2key, colt], w=[a_])
        lw = sgm
        K0 = -0.6065306597126334
        P.op("dve", lambda e: e.tensor_tensor_scan(out=PI.ap, data0=rmask.ap, data1=lw.ap, initial=0.0, op0=ALU.mult, op1=ALU.add), r=[rmask, lw], w=[PI])
        P.op("pool", lambda e: e.tensor_tensor(out=X1.ap, in0=PI.ap, in1=lw.ap, op=ALU.subtract), r=[PI, lw], w=[X1])
        PI3 = PI.ap.rearrange("p (c t) -> p c t", c=8)
        P.op("dve", lambda e: e.tensor_tensor(out=X2.ap.rearrange("p (c t) -> p c t", c=8), in0=PI3[:, :, 63:64].to_broadcast([128, 8, 64]), in1=PI3, op=ALU.subtract), r=[PI], w=[X2])
        if d_ == "f":
            srcs = [(X1, K0), (PI, -K0), (PI, K0), (X2, K0)]
        else:
            P.op("pool", lambda e: e.tensor_tensor(out=X3.ap, in0=X2.ap, in1=lw.ap, op=ALU.add), r=[X2, lw], w=[X3])
            srcs = [(X2, K0), (X3, -K0), (X3, K0), (X1, K0)]
        P.op("dve", lambda e: e.tensor_scalar(out=kd.ap, in0=a_.ap, scalar1=col("k_a", j), scalar2=col("omka", j), op0=ALU.mult, op1=ALU.add), r=[a_, colt], w=[kd])
        P.op("dve", lambda e: e.tensor_tensor(out=kd.ap, in0=kd.ap, in1=zk.ap[:, j, :], op=ALU.mult), r=[kd, (zk, j)], w=[kd])
        P.op("pool", lambda e: e.tensor_tensor(out=b_.ap, in0=kkn.ap, in1=a_.ap, op=ALU.mult), r=[kkn, a_], w=[b_])
        if di == 0:
            P.op("pool", lambda e: e.tensor_copy(out=ksum.ap, in_=kd.ap), r=[kd], w=[ksum])
        else:
            P.op("pool", lambda e: e.tensor_tensor(out=ksum.ap, in0=ksum.ap, in1=kd.ap, op=ALU.add), r=[kd, ksum], w=[ksum])

        def ex(Et, k):
            src_, sc = srcs[k]
            P.op("act", lambda e: e.activation(out=Et.ap, in_=src_.ap, func=AF.Exp, scale=sc), r=[src_], w=[Et])

        ex(Ea, 0)
        P.op("dve", lambda e: e.scalar_tensor_tensor(out=At.ap, in0=kkn.ap, scalar=-1.0, in1=Ea.ap, op0=ALU.mult, op1=ALU.mult), r=[kkn, Ea], w=[At])
        ex(Eb, 1)
        P.op("dve", lambda e: e.tensor_tensor(out=Bt.ap, in0=b_.ap, in1=Eb.ap, op=ALU.mult), r=[b_, Eb], w=[Bt])
        P.op("dve", lambda e: e.tensor_tensor(out=Kt.ap, in0=kd.ap, in1=Eb.ap, op=ALU.mult), r=[kd, Eb], w=[Kt])
        ex(Ea, 2)
        P.op("dve", lambda e: e.tensor_tensor(out=Rt.ap, in0=zr.ap[:, j, :], in1=Ea.ap, op=ALU.mult), r=[(zr, j), Ea], w=[Rt])
        E33 = Ea.ap.rearrange("p (c t) -> p c t", c=8)
        cc_ = 63 if d_ == "f" else 0
        P.op("pool", lambda e: e.tensor_copy(out=gcs[d_].ap[:, j, ck0:ck0 + 8], in_=E33[:, :, cc_]), r=[Ea], w=[gcs[d_]])
        ex(Eb, 3)
        P.op("pool", lambda e: e.tensor_tensor(out=BH.ap, in0=b_.ap, in1=Eb.ap, op=ALU.mult), r=[b_, Eb], w=[BH])
        P.op("pool", lambda e: e.tensor_tensor(out=KH.ap, in0=kd.ap, in1=Eb.ap, op=ALU.mult), r=[kd, Eb], w=[KH])
        for nm, tl in (("A", At), ("B", Bt), ("K", Kt), ("R", Rt)):
            dst = FM[nm + d_][ck0:ck0 + 8].rearrange("c p (j t) -> p c j t", j=NJ)[:, :, j, :]
            P.dma("sp", dst, tl.ap.rearrange("p (c t) -> p c t", c=8), r=[tl])
        transposes_to_tm(lambda b: BH.ap[:, b * 128:(b + 1) * 128], [BH], "BH" + d_, j, t0)
        transposes_to_tm(lambda b: KH.ap[:, b * 128:(b + 1) * 128], [KH], "KH" + d_, j, t0)

    for ti in range(NT):
        do_tile(ti)


def col_eps(env):
    return RMS_EPS


def phase2(nc, P, ar, env):
    NCK, NS, CPS = env["NCK"], env["NS"], env["CPS"]
    FM, TM = env["FM"], env["TM"]
    gcs, cmt, bk, banks, consts_d = env["gcs"], env["cmt"], env["bk"], env["banks"], env["consts_d"]
    mk = ar.alloc("masks", [64, 4, 512])
    P.dma("sp", mk.ap, consts_d[0:64, 0:2048].rearrange("p (m c) -> p m c", m=4), w=[mk])
    irep = ar.alloc("irep", [64, 8, 64])
    P.dma("sp", irep.ap, consts_d[0:64, 2816:3328].rearrange("p (a b) -> p a b", a=8), w=[irep])
    S = {}
    for d_ in "fb":
        s = {}
        s["fm"] = [{nm: ar.alloc("fm" + nm + d_ + str(q), [128, NJ, C]) for nm in "ABKR"} for q in range(2)]
        for nm in ("V", "BH", "KH"):
            s[nm] = ar.alloc(nm + d_, [64, D])
        s["ARbd"] = ar.alloc("ARbd" + d_, [128, NJ, 256])
        s["Bbd"] = ar.alloc("Bbd" + d_, [128, NJ, 128])
        s["A_sb"] = ar.alloc("A_sb" + d_, [64, 16, 64])
        s["BP"] = ar.alloc("BP" + d_, [64, 16, 128])
        s["ArbT"] = ar.alloc("ArbT" + d_, [64, 16, 64])
        s["AakT"] = ar.alloc("AakT" + d_, [64, 16, 64])
        s["ArkT"] = ar.alloc("ArkT" + d_, [64, 16, 64])
        s["W"] = ar.alloc("W" + d_, [64, D])
        s["U"] = ar.alloc("U" + d_, [64, D])
        s["H"] = ar.alloc("H" + d_, [128, NJ, 128])
        for nm in ("ARbd", "Bbd", "H"):
            t_ = s[nm]
            P.op("pool", lambda e, t_=t_: e.memset(t_.ap, 0.0), w=[t_] + [(t_, o0, pp) for o0 in (0, 128) for pp in (0, 1)])
        S[d_] = s
    print("phase2 arena cols", ar.off)
    Q = [banks[i].ap for i in range(4)]
    QK = [[("ps", 2 * i), ("ps", 2 * i + 1)] for i in range(4)]
    MIDX = {"f": dict(A=2, B=0, R=1), "b": dict(A=0, B=2, R=3)}

    def load(d_, c, q):
        s = S[d_]
        for nm in "ABKR":
            t_ = s["fm"][q][nm]
            P.dma("sp", t_.ap.rearrange("p j t -> p (j t)"), FM[nm + d_][c], w=[t_])
        for nm, src_ in (("V", "V"), ("BH", "BH" + d_), ("KH", "KH" + d_)):
            P.dma("sp", s[nm].ap, TM[src_][c * C:(c + 1) * C, :], w=[s[nm]])

    def scan(d_, c, q):
        s = S[d_]
        fm = s["fm"][q]
        A, B, K, R = fm["A"], fm["B"], fm["K"], fm["R"]
        V, BH, KH, ARbd, Bbd, A_sb, BP, ArbT, AakT, ArkT, W, U, H = (s[k] for k in ("V", "BH", "KH", "ARbd", "Bbd", "A_sb", "BP", "ArbT", "AakT", "ArkT", "W", "U", "H"))
        mi = MIDX[d_]
        gc = gcs[d_]
        if d_ == "f" and c % CPS == 0 and c > 0:
            idx = c // CPS
            P.op("pool", lambda e: e.tensor_scalar(out=H.ap, in0=H.ap, scalar1=cmt.ap[:, idx:idx + 1], scalar2=None, op0=ALU.mult), r=[H, cmt], w=[H])
        if d_ == "b" and c % CPS == CPS - 1 and c < NCK - 1:
            idx = NS + c // CPS
            P.op("pool", lambda e: e.tensor_scalar(out=H.ap, in0=H.ap, scalar1=cmt.ap[:, idx:idx + 1], scalar2=None, op0=ALU.mult), r=[H, cmt], w=[H])
        for (dst, src_, o0, eng_) in ((ARbd, A, 0, "pool"), (Bbd, B, 0, "pool"), (ARbd, R, 128, "act")):
            if eng_ == "pool":
                P.op("pool", lambda e, dst=dst, src_=src_, o0=o0: e.tensor_copy(out=dst.ap[0:64, :, o0:o0 + 64], in_=src_.ap[0:64, :, :]), r=[src_], w=[(dst, o0, 0)])
                P.op("pool", lambda e, dst=dst, src_=src_, o0=o0: e.tensor_copy(out=dst.ap[64:128, :, o0 + 64:o0 + 128], in_=src_.ap[64:128, :, :]), r=[src_], w=[(dst, o0, 1)])
            else:
                P.op("act", lambda e, dst=dst, src_=src_, o0=o0: e.activation(out=dst.ap[0:64, :, o0:o0 + 64], in_=src_.ap[0:64, :, :], func=AF.Copy), r=[src_], w=[(dst, o0, 0)])
                P.op("act", lambda e, dst=dst, src_=src_, o0=o0: e.activation(out=dst.ap[64:128, :, o0 + 64:o0 + 128], in_=src_.ap[64:128, :, :], func=AF.Copy), r=[src_], w=[(dst, o0, 1)])
        yield
        for hf in range(2):
            yield
            pA, pAk = bk(4)
            for jl in range(4):
                jj = 4 * hf + jl
                P.op("pe", lambda e, jl=jl, jj=jj: e.matmul(pA[0:64, jl * 128:(jl + 1) * 128], lhsT=A.ap[:, jj, :], rhs=Bbd.ap[:, jj, :], start=True, stop=True), r=[A, (Bbd, 0, 0), (Bbd, 0, 1)], w=[pAk])
                P.op("pe", lambda e, jl=jl, jj=jj: e.matmul(Q[0][0:64, jl * 256:(jl + 1) * 256], lhsT=B.ap[:, jj, :], rhs=ARbd.ap[:, jj, :], start=True, stop=True), r=[B, (ARbd, 0, 0), (ARbd, 0, 1), (ARbd, 128, 0), (ARbd, 128, 1)], w=QK[0])
                P.op("pe", lambda e, jl=jl, jj=jj: e.matmul(Q[1][0:64, jl * 256:(jl + 1) * 256], lhsT=K.ap[:, jj, :], rhs=ARbd.ap[:, jj, :], start=True, stop=True), r=[K, (ARbd, 0, 0), (ARbd, 0, 1), (ARbd, 128, 0), (ARbd, 128, 1)], w=QK[1])
            hs = slice(8 * hf, 8 * hf + 8)
            P.op("dve", lambda e, hs=hs: e.tensor_tensor(out=A_sb.ap[:, hs, :], in0=pA[0:64, :].rearrange("p (h s) -> p h s", h=8), in1=mk.ap[:, mi["A"], :].rearrange("p (h s) -> p h s", h=8), op=ALU.mult), r=[pAk, mk], w=[A_sb])
            q0v = Q[0][0:64, :].rearrange("p (j q s) -> p j q s", j=4, q=4)
            q1v = Q[1][0:64, :].rearrange("p (j q s) -> p j q s", j=4, q=4)
            mB = mk.ap[:, mi["B"], :].rearrange("p (j q s) -> p j q s", j=4, q=2)
            mR = mk.ap[:, mi["R"], :].rearrange("p (j q s) -> p j q s", j=4, q=2)
            v4 = lambda ap: ap.rearrange("p (j q) s -> p j q s", j=4)
            P.op("dve", lambda e, hs=hs, q0v=q0v, mB=mB: e.tensor_tensor(out=v4(BP.ap[:, hs, 0:64]), in0=q0v[:, :, 0:2, :], in1=mB, op=ALU.mult), r=QK[0] + [mk], w=[BP])
            P.op("dve", lambda e, hs=hs, q0v=q0v, mR=mR: e.tensor_tensor(out=v4(ArbT.ap[:, hs, :]), in0=q0v[:, :, 2:4, :], in1=mR, op=ALU.mult), r=QK[0] + [mk], w=[ArbT])
            P.op("dve", lambda e, hs=hs, q1v=q1v, mB=mB: e.tensor_tensor(out=v4(AakT.ap[:, hs, :]), in0=q1v[:, :, 0:2, :], in1=mB, op=ALU.mult), r=QK[1] + [mk], w=[AakT])
            P.op("dve", lambda e, hs=hs, q1v=q1v, mR=mR: e.tensor_tensor(out=v4(ArkT.ap[:, hs, :]), in0=q1v[:, :, 2:4, :], in1=mR, op=ALU.mult), r=QK[1] + [mk], w=[ArkT])
            P.op("pool", lambda e, hs=hs: e.tensor_tensor(out=BP.ap[:, hs, 64:128], in0=BP.ap[:, hs, 0:64], in1=irep.ap, op=ALU.add), r=[BP, irep], w=[BP])
        pset = [(Q[0], QK[0], bk(4)), (Q[1], QK[1], bk(5))]
        for st_ in range(6):
            yield
            for hf in range(2):
                pX, pXk, (pY, pYk) = pset[hf]
                hs = slice(8 * hf, 8 * hf + 8)
                for hl in range(8):
                    h = 8 * hf + hl
                    if st_ == 0:
                        P.op("pe", lambda e, hl=hl, h=h, pX=pX: e.matmul(pX[0:64, hl * 128:hl * 128 + 64], lhsT=A_sb.ap[:, h, :], rhs=BP.ap[:, h, 0:64], start=True, stop=True), r=[A_sb, BP], w=pXk)
                    elif st_ < 5:
                        P.op("pe", lambda e, hl=hl, h=h, pX=pX: e.matmul(pX[0:64, hl * 128:(hl + 1) * 128], lhsT=A_sb.ap[:, h, :], rhs=BP.ap[:, h, :], start=True, stop=True), r=[A_sb, BP], w=pXk)
                    else:
                        P.op("pe", lambda e, hl=hl, h=h, pX=pX: e.matmul(pX[0:64, hl * 128 + 64:(hl + 1) * 128], lhsT=A_sb.ap[:, h, :], rhs=BP.ap[:, h, 64:128], start=True, stop=True), r=[A_sb, BP], w=pXk)
                    if st_ < 5:
                        P.op("pe", lambda e, hl=hl, h=h, pY=pY: e.matmul(pY[0:64, hl * 64:(hl + 1) * 64], lhsT=BP.ap[:, h, 0:64], rhs=A_sb.ap[:, h, :], start=True, stop=True), r=[A_sb, BP], w=[pYk])
                pXv = pX[0:64, :].rearrange("p (h c) -> p h c", h=8)
                if st_ < 5:
                    P.op("act", lambda e, hs=hs, pXv=pXv: e.activation(out=BP.ap[:, hs, 0:64], in_=pXv[:, :, 0:64], func=AF.Copy), r=pXk, w=[BP])
                    P.op("act", lambda e, hs=hs, pY=pY: e.activation(out=A_sb.ap[:, hs, :], in_=pY[0:64, :].rearrange("p (h s) -> p h s", h=8), func=AF.Copy), r=[pYk], w=[A_sb])
                if st_ > 0:
                    P.op("dve", lambda e, hs=hs, pXv=pXv: e.tensor_tensor(out=BP.ap[:, hs, 64:128], in0=BP.ap[:, hs, 64:128], in1=pXv[:, :, 64:128], op=ALU.add), r=pXk + [BP], w=[BP])
        yield
        for jj in range(NJ):
            P.op("pe", lambda e, jj=jj: e.matmul(Q[3][0:64, jj * 128:(jj + 1) * 128], lhsT=A.ap[:, jj, :], rhs=H.ap[:, jj, :], start=True, stop=False, skip_group_check=True), r=[A, H], w=QK[3])
            for par in range(2):
                h = 2 * jj + par
                P.op("pe", lambda e, h=h: e.matmul(Q[3][0:64, h * 64:(h + 1) * 64], lhsT=AakT.ap[:, h, :], rhs=V.ap[:, h * 64:(h + 1) * 64], start=False, stop=True, skip_group_check=True), r=[AakT, V], w=QK[3])
        P.op("act", lambda e: e.activation(out=W.ap, in_=Q[3][0:64, :], func=AF.Copy), r=QK[3], w=[W])
        yield
        for h in range(16):
            P.op("pe", lambda e, h=h: e.matmul(Q[3][0:64, h * 64:(h + 1) * 64], lhsT=BP.ap[:, h, 64:128], rhs=W.ap[:, h * 64:(h + 1) * 64], start=True, stop=True), r=[BP, W], w=QK[3])
        P.op("dve", lambda e: e.tensor_copy(out=U.ap, in_=Q[3][0:64, :]), r=QK[3], w=[U])
        yield
        for jj in range(NJ):
            P.op("pe", lambda e, jj=jj: e.matmul(Q[3][0:64, jj * 128:(jj + 1) * 128], lhsT=R.ap[:, jj, :], rhs=H.ap[:, jj, :], start=True, stop=False, skip_group_check=True), r=[R, H], w=QK[3])
            for par in range(2):
                h = 2 * jj + par
                P.op("pe", lambda e, h=h: e.matmul(Q[3][0:64, h * 64:(h + 1) * 64], lhsT=ArbT.ap[:, h, :], rhs=U.ap[:, h * 64:(h + 1) * 64], start=False, stop=False, skip_group_check=True), r=[ArbT, U], w=QK[3])
                P.op("pe", lambda e, h=h: e.matmul(Q[3][0:64, h * 64:(h + 1) * 64], lhsT=ArkT.ap[:, h, :], rhs=V.ap[:, h * 64:(h + 1) * 64], start=False, stop=True, skip_group_check=True), r=[ArkT, V], w=QK[3])
        P.op("act", lambda e: e.activation(out=W.ap, in_=Q[3][0:64, :], func=AF.Copy), r=QK[3], w=[W])
        P.dma("pool", TM["Y" + d_][c * C:(c + 1) * C, :], W.ap, r=[W])
        yield
        for jj in range(NJ):
            P.op("pe", lambda e, jj=jj: e.matmul(Q[2][:, jj * 128:(jj + 1) * 128], lhsT=BH.ap[:, jj * 128:(jj + 1) * 128], rhs=U.ap[:, jj * 128:(jj + 1) * 128], start=True, stop=False), r=[BH, U], w=QK[2])
            P.op("pe", lambda e, jj=jj: e.matmul(Q[2][:, jj * 128:(jj + 1) * 128], lhsT=KH.ap[:, jj * 128:(jj + 1) * 128], rhs=V.ap[:, jj * 128:(jj + 1) * 128], start=False, stop=True), r=[KH, V], w=QK[2])
        q2v = Q[2].rearrange("p (j c) -> p j c", j=NJ)
        for (p0, c0_) in ((0, 0), (64, 64)):
            hb = H.ap[p0:p0 + 64, :, c0_:c0_ + 64]
            P.op("pool", lambda e, hb=hb, p0=p0: e.tensor_tensor(out=hb, in0=hb, in1=gc.ap[p0:p0 + 64, :, c:c + 1].to_broadcast([64, NJ, 64]), op=ALU.mult), r=[H, gc], w=[H])
            P.op("dve", lambda e, hb=hb, p0=p0, c0_=c0_: e.tensor_tensor(out=hb, in0=hb, in1=q2v[p0:p0 + 64, :, c0_:c0_ + 64], op=ALU.add), r=[H] + QK[2], w=[H])

    from itertools import zip_longest
    load("f", 0, 0)
    load("b", NCK - 1, 0)
    for i in range(NCK):
        q = i % 2
        if i + 1 < NCK:
            for nm in "ABKR":
                for d_, cn in (("f", i + 1), ("b", NCK - 2 - i)):
                    t_ = S[d_]["fm"][1 - q][nm]
                    P.dma("sp", t_.ap.rearrange("p j t -> p (j t)"), FM[nm + d_][cn], w=[t_])
        for _ in zip_longest(scan("f", i, q), scan("b", NCK - 1 - i, q)):
            pass
        if i + 1 < NCK:
            for d_, cn in (("f", i + 1), ("b", NCK - 2 - i)):
                for nm, src_ in (("V", "V"), ("BH", "BH" + d_), ("KH", "KH" + d_)):
                    P.dma("sp", S[d_][nm].ap, TM[src_][cn * C:(cn + 1) * C, :], w=[S[d_][nm]])


def phase3(nc, P, ar, env):
    NT = env["NT"]
    TM, GATE, P2, xpad, out_d, rows_d = env["TM"], env["GATE"], env["P2"], env["xpad"], env["out_d"], env["rows_d"]
    colt, ident, col, bk = env["colt"], env["ident"], env["col"], env["bk"]
    w_pb_d, w_rb_d, w_out_d, w_ff1_d, w_ff2_d = env["w_pb_d"], env["w_rb_d"], env["w_out_d"], env["w_ff1_d"], env["w_ff2_d"]
    rowt = [ar.alloc("row%d" % i, [128, D]) for i in range(3)]
    for i in range(3):
        P.dma("sp", rowt[i].ap, rows_d[i:i + 1, :].partition_broadcast(128), w=[rowt[i]])
    lnw, lnb, gfin = rowt
    L = [ar.alloc("L%d" % i, [128, D]) for i in range(4)]
    W1 = ar.alloc("W1", [128, D])
    W2 = ar.alloc("W2", [128, D])
    rw = [ar.alloc("rw%d" % i, [128, D]) for i in range(4)]
    rwT = ar.alloc("rwT", [128, NJ, TT])
    mT = ar.alloc("mT", [128, NJ, TT])
    gtr = [ar.alloc("gtr%d" % i, [128, TT]) for i in range(2)]
    gtp = [ar.alloc("gtp%d" % i, [128, TT]) for i in range(2)]
    x1 = [ar.alloc("x1%d" % i, [128, D]) for i in range(4)]
    xr = [ar.alloc("xr%d" % i, [128, D]) for i in range(2)]
    h2 = [ar.alloc("h2%d" % i, [128, 256]) for i in range(3)]
    tmpf = [ar.alloc("tmpf%d" % i, [128, 256]) for i in range(2)]
    wrb = [ar.alloc("wrb%d" % i, [128, NJ, 128]) for i in range(2)]
    wpb = [ar.alloc("wpb%d" % i, [128, 4, 128]) for i in range(2)]
    wo = [ar.alloc("wo%d" % i, [128, D]) for i in range(2)]
    wf1 = [ar.alloc("wf1%d" % i, [128, NJ, 128]) for i in range(3)]
    wf2 = [ar.alloc("wf2%d" % i, [128, D]) for i in range(3)]
    stt = ar.alloc("stt", [128, 64])
    print("phase3 arena cols", ar.off)
    h3 = lambda ap: ap.rearrange("p (h c) -> p h c", h=16)
    bc3 = lambda ap: ap.rearrange("p (h o) -> p h o", o=1).to_broadcast([128, 16, 64])

    def do_block(ti, b):
        t0 = ti * TT + b * 128
        for i, nm in enumerate(("Yf", "Yb", "BV", "G")):
            P.dma("sp", L[i].ap, TM[nm][t0:t0 + 128, :], w=[L[i]])
        P.op("dve", lambda e: e.tensor_tensor(out=L[0].ap, in0=L[0].ap, in1=L[1].ap, op=ALU.add), r=[L[0], L[1]], w=[L[0]])
        P.op("dve", lambda e: e.tensor_reduce(out=stt.ap[:, 0:16], in_=h3(L[0].ap), axis=AX.X, op=ALU.add), r=[L[0]], w=[stt])
        P.op("dve", lambda e: e.tensor_scalar(out=stt.ap[:, 16:32], in0=stt.ap[:, 0:16], scalar1=1.0 / 64, scalar2=None, op0=ALU.mult), r=[stt], w=[stt])
        P.op("dve", lambda e: e.tensor_tensor(out=h3(W1.ap), in0=h3(L[0].ap), in1=bc3(stt.ap[:, 16:32]), op=ALU.subtract), r=[L[0], stt], w=[W1])
        P.op("pool", lambda e: e.tensor_tensor(out=W2.ap, in0=W1.ap, in1=W1.ap, op=ALU.mult), r=[W1], w=[W2])
        P.op("dve", lambda e: e.tensor_reduce(out=stt.ap[:, 32:48], in_=h3(W2.ap), axis=AX.X, op=ALU.add), r=[W2], w=[stt])
        P.op("act", lambda e: e.activation(out=stt.ap[:, 48:64], in_=stt.ap[:, 32:48], func=AF.Sqrt, scale=1.0 / 64, bias=gn_eps_ap(env)), r=[stt], w=[stt])
        P.op("dve", lambda e: e.reciprocal(out=stt.ap[:, 48:64], in_=stt.ap[:, 48:64]), r=[stt], w=[stt])
        P.op("dve", lambda e: e.tensor_tensor(out=h3(W1.ap), in0=h3(W1.ap), in1=bc3(stt.ap[:, 48:64]), op=ALU.mult), r=[W1, stt], w=[W1])
        P.op("pool", lambda e: e.tensor_tensor(out=W1.ap, in0=W1.ap, in1=lnw.ap, op=ALU.mult), r=[W1, lnw], w=[W1])
        P.op("pool", lambda e: e.tensor_tensor(out=W1.ap, in0=W1.ap, in1=lnb.ap, op=ALU.add), r=[W1, lnb], w=[W1])
        P.op("dve", lambda e: e.tensor_tensor(out=W1.ap, in0=W1.ap, in1=L[2].ap, op=ALU.add), r=[W1, L[2]], w=[W1])
        P.op("dve", lambda e: e.tensor_tensor(out=rw[b].ap, in0=W1.ap, in1=L[3].ap, op=ALU.mult), r=[W1, L[3]], w=[rw[b]])

    def rstd_of(src, colidx, junk):
        P.op("act", lambda e: e.activation(out=junk.ap, in_=src.ap, func=AF.Square, accum_out=stt.ap[:, colidx:colidx + 1]), r=[src], w=[junk, stt])
        P.op("act", lambda e: e.activation(out=stt.ap[:, colidx:colidx + 1], in_=stt.ap[:, colidx:colidx + 1], func=AF.Sqrt, scale=1.0 / D, bias=RMS_EPS), r=[stt], w=[stt])
        P.op("dve", lambda e: e.reciprocal(out=stt.ap[:, colidx:colidx + 1], in_=stt.ap[:, colidx:colidx + 1]), r=[stt], w=[stt])

    def do_tile(ti):
        t0 = ti * TT
        for b in range(4):
            do_block(ti, b)
        for j in range(NJ):
            bap, bkey = bk(j % 2)
            for b in range(4):
                P.op("pe", lambda e, b=b, j=j, bap=bap: e.transpose(out=bap[:, b * 128:(b + 1) * 128], in_=rw[b].ap[:, j * 128:(j + 1) * 128], identity=ident.ap), r=[rw[b], ident], w=[bkey])
            P.op("act", lambda e, j=j, bap=bap: e.activation(out=rwT.ap[:, j, :], in_=bap, func=AF.Copy), r=[bkey], w=[rwT])
        for g in range(4):
            P.dma("sp", L[g].ap[:, 0:TT], P2[g, :, t0:t0 + TT], w=[L[g]])
        for ec in range(NJ):
            wr, wp, gr, gp = wrb[ec % 2], wpb[ec % 2], gtr[ec % 2], gtp[ec % 2]
            P.dma("sp", wr.ap, w_rb_d[:, :, ec * 128:(ec + 1) * 128], w=[wr])
            P.dma("sp", wp.ap, w_pb_d[:, :, ec * 128:(ec + 1) * 128], w=[wp])
            P.dma("sp", gr.ap, GATE[8 + ec, :, t0:t0 + TT], w=[gr])
            P.dma("sp", gp.ap, GATE[ec, :, t0:t0 + TT], w=[gp])
            bA, kA = bk(ec % 2)
            bB, kB = bk(2 + ec % 2)
            for j in range(NJ):
                P.op("pe", lambda e, j=j, wr=wr, bA=bA: e.matmul(bA, lhsT=wr.ap[:, j, :], rhs=rwT.ap[:, j, :], start=(j == 0), stop=(j == NJ - 1)), r=[wr, rwT], w=[kA])
            for g in range(4):
                P.op("pe", lambda e, g=g, wp=wp, bB=bB: e.matmul(bB, lhsT=wp.ap[:, g, :], rhs=L[g].ap[:, 0:TT], start=(g == 0), stop=(g == 3)), r=[wp, L[g]], w=[kB])
            P.op("dve", lambda e, ec=ec, bA=bA, gr=gr: e.tensor_tensor(out=mT.ap[:, ec, :], in0=bA, in1=gr.ap, op=ALU.mult), r=[kA, gr], w=[(mT, ec)])
            P.op("dve", lambda e, bB=bB, gp=gp: e.tensor_tensor(out=W2.ap[:, 0:TT], in0=bB, in1=gp.ap, op=ALU.mult), r=[kB, gp], w=[W2])
            P.op("pool", lambda e, ec=ec: e.tensor_tensor(out=mT.ap[:, ec, :], in0=mT.ap[:, ec, :], in1=W2.ap[:, 0:TT], op=ALU.add), r=[(mT, ec), W2], w=[(mT, ec)])
        for ec in range(NJ):
            wt = wo[ec % 2]
            P.dma("sp", wt.ap, w_out_d[ec], w=[wt])
            for b in range(4):
                for hh in range(2):
                    bap, bkey = bk(b * 2 + hh)
                    P.op("pe", lambda e, ec=ec, b=b, hh=hh, wt=wt, bap=bap: e.matmul(bap[:, :], lhsT=mT.ap[:, ec, b * 128:(b + 1) * 128], rhs=wt.ap[:, hh * 512:(hh + 1) * 512], start=(ec == 0), stop=(ec == NJ - 1)), r=[wt, (mT, ec)], w=[bkey])
        for b in range(4):
            xt_ = xr[b % 2]
            P.dma("sp", xt_.ap, xpad[8 + t0 + b * 128:8 + t0 + (b + 1) * 128, :], w=[xt_])
            for hh in range(2):
                bap, bkey = bk(b * 2 + hh)
                P.op("dve", lambda e, b=b, hh=hh, bap=bap, xt_=xt_: e.tensor_tensor(out=x1[b].ap[:, hh * 512:(hh + 1) * 512], in0=bap, in1=xt_.ap[:, hh * 512:(hh + 1) * 512], op=ALU.add), r=[bkey, xt_], w=[x1[b]])
        hsb = [W1, W2]
        for sb in range(2):
            for bl in range(2):
                b = 2 * sb + bl
                rstd_of(x1[b], bl, L[0])
                P.op("dve", lambda e, b=b, bl=bl: e.tensor_scalar(out=hsb[bl].ap, in0=x1[b].ap, scalar1=stt.ap[:, bl:bl + 1], scalar2=None, op0=ALU.mult), r=[x1[b], stt], w=[hsb[bl]])
            for j in range(NJ):
                bap, bkey = bk(4 + j % 2)
                for bl in range(2):
                    P.op("pe", lambda e, j=j, bl=bl, bap=bap: e.transpose(out=bap[:, bl * 128:(bl + 1) * 128], in_=hsb[bl].ap[:, j * 128:(j + 1) * 128], identity=ident.ap), r=[hsb[bl], ident], w=[bkey])
                P.op("act", lambda e, j=j, bap=bap: e.activation(out=rwT.ap[:, j, 0:256], in_=bap[:, 0:256], func=AF.Copy, scale=col("g_ffn", j)), r=[bkey, colt], w=[rwT])
            for fc in range(32):
                w1_, w2_ = wf1[fc % 3], wf2[fc % 3]
                P.dma("sp", w1_.ap, w_ff1_d[fc], w=[w1_])
                P.dma("sp", w2_.ap, w_ff2_d[fc], w=[w2_])
                bap, bkey = bk(4 + fc % 2)
                for j in range(NJ):
                    P.op("pe", lambda e, j=j, w1_=w1_, bap=bap: e.matmul(bap[:, 0:256], lhsT=w1_.ap[:, j, :], rhs=rwT.ap[:, j, 0:256], start=(j == 0), stop=(j == NJ - 1)), r=[w1_, rwT], w=[bkey])
                tf, hh_ = tmpf[fc % 2], h2[fc % 3]
                P.op("act", lambda e, bap=bap, tf=tf: e.activation(out=tf.ap, in_=bap[:, 0:256], func=AF.Copy), r=[bkey], w=[tf])
                P.op("dve", lambda e, tf=tf, hh_=hh_: e.scalar_tensor_tensor(out=hh_.ap, in0=tf.ap, scalar=0.0, in1=tf.ap, op0=ALU.max, op1=ALU.mult), r=[tf], w=[hh_])
                for bl in range(2):
                    for hh in range(2):
                        oap, okey = bk(bl * 2 + hh)
                        P.op("pe", lambda e, fc=fc, bl=bl, hh=hh, hh_=hh_, w2_=w2_, oap=oap: e.matmul(oap, lhsT=hh_.ap[:, bl * 128:(bl + 1) * 128], rhs=w2_.ap[:, hh * 512:(hh + 1) * 512], start=(fc == 0), stop=(fc == 31)), r=[hh_, w2_], w=[okey])
            for bl in range(2):
                b = 2 * sb + bl
                for hh in range(2):
                    oap, okey = bk(bl * 2 + hh)
                    P.op("dve", lambda e, b=b, hh=hh, oap=oap: e.tensor_tensor(out=x1[b].ap[:, hh * 512:(hh + 1) * 512], in0=oap, in1=x1[b].ap[:, hh * 512:(hh + 1) * 512], op=ALU.add), r=[okey, x1[b]], w=[x1[b]])
                rstd_of(x1[b], 2 + bl, L[0])
                P.op("dve", lambda e, b=b, bl=bl: e.scalar_tensor_tensor(out=x1[b].ap, in0=x1[b].ap, scalar=stt.ap[:, 2 + bl:3 + bl], in1=gfin.ap, op0=ALU.mult, op1=ALU.mult), r=[x1[b], stt, gfin], w=[x1[b]])
                P.dma("pool", out_d[t0 + b * 128:t0 + (b + 1) * 128, :], x1[b].ap, r=[x1[b]])

    for ti in range(NT):
        do_tile(ti)


def gn_eps_ap(env):
    return GN_EPS


def _colpack(v, n):
    return np.ascontiguousarray(np.asarray(v, np.float32).reshape(n, 128).T)


def prep_shared(inp):
    f = lambda a: np.asarray(a, np.float32)
    cols = np.zeros((128, NCOLS), np.float32)
    src = {"g_mix": inp["g_mix"][0], "b_gate": inp["b_gate"][0], "mu_prev": inp["mu_prev"][0], "mu_next": inp["mu_next"][0],
           "pool_scale": inp["pool_scale"][0], "k_k": inp["k_k"][0], "k_a": inp["k_a"][0], "r_k": f(inp["r_k"][0]).reshape(-1),
           "w0_f": inp["w0_f"][0], "a0_f": inp["a0_f"][0], "w0_b": inp["w0_b"][0], "a0_b": inp["a0_b"][0], "g_ffn": inp["g_ffn"][0]}
    for n, c in COLSPEC:
        if n in src:
            cols[:, COLOFF[n]:COLOFF[n] + c] = _colpack(src[n], c)
    consts = np.zeros((128, 3328), np.float32)
    s = np.arange(64)[:, None]
    t = np.arange(64)[None, :]
    masks = [(s < t), (s <= t), (s > t), (s >= t)]
    for i, m in enumerate(masks):
        consts[0:64, i * 512:(i + 1) * 512] = np.tile(m.astype(np.float32), (1, 8))
    rm = np.ones(512, np.float32)
    rm[::64] = 0.0
    consts[:, 2048:2560] = rm[None, :]
    consts[:, 2560:2688] = np.eye(128, dtype=np.float32)
    b1 = np.zeros((128, 128), np.float32)
    b1[0:64, 0:64] = 1.0
    b1[64:128, 64:128] = 1.0
    consts[:, 2688:2816] = b1
    consts[0:64, 2816:3328] = np.tile(np.eye(64, dtype=np.float32), (1, 8))
    rows = np.stack([f(inp["ln_w"][0]), f(inp["ln_b"][0]), f(inp["g_final"])], 0)
    lora = np.zeros((128, 2, D), np.float32)
    lora[0:64, 0] = inp["w_up_f"][0]
    lora[64:128, 0] = inp["a_up_f"][0]
    lora[0:64, 1] = inp["w_up_b"][0]
    lora[64:128, 1] = inp["a_up_b"][0]
    sh = {
        "cols": cols, "consts": consts, "rows": rows,
        "w_in": np.ascontiguousarray(f(inp["w_in"][0]).reshape(NJ, 128, NCC, 128).transpose(2, 1, 0, 3)),
        "pool_w": np.ascontiguousarray(f(inp["pool_w"][0]).transpose(1, 0, 2)),
        "lora": lora, "g_up": np.ascontiguousarray(f(inp["g_up"][0])),
        "w_pb": np.ascontiguousarray(f(inp["w_pool_br"][0]).reshape(4, 128, D).transpose(1, 0, 2)),
        "w_rb": np.ascontiguousarray(f(inp["w_rwkv_br"][0]).reshape(NJ, 128, D).transpose(1, 0, 2)),
        "w_out": np.ascontiguousarray(f(inp["w_out"][0]).reshape(NJ, 128, D)),
        "w_ff1": np.ascontiguousarray(f(inp["w_ff1"][0]).reshape(NJ, 128, 32, 128).transpose(2, 1, 0, 3)),
        "w_ff2": np.ascontiguousarray(f(inp["w_ff2"][0]).reshape(32, 128, D)),
    }
    return sh


def prep_core(segs, NS, SL):
    N = NS * SL
    NT = N // TT
    xpad = np.zeros((N + 16, D), np.float32)
    segid = np.full(NS, -1, np.int64)
    rcnt = np.ones((4, N), np.float32)
    cover = np.zeros(NS, bool)
    allsegs = list(segs)
    for s0, ns, arr in segs:
        cover[s0:s0 + ns] = True
    for s in range(NS):
        if not cover[s]:
            allsegs.append((s, 1, None))
    for i, (s0, ns, arr) in enumerate(allsegs):
        segid[s0:s0 + ns] = i
        S = ns * SL
        if arr is not None:
            xpad[8 + s0 * SL: 8 + s0 * SL + S] = arr
        pos = np.arange(S)
        for g, w in enumerate((2, 4, 8, 16)):
            lo = np.maximum(pos - w // 2, 0)
            hi = np.minimum(pos + w // 2 - 1, S - 1)
            rcnt[g, s0 * SL:s0 * SL + S] = 1.0 / (hi - lo + 1).astype(np.float32)
    hm = np.ones((16, NT), np.float32)
    for ti in range(NT):
        t0 = ti * TT
        sl = t0 // SL
        if t0 % SL == 0 and (sl == 0 or segid[sl - 1] != segid[sl]):
            hm[0:8, ti] = 0.0
        t1 = t0 + TT
        sl1 = (t1 - 1) // SL
        if t1 % SL == 0 and (sl1 == NS - 1 or segid[sl1 + 1] != segid[sl1]):
            hm[8:16, ti] = 0.0
    cm = np.zeros((128, 2 * NS), np.float32)
    for s in range(NS):
        if s > 0 and segid[s - 1] == segid[s]:
            cm[:, s] = 1.0
        if s < NS - 1 and segid[s + 1] == segid[s]:
            cm[:, NS + s] = 1.0
    return {"xpad": xpad, "hm": hm, "rcnt": rcnt, "cm": cm}


_NC_CACHE = {}


def kernel(**inputs):
    NS, SL = 8, 2048
    xp = np.asarray(inputs["x_prompt"], np.float32)
    xs = np.asarray(inputs["x_sample"], np.float32)
    sh = prep_shared(inputs)
    plan = []
    plan.append([(0, 8, xs[0])])
    plan.append([(0, 8, xs[1])])
    counts = [6, 6, 5, 5, 5, 5]
    nxt = 0
    owners = []
    for c in counts:
        segs = []
        own = []
        for i in range(c):
            segs.append((i, 1, xp[nxt]))
            own.append(nxt)
            nxt += 1
        plan.append(segs)
        owners.append(own)
    in_maps = []
    for segs in plan:
        m = dict(sh)
        m.update(prep_core(segs, NS, SL))
        in_maps.append(m)
    key = (NS, SL)
    if key not in _NC_CACHE:
        _NC_CACHE[key] = build(NS, SL)
    nc = _NC_CACHE[key]
    res = run_bass_kernel_spmd(nc, in_maps, core_ids=list(range(8)))
    y_prompt = np.empty_like(xp)
    y_sample = np.empty_like(xs)
    for c in range(2):
        y_sample[c] = np.asarray(res.results[c]["out"]).reshape(NS * SL, D)
    for ci, own in enumerate(owners):
        o = np.asarray(res.results[2 + ci]["out"]).reshape(NS, SL, D)
        for i, b in enumerate(own):
            y_prompt[b] = o[i]
    return (y_prompt, y_sample)
```

```python
from contextlib import ExitStack
import numpy as np
import concourse.bass as bass
import concourse.mybir as mybir
from concourse.bass_utils import run_bass_kernel_spmd

F32 = mybir.dt.float32
AF = mybir.ActivationFunctionType
ALU = mybir.AluOpType
AX = mybir.AxisListType

D = 1024
NJ = 8
TT = 512
C = 64
IN_COLS = 5888
NCC = IN_COLS // 128
RMS_EPS = 1e-6
GN_EPS = 64e-5
CENGS = ("pe", "act", "dve", "pool")
ENGS = ("pe", "act", "dve", "pool", "sp")
NSLOT = 8
PHASES = 3
INLINE_WAIT = True
EPOCH = 30000


class T:
    def __init__(self, name, ap):
        self.name = name
        self.ap = ap

    def __getitem__(self, k):
        return self.ap[k]


class Prog:
    def __init__(self):
        self.streams = {e: [] for e in ENGS}
        self.cnt = {e: 0 for e in CENGS}
        self.seen = {e: {} for e in ENGS}
        self.lastw = {}
        self.readers = {}
        self.semkeys = {}
        self.dma_slot = {e: 0 for e in ENGS}
        self.dma_val = {}
        self.nops = 0

    @staticmethod
    def _k(x):
        if isinstance(x, T):
            return x.name
        if isinstance(x, tuple) and isinstance(x[0], T):
            return (x[0].name,) + tuple(x[1:])
        return x

    def _need(self, eng, ev):
        if ev is None:
            return
        k, v = ev
        if k[0] == eng and eng == "pe":
            return
        if self.seen[eng].get(k, 0) >= v:
            return
        self.seen[eng][k] = v
        self.streams[eng].append(("wait", k, v))

    def _deps(self, eng, r, w):
        for key in r:
            self._need(eng, self.lastw.get(key))
        for key in w:
            self._need(eng, self.lastw.get(key))
            for ev in list(self.readers.get(key, {}).items()):
                self._need(eng, ev)

    def _commit(self, ev, r, w):
        for key in r:
            d = self.readers.setdefault(key, {})
            d[ev[0]] = max(d.get(ev[0], 0), ev[1])
        for key in w:
            self.lastw[key] = ev
            self.readers[key] = {}

    def op(self, eng, fn, r=(), w=()):
        r = [self._k(x) for x in r]
        w = [self._k(x) for x in w]
        self._deps(eng, r, w)
        self.cnt[eng] += 1
        n = self.cnt[eng]
        ep = (n - 1) // EPOCH
        k = (eng, ep)
        self.semkeys[k] = 1
        self.streams[eng].append(("op", fn, k, 1))
        self._commit((k, n - ep * EPOCH), r, w)
        self.nops += 1

    def dma(self, q, out, in_, r=(), w=()):
        r = [self._k(x) for x in r]
        w = [self._k(x) for x in w]
        self._deps(q, r, w)
        slot = self.dma_slot[q]
        self.dma_slot[q] = (slot + 1) % NSLOT
        k = ("dma", q, slot)
        self.semkeys[k] = 1
        prev = self.dma_val.get(k, 0)
        if prev:
            self._need(q, (k, prev))
        v = prev + 16
        self.dma_val[k] = v
        self.streams[q].append(("op", lambda e, o=out, i=in_: e.dma_start(out=o, in_=i), k, 16))
        self._commit((k, v), r, w)
        self.nops += 1

    def barrier(self):
        evs = []
        for e in CENGS:
            n = self.cnt[e]
            if n:
                ep = (n - 1) // EPOCH
                evs.append(((e, ep), n - ep * EPOCH))
        for k, v in self.dma_val.items():
            evs.append((k, v))
        for e in ENGS:
            for ev in evs:
                self._need(e, ev)

    def emit(self, nc):
        blockname = {"pe": "tensor", "act": "scalar", "dve": "vector", "pool": "gpsimd", "sp": "sync"}
        waited = {}
        for eng in ENGS:
            for it in self.streams[eng]:
                if it[0] == "wait" and it[1][0] != "dma":
                    waited.setdefault(it[1], set()).add(it[2])
        rank = {k: {v: i + 1 for i, v in enumerate(sorted(vs))} for k, vs in waited.items()}
        with ExitStack() as st:
            sems = {}
            for i, k in enumerate(self.semkeys):
                sems[k] = st.enter_context(nc.semaphore("s%d" % i))

            def do_wait(e, w_, ins=None):
                k, v = w_[1], w_[2]
                if k[0] != "dma":
                    v = rank[k][v]
                if ins is None:
                    e.wait_ge(sems[k], v)
                else:
                    ins._wait_ge(sems[k], v)

            with nc.Block() as block:
                for eng in ENGS:
                    items = self.streams[eng]

                    def body(e, items=items):
                        pend = []
                        cnt = {}
                        for it in items:
                            if it[0] == "wait":
                                pend.append(it)
                            else:
                                if INLINE_WAIT and pend:
                                    for w_ in pend[:-1]:
                                        do_wait(e, w_)
                                    ins = it[1](e)
                                    do_wait(e, pend[-1], ins)
                                else:
                                    for w_ in pend:
                                        do_wait(e, w_)
                                    ins = it[1](e)
                                pend = []
                                k = it[2]
                                if k[0] == "dma":
                                    ins.then_inc(sems[k], it[3])
                                else:
                                    n = cnt.get(k, 0) + 1
                                    cnt[k] = n
                                    if n in waited.get(k, ()):
                                        ins.then_inc(sems[k], 1)
                        for w_ in pend:
                            do_wait(e, w_)

                    getattr(block, blockname[eng])(body)


class Arena:
    def __init__(self, nc, st, name, ncols):
        self.t = st.enter_context(nc.sbuf_tensor(name, [128, ncols], F32))
        self.ncols = ncols
        self.off = 0
        self.uid = 0
        self.name = name

    def reset(self):
        self.off = 0

    def alloc(self, name, shape):
        n = int(np.prod(shape[1:]))
        assert self.off + n <= self.ncols, ("SBUF arena overflow", name, self.off + n, self.ncols)
        ap = self.t[0:shape[0], self.off:self.off + n]
        self.off += n
        if len(shape) == 3:
            ap = ap.rearrange("p (a b) -> p a b", a=shape[1])
        elif len(shape) == 4:
            ap = ap.rearrange("p (a b c) -> p a b c", a=shape[1], b=shape[2])
        self.uid += 1
        return T("%s.%s.%d" % (self.name, name, self.uid), ap)


def build(NS, SL, dbg=False):
    N = NS * SL
    NT = N // TT
    NCK = N // C
    CPS = SL // C
    nc = bass.Bass("TRN2", target_bir_lowering=False)
    P = Prog()

    def din(name, shape):
        return nc.dram_tensor(name, list(shape), F32, kind="ExternalInput").ap()

    def dscr(name, shape):
        kind = "ExternalOutput" if dbg else "Internal"
        return nc.dram_tensor(name, list(shape), F32, kind=kind).ap()

    xpad = din("xpad", [N + 16, D])
    hm_d = din("hm", [16, NT])
    rcnt_d = din("rcnt", [4, N])
    cm_d = din("cm", [128, 2 * NS])
    cols_d = din("cols", [128, NCOLS])
    consts_d = din("consts", [128, 3840])
    rows_d = din("rows", [3, D])
    w_in_d = din("w_in", [NCC, 128, NJ, 128])
    pool_w_d = din("pool_w", [128, 4, 128])
    lora_d = din("lora", [128, 2, D])
    g_up_d = din("g_up", [128, D])
    w_pb_d = din("w_pb", [128, 4, D])
    w_rb_d = din("w_rb", [128, NJ, D])
    w_out_d = din("w_out", [NJ, 128, D])
    w_ff1_d = din("w_ff1", [32, 128, NJ, 128])
    w_ff2_d = din("w_ff2", [32, 128, D])
    out_d = nc.dram_tensor("out", [N, D], F32, kind="ExternalOutput").ap()

    FM = {}
    for d_ in "fb":
        for nm in ("A", "B", "K", "R"):
            FM[nm + d_] = dscr("FM_%s%s" % (nm, d_), [NCK, 128, NJ * C])
    TMn = ["V", "BV", "BHf", "KHf", "BHb", "KHb", "G", "Yf", "Yb"]
    TM = {nm: dscr("TM_" + nm, [N, D]) for nm in TMn}
    GATE = dscr("GATE", [16, 128, N])
    P2 = dscr("P2", [4, 128, N])

    with ExitStack() as st:
        ar = Arena(nc, st, "ar", 43000)
        cst = Arena(nc, st, "cst", 2 * 512 + 128 + 128 + NCOLS + 2 * NJ * NCK + 2 * NS + NT + 64)
        banks = [T("bank%d" % i, st.enter_context(nc.psum_tensor("bank%d" % i, [128, 1024], F32))) for i in range(4)]

        def bk(i):
            return banks[i // 2].ap[:, (i % 2) * 512:(i % 2 + 1) * 512], ("ps", i)

        colt = cst.alloc("cols", [128, NCOLS])
        ident = cst.alloc("ident", [128, 128])
        blk1 = cst.alloc("blk1", [128, 128])
        rmask = cst.alloc("rmask", [128, 512])
        gcs = {"f": cst.alloc("gcf", [128, NJ, NCK]), "b": cst.alloc("gcb", [128, NJ, NCK])}
        cmt = cst.alloc("cm", [128, 2 * NS])
        hmt = cst.alloc("hm", [16, NT])
        P.dma("sp", colt.ap, cols_d, w=[colt])
        P.dma("sp", ident.ap, consts_d[:, 2560:2688], w=[ident])
        P.dma("sp", blk1.ap, consts_d[:, 2688:2816], w=[blk1])
        P.dma("sp", rmask.ap, consts_d[:, 2048:2560], w=[rmask])
        P.dma("sp", cmt.ap, cm_d, w=[cmt])
        P.dma("sp", hmt.ap, hm_d, w=[hmt])

        def col(name, j):
            o = COLOFF[name] + j
            return colt.ap[:, o:o + 1]

        for i in range(26):
            P.op("dve", lambda e, i=i: e.tensor_tensor(out=col("c0", i), in0=col("mu_prev", i), in1=col("mu_next", i), op=ALU.add), r=[colt], w=[colt])
        P.op("dve", lambda e: e.tensor_scalar(out=colt.ap[:, COLOFF["c0"]:COLOFF["c0"] + 26], in0=colt.ap[:, COLOFF["c0"]:COLOFF["c0"] + 26], scalar1=-1.0, scalar2=1.0, op0=ALU.mult, op1=ALU.add), r=[colt], w=[colt])
        P.op("dve", lambda e: e.tensor_scalar(out=colt.ap[:, COLOFF["omka"]:COLOFF["omka"] + 8], in0=colt.ap[:, COLOFF["k_a"]:COLOFF["k_a"] + 8], scalar1=-1.0, scalar2=1.0, op0=ALU.mult, op1=ALU.add), r=[colt], w=[colt])

        phase1(nc, P, ar, locals())
        P.barrier()
        ar.reset()
        if PHASES >= 2:
            phase2(nc, P, ar, locals())
            P.barrier()
            ar.reset()
        if PHASES >= 3:
            phase3(nc, P, ar, locals())
            P.barrier()
        P.emit(nc)
    return nc


COLSPEC = [("g_mix", 8), ("b_gate", 16), ("mu_prev", 26), ("mu_next", 26), ("pool_scale", 4), ("k_k", 8), ("k_a", 8),
           ("r_k", 8), ("w0_f", 8), ("a0_f", 8), ("w0_b", 8), ("a0_b", 8), ("g_ffn", 8), ("c0", 26), ("omka", 8)]
COLOFF = {}
_o = 0
for _n, _c in COLSPEC:
    COLOFF[_n] = _o
    _o += _c
NCOLS = _o


def phase1(nc, P, ar, env):
    NT, N, NCK = env["NT"], env["N"], env["NCK"]
    xpad, rcnt_d, w_in_d = env["xpad"], env["rcnt_d"], env["w_in_d"]
    FM, TM, GATE, P2 = env["FM"], env["TM"], env["GATE"], env["P2"]
    colt, ident, blk1, rmask, gcs, hmt = env["colt"], env["ident"], env["blk1"], env["rmask"], env["gcs"], env["hmt"]
    col, bk = env["col"], env["bk"]

    poolw = ar.alloc("poolw", [128, 4, 128])
    lora = ar.alloc("lora", [128, 2, D])
    gup = ar.alloc("gup", [128, D])
    P.dma("sp", poolw.ap, env["pool_w_d"], w=[poolw])
    P.dma("sp", lora.ap, env["lora_d"], w=[lora])
    P.dma("sp", gup.ap, env["g_up_d"], w=[gup])

    xt = [ar.alloc("xt%d" % b, [128, D]) for b in range(4)]
    xh = ar.alloc("xh", [16, D])
    ss = ar.alloc("ss", [128, 8])
    xnT = ar.alloc("xnT", [128, NJ, TT])
    xnTh = ar.alloc("xnTh", [128, NJ, 16])
    wring = [ar.alloc("w%d" % i, [128, NJ, 128]) for i in range(3)]
    zext = [ar.alloc("zext%d" % i, [128, 528]) for i in range(2)]
    zr = ar.alloc("zr", [128, NJ, TT])
    zk = ar.alloc("zk", [128, NJ, TT])
    zv = ar.alloc("zv", [128, NJ, TT])
    twza = ar.alloc("twza", [128, TT])
    sg = ar.alloc("sg", [128, TT])
    rc = ar.alloc("rc", [128, TT])
    pt = [ar.alloc("pt%d" % i, [128, 528]) for i in range(2)]
    p2T = ar.alloc("p2T", [128, TT])
    gst = [ar.alloc("gst%d" % i, [128, TT]) for i in range(2)]
    tmst = [ar.alloc("tmst%d" % i, [128, 4, 128]) for i in range(3)]
    gtm = [ar.alloc("gtm%d" % i, [128, D]) for i in range(1)]
    junk = gtm[0]
    ded = [ar.alloc("ded%d" % i, [128, TT]) for i in range(13)]
    xsl = [T((xnT.name, j), xnT.ap[:, j, :]) for j in range(NJ)]
    xth = [T((xt[b].name, h), xt[b].ap[:, h * 512:(h + 1) * 512]) for b in range(4) for h in range(2)]
    xtk = lambda b: [(xt[b], 0), (xt[b], 1)]
    print("phase1 arena cols", ar.off)
    PT = dict(kk=ded[0], kkn=ded[1], ksum=ded[2], sq=ded[3], rn=ded[4], bv=ded[5])
    DT1 = dict(sgm=xsl[0], a_=xsl[1], lw=xsl[2], PI=xsl[3], X1=xsl[4], X2=xsl[5], X3=xsl[6], Ea=xsl[7], Eb=xth[0], kd=xth[1], b_=xth[2])
    outs = [xth[3], xth[4], xth[5], xth[6], xth[7]] + ded[6:13]
    DT2 = [dict(At=outs[0], Bt=outs[1], Kt=outs[2], Rt=outs[3], BH=outs[4], KH=outs[5]),
           dict(At=outs[6], Bt=outs[7], Kt=outs[8], Rt=outs[9], BH=outs[10], KH=outs[11])]
    ptmp = [zext[0], zext[1]]
    tctr = [0]
    mctr = [0]
    wctr = [0]
    dctr = [0]

    def transposes_to_tm(src_fn, src_keys, name, j, t0):
        bi = 4 + (mctr[0] % 2)
        bap, bkey = bk(bi)
        for b in range(4):
            P.op("pe", lambda e, b=b: e.transpose(out=bap[:, b * 128:(b + 1) * 128], in_=src_fn(b), identity=ident.ap), r=list(src_keys) + [ident], w=[bkey])
        stg = tmst[mctr[0] % 3]
        mctr[0] += 1
        P.op("act", lambda e: e.activation(out=stg.ap, in_=bap.rearrange("p (b c) -> p b c", b=4), func=AF.Copy), r=[bkey], w=[stg])
        dst = TM[name][t0:t0 + TT, j * 128:(j + 1) * 128].rearrange("(b t) c -> t b c", b=4)
        P.dma("pool", dst, stg.ap, r=[stg])

    def do_tile(ti):
        t0 = ti * TT
        for b in range(4):
            P.dma("sp", xt[b].ap, xpad[8 + t0 + b * 128: 8 + t0 + (b + 1) * 128, :], w=xtk(b))
        P.dma("sp", xh.ap[0:8, :], xpad[t0:t0 + 8, :], w=[xh])
        P.dma("sp", xh.ap[8:16, :], xpad[t0 + 520:t0 + 528, :], w=[xh])
        for b in range(4):
            P.op("act", lambda e, b=b: e.activation(out=junk.ap, in_=xt[b].ap, func=AF.Square, accum_out=ss.ap[:, b:b + 1]), r=xtk(b), w=[junk, ss])
        P.op("act", lambda e: e.activation(out=junk.ap[0:16, :], in_=xh.ap, func=AF.Square, accum_out=ss.ap[0:16, 4:5]), r=[xh], w=[junk, ss])
        P.op("act", lambda e: e.activation(out=ss.ap[:, 0:5], in_=ss.ap[:, 0:5], func=AF.Sqrt, scale=1.0 / D, bias=col_eps(env)), r=[ss], w=[ss])
        P.op("dve", lambda e: e.reciprocal(out=ss.ap[:, 0:5], in_=ss.ap[:, 0:5]), r=[ss], w=[ss])
        P.op("dve", lambda e, ti=ti: e.tensor_tensor(out=ss.ap[0:16, 4:5], in0=ss.ap[0:16, 4:5], in1=hmt.ap[0:16, ti:ti + 1], op=ALU.mult), r=[ss, hmt], w=[ss])
        for b in range(4):
            P.op("dve", lambda e, b=b: e.tensor_scalar(out=xt[b].ap, in0=xt[b].ap, scalar1=ss.ap[:, b:b + 1], scalar2=None, op0=ALU.mult), r=xtk(b) + [ss], w=xtk(b))
        P.op("dve", lambda e: e.tensor_scalar(out=xh.ap, in0=xh.ap, scalar1=ss.ap[0:16, 4:5], scalar2=None, op0=ALU.mult), r=[xh, ss], w=[xh])
        for j in range(NJ):
            bap, bkey = bk(j % 2)
            for b in range(4):
                P.op("pe", lambda e, b=b, j=j, bap=bap: e.transpose(out=bap[:, b * 128:(b + 1) * 128], in_=xt[b].ap[:, j * 128:(j + 1) * 128], identity=ident.ap), r=xtk(b) + [ident], w=[bkey])
            P.op("act", lambda e, j=j, bap=bap: e.activation(out=xnT.ap[:, j, :], in_=bap, func=AF.Copy, scale=col("g_mix", j)), r=[bkey, colt], w=[(xnT, j)])
            hap, hkey = bk(2 + j % 2)
            P.op("pe", lambda e, j=j, hap=hap: e.transpose(out=hap[:, 0:16], in_=xh.ap[0:16, j * 128:(j + 1) * 128], identity=ident.ap[0:16, 0:16]), r=[xh, ident], w=[hkey])
            P.op("act", lambda e, j=j, hap=hap: e.activation(out=xnTh.ap[:, j, :], in_=hap[:, 0:16], func=AF.Copy, scale=col("g_mix", j)), r=[hkey, colt], w=[xnTh])
        xkeys = [(xnT, j) for j in range(NJ)]

        def zchunk(cc, halo):
            wt = wring[wctr[0] % 3]
            wctr[0] += 1
            P.dma("sp", wt.ap, w_in_d[cc], w=[wt])
            bi = wctr[0] % 2
            bap, bkey = bk(bi)
            for j in range(NJ):
                P.op("pe", lambda e, j=j: e.matmul(bap, lhsT=wt.ap[:, j, :], rhs=xnT.ap[:, j, :], start=(j == 0), stop=(j == NJ - 1)), r=[wt, (xnT, j)], w=[bkey])
            hap = hkey = None
            if halo:
                hap, hkey = bk(2 + bi)
                for j in range(NJ):
                    P.op("pe", lambda e, j=j: e.matmul(hap[:, 0:16], lhsT=wt.ap[:, j, :], rhs=xnTh.ap[:, j, :], start=(j == 0), stop=(j == NJ - 1)), r=[wt, xnTh], w=[hkey])
            return bap, bkey, hap, hkey

        def to_zext(cc):
            bap, bkey, hap, hkey = zchunk(cc, True)
            ze = zext[cc % 2]
            P.op("act", lambda e: e.activation(out=ze.ap[:, 8:520], in_=bap, func=AF.Copy), r=[bkey], w=[ze])
            P.op("dve", lambda e: e.tensor_copy(out=ze.ap[:, 0:8], in_=hap[:, 0:8]), r=[hkey], w=[ze])
            P.op("dve", lambda e: e.tensor_copy(out=ze.ap[:, 520:528], in_=hap[:, 8:16]), r=[hkey], w=[ze])
            return ze

        for g in range(4):
            ze = to_zext(g)
            P.dma("sp", rc.ap, rcnt_d[g:g + 1, t0:t0 + TT].partition_broadcast(128), w=[rc])
            cur, L = ze, 528
            sh = 1
            for lev in range(g + 1):
                nxt = pt[lev % 2]
                L2 = L - sh
                P.op("pool", lambda e, cur=cur, nxt=nxt, L2=L2, sh=sh: e.tensor_tensor(out=nxt.ap[:, 0:L2], in0=cur.ap[:, 0:L2], in1=cur.ap[:, sh:sh + L2], op=ALU.add), r=[cur], w=[nxt])
                cur, L, sh = nxt, L2, sh * 2
            w2 = 1 << g
            pl = ded[g % 2]
            P.op("pool", lambda e, cur=cur, w2=w2, pl=pl: e.tensor_tensor(out=pl.ap, in0=cur.ap[:, 8 - w2:8 - w2 + TT], in1=rc.ap, op=ALU.mult), r=[cur, rc], w=[pl])
            P.op("pool", lambda e, pl=pl, ze=ze: e.tensor_tensor(out=pl.ap, in0=pl.ap, in1=ze.ap[:, 8:520], op=ALU.subtract), r=[pl, ze], w=[pl])
            bap, bkey = bk(4)
            P.op("pe", lambda e, g=g, pl=pl: e.matmul(bap, lhsT=poolw.ap[:, g, :], rhs=pl.ap, start=True, stop=True), r=[poolw, pl], w=[bkey])
            P.op("act", lambda e, g=g: e.activation(out=p2T.ap, in_=bap, func=AF.Copy, scale=col("pool_scale", g)), r=[bkey, colt], w=[p2T])
            P.dma("pool", P2[g, :, t0:t0 + TT], p2T.ap, r=[p2T])

        def mix(ze, i, dst_ap, dkeys):
            t1 = ded[2 + (i % 2)]
            P.op("dve", lambda e: e.tensor_scalar(out=t1.ap, in0=ze.ap[:, 8:520], scalar1=col("c0", i), scalar2=None, op0=ALU.mult), r=[ze, colt], w=[t1])
            P.op("dve", lambda e: e.scalar_tensor_tensor(out=t1.ap, in0=ze.ap[:, 7:519], scalar=col("mu_prev", i), in1=t1.ap, op0=ALU.mult, op1=ALU.add), r=[ze, colt, t1], w=[t1])
            P.op("dve", lambda e: e.scalar_tensor_tensor(out=dst_ap, in0=ze.ap[:, 9:521], scalar=col("mu_next", i), in1=t1.ap, op0=ALU.mult, op1=ALU.add), r=[ze, colt, t1], w=dkeys)

        for i in range(26):
            ze = to_zext(4 + i)
            if i < 8:
                mix(ze, i, zr.ap[:, i, :], [(zr, i)])
            elif i < 16:
                mix(ze, i, zk.ap[:, i - 8, :], [(zk, i - 8)])
            elif i < 24:
                mix(ze, i, zv.ap[:, i - 16, :], [(zv, i - 16)])
            elif i == 24:
                mix(ze, i, twza.ap, [twza])
                P.op("act", lambda e: e.activation(out=twza.ap[0:64, :], in_=twza.ap[0:64, :], func=AF.Tanh), r=[twza], w=[twza])
            else:
                mix(ze, i, sg.ap, [sg])
                P.op("act", lambda e: e.activation(out=sg.ap, in_=sg.ap, func=AF.Sigmoid), r=[sg], w=[sg])

        for i in range(16):
            bap, bkey, _, _ = zchunk(30 + i, False)
            gt = gst[i % 2]
            P.op("act", lambda e, i=i, bap=bap, gt=gt: e.activation(out=gt.ap, in_=bap, func=AF.Sigmoid, bias=col("b_gate", i)), r=[bkey, colt], w=[gt])
            P.dma("pool", GATE[i, :, t0:t0 + TT], gt.ap, r=[gt])

        for b in range(4):
            gt_ = gtm[0]
            for hh in range(2):
                bap, bkey = bk(6 + hh)
                P.op("pe", lambda e, b=b, hh=hh, bap=bap: e.matmul(bap, lhsT=sg.ap[:, b * 128:(b + 1) * 128], rhs=gup.ap[:, hh * 512:(hh + 1) * 512], start=True, stop=True), r=[sg, gup], w=[bkey])
                P.op("act", lambda e, hh=hh, bap=bap, gt_=gt_: e.activation(out=gt_.ap[:, hh * 512:(hh + 1) * 512], in_=bap, func=AF.Copy), r=[bkey], w=[gt_])
            P.dma("pool", TM["G"][t0 + b * 128:t0 + (b + 1) * 128, :], gt_.ap, r=[gt_])

        ck0 = ti * (TT // C)
        for j in range(NJ):
            do_pair(j, t0, ck0)

    def do_pair(j, t0, ck0):
        kk, sq, rn, kkn, ksum, bv = PT["kk"], PT["sq"], PT["rn"], PT["kkn"], PT["ksum"], PT["bv"]
        P.op("dve", lambda e: e.tensor_scalar(out=kk.ap, in0=zk.ap[:, j, :], scalar1=col("k_k", j), scalar2=None, op0=ALU.mult), r=[(zk, j), colt], w=[kk])
        P.op("pool", lambda e: e.tensor_tensor(out=sq.ap, in0=kk.ap, in1=kk.ap, op=ALU.mult), r=[kk], w=[sq])
        bap, bkey = bk(6)
        P.op("pe", lambda e: e.matmul(bap, lhsT=blk1.ap, rhs=sq.ap, start=True, stop=True), r=[blk1, sq], w=[bkey])
        P.op("act", lambda e: e.activation(out=rn.ap, in_=bap, func=AF.Sqrt), r=[bkey], w=[rn])
        P.op("dve", lambda e: e.tensor_scalar(out=rn.ap, in0=rn.ap, scalar1=1e-12, scalar2=None, op0=ALU.max), r=[rn], w=[rn])
        P.op("dve", lambda e: e.reciprocal(out=rn.ap, in_=rn.ap), r=[rn], w=[rn])
        P.op("dve", lambda e: e.tensor_tensor(out=kkn.ap, in0=kk.ap, in1=rn.ap, op=ALU.mult), r=[kk, rn], w=[kkn])
        for di, d_ in enumerate("fb"):
            do_dir(j, t0, ck0, di, d_, kkn, ksum)
        t1 = sq
        P.op("dve", lambda e: e.scalar_tensor_tensor(out=t1.ap, in0=ksum.ap, scalar=col("r_k", j), in1=zr.ap[:, j, :], op0=ALU.mult, op1=ALU.mult), r=[ksum, colt, (zr, j)], w=[t1])
        b7ap, b7key = bk(7)
        P.op("pe", lambda e: e.matmul(b7ap, lhsT=blk1.ap, rhs=t1.ap, start=True, stop=True), r=[blk1, t1], w=[b7key])
        P.op("dve", lambda e: e.tensor_tensor(out=bv.ap, in0=b7ap, in1=zv.ap[:, j, :], op=ALU.mult), r=[b7key, (zv, j)], w=[bv])
        transposes_to_tm(lambda b: bv.ap[:, b * 128:(b + 1) * 128], [bv], "BV", j, t0)
        transposes_to_tm(lambda b: zv.ap[:, j, b * 128:(b + 1) * 128], [(zv, j)], "V", j, t0)

    def do_dir(j, t0, ck0, di, d_, kkn, ksum):
        g_ = DT1
        sgm, a_, lw, PI, X1, X2, X3, Ea, Eb, kd, b_ = (g_[k] for k in ("sgm", "a_", "lw", "PI", "X1", "X2", "X3", "Ea", "Eb", "kd", "b_"))
        o_ = DT2[dctr[0] % 2]
        dctr[0] += 1
        At, Bt, Kt, Rt, BH, KH = (o_[k] for k in ("At", "Bt", "Kt", "Rt", "BH", "KH"))
        b1ap, b1key = bk(7)
        P.op("pe", lambda e: e.matmul(b1ap, lhsT=lora.ap[0:64, di, j * 128:(j + 1) * 128], rhs=twza.ap[0:64, :], start=True, stop=True), r=[lora, twza], w=[b1key])
        P.op("act", lambda e: e.activation(out=sgm.ap, in_=b1ap, func=AF.Sigmoid, bias=col("w0_" + d_, j)), r=[b1key, colt], w=[sgm])
        b2ap, b2key = bk(6)
        P.op("pe", lambda e: e.matmul(b2ap, lhsT=lora.ap[64:128, di, j * 128:(j + 1) * 128], rhs=twza.ap[64:128, :], start=True, stop=True), r=[lora, twza], w=[b2key])
        P.op("act", lambda e: e.activation(out=a_.ap, in_=b2ap, func=AF.Sigmoid, bias=col("a0_" + d_, j)), r=[b2key, colt], w=[a_])
        lw = sgm
        K0 = -0.6065306597126334
        P.op("dve", lambda e: e.tensor_tensor_scan(out=PI.ap, data0=rmask.ap, data1=lw.ap, initial=0.0, op0=ALU.mult, op1=ALU.add), r=[rmask, lw], w=[PI])
        P.op("pool", lambda e: e.tensor_tensor(out=X1.ap, in0=PI.ap, in1=lw.ap, op=ALU.subtract), r=[PI, lw], w=[X1])
        PI3 = PI.ap.rearrange("p (c t) -> p c t", c=8)
        P.op("dve", lambda e: e.tensor_tensor(out=X2.ap.rearrange("p (c t) -> p c t", c=8), in0=PI3[:, :, 63:64].to_broadcast([128, 8, 64]), in1=PI3, op=ALU.subtract), r=[PI], w=[X2])
        if d_ == "f":
            srcs = [(X1, K0), (PI, -K0), (PI, K0), (X2, K0)]
        else:
            P.op("pool", lambda e: e.tensor_tensor(out=X3.ap, in0=X2.ap, in1=lw.ap, op=ALU.add), r=[X2, lw], w=[X3])
            srcs = [(X2, K0), (X3, -K0), (X3, K0), (X1, K0)]
        P.op("dve", lambda e: e.tensor_scalar(out=kd.ap, in0=a_.ap, scalar1=col("k_a", j), scalar2=col("omka", j), op0=ALU.mult, op1=ALU.add), r=[a_, colt], w=[kd])
        P.op("dve", lambda e: e.tensor_tensor(out=kd.ap, in0=kd.ap, in1=zk.ap[:, j, :], op=ALU.mult), r=[kd, (zk, j)], w=[kd])
        P.op("pool", lambda e: e.tensor_tensor(out=b_.ap, in0=kkn.ap, in1=a_.ap, op=ALU.mult), r=[kkn, a_], w=[b_])
        if di == 0:
            P.op("pool", lambda e: e.tensor_copy(out=ksum.ap, in_=kd.ap), r=[kd], w=[ksum])
        else:
            P.op("pool", lambda e: e.tensor_tensor(out=ksum.ap, in0=ksum.ap, in1=kd.ap, op=ALU.add), r=[kd, ksum], w=[ksum])

        def ex(Et, k):
            src_, sc = srcs[k]
            P.op("act", lambda e: e.activation(out=Et.ap, in_=src_.ap, func=AF.Exp, scale=sc), r=[src_], w=[Et])

        ex(Ea, 0)
        P.op("dve", lambda e: e.scalar_tensor_tensor(out=At.ap, in0=kkn.ap, scalar=-1.0, in1=Ea.ap, op0=ALU.mult, op1=ALU.mult), r=[kkn, Ea], w=[At])
        ex(Eb, 1)
        P.op("dve", lambda e: e.tensor_tensor(out=Bt.ap, in0=b_.ap, in1=Eb.ap, op=ALU.mult), r=[b_, Eb], w=[Bt])
        P.op("dve", lambda e: e.tensor_tensor(out=Kt.ap, in0=kd.ap, in1=Eb.ap, op=ALU.mult), r=[kd, Eb], w=[Kt])
        ex(Ea, 2)
        P.op("dve", lambda e: e.tensor_tensor(out=Rt.ap, in0=zr.ap[:, j, :], in1=Ea.ap, op=ALU.mult), r=[(zr, j), Ea], w=[Rt])
        E33 = Ea.ap.rearrange("p (c t) -> p c t", c=8)
        cc_ = 63 if d_ == "f" else 0
        P.op("pool", lambda e: e.tensor_copy(out=gcs[d_].ap[:, j, ck0:ck0 + 8], in_=E33[:, :, cc_]), r=[Ea], w=[gcs[d_]])
        ex(Eb, 3)
        P.op("pool", lambda e: e.tensor_tensor(out=BH.ap, in0=b_.ap, in1=Eb.ap, op=ALU.mult), r=[b_, Eb], w=[BH])
        P.op("pool", lambda e: e.tensor_tensor(out=KH.ap, in0=kd.ap, in1=Eb.ap, op=ALU.mult), r=[kd, Eb], w=[KH])
        for nm, tl in (("A", At), ("B", Bt), ("K", Kt), ("R", Rt)):
            dst = FM[nm + d_][ck0:ck0 + 8].rearrange("c p (j t) -> p c j t", j=NJ)[:, :, j, :]
            P.dma("sp", dst, tl.ap.rearrange("p (c t) -> p c t", c=8), r=[tl])
        transposes_to_tm(lambda b: BH.ap[:, b * 128:(b + 1) * 128], [BH], "BH" + d_, j, t0)
        transposes_to_tm(lambda b: KH.ap[:, b * 128:(b + 1) * 128], [KH], "KH" + d_, j, t0)

    for ti in range(NT):
        do_tile(ti)


def col_eps(env):
    return RMS_EPS


def phase2(nc, P, ar, env):
    NCK, NS, CPS = env["NCK"], env["NS"], env["CPS"]
    FM, TM = env["FM"], env["TM"]
    gcs, cmt, bk, banks, consts_d = env["gcs"], env["cmt"], env["bk"], env["banks"], env["consts_d"]
    mk = ar.alloc("masks", [64, 4, 512])
    P.dma("sp", mk.ap, consts_d[0:64, 0:2048].rearrange("p (m c) -> p m c", m=4), w=[mk])
    mkbd = ar.alloc("mkbd", [128, 4, 128])
    P.dma("sp", mkbd.ap, consts_d[:, 3328:3840].rearrange("p (m c) -> p m c", m=4), w=[mkbd])
    ident = env["ident"]
    bc4 = lambda ap: ap.rearrange("p (o c) -> p o c", o=1).to_broadcast([128, 4, 128])
    S = {}
    for d_ in "fb":
        s = {}
        s["fm"] = [{nm: ar.alloc("fm" + nm + d_ + str(q), [128, NJ, C]) for nm in "ABKR"} for q in range(2)]
        for nm in ("V", "BH", "KH"):
            s[nm] = ar.alloc(nm + d_, [64, D])
        s["ARbd"] = ar.alloc("ARbd" + d_, [128, NJ, 256])
        s["Bbd"] = ar.alloc("Bbd" + d_, [128, NJ, 128])
        s["A_sb"] = ar.alloc("A_bd" + d_, [128, NJ, 128])
        s["BP"] = ar.alloc("BPbd" + d_, [128, NJ, 256])
        s["To"] = ar.alloc("To" + d_, [64, NJ, 64])
        s["ArbT"] = ar.alloc("ArbT" + d_, [64, 16, 64])
        s["AakT"] = ar.alloc("AakT" + d_, [64, 16, 64])
        s["ArkT"] = ar.alloc("ArkT" + d_, [64, 16, 64])
        s["W"] = ar.alloc("W" + d_, [64, D])
        s["U"] = ar.alloc("U" + d_, [64, D])
        s["H"] = ar.alloc("H" + d_, [128, NJ, 128])
        for nm in ("ARbd", "Bbd", "H"):
            t_ = s[nm]
            P.op("pool", lambda e, t_=t_: e.memset(t_.ap, 0.0), w=[t_] + [(t_, o0, pp) for o0 in (0, 128) for pp in (0, 1)])
        S[d_] = s
    print("phase2 arena cols", ar.off)
    Q = [banks[i].ap for i in range(4)]
    QK = [[("ps", 2 * i), ("ps", 2 * i + 1)] for i in range(4)]
    MIDX = {"f": dict(A=2, B=0, R=1), "b": dict(A=0, B=2, R=3)}

    def load(d_, c, q):
        s = S[d_]
        for nm in "ABKR":
            t_ = s["fm"][q][nm]
            P.dma("sp", t_.ap.rearrange("p j t -> p (j t)"), FM[nm + d_][c], w=[t_])
        for nm, src_ in (("V", "V"), ("BH", "BH" + d_), ("KH", "KH" + d_)):
            P.dma("sp", s[nm].ap, TM[src_][c * C:(c + 1) * C, :], w=[s[nm]])

    def scan(d_, c, q):
        s = S[d_]
        fm = s["fm"][q]
        A, B, K, R = fm["A"], fm["B"], fm["K"], fm["R"]
        V, BH, KH, ARbd, Bbd, A_sb, BP, ArbT, AakT, ArkT, W, U, H, To = (s[k] for k in ("V", "BH", "KH", "ARbd", "Bbd", "A_sb", "BP", "ArbT", "AakT", "ArkT", "W", "U", "H", "To"))
        mi = MIDX[d_]
        gc = gcs[d_]
        if d_ == "f" and c % CPS == 0 and c > 0:
            idx = c // CPS
            P.op("pool", lambda e: e.tensor_scalar(out=H.ap, in0=H.ap, scalar1=cmt.ap[:, idx:idx + 1], scalar2=None, op0=ALU.mult), r=[H, cmt], w=[H])
        if d_ == "b" and c % CPS == CPS - 1 and c < NCK - 1:
            idx = NS + c // CPS
            P.op("pool", lambda e: e.tensor_scalar(out=H.ap, in0=H.ap, scalar1=cmt.ap[:, idx:idx + 1], scalar2=None, op0=ALU.mult), r=[H, cmt], w=[H])
        for (dst, src_, o0, eng_) in ((ARbd, A, 0, "pool"), (Bbd, B, 0, "pool"), (ARbd, R, 128, "act")):
            if eng_ == "pool":
                P.op("pool", lambda e, dst=dst, src_=src_, o0=o0: e.tensor_copy(out=dst.ap[0:64, :, o0:o0 + 64], in_=src_.ap[0:64, :, :]), r=[src_], w=[(dst, o0, 0)])
                P.op("pool", lambda e, dst=dst, src_=src_, o0=o0: e.tensor_copy(out=dst.ap[64:128, :, o0 + 64:o0 + 128], in_=src_.ap[64:128, :, :]), r=[src_], w=[(dst, o0, 1)])
            else:
                P.op("act", lambda e, dst=dst, src_=src_, o0=o0: e.activation(out=dst.ap[0:64, :, o0:o0 + 64], in_=src_.ap[0:64, :, :], func=AF.Copy), r=[src_], w=[(dst, o0, 0)])
                P.op("act", lambda e, dst=dst, src_=src_, o0=o0: e.activation(out=dst.ap[64:128, :, o0 + 64:o0 + 128], in_=src_.ap[64:128, :, :], func=AF.Copy), r=[src_], w=[(dst, o0, 1)])
        yield
        for hf in range(2):
            yield
            pA, pAk = bk(4)
            b0, b0k = bk(0)
            b1, b1k = bk(1)
            ARk = [(ARbd, 0, 0), (ARbd, 0, 1)]
            RRk = [(ARbd, 128, 0), (ARbd, 128, 1)]
            Bk = [(Bbd, 0, 0), (Bbd, 0, 1)]
            for jl in range(4):
                jj = 4 * hf + jl
                P.op("pe", lambda e, jl=jl, jj=jj: e.matmul(pA[:, jl * 128:(jl + 1) * 128], lhsT=ARbd.ap[:, jj, 0:128], rhs=Bbd.ap[:, jj, :], start=True, stop=True), r=ARk + Bk, w=[pAk])
                P.op("pe", lambda e, jl=jl, jj=jj: e.matmul(b0[:, jl * 128:(jl + 1) * 128], lhsT=Bbd.ap[:, jj, :], rhs=ARbd.ap[:, jj, 0:128], start=True, stop=True), r=ARk + Bk, w=[b0k])
                P.op("pe", lambda e, jl=jl, jj=jj: e.matmul(b1[0:64, jl * 128:(jl + 1) * 128], lhsT=B.ap[:, jj, :], rhs=ARbd.ap[:, jj, 128:256], start=True, stop=True), r=[B] + RRk, w=[b1k])
                P.op("pe", lambda e, jl=jl, jj=jj: e.matmul(Q[1][0:64, jl * 256:(jl + 1) * 256], lhsT=K.ap[:, jj, :], rhs=ARbd.ap[:, jj, :], start=True, stop=True), r=[K] + ARk + RRk, w=QK[1])
            hs = slice(8 * hf, 8 * hf + 8)
            ps = slice(4 * hf, 4 * hf + 4)
            P.op("dve", lambda e, ps=ps: e.tensor_tensor(out=A_sb.ap[:, ps, :], in0=pA.rearrange("p (j c) -> p j c", j=4), in1=bc4(mkbd.ap[:, mi["A"], :]), op=ALU.mult), r=[pAk, mkbd], w=[A_sb])
            P.op("dve", lambda e, ps=ps: e.tensor_tensor(out=BP.ap[:, ps, 0:128], in0=b0.rearrange("p (j c) -> p j c", j=4), in1=bc4(mkbd.ap[:, mi["B"], :]), op=ALU.mult), r=[b0k, mkbd], w=[BP])
            P.op("pool", lambda e, ps=ps: e.tensor_tensor(out=BP.ap[:, ps, 128:256], in0=BP.ap[:, ps, 0:128], in1=bc4(ident.ap), op=ALU.add), r=[BP, ident], w=[BP])
            P.op("dve", lambda e, hs=hs: e.tensor_tensor(out=ArbT.ap[:, hs, :], in0=b1[0:64, :].rearrange("p (h s) -> p h s", h=8), in1=mk.ap[:, mi["R"], :].rearrange("p (h s) -> p h s", h=8), op=ALU.mult), r=[b1k, mk], w=[ArbT])
            q1v = Q[1][0:64, :].rearrange("p (j q s) -> p j q s", j=4, q=4)
            mB = mk.ap[:, mi["B"], :].rearrange("p (j q s) -> p j q s", j=4, q=2)
            mR = mk.ap[:, mi["R"], :].rearrange("p (j q s) -> p j q s", j=4, q=2)
            v4 = lambda ap: ap.rearrange("p (j q) s -> p j q s", j=4)
            P.op("dve", lambda e, hs=hs, q1v=q1v, mB=mB: e.tensor_tensor(out=v4(AakT.ap[:, hs, :]), in0=q1v[:, :, 0:2, :], in1=mB, op=ALU.mult), r=QK[1] + [mk], w=[AakT])
            P.op("dve", lambda e, hs=hs, q1v=q1v, mR=mR: e.tensor_tensor(out=v4(ArkT.ap[:, hs, :]), in0=q1v[:, :, 2:4, :], in1=mR, op=ALU.mult), r=QK[1] + [mk], w=[ArkT])
        pset = [(Q[0], QK[0], bk(4)), (Q[1], QK[1], bk(5))]
        for st_ in range(6):
            yield
            for hf in range(2):
                pX, pXk, (pY, pYk) = pset[hf]
                ps = slice(4 * hf, 4 * hf + 4)
                for jl in range(4):
                    jj = 4 * hf + jl
                    if st_ == 0:
                        P.op("pe", lambda e, jl=jl, jj=jj, pX=pX: e.matmul(pX[:, jl * 256:jl * 256 + 128], lhsT=A_sb.ap[:, jj, :], rhs=BP.ap[:, jj, 0:128], start=True, stop=True), r=[A_sb, BP], w=pXk)
                    elif st_ < 5:
                        P.op("pe", lambda e, jl=jl, jj=jj, pX=pX: e.matmul(pX[:, jl * 256:(jl + 1) * 256], lhsT=A_sb.ap[:, jj, :], rhs=BP.ap[:, jj, :], start=True, stop=True), r=[A_sb, BP], w=pXk)
                    else:
                        P.op("pe", lambda e, jl=jl, jj=jj, pX=pX: e.matmul(pX[:, jl * 256 + 128:(jl + 1) * 256], lhsT=A_sb.ap[:, jj, :], rhs=BP.ap[:, jj, 128:256], start=True, stop=True), r=[A_sb, BP], w=pXk)
                    if st_ < 5:
                        P.op("pe", lambda e, jl=jl, jj=jj, pY=pY: e.matmul(pY[:, jl * 128:(jl + 1) * 128], lhsT=BP.ap[:, jj, 0:128], rhs=A_sb.ap[:, jj, :], start=True, stop=True), r=[A_sb, BP], w=[pYk])
                pXv = pX.rearrange("p (j c) -> p j c", j=4)
                if st_ < 5:
                    P.op("act", lambda e, ps=ps, pXv=pXv: e.activation(out=BP.ap[:, ps, 0:128], in_=pXv[:, :, 0:128], func=AF.Copy), r=pXk, w=[BP])
                    P.op("act", lambda e, ps=ps, pY=pY: e.activation(out=A_sb.ap[:, ps, :], in_=pY.rearrange("p (j c) -> p j c", j=4), func=AF.Copy), r=[pYk], w=[A_sb])
                if st_ > 0:
                    P.op("dve", lambda e, ps=ps, pXv=pXv: e.tensor_tensor(out=BP.ap[:, ps, 128:256], in0=BP.ap[:, ps, 128:256], in1=pXv[:, :, 128:256], op=ALU.add), r=pXk + [BP], w=[BP])
        yield
        b4, b4k = bk(4)
        P.op("pe", lambda e: e.matmul(b4[0:64, :], lhsT=ident.ap[:, 64:128], rhs=BP.ap[:, :, 192:256], start=True, stop=True), r=[BP, ident], w=[b4k])
        P.op("act", lambda e: e.activation(out=To.ap, in_=b4[0:64, :].rearrange("p (j c) -> p j c", j=NJ), func=AF.Copy), r=[b4k], w=[To])
        yield
        for jj in range(NJ):
            P.op("pe", lambda e, jj=jj: e.matmul(Q[3][0:64, jj * 128:(jj + 1) * 128], lhsT=A.ap[:, jj, :], rhs=H.ap[:, jj, :], start=True, stop=False, skip_group_check=True), r=[A, H], w=QK[3])
            for par in range(2):
                h = 2 * jj + par
                P.op("pe", lambda e, h=h: e.matmul(Q[3][0:64, h * 64:(h + 1) * 64], lhsT=AakT.ap[:, h, :], rhs=V.ap[:, h * 64:(h + 1) * 64], start=False, stop=True, skip_group_check=True), r=[AakT, V], w=QK[3])
        P.op("act", lambda e: e.activation(out=W.ap, in_=Q[3][0:64, :], func=AF.Copy), r=QK[3], w=[W])
        yield
        for h in range(16):
            P.op("pe", lambda e, h=h: e.matmul(Q[3][0:64, h * 64:(h + 1) * 64], lhsT=(BP.ap[0:64, h // 2, 128:192] if h % 2 == 0 else To.ap[:, h // 2, :]), rhs=W.ap[:, h * 64:(h + 1) * 64], start=True, stop=True), r=[BP, To, W], w=QK[3])
        P.op("dve", lambda e: e.tensor_copy(out=U.ap, in_=Q[3][0:64, :]), r=QK[3], w=[U])
        yield
        for jj in range(NJ):
            P.op("pe", lambda e, jj=jj: e.matmul(Q[3][0:64, jj * 128:(jj + 1) * 128], lhsT=R.ap[:, jj, :], rhs=H.ap[:, jj, :], start=True, stop=False, skip_group_check=True), r=[R, H], w=QK[3])
            for par in range(2):
                h = 2 * jj + par
                P.op("pe", lambda e, h=h: e.matmul(Q[3][0:64, h * 64:(h + 1) * 64], lhsT=ArbT.ap[:, h, :], rhs=U.ap[:, h * 64:(h + 1) * 64], start=False, stop=False, skip_group_check=True), r=[ArbT, U], w=QK[3])
                P.op("pe", lambda e, h=h: e.matmul(Q[3][0:64, h * 64:(h + 1) * 64], lhsT=ArkT.ap[:, h, :], rhs=V.ap[:, h * 64:(h + 1) * 64], start=False, stop=True, skip_group_check=True), r=[ArkT, V], w=QK[3])
        P.op("act", lambda e: e.activation(out=W.ap, in_=Q[3][0:64, :], func=AF.Copy), r=QK[3], w=[W])
        P.dma("pool", TM["Y" + d_][c * C:(c + 1) * C, :], W.ap, r=[W])
        yield
        for jj in range(NJ):
            P.op("pe", lambda e, jj=jj: e.matmul(Q[2][:, jj * 128:(jj + 1) * 128], lhsT=BH.ap[:, jj * 128:(jj + 1) * 128], rhs=U.ap[:, jj * 128:(jj + 1) * 128], start=True, stop=False), r=[BH, U], w=QK[2])
            P.op("pe", lambda e, jj=jj: e.matmul(Q[2][:, jj * 128:(jj + 1) * 128], lhsT=KH.ap[:, jj * 128:(jj + 1) * 128], rhs=V.ap[:, jj * 128:(jj + 1) * 128], start=False, stop=True), r=[KH, V], w=QK[2])
        q2v = Q[2].rearrange("p (j c) -> p j c", j=NJ)
        for (p0, c0_) in ((0, 0), (64, 64)):
            hb = H.ap[p0:p0 + 64, :, c0_:c0_ + 64]
            P.op("pool", lambda e, hb=hb, p0=p0: e.tensor_tensor(out=hb, in0=hb, in1=gc.ap[p0:p0 + 64, :, c:c + 1].to_broadcast([64, NJ, 64]), op=ALU.mult), r=[H, gc], w=[H])
            P.op("dve", lambda e, hb=hb, p0=p0, c0_=c0_: e.tensor_tensor(out=hb, in0=hb, in1=q2v[p0:p0 + 64, :, c0_:c0_ + 64], op=ALU.add), r=[H] + QK[2], w=[H])

    from itertools import zip_longest
    load("f", 0, 0)
    load("b", NCK - 1, 0)
    for i in range(NCK):
        q = i % 2
        if i + 1 < NCK:
            for nm in "ABKR":
                for d_, cn in (("f", i + 1), ("b", NCK - 2 - i)):
                    t_ = S[d_]["fm"][1 - q][nm]
                    P.dma("sp", t_.ap.rearrange("p j t -> p (j t)"), FM[nm + d_][cn], w=[t_])
        for _ in zip_longest(scan("f", i, q), scan("b", NCK - 1 - i, q)):
            pass
        if i + 1 < NCK:
            for d_, cn in (("f", i + 1), ("b", NCK - 2 - i)):
                for nm, src_ in (("V", "V"), ("BH", "BH" + d_), ("KH", "KH" + d_)):
                    P.dma("sp", S[d_][nm].ap, TM[src_][cn * C:(cn + 1) * C, :], w=[S[d_][nm]])


def phase3(nc, P, ar, env):
    NT = env["NT"]
    TM, GATE, P2, xpad, out_d, rows_d = env["TM"], env["GATE"], env["P2"], env["xpad"], env["out_d"], env["rows_d"]
    colt, ident, col, bk = env["colt"], env["ident"], env["col"], env["bk"]
    w_pb_d, w_rb_d, w_out_d, w_ff1_d, w_ff2_d = env["w_pb_d"], env["w_rb_d"], env["w_out_d"], env["w_ff1_d"], env["w_ff2_d"]
    rowt = [ar.alloc("row%d" % i, [128, D]) for i in range(3)]
    for i in range(3):
        P.dma("sp", rowt[i].ap, rows_d[i:i + 1, :].partition_broadcast(128), w=[rowt[i]])
    lnw, lnb, gfin = rowt
    L = [ar.alloc("L%d" % i, [128, D]) for i in range(4)]
    W1 = ar.alloc("W1", [128, D])
    W2 = ar.alloc("W2", [128, D])
    rw = [ar.alloc("rw%d" % i, [128, D]) for i in range(4)]
    rwT = ar.alloc("rwT", [128, NJ, TT])
    mT = ar.alloc("mT", [128, NJ, TT])
    gtr = [ar.alloc("gtr%d" % i, [128, TT]) for i in range(2)]
    gtp = [ar.alloc("gtp%d" % i, [128, TT]) for i in range(2)]
    x1 = [ar.alloc("x1%d" % i, [128, D]) for i in range(4)]
    xr = [ar.alloc("xr%d" % i, [128, D]) for i in range(2)]
    h2 = [ar.alloc("h2%d" % i, [128, 256]) for i in range(3)]
    tmpf = [ar.alloc("tmpf%d" % i, [128, 256]) for i in range(2)]
    wrb = [ar.alloc("wrb%d" % i, [128, NJ, 128]) for i in range(2)]
    wpb = [ar.alloc("wpb%d" % i, [128, 4, 128]) for i in range(2)]
    wo = [ar.alloc("wo%d" % i, [128, D]) for i in range(2)]
    wf1 = [ar.alloc("wf1%d" % i, [128, NJ, 128]) for i in range(3)]
    wf2 = [ar.alloc("wf2%d" % i, [128, D]) for i in range(3)]
    stt = ar.alloc("stt", [128, 64])
    print("phase3 arena cols", ar.off)
    h3 = lambda ap: ap.rearrange("p (h c) -> p h c", h=16)
    bc3 = lambda ap: ap.rearrange("p (h o) -> p h o", o=1).to_broadcast([128, 16, 64])

    def do_block(ti, b):
        t0 = ti * TT + b * 128
        for i, nm in enumerate(("Yf", "Yb", "BV", "G")):
            P.dma("sp", L[i].ap, TM[nm][t0:t0 + 128, :], w=[L[i]])
        P.op("dve", lambda e: e.tensor_tensor(out=L[0].ap, in0=L[0].ap, in1=L[1].ap, op=ALU.add), r=[L[0], L[1]], w=[L[0]])
        P.op("dve", lambda e: e.tensor_reduce(out=stt.ap[:, 0:16], in_=h3(L[0].ap), axis=AX.X, op=ALU.add), r=[L[0]], w=[stt])
        P.op("dve", lambda e: e.tensor_scalar(out=stt.ap[:, 16:32], in0=stt.ap[:, 0:16], scalar1=1.0 / 64, scalar2=None, op0=ALU.mult), r=[stt], w=[stt])
        P.op("dve", lambda e: e.tensor_tensor(out=h3(W1.ap), in0=h3(L[0].ap), in1=bc3(stt.ap[:, 16:32]), op=ALU.subtract), r=[L[0], stt], w=[W1])
        P.op("pool", lambda e: e.tensor_tensor(out=W2.ap, in0=W1.ap, in1=W1.ap, op=ALU.mult), r=[W1], w=[W2])
        P.op("dve", lambda e: e.tensor_reduce(out=stt.ap[:, 32:48], in_=h3(W2.ap), axis=AX.X, op=ALU.add), r=[W2], w=[stt])
        P.op("act", lambda e: e.activation(out=stt.ap[:, 48:64], in_=stt.ap[:, 32:48], func=AF.Sqrt, scale=1.0 / 64, bias=gn_eps_ap(env)), r=[stt], w=[stt])
        P.op("dve", lambda e: e.reciprocal(out=stt.ap[:, 48:64], in_=stt.ap[:, 48:64]), r=[stt], w=[stt])
        P.op("dve", lambda e: e.tensor_tensor(out=h3(W1.ap), in0=h3(W1.ap), in1=bc3(stt.ap[:, 48:64]), op=ALU.mult), r=[W1, stt], w=[W1])
        P.op("pool", lambda e: e.tensor_tensor(out=W1.ap, in0=W1.ap, in1=lnw.ap, op=ALU.mult), r=[W1, lnw], w=[W1])
        P.op("pool", lambda e: e.tensor_tensor(out=W1.ap, in0=W1.ap, in1=lnb.ap, op=ALU.add), r=[W1, lnb], w=[W1])
        P.op("dve", lambda e: e.tensor_tensor(out=W1.ap, in0=W1.ap, in1=L[2].ap, op=ALU.add), r=[W1, L[2]], w=[W1])
        P.op("dve", lambda e: e.tensor_tensor(out=rw[b].ap, in0=W1.ap, in1=L[3].ap, op=ALU.mult), r=[W1, L[3]], w=[rw[b]])

    def rstd_of(src, colidx, junk):
        P.op("act", lambda e: e.activation(out=junk.ap, in_=src.ap, func=AF.Square, accum_out=stt.ap[:, colidx:colidx + 1]), r=[src], w=[junk, stt])
        P.op("act", lambda e: e.activation(out=stt.ap[:, colidx:colidx + 1], in_=stt.ap[:, colidx:colidx + 1], func=AF.Sqrt, scale=1.0 / D, bias=RMS_EPS), r=[stt], w=[stt])
        P.op("dve", lambda e: e.reciprocal(out=stt.ap[:, colidx:colidx + 1], in_=stt.ap[:, colidx:colidx + 1]), r=[stt], w=[stt])

    def do_tile(ti):
        t0 = ti * TT
        for b in range(4):
            do_block(ti, b)
        for j in range(NJ):
            bap, bkey = bk(j % 2)
            for b in range(4):
                P.op("pe", lambda e, b=b, j=j, bap=bap: e.transpose(out=bap[:, b * 128:(b + 1) * 128], in_=rw[b].ap[:, j * 128:(j + 1) * 128], identity=ident.ap), r=[rw[b], ident], w=[bkey])
            P.op("act", lambda e, j=j, bap=bap: e.activation(out=rwT.ap[:, j, :], in_=bap, func=AF.Copy), r=[bkey], w=[rwT])
        for g in range(4):
            P.dma("sp", L[g].ap[:, 0:TT], P2[g, :, t0:t0 + TT], w=[L[g]])
        for ec in range(NJ):
            wr, wp, gr, gp = wrb[ec % 2], wpb[ec % 2], gtr[ec % 2], gtp[ec % 2]
            P.dma("sp", wr.ap, w_rb_d[:, :, ec * 128:(ec + 1) * 128], w=[wr])
            P.dma("sp", wp.ap, w_pb_d[:, :, ec * 128:(ec + 1) * 128], w=[wp])
            P.dma("sp", gr.ap, GATE[8 + ec, :, t0:t0 + TT], w=[gr])
            P.dma("sp", gp.ap, GATE[ec, :, t0:t0 + TT], w=[gp])
            bA, kA = bk(ec % 2)
            bB, kB = bk(2 + ec % 2)
            for j in range(NJ):
                P.op("pe", lambda e, j=j, wr=wr, bA=bA: e.matmul(bA, lhsT=wr.ap[:, j, :], rhs=rwT.ap[:, j, :], start=(j == 0), stop=(j == NJ - 1)), r=[wr, rwT], w=[kA])
            for g in range(4):
                P.op("pe", lambda e, g=g, wp=wp, bB=bB: e.matmul(bB, lhsT=wp.ap[:, g, :], rhs=L[g].ap[:, 0:TT], start=(g == 0), stop=(g == 3)), r=[wp, L[g]], w=[kB])
            P.op("dve", lambda e, ec=ec, bA=bA, gr=gr: e.tensor_tensor(out=mT.ap[:, ec, :], in0=bA, in1=gr.ap, op=ALU.mult), r=[kA, gr], w=[(mT, ec)])
            P.op("dve", lambda e, bB=bB, gp=gp: e.tensor_tensor(out=W2.ap[:, 0:TT], in0=bB, in1=gp.ap, op=ALU.mult), r=[kB, gp], w=[W2])
            P.op("pool", lambda e, ec=ec: e.tensor_tensor(out=mT.ap[:, ec, :], in0=mT.ap[:, ec, :], in1=W2.ap[:, 0:TT], op=ALU.add), r=[(mT, ec), W2], w=[(mT, ec)])
        for ec in range(NJ):
            wt = wo[ec % 2]
            P.dma("sp", wt.ap, w_out_d[ec], w=[wt])
            for b in range(4):
                for hh in range(2):
                    bap, bkey = bk(b * 2 + hh)
                    P.op("pe", lambda e, ec=ec, b=b, hh=hh, wt=wt, bap=bap: e.matmul(bap[:, :], lhsT=mT.ap[:, ec, b * 128:(b + 1) * 128], rhs=wt.ap[:, hh * 512:(hh + 1) * 512], start=(ec == 0), stop=(ec == NJ - 1)), r=[wt, (mT, ec)], w=[bkey])
        for b in range(4):
            xt_ = xr[b % 2]
            P.dma("sp", xt_.ap, xpad[8 + t0 + b * 128:8 + t0 + (b + 1) * 128, :], w=[xt_])
            for hh in range(2):
                bap, bkey = bk(b * 2 + hh)
                P.op("dve", lambda e, b=b, hh=hh, bap=bap, xt_=xt_: e.tensor_tensor(out=x1[b].ap[:, hh * 512:(hh + 1) * 512], in0=bap, in1=xt_.ap[:, hh * 512:(hh + 1) * 512], op=ALU.add), r=[bkey, xt_], w=[x1[b]])
        hsb = [W1, W2]
        for sb in range(2):
            for bl in range(2):
                b = 2 * sb + bl
                rstd_of(x1[b], bl, L[0])
                P.op("dve", lambda e, b=b, bl=bl: e.tensor_scalar(out=hsb[bl].ap, in0=x1[b].ap, scalar1=stt.ap[:, bl:bl + 1], scalar2=None, op0=ALU.mult), r=[x1[b], stt], w=[hsb[bl]])
            for j in range(NJ):
                bap, bkey = bk(4 + j % 2)
                for bl in range(2):
                    P.op("pe", lambda e, j=j, bl=bl, bap=bap: e.transpose(out=bap[:, bl * 128:(bl + 1) * 128], in_=hsb[bl].ap[:, j * 128:(j + 1) * 128], identity=ident.ap), r=[hsb[bl], ident], w=[bkey])
                P.op("act", lambda e, j=j, bap=bap: e.activation(out=rwT.ap[:, j, 0:256], in_=bap[:, 0:256], func=AF.Copy, scale=col("g_ffn", j)), r=[bkey, colt], w=[rwT])
            for fc in range(32):
                w1_, w2_ = wf1[fc % 3], wf2[fc % 3]
                P.dma("sp", w1_.ap, w_ff1_d[fc], w=[w1_])
                P.dma("sp", w2_.ap, w_ff2_d[fc], w=[w2_])
                bap, bkey = bk(4 + fc % 2)
                for j in range(NJ):
                    P.op("pe", lambda e, j=j, w1_=w1_, bap=bap: e.matmul(bap[:, 0:256], lhsT=w1_.ap[:, j, :], rhs=rwT.ap[:, j, 0:256], start=(j == 0), stop=(j == NJ - 1)), r=[w1_, rwT], w=[bkey])
                tf, hh_ = tmpf[fc % 2], h2[fc % 3]
                P.op("act", lambda e, bap=bap, tf=tf: e.activation(out=tf.ap, in_=bap[:, 0:256], func=AF.Copy), r=[bkey], w=[tf])
                P.op("dve", lambda e, tf=tf, hh_=hh_: e.scalar_tensor_tensor(out=hh_.ap, in0=tf.ap, scalar=0.0, in1=tf.ap, op0=ALU.max, op1=ALU.mult), r=[tf], w=[hh_])
                for bl in range(2):
                    for hh in range(2):
                        oap, okey = bk(bl * 2 + hh)
                        P.op("pe", lambda e, fc=fc, bl=bl, hh=hh, hh_=hh_, w2_=w2_, oap=oap: e.matmul(oap, lhsT=hh_.ap[:, bl * 128:(bl + 1) * 128], rhs=w2_.ap[:, hh * 512:(hh + 1) * 512], start=(fc == 0), stop=(fc == 31)), r=[hh_, w2_], w=[okey])
            for bl in range(2):
                b = 2 * sb + bl
                for hh in range(2):
                    oap, okey = bk(bl * 2 + hh)
                    P.op("dve", lambda e, b=b, hh=hh, oap=oap: e.tensor_tensor(out=x1[b].ap[:, hh * 512:(hh + 1) * 512], in0=oap, in1=x1[b].ap[:, hh * 512:(hh + 1) * 512], op=ALU.add), r=[okey, x1[b]], w=[x1[b]])
                rstd_of(x1[b], 2 + bl, L[0])
                P.op("dve", lambda e, b=b, bl=bl: e.scalar_tensor_tensor(out=x1[b].ap, in0=x1[b].ap, scalar=stt.ap[:, 2 + bl:3 + bl], in1=gfin.ap, op0=ALU.mult, op1=ALU.mult), r=[x1[b], stt, gfin], w=[x1[b]])
                P.dma("pool", out_d[t0 + b * 128:t0 + (b + 1) * 128, :], x1[b].ap, r=[x1[b]])

    for ti in range(NT):
        do_tile(ti)


def gn_eps_ap(env):
    return GN_EPS


def _colpack(v, n):
    return np.ascontiguousarray(np.asarray(v, np.float32).reshape(n, 128).T)


def prep_shared(inp):
    f = lambda a: np.asarray(a, np.float32)
    cols = np.zeros((128, NCOLS), np.float32)
    src = {"g_mix": inp["g_mix"][0], "b_gate": inp["b_gate"][0], "mu_prev": inp["mu_prev"][0], "mu_next": inp["mu_next"][0],
           "pool_scale": inp["pool_scale"][0], "k_k": inp["k_k"][0], "k_a": inp["k_a"][0], "r_k": f(inp["r_k"][0]).reshape(-1),
           "w0_f": inp["w0_f"][0], "a0_f": inp["a0_f"][0], "w0_b": inp["w0_b"][0], "a0_b": inp["a0_b"][0], "g_ffn": inp["g_ffn"][0]}
    for n, c in COLSPEC:
        if n in src:
            cols[:, COLOFF[n]:COLOFF[n] + c] = _colpack(src[n], c)
    consts = np.zeros((128, 3840), np.float32)
    s = np.arange(64)[:, None]
    t = np.arange(64)[None, :]
    masks = [(s < t), (s <= t), (s > t), (s >= t)]
    for i, m in enumerate(masks):
        consts[0:64, i * 512:(i + 1) * 512] = np.tile(m.astype(np.float32), (1, 8))
    for i, m in enumerate(masks):
        mf = m.astype(np.float32)
        consts[0:64, 3328 + i * 128:3328 + i * 128 + 64] = mf
        consts[64:128, 3328 + i * 128 + 64:3328 + (i + 1) * 128] = mf
    rm = np.ones(512, np.float32)
    rm[::64] = 0.0
    consts[:, 2048:2560] = rm[None, :]
    consts[:, 2560:2688] = np.eye(128, dtype=np.float32)
    b1 = np.zeros((128, 128), np.float32)
    b1[0:64, 0:64] = 1.0
    b1[64:128, 64:128] = 1.0
    consts[:, 2688:2816] = b1
    consts[0:64, 2816:3328] = np.tile(np.eye(64, dtype=np.float32), (1, 8))
    rows = np.stack([f(inp["ln_w"][0]), f(inp["ln_b"][0]), f(inp["g_final"])], 0)
    lora = np.zeros((128, 2, D), np.float32)
    lora[0:64, 0] = inp["w_up_f"][0]
    lora[64:128, 0] = inp["a_up_f"][0]
    lora[0:64, 1] = inp["w_up_b"][0]
    lora[64:128, 1] = inp["a_up_b"][0]
    sh = {
        "cols": cols, "consts": consts, "rows": rows,
        "w_in": np.ascontiguousarray(f(inp["w_in"][0]).reshape(NJ, 128, NCC, 128).transpose(2, 1, 0, 3)),
        "pool_w": np.ascontiguousarray(f(inp["pool_w"][0]).transpose(1, 0, 2)),
        "lora": lora, "g_up": np.ascontiguousarray(f(inp["g_up"][0])),
        "w_pb": np.ascontiguousarray(f(inp["w_pool_br"][0]).reshape(4, 128, D).transpose(1, 0, 2)),
        "w_rb": np.ascontiguousarray(f(inp["w_rwkv_br"][0]).reshape(NJ, 128, D).transpose(1, 0, 2)),
        "w_out": np.ascontiguousarray(f(inp["w_out"][0]).reshape(NJ, 128, D)),
        "w_ff1": np.ascontiguousarray(f(inp["w_ff1"][0]).reshape(NJ, 128, 32, 128).transpose(2, 1, 0, 3)),
        "w_ff2": np.ascontiguousarray(f(inp["w_ff2"][0]).reshape(32, 128, D)),
    }
    return sh


def prep_core(segs, NS, SL):
    N = NS * SL
    NT = N // TT
    xpad = np.zeros((N + 16, D), np.float32)
    segid = np.full(NS, -1, np.int64)
    rcnt = np.ones((4, N), np.float32)
    cover = np.zeros(NS, bool)
    allsegs = list(segs)
    for s0, ns, arr in segs:
        cover[s0:s0 + ns] = True
    for s in range(NS):
        if not cover[s]:
            allsegs.append((s, 1, None))
    for i, (s0, ns, arr) in enumerate(allsegs):
        segid[s0:s0 + ns] = i
        S = ns * SL
        if arr is not None:
            xpad[8 + s0 * SL: 8 + s0 * SL + S] = arr
        pos = np.arange(S)
        for g, w in enumerate((2, 4, 8, 16)):
            lo = np.maximum(pos - w // 2, 0)
            hi = np.minimum(pos + w // 2 - 1, S - 1)
            rcnt[g, s0 * SL:s0 * SL + S] = 1.0 / (hi - lo + 1).astype(np.float32)
    hm = np.ones((16, NT), np.float32)
    for ti in range(NT):
        t0 = ti * TT
        sl = t0 // SL
        if t0 % SL == 0 and (sl == 0 or segid[sl - 1] != segid[sl]):
            hm[0:8, ti] = 0.0
        t1 = t0 + TT
        sl1 = (t1 - 1) // SL
        if t1 % SL == 0 and (sl1 == NS - 1 or segid[sl1 + 1] != segid[sl1]):
            hm[8:16, ti] = 0.0
    cm = np.zeros((128, 2 * NS), np.float32)
    for s in range(NS):
        if s > 0 and segid[s - 1] == segid[s]:
            cm[:, s] = 1.0
        if s < NS - 1 and segid[s + 1] == segid[s]:
            cm[:, NS + s] = 1.0
    return {"xpad": xpad, "hm": hm, "rcnt": rcnt, "cm": cm}


_NC_CACHE = {}


def kernel(**inputs):
    NS, SL = 8, 2048
    xp = np.asarray(inputs["x_prompt"], np.float32)
    xs = np.asarray(inputs["x_sample"], np.float32)
    sh = prep_shared(inputs)
    plan = []
    plan.append([(0, 8, xs[0])])
    plan.append([(0, 8, xs[1])])
    counts = [6, 6, 5, 5, 5, 5]
    nxt = 0
    owners = []
    for c in counts:
        segs = []
        own = []
        for i in range(c):
            segs.append((i, 1, xp[nxt]))
            own.append(nxt)
            nxt += 1
        plan.append(segs)
        owners.append(own)
    in_maps = []
    for segs in plan:
        m = dict(sh)
        m.update(prep_core(segs, NS, SL))
        in_maps.append(m)
    key = (NS, SL)
    if key not in _NC_CACHE:
        _NC_CACHE[key] = build(NS, SL)
    nc = _NC_CACHE[key]
    res = run_bass_kernel_spmd(nc, in_maps, core_ids=list(range(8)))
    y_prompt = np.empty_like(xp)
    y_sample = np.empty_like(xs)
    for c in range(2):
        y_sample[c] = np.asarray(res.results[c]["out"]).reshape(NS * SL, D)
    for ci, own in enumerate(owners):
        o = np.asarray(res.results[2 + ci]["out"]).reshape(NS, SL, D)
        for i, b in enumerate(own):
            y_prompt[b] = o[i]
    return (y_prompt, y_sample)
```

```python
from contextlib import ExitStack
import numpy as np
import concourse.bass as bass
import concourse.mybir as mybir
from concourse.bass_utils import run_bass_kernel_spmd

F32 = mybir.dt.float32
AF = mybir.ActivationFunctionType
ALU = mybir.AluOpType
AX = mybir.AxisListType

D = 1024
NJ = 8
TT = 512
C = 64
IN_COLS = 5888
NCC = IN_COLS // 128
RMS_EPS = 1e-6
GN_EPS = 64e-5
CENGS = ("pe", "act", "dve", "pool")
ENGS = ("pe", "act", "dve", "pool", "sp")
NSLOT = 8
PHASES = 3
INLINE_WAIT = True
EPOCH = 30000


class T:
    def __init__(self, name, ap):
        self.name = name
        self.ap = ap

    def __getitem__(self, k):
        return self.ap[k]


class Prog:
    def __init__(self):
        self.streams = {e: [] for e in ENGS}
        self.cnt = {e: 0 for e in CENGS}
        self.seen = {e: {} for e in ENGS}
        self.lastw = {}
        self.readers = {}
        self.semkeys = {}
        self.dma_slot = {e: 0 for e in ENGS}
        self.dma_val = {}
        self.nops = 0

    @staticmethod
    def _k(x):
        if isinstance(x, T):
            return x.name
        if isinstance(x, tuple) and isinstance(x[0], T):
            return (x[0].name,) + tuple(x[1:])
        return x

    def _need(self, eng, ev):
        if ev is None:
            return
        k, v = ev
        if k[0] == eng and eng == "pe":
            return
        if self.seen[eng].get(k, 0) >= v:
            return
        self.seen[eng][k] = v
        self.streams[eng].append(("wait", k, v))

    def _deps(self, eng, r, w):
        for key in r:
            self._need(eng, self.lastw.get(key))
        for key in w:
            self._need(eng, self.lastw.get(key))
            for ev in list(self.readers.get(key, {}).items()):
                self._need(eng, ev)

    def _commit(self, ev, r, w):
        for key in r:
            d = self.readers.setdefault(key, {})
            d[ev[0]] = max(d.get(ev[0], 0), ev[1])
        for key in w:
            self.lastw[key] = ev
            self.readers[key] = {}

    def op(self, eng, fn, r=(), w=()):
        r = [self._k(x) for x in r]
        w = [self._k(x) for x in w]
        self._deps(eng, r, w)
        self.cnt[eng] += 1
        n = self.cnt[eng]
        ep = (n - 1) // EPOCH
        k = (eng, ep)
        self.semkeys[k] = 1
        self.streams[eng].append(("op", fn, k, 1))
        self._commit((k, n - ep * EPOCH), r, w)
        self.nops += 1

    def dma(self, q, out, in_, r=(), w=()):
        r = [self._k(x) for x in r]
        w = [self._k(x) for x in w]
        self._deps(q, r, w)
        slot = self.dma_slot[q]
        self.dma_slot[q] = (slot + 1) % NSLOT
        k = ("dma", q, slot)
        self.semkeys[k] = 1
        prev = self.dma_val.get(k, 0)
        if prev:
            self._need(q, (k, prev))
        v = prev + 16
        self.dma_val[k] = v
        self.streams[q].append(("op", lambda e, o=out, i=in_: e.dma_start(out=o, in_=i), k, 16))
        self._commit((k, v), r, w)
        self.nops += 1

    def barrier(self):
        evs = []
        for e in CENGS:
            n = self.cnt[e]
            if n:
                ep = (n - 1) // EPOCH
                evs.append(((e, ep), n - ep * EPOCH))
        for k, v in self.dma_val.items():
            evs.append((k, v))
        for e in ENGS:
            for ev in evs:
                self._need(e, ev)

    def emit(self, nc):
        blockname = {"pe": "tensor", "act": "scalar", "dve": "vector", "pool": "gpsimd", "sp": "sync"}
        waited = {}
        for eng in ENGS:
            for it in self.streams[eng]:
                if it[0] == "wait" and it[1][0] != "dma":
                    waited.setdefault(it[1], set()).add(it[2])
        rank = {k: {v: i + 1 for i, v in enumerate(sorted(vs))} for k, vs in waited.items()}
        with ExitStack() as st:
            sems = {}
            for i, k in enumerate(self.semkeys):
                sems[k] = st.enter_context(nc.semaphore("s%d" % i))

            def do_wait(e, w_, ins=None):
                k, v = w_[1], w_[2]
                if k[0] != "dma":
                    v = rank[k][v]
                if ins is None:
                    e.wait_ge(sems[k], v)
                else:
                    ins._wait_ge(sems[k], v)

            with nc.Block() as block:
                for eng in ENGS:
                    items = self.streams[eng]

                    def body(e, items=items):
                        pend = []
                        cnt = {}
                        for it in items:
                            if it[0] == "wait":
                                pend.append(it)
                            else:
                                if INLINE_WAIT and pend:
                                    for w_ in pend[:-1]:
                                        do_wait(e, w_)
                                    ins = it[1](e)
                                    do_wait(e, pend[-1], ins)
                                else:
                                    for w_ in pend:
                                        do_wait(e, w_)
                                    ins = it[1](e)
                                pend = []
                                k = it[2]
                                if k[0] == "dma":
                                    ins.then_inc(sems[k], it[3])
                                else:
                                    n = cnt.get(k, 0) + 1
                                    cnt[k] = n
                                    if n in waited.get(k, ()):
                                        ins.then_inc(sems[k], 1)
                        for w_ in pend:
                            do_wait(e, w_)

                    getattr(block, blockname[eng])(body)


class Arena:
    def __init__(self, nc, st, name, ncols):
        self.t = st.enter_context(nc.sbuf_tensor(name, [128, ncols], F32))
        self.ncols = ncols
        self.off = 0
        self.uid = 0
        self.name = name

    def reset(self):
        self.off = 0

    def alloc(self, name, shape):
        n = int(np.prod(shape[1:]))
        assert self.off + n <= self.ncols, ("SBUF arena overflow", name, self.off + n, self.ncols)
        ap = self.t[0:shape[0], self.off:self.off + n]
        self.off += n
        if len(shape) == 3:
            ap = ap.rearrange("p (a b) -> p a b", a=shape[1])
        elif len(shape) == 4:
            ap = ap.rearrange("p (a b c) -> p a b c", a=shape[1], b=shape[2])
        self.uid += 1
        return T("%s.%s.%d" % (self.name, name, self.uid), ap)


def build(NS, SL, dbg=False):
    N = NS * SL
    NT = N // TT
    NCK = N // C
    CPS = SL // C
    nc = bass.Bass("TRN2", target_bir_lowering=False)
    P = Prog()

    def din(name, shape):
        return nc.dram_tensor(name, list(shape), F32, kind="ExternalInput").ap()

    def dscr(name, shape):
        kind = "ExternalOutput" if dbg else "Internal"
        return nc.dram_tensor(name, list(shape), F32, kind=kind).ap()

    xpad = din("xpad", [N + 16, D])
    hm_d = din("hm", [16, NT])
    rcnt_d = din("rcnt", [4, N])
    cm_d = din("cm", [128, 2 * NS])
    cols_d = din("cols", [128, NCOLS])
    consts_d = din("consts", [128, 3840])
    rows_d = din("rows", [3, D])
    w_in_d = din("w_in", [NCC, 128, NJ, 128])
    pool_w_d = din("pool_w", [128, 4, 128])
    lora_d = din("lora", [128, 2, D])
    g_up_d = din("g_up", [128, D])
    w_pb_d = din("w_pb", [128, 4, D])
    w_rb_d = din("w_rb", [128, NJ, D])
    w_out_d = din("w_out", [NJ, 128, D])
    w_ff1_d = din("w_ff1", [32, 128, NJ, 128])
    w_ff2_d = din("w_ff2", [32, 128, D])
    out_d = nc.dram_tensor("out", [N, D], F32, kind="ExternalOutput").ap()

    FM = {}
    for d_ in "fb":
        for nm in ("A", "B", "K", "R"):
            FM[nm + d_] = dscr("FM_%s%s" % (nm, d_), [NCK, 128, NJ * C])
    TMn = ["V", "BV", "BHf", "KHf", "BHb", "KHb", "G", "Yf", "Yb"]
    TM = {nm: dscr("TM_" + nm, [N, D]) for nm in TMn}
    GATE = dscr("GATE", [16, 128, N])
    P2 = dscr("P2", [4, 128, N])

    with ExitStack() as st:
        ar = Arena(nc, st, "ar", 43000)
        cst = Arena(nc, st, "cst", 2 * 512 + 128 + 128 + NCOLS + 2 * NJ * NCK + 2 * NS + NT + 64)
        banks = [T("bank%d" % i, st.enter_context(nc.psum_tensor("bank%d" % i, [128, 1024], F32))) for i in range(4)]

        def bk(i):
            return banks[i // 2].ap[:, (i % 2) * 512:(i % 2 + 1) * 512], ("ps", i)

        colt = cst.alloc("cols", [128, NCOLS])
        ident = cst.alloc("ident", [128, 128])
        blk1 = cst.alloc("blk1", [128, 128])
        rmask = cst.alloc("rmask", [128, 512])
        gcs = {"f": cst.alloc("gcf", [128, NJ, NCK]), "b": cst.alloc("gcb", [128, NJ, NCK])}
        cmt = cst.alloc("cm", [128, 2 * NS])
        hmt = cst.alloc("hm", [16, NT])
        P.dma("sp", colt.ap, cols_d, w=[colt])
        P.dma("sp", ident.ap, consts_d[:, 2560:2688], w=[ident])
        P.dma("sp", blk1.ap, consts_d[:, 2688:2816], w=[blk1])
        P.dma("sp", rmask.ap, consts_d[:, 2048:2560], w=[rmask])
        P.dma("sp", cmt.ap, cm_d, w=[cmt])
        P.dma("sp", hmt.ap, hm_d, w=[hmt])

        def col(name, j):
            o = COLOFF[name] + j
            return colt.ap[:, o:o + 1]

        for i in range(26):
            P.op("dve", lambda e, i=i: e.tensor_tensor(out=col("c0", i), in0=col("mu_prev", i), in1=col("mu_next", i), op=ALU.add), r=[colt], w=[colt])
        P.op("dve", lambda e: e.tensor_scalar(out=colt.ap[:, COLOFF["c0"]:COLOFF["c0"] + 26], in0=colt.ap[:, COLOFF["c0"]:COLOFF["c0"] + 26], scalar1=-1.0, scalar2=1.0, op0=ALU.mult, op1=ALU.add), r=[colt], w=[colt])
        P.op("dve", lambda e: e.tensor_scalar(out=colt.ap[:, COLOFF["omka"]:COLOFF["omka"] + 8], in0=colt.ap[:, COLOFF["k_a"]:COLOFF["k_a"] + 8], scalar1=-1.0, scalar2=1.0, op0=ALU.mult, op1=ALU.add), r=[colt], w=[colt])

        phase1(nc, P, ar, locals())
        P.barrier()
        ar.reset()
        if PHASES >= 2:
            phase2(nc, P, ar, locals())
            P.barrier()
            ar.reset()
        if PHASES >= 3:
            phase3(nc, P, ar, locals())
            P.barrier()
        P.emit(nc)
    return nc


COLSPEC = [("g_mix", 8), ("b_gate", 16), ("mu_prev", 26), ("mu_next", 26), ("pool_scale", 4), ("k_k", 8), ("k_a", 8),
           ("r_k", 8), ("w0_f", 8), ("a0_f", 8), ("w0_b", 8), ("a0_b", 8), ("g_ffn", 8), ("c0", 26), ("omka", 8)]
COLOFF = {}
_o = 0
for _n, _c in COLSPEC:
    COLOFF[_n] = _o
    _o += _c
NCOLS = _o


def phase1(nc, P, ar, env):
    NT, N, NCK = env["NT"], env["N"], env["NCK"]
    xpad, rcnt_d, w_in_d = env["xpad"], env["rcnt_d"], env["w_in_d"]
    FM, TM, GATE, P2 = env["FM"], env["TM"], env["GATE"], env["P2"]
    colt, ident, blk1, rmask, gcs, hmt = env["colt"], env["ident"], env["blk1"], env["rmask"], env["gcs"], env["hmt"]
    col, bk = env["col"], env["bk"]

    poolw = ar.alloc("poolw", [128, 4, 128])
    lora = ar.alloc("lora", [128, 2, D])
    gup = ar.alloc("gup", [128, D])
    P.dma("sp", poolw.ap, env["pool_w_d"], w=[poolw])
    P.dma("sp", lora.ap, env["lora_d"], w=[lora])
    P.dma("sp", gup.ap, env["g_up_d"], w=[gup])

    xt = [ar.alloc("xt%d" % b, [128, D]) for b in range(4)]
    xh = ar.alloc("xh", [16, D])
    ss = ar.alloc("ss", [128, 8])
    P.op("pool", lambda e: e.memset(ss.ap, 1.0), w=[ss])
    xnT = ar.alloc("xnT", [128, NJ, TT])
    xnTh = ar.alloc("xnTh", [128, NJ, 16])
    wring = [ar.alloc("w%d" % i, [128, NJ, 128]) for i in range(3)]
    zext = [ar.alloc("zext%d" % i, [128, 528]) for i in range(2)]
    zr = ar.alloc("zr", [128, NJ, TT])
    zk = ar.alloc("zk", [128, NJ, TT])
    zv = ar.alloc("zv", [128, NJ, TT])
    twza = ar.alloc("twza", [128, TT])
    sg = ar.alloc("sg", [128, TT])
    rc = ar.alloc("rc", [128, TT])
    pt = [ar.alloc("pt%d" % i, [128, 528]) for i in range(2)]
    p2T = ar.alloc("p2T", [128, TT])
    gst = [ar.alloc("gst%d" % i, [128, TT]) for i in range(2)]
    tmst = [ar.alloc("tmst%d" % i, [128, 4, 128]) for i in range(3)]
    gtm = [ar.alloc("gtm%d" % i, [128, D]) for i in range(1)]
    junk = gtm[0]
    ded = [ar.alloc("ded%d" % i, [128, TT]) for i in range(13)]
    xsl = [T((xnT.name, j), xnT.ap[:, j, :]) for j in range(NJ)]
    xth = [T((xt[b].name, h), xt[b].ap[:, h * 512:(h + 1) * 512]) for b in range(4) for h in range(2)]
    xtk = lambda b: [(xt[b], 0), (xt[b], 1)]
    print("phase1 arena cols", ar.off)
    PT = dict(kk=ded[0], kkn=ded[1], ksum=ded[2], sq=ded[3], rn=ded[4], bv=ded[5])
    DT1 = dict(sgm=xsl[0], a_=xsl[1], lw=xsl[2], PI=xsl[3], X1=xsl[4], X2=xsl[5], X3=xsl[6], Ea=xsl[7], Eb=xth[0], kd=xth[1], b_=xth[2])
    outs = [xth[3], xth[4], xth[5], xth[6], xth[7]] + ded[6:13]
    DT2 = [dict(At=outs[0], Bt=outs[1], Kt=outs[2], Rt=outs[3], BH=outs[4], KH=outs[5]),
           dict(At=outs[6], Bt=outs[7], Kt=outs[8], Rt=outs[9], BH=outs[10], KH=outs[11])]
    ptmp = [zext[0], zext[1]]
    tctr = [0]
    mctr = [0]
    wctr = [0]
    dctr = [0]

    def transposes_to_tm(src_fn, src_keys, name, j, t0):
        bi = 4 + (mctr[0] % 2)
        bap, bkey = bk(bi)
        for b in range(4):
            P.op("pe", lambda e, b=b: e.transpose(out=bap[:, b * 128:(b + 1) * 128], in_=src_fn(b), identity=ident.ap), r=list(src_keys) + [ident], w=[bkey])
        stg = tmst[mctr[0] % 3]
        mctr[0] += 1
        P.op("act", lambda e: e.activation(out=stg.ap, in_=bap.rearrange("p (b c) -> p b c", b=4), func=AF.Copy), r=[bkey], w=[stg])
        dst = TM[name][t0:t0 + TT, j * 128:(j + 1) * 128].rearrange("(b t) c -> t b c", b=4)
        P.dma("pool", dst, stg.ap, r=[stg])

    def do_tile(ti):
        t0 = ti * TT
        for b in range(4):
            P.dma("sp", xt[b].ap, xpad[8 + t0 + b * 128: 8 + t0 + (b + 1) * 128, :], w=xtk(b))
        P.dma("sp", xh.ap[0:8, :], xpad[t0:t0 + 8, :], w=[xh])
        P.dma("sp", xh.ap[8:16, :], xpad[t0 + 520:t0 + 528, :], w=[xh])
        for b in range(4):
            P.op("act", lambda e, b=b: e.activation(out=junk.ap, in_=xt[b].ap, func=AF.Square, accum_out=ss.ap[:, b:b + 1]), r=xtk(b), w=[junk, ss])
        P.op("act", lambda e: e.activation(out=junk.ap[0:16, :], in_=xh.ap, func=AF.Square, accum_out=ss.ap[0:16, 4:5]), r=[xh], w=[junk, ss])
        P.op("act", lambda e: e.activation(out=ss.ap[:, 0:5], in_=ss.ap[:, 0:5], func=AF.Sqrt, scale=1.0 / D, bias=col_eps(env)), r=[ss], w=[ss])
        P.op("dve", lambda e: e.reciprocal(out=ss.ap[:, 0:5], in_=ss.ap[:, 0:5]), r=[ss], w=[ss])
        P.op("dve", lambda e, ti=ti: e.tensor_tensor(out=ss.ap[0:16, 4:5], in0=ss.ap[0:16, 4:5], in1=hmt.ap[0:16, ti:ti + 1], op=ALU.mult), r=[ss, hmt], w=[ss])
        for b in range(4):
            P.op("dve", lambda e, b=b: e.tensor_scalar(out=xt[b].ap, in0=xt[b].ap, scalar1=ss.ap[:, b:b + 1], scalar2=None, op0=ALU.mult), r=xtk(b) + [ss], w=xtk(b))
        P.op("dve", lambda e: e.tensor_scalar(out=xh.ap, in0=xh.ap, scalar1=ss.ap[0:16, 4:5], scalar2=None, op0=ALU.mult), r=[xh, ss], w=[xh])
        for j in range(NJ):
            bap, bkey = bk(j % 2)
            for b in range(4):
                P.op("pe", lambda e, b=b, j=j, bap=bap: e.transpose(out=bap[:, b * 128:(b + 1) * 128], in_=xt[b].ap[:, j * 128:(j + 1) * 128], identity=ident.ap), r=xtk(b) + [ident], w=[bkey])
            P.op("act", lambda e, j=j, bap=bap: e.activation(out=xnT.ap[:, j, :], in_=bap, func=AF.Copy, scale=col("g_mix", j)), r=[bkey, colt], w=[(xnT, j)])
            hap, hkey = bk(2 + j % 2)
            P.op("pe", lambda e, j=j, hap=hap: e.transpose(out=hap[:, 0:16], in_=xh.ap[0:16, j * 128:(j + 1) * 128], identity=ident.ap[0:16, 0:16]), r=[xh, ident], w=[hkey])
            P.op("act", lambda e, j=j, hap=hap: e.activation(out=xnTh.ap[:, j, :], in_=hap[:, 0:16], func=AF.Copy, scale=col("g_mix", j)), r=[hkey, colt], w=[xnTh])
        xkeys = [(xnT, j) for j in range(NJ)]

        def zchunk(cc, halo):
            wt = wring[wctr[0] % 3]
            wctr[0] += 1
            P.dma("sp", wt.ap, w_in_d[cc], w=[wt])
            bi = wctr[0] % 2
            bap, bkey = bk(bi)
            for j in range(NJ):
                P.op("pe", lambda e, j=j: e.matmul(bap, lhsT=wt.ap[:, j, :], rhs=xnT.ap[:, j, :], start=(j == 0), stop=(j == NJ - 1)), r=[wt, (xnT, j)], w=[bkey])
            hap = hkey = None
            if halo:
                hap, hkey = bk(2 + bi)
                for j in range(NJ):
                    P.op("pe", lambda e, j=j: e.matmul(hap[:, 0:16], lhsT=wt.ap[:, j, :], rhs=xnTh.ap[:, j, :], start=(j == 0), stop=(j == NJ - 1)), r=[wt, xnTh], w=[hkey])
            return bap, bkey, hap, hkey

        def to_zext(cc):
            bap, bkey, hap, hkey = zchunk(cc, True)
            ze = zext[cc % 2]
            P.op("act", lambda e: e.activation(out=ze.ap[:, 8:520], in_=bap, func=AF.Copy), r=[bkey], w=[ze])
            P.op("dve", lambda e: e.tensor_copy(out=ze.ap[:, 0:8], in_=hap[:, 0:8]), r=[hkey], w=[ze])
            P.op("dve", lambda e: e.tensor_copy(out=ze.ap[:, 520:528], in_=hap[:, 8:16]), r=[hkey], w=[ze])
            return ze

        for g in range(4):
            ze = to_zext(g)
            P.dma("sp", rc.ap, rcnt_d[g:g + 1, t0:t0 + TT].partition_broadcast(128), w=[rc])
            cur, L = ze, 528
            sh = 1
            for lev in range(g + 1):
                nxt = pt[lev % 2]
                L2 = L - sh
                P.op("pool", lambda e, cur=cur, nxt=nxt, L2=L2, sh=sh: e.tensor_tensor(out=nxt.ap[:, 0:L2], in0=cur.ap[:, 0:L2], in1=cur.ap[:, sh:sh + L2], op=ALU.add), r=[cur], w=[nxt])
                cur, L, sh = nxt, L2, sh * 2
            w2 = 1 << g
            pl = ded[g % 2]
            P.op("pool", lambda e, cur=cur, w2=w2, pl=pl: e.tensor_tensor(out=pl.ap, in0=cur.ap[:, 8 - w2:8 - w2 + TT], in1=rc.ap, op=ALU.mult), r=[cur, rc], w=[pl])
            P.op("pool", lambda e, pl=pl, ze=ze: e.tensor_tensor(out=pl.ap, in0=pl.ap, in1=ze.ap[:, 8:520], op=ALU.subtract), r=[pl, ze], w=[pl])
            bap, bkey = bk(4)
            P.op("pe", lambda e, g=g, pl=pl: e.matmul(bap, lhsT=poolw.ap[:, g, :], rhs=pl.ap, start=True, stop=True), r=[poolw, pl], w=[bkey])
            P.op("act", lambda e, g=g: e.activation(out=p2T.ap, in_=bap, func=AF.Copy, scale=col("pool_scale", g)), r=[bkey, colt], w=[p2T])
            P.dma("pool", P2[g, :, t0:t0 + TT], p2T.ap, r=[p2T])

        def mix(ze, i, dst_ap, dkeys):
            t1 = ded[2 + (i % 2)]
            P.op("dve", lambda e: e.tensor_scalar(out=t1.ap, in0=ze.ap[:, 8:520], scalar1=col("c0", i), scalar2=None, op0=ALU.mult), r=[ze, colt], w=[t1])
            P.op("dve", lambda e: e.scalar_tensor_tensor(out=t1.ap, in0=ze.ap[:, 7:519], scalar=col("mu_prev", i), in1=t1.ap, op0=ALU.mult, op1=ALU.add), r=[ze, colt, t1], w=[t1])
            P.op("dve", lambda e: e.scalar_tensor_tensor(out=dst_ap, in0=ze.ap[:, 9:521], scalar=col("mu_next", i), in1=t1.ap, op0=ALU.mult, op1=ALU.add), r=[ze, colt, t1], w=dkeys)

        for i in range(26):
            ze = to_zext(4 + i)
            if i < 8:
                mix(ze, i, zr.ap[:, i, :], [(zr, i)])
            elif i < 16:
                mix(ze, i, zk.ap[:, i - 8, :], [(zk, i - 8)])
            elif i < 24:
                mix(ze, i, zv.ap[:, i - 16, :], [(zv, i - 16)])
            elif i == 24:
                mix(ze, i, twza.ap, [twza])
                P.op("act", lambda e: e.activation(out=twza.ap[0:64, :], in_=twza.ap[0:64, :], func=AF.Tanh), r=[twza], w=[twza])
            else:
                mix(ze, i, sg.ap, [sg])
                P.op("act", lambda e: e.activation(out=sg.ap, in_=sg.ap, func=AF.Sigmoid), r=[sg], w=[sg])

        for i in range(16):
            bap, bkey, _, _ = zchunk(30 + i, False)
            gt = gst[i % 2]
            P.op("act", lambda e, i=i, bap=bap, gt=gt: e.activation(out=gt.ap, in_=bap, func=AF.Sigmoid, bias=col("b_gate", i)), r=[bkey, colt], w=[gt])
            P.dma("pool", GATE[i, :, t0:t0 + TT], gt.ap, r=[gt])

        for b in range(4):
            gt_ = gtm[0]
            for hh in range(2):
                bap, bkey = bk(6 + hh)
                P.op("pe", lambda e, b=b, hh=hh, bap=bap: e.matmul(bap, lhsT=sg.ap[:, b * 128:(b + 1) * 128], rhs=gup.ap[:, hh * 512:(hh + 1) * 512], start=True, stop=True), r=[sg, gup], w=[bkey])
                P.op("act", lambda e, hh=hh, bap=bap, gt_=gt_: e.activation(out=gt_.ap[:, hh * 512:(hh + 1) * 512], in_=bap, func=AF.Copy), r=[bkey], w=[gt_])
            P.dma("pool", TM["G"][t0 + b * 128:t0 + (b + 1) * 128, :], gt_.ap, r=[gt_])

        ck0 = ti * (TT // C)
        for j in range(NJ):
            do_pair(j, t0, ck0)

    def do_pair(j, t0, ck0):
        kk, sq, rn, kkn, ksum, bv = PT["kk"], PT["sq"], PT["rn"], PT["kkn"], PT["ksum"], PT["bv"]
        P.op("dve", lambda e: e.tensor_scalar(out=kk.ap, in0=zk.ap[:, j, :], scalar1=col("k_k", j), scalar2=None, op0=ALU.mult), r=[(zk, j), colt], w=[kk])
        P.op("pool", lambda e: e.tensor_tensor(out=sq.ap, in0=kk.ap, in1=kk.ap, op=ALU.mult), r=[kk], w=[sq])
        bap, bkey = bk(6)
        P.op("pe", lambda e: e.matmul(bap, lhsT=blk1.ap, rhs=sq.ap, start=True, stop=True), r=[blk1, sq], w=[bkey])
        P.op("act", lambda e: e.activation(out=rn.ap, in_=bap, func=AF.Sqrt), r=[bkey], w=[rn])
        P.op("dve", lambda e: e.tensor_scalar(out=rn.ap, in0=rn.ap, scalar1=1e-12, scalar2=None, op0=ALU.max), r=[rn], w=[rn])
        P.op("dve", lambda e: e.reciprocal(out=rn.ap, in_=rn.ap), r=[rn], w=[rn])
        P.op("dve", lambda e: e.tensor_tensor(out=kkn.ap, in0=kk.ap, in1=rn.ap, op=ALU.mult), r=[kk, rn], w=[kkn])
        for di, d_ in enumerate("fb"):
            do_dir(j, t0, ck0, di, d_, kkn, ksum)
        t1 = sq
        P.op("dve", lambda e: e.scalar_tensor_tensor(out=t1.ap, in0=ksum.ap, scalar=col("r_k", j), in1=zr.ap[:, j, :], op0=ALU.mult, op1=ALU.mult), r=[ksum, colt, (zr, j)], w=[t1])
        b7ap, b7key = bk(7)
        P.op("pe", lambda e: e.matmul(b7ap, lhsT=blk1.ap, rhs=t1.ap, start=True, stop=True), r=[blk1, t1], w=[b7key])
        P.op("dve", lambda e: e.tensor_tensor(out=bv.ap, in0=b7ap, in1=zv.ap[:, j, :], op=ALU.mult), r=[b7key, (zv, j)], w=[bv])
        transposes_to_tm(lambda b: bv.ap[:, b * 128:(b + 1) * 128], [bv], "BV", j, t0)
        transposes_to_tm(lambda b: zv.ap[:, j, b * 128:(b + 1) * 128], [(zv, j)], "V", j, t0)

    def do_dir(j, t0, ck0, di, d_, kkn, ksum):
        g_ = DT1
        sgm, a_, lw, PI, X1, X2, X3, Ea, Eb, kd, b_ = (g_[k] for k in ("sgm", "a_", "lw", "PI", "X1", "X2", "X3", "Ea", "Eb", "kd", "b_"))
        o_ = DT2[dctr[0] % 2]
        dctr[0] += 1
        At, Bt, Kt, Rt, BH, KH = (o_[k] for k in ("At", "Bt", "Kt", "Rt", "BH", "KH"))
        b1ap, b1key = bk(7)
        P.op("pe", lambda e: e.matmul(b1ap, lhsT=lora.ap[0:64, di, j * 128:(j + 1) * 128], rhs=twza.ap[0:64, :], start=True, stop=True), r=[lora, twza], w=[b1key])
        P.op("act", lambda e: e.activation(out=sgm.ap, in_=b1ap, func=AF.Sigmoid, bias=col("w0_" + d_, j)), r=[b1key, colt], w=[sgm])
        b2ap, b2key = bk(6)
        P.op("pe", lambda e: e.matmul(b2ap, lhsT=lora.ap[64:128, di, j * 128:(j + 1) * 128], rhs=twza.ap[64:128, :], start=True, stop=True), r=[lora, twza], w=[b2key])
        P.op("act", lambda e: e.activation(out=a_.ap, in_=b2ap, func=AF.Sigmoid, bias=col("a0_" + d_, j)), r=[b2key, colt], w=[a_])
        lw = sgm
        K0 = -0.6065306597126334
        P.op("dve", lambda e: e.tensor_tensor_scan(out=PI.ap, data0=rmask.ap, data1=lw.ap, initial=0.0, op0=ALU.mult, op1=ALU.add), r=[rmask, lw], w=[PI])
        P.op("pool", lambda e: e.tensor_tensor(out=X1.ap, in0=PI.ap, in1=lw.ap, op=ALU.subtract), r=[PI, lw], w=[X1])
        PI3 = PI.ap.rearrange("p (c t) -> p c t", c=8)
        P.op("dve", lambda e: e.tensor_tensor(out=X2.ap.rearrange("p (c t) -> p c t", c=8), in0=PI3[:, :, 63:64].to_broadcast([128, 8, 64]), in1=PI3, op=ALU.subtract), r=[PI], w=[X2])
        if d_ == "f":
            srcs = [(X1, K0), (PI, -K0), (PI, K0), (X2, K0)]
        else:
            P.op("pool", lambda e: e.tensor_tensor(out=X3.ap, in0=X2.ap, in1=lw.ap, op=ALU.add), r=[X2, lw], w=[X3])
            srcs = [(X2, K0), (X3, -K0), (X3, K0), (X1, K0)]
        P.op("dve", lambda e: e.tensor_scalar(out=kd.ap, in0=a_.ap, scalar1=col("k_a", j), scalar2=col("omka", j), op0=ALU.mult, op1=ALU.add), r=[a_, colt], w=[kd])
        P.op("dve", lambda e: e.tensor_tensor(out=kd.ap, in0=kd.ap, in1=zk.ap[:, j, :], op=ALU.mult), r=[kd, (zk, j)], w=[kd])
        P.op("pool", lambda e: e.tensor_tensor(out=b_.ap, in0=kkn.ap, in1=a_.ap, op=ALU.mult), r=[kkn, a_], w=[b_])
        if di == 0:
            P.op("pool", lambda e: e.tensor_copy(out=ksum.ap, in_=kd.ap), r=[kd], w=[ksum])
        else:
            P.op("pool", lambda e: e.tensor_tensor(out=ksum.ap, in0=ksum.ap, in1=kd.ap, op=ALU.add), r=[kd, ksum], w=[ksum])

        def ex(Et, k):
            src_, sc = srcs[k]
            P.op("act", lambda e: e.activation(out=Et.ap, in_=src_.ap, func=AF.Exp, scale=sc), r=[src_], w=[Et])

        ex(Ea, 0)
        P.op("dve", lambda e: e.scalar_tensor_tensor(out=At.ap, in0=kkn.ap, scalar=-1.0, in1=Ea.ap, op0=ALU.mult, op1=ALU.mult), r=[kkn, Ea], w=[At])
        ex(Eb, 1)
        P.op("dve", lambda e: e.tensor_tensor(out=Bt.ap, in0=b_.ap, in1=Eb.ap, op=ALU.mult), r=[b_, Eb], w=[Bt])
        P.op("dve", lambda e: e.tensor_tensor(out=Kt.ap, in0=kd.ap, in1=Eb.ap, op=ALU.mult), r=[kd, Eb], w=[Kt])
        ex(Ea, 2)
        P.op("dve", lambda e: e.tensor_tensor(out=Rt.ap, in0=zr.ap[:, j, :], in1=Ea.ap, op=ALU.mult), r=[(zr, j), Ea], w=[Rt])
        E33 = Ea.ap.rearrange("p (c t) -> p c t", c=8)
        cc_ = 63 if d_ == "f" else 0
        P.op("pool", lambda e: e.tensor_copy(out=gcs[d_].ap[:, j, ck0:ck0 + 8], in_=E33[:, :, cc_]), r=[Ea], w=[gcs[d_]])
        ex(Eb, 3)
        P.op("pool", lambda e: e.tensor_tensor(out=BH.ap, in0=b_.ap, in1=Eb.ap, op=ALU.mult), r=[b_, Eb], w=[BH])
        P.op("pool", lambda e: e.tensor_tensor(out=KH.ap, in0=kd.ap, in1=Eb.ap, op=ALU.mult), r=[kd, Eb], w=[KH])
        for nm, tl in (("A", At), ("B", Bt), ("K", Kt), ("R", Rt)):
            dst = FM[nm + d_][ck0:ck0 + 8].rearrange("c p (j t) -> p c j t", j=NJ)[:, :, j, :]
            P.dma("sp", dst, tl.ap.rearrange("p (c t) -> p c t", c=8), r=[tl])
        transposes_to_tm(lambda b: BH.ap[:, b * 128:(b + 1) * 128], [BH], "BH" + d_, j, t0)
        transposes_to_tm(lambda b: KH.ap[:, b * 128:(b + 1) * 128], [KH], "KH" + d_, j, t0)

    for ti in range(NT):
        do_tile(ti)


def col_eps(env):
    return RMS_EPS


def phase2(nc, P, ar, env):
    NCK, NS, CPS = env["NCK"], env["NS"], env["CPS"]
    FM, TM = env["FM"], env["TM"]
    gcs, cmt, bk, banks, consts_d = env["gcs"], env["cmt"], env["bk"], env["banks"], env["consts_d"]
    mk = ar.alloc("masks", [64, 4, 512])
    P.dma("sp", mk.ap, consts_d[0:64, 0:2048].rearrange("p (m c) -> p m c", m=4), w=[mk])
    mkbd = ar.alloc("mkbd", [128, 4, 128])
    P.dma("sp", mkbd.ap, consts_d[:, 3328:3840].rearrange("p (m c) -> p m c", m=4), w=[mkbd])
    ident = env["ident"]
    bc4 = lambda ap: ap.rearrange("p (o c) -> p o c", o=1).to_broadcast([128, 4, 128])
    S = {}
    for d_ in "fb":
        s = {}
        s["fm"] = [{nm: ar.alloc("fm" + nm + d_ + str(q), [128, NJ, C]) for nm in "ABKR"} for q in range(2)]
        for nm in ("V", "BH", "KH"):
            s[nm] = ar.alloc(nm + d_, [64, D])
        s["ARbd"] = ar.alloc("ARbd" + d_, [128, NJ, 256])
        s["Bbd"] = ar.alloc("Bbd" + d_, [128, NJ, 128])
        s["A_sb"] = ar.alloc("A_bd" + d_, [128, NJ, 128])
        s["BP"] = ar.alloc("BPbd" + d_, [128, NJ, 256])
        s["To"] = ar.alloc("To" + d_, [64, NJ, 64])
        s["ArbT"] = ar.alloc("ArbT" + d_, [64, 16, 64])
        s["AakT"] = ar.alloc("AakT" + d_, [64, 16, 64])
        s["ArkT"] = ar.alloc("ArkT" + d_, [64, 16, 64])
        s["W"] = ar.alloc("W" + d_, [64, D])
        s["U"] = ar.alloc("U" + d_, [64, D])
        s["H"] = ar.alloc("H" + d_, [128, NJ, 128])
        for nm in ("ARbd", "Bbd", "H"):
            t_ = s[nm]
            P.op("pool", lambda e, t_=t_: e.memset(t_.ap, 0.0), w=[t_] + [(t_, o0, pp) for o0 in (0, 128) for pp in (0, 1)])
        S[d_] = s
    print("phase2 arena cols", ar.off)
    Q = [banks[i].ap for i in range(4)]
    QK = [[("ps", 2 * i), ("ps", 2 * i + 1)] for i in range(4)]
    MIDX = {"f": dict(A=2, B=0, R=1), "b": dict(A=0, B=2, R=3)}

    def load(d_, c, q):
        s = S[d_]
        for nm in "ABKR":
            t_ = s["fm"][q][nm]
            P.dma("sp", t_.ap.rearrange("p j t -> p (j t)"), FM[nm + d_][c], w=[t_])
        for nm, src_ in (("V", "V"), ("BH", "BH" + d_), ("KH", "KH" + d_)):
            P.dma("sp", s[nm].ap, TM[src_][c * C:(c + 1) * C, :], w=[s[nm]])

    def scan(d_, c, q):
        s = S[d_]
        fm = s["fm"][q]
        A, B, K, R = fm["A"], fm["B"], fm["K"], fm["R"]
        V, BH, KH, ARbd, Bbd, A_sb, BP, ArbT, AakT, ArkT, W, U, H, To = (s[k] for k in ("V", "BH", "KH", "ARbd", "Bbd", "A_sb", "BP", "ArbT", "AakT", "ArkT", "W", "U", "H", "To"))
        mi = MIDX[d_]
        gc = gcs[d_]
        if d_ == "f" and c % CPS == 0 and c > 0:
            idx = c // CPS
            P.op("pool", lambda e: e.tensor_scalar(out=H.ap, in0=H.ap, scalar1=cmt.ap[:, idx:idx + 1], scalar2=None, op0=ALU.mult), r=[H, cmt], w=[H])
        if d_ == "b" and c % CPS == CPS - 1 and c < NCK - 1:
            idx = NS + c // CPS
            P.op("pool", lambda e: e.tensor_scalar(out=H.ap, in0=H.ap, scalar1=cmt.ap[:, idx:idx + 1], scalar2=None, op0=ALU.mult), r=[H, cmt], w=[H])
        for (dst, src_, o0, eng_) in ((ARbd, A, 0, "pool"), (Bbd, B, 0, "pool"), (ARbd, R, 128, "act")):
            if eng_ == "pool":
                P.op("pool", lambda e, dst=dst, src_=src_, o0=o0: e.tensor_copy(out=dst.ap[0:64, :, o0:o0 + 64], in_=src_.ap[0:64, :, :]), r=[src_], w=[(dst, o0, 0)])
                P.op("pool", lambda e, dst=dst, src_=src_, o0=o0: e.tensor_copy(out=dst.ap[64:128, :, o0 + 64:o0 + 128], in_=src_.ap[64:128, :, :]), r=[src_], w=[(dst, o0, 1)])
            else:
                P.op("act", lambda e, dst=dst, src_=src_, o0=o0: e.activation(out=dst.ap[0:64, :, o0:o0 + 64], in_=src_.ap[0:64, :, :], func=AF.Copy), r=[src_], w=[(dst, o0, 0)])
                P.op("act", lambda e, dst=dst, src_=src_, o0=o0: e.activation(out=dst.ap[64:128, :, o0 + 64:o0 + 128], in_=src_.ap[64:128, :, :], func=AF.Copy), r=[src_], w=[(dst, o0, 1)])
        yield
        for hf in range(2):
            yield
            pA, pAk = bk(4)
            b0, b0k = bk(0)
            b1, b1k = bk(1)
            ARk = [(ARbd, 0, 0), (ARbd, 0, 1)]
            RRk = [(ARbd, 128, 0), (ARbd, 128, 1)]
            Bk = [(Bbd, 0, 0), (Bbd, 0, 1)]
            for jl in range(4):
                jj = 4 * hf + jl
                P.op("pe", lambda e, jl=jl, jj=jj: e.matmul(pA[:, jl * 128:(jl + 1) * 128], lhsT=ARbd.ap[:, jj, 0:128], rhs=Bbd.ap[:, jj, :], start=True, stop=True), r=ARk + Bk, w=[pAk])
                P.op("pe", lambda e, jl=jl, jj=jj: e.matmul(b0[:, jl * 128:(jl + 1) * 128], lhsT=Bbd.ap[:, jj, :], rhs=ARbd.ap[:, jj, 0:128], start=True, stop=True), r=ARk + Bk, w=[b0k])
                P.op("pe", lambda e, jl=jl, jj=jj: e.matmul(b1[0:64, jl * 128:(jl + 1) * 128], lhsT=B.ap[:, jj, :], rhs=ARbd.ap[:, jj, 128:256], start=True, stop=True), r=[B] + RRk, w=[b1k])
                P.op("pe", lambda e, jl=jl, jj=jj: e.matmul(Q[1][0:64, jl * 256:(jl + 1) * 256], lhsT=K.ap[:, jj, :], rhs=ARbd.ap[:, jj, :], start=True, stop=True), r=[K] + ARk + RRk, w=QK[1])
            hs = slice(8 * hf, 8 * hf + 8)
            ps = slice(4 * hf, 4 * hf + 4)
            P.op("dve", lambda e, ps=ps: e.tensor_tensor(out=A_sb.ap[:, ps, :], in0=pA.rearrange("p (j c) -> p j c", j=4), in1=bc4(mkbd.ap[:, mi["A"], :]), op=ALU.mult), r=[pAk, mkbd], w=[A_sb])
            P.op("dve", lambda e, ps=ps: e.tensor_tensor(out=BP.ap[:, ps, 0:128], in0=b0.rearrange("p (j c) -> p j c", j=4), in1=bc4(mkbd.ap[:, mi["B"], :]), op=ALU.mult), r=[b0k, mkbd], w=[BP])
            P.op("pool", lambda e, ps=ps: e.tensor_tensor(out=BP.ap[:, ps, 128:256], in0=BP.ap[:, ps, 0:128], in1=bc4(ident.ap), op=ALU.add), r=[BP, ident], w=[BP])
            P.op("dve", lambda e, hs=hs: e.tensor_tensor(out=ArbT.ap[:, hs, :], in0=b1[0:64, :].rearrange("p (h s) -> p h s", h=8), in1=mk.ap[:, mi["R"], :].rearrange("p (h s) -> p h s", h=8), op=ALU.mult), r=[b1k, mk], w=[ArbT])
            q1v = Q[1][0:64, :].rearrange("p (j q s) -> p j q s", j=4, q=4)
            mB = mk.ap[:, mi["B"], :].rearrange("p (j q s) -> p j q s", j=4, q=2)
            mR = mk.ap[:, mi["R"], :].rearrange("p (j q s) -> p j q s", j=4, q=2)
            v4 = lambda ap: ap.rearrange("p (j q) s -> p j q s", j=4)
            P.op("dve", lambda e, hs=hs, q1v=q1v, mB=mB: e.tensor_tensor(out=v4(AakT.ap[:, hs, :]), in0=q1v[:, :, 0:2, :], in1=mB, op=ALU.mult), r=QK[1] + [mk], w=[AakT])
            P.op("dve", lambda e, hs=hs, q1v=q1v, mR=mR: e.tensor_tensor(out=v4(ArkT.ap[:, hs, :]), in0=q1v[:, :, 2:4, :], in1=mR, op=ALU.mult), r=QK[1] + [mk], w=[ArkT])
        pset = [(Q[0], QK[0], bk(4)), (Q[1], QK[1], bk(5))]
        for st_ in range(6):
            yield
            for hf in range(2):
                pX, pXk, (pY, pYk) = pset[hf]
                ps = slice(4 * hf, 4 * hf + 4)
                for jl in range(4):
                    jj = 4 * hf + jl
                    if st_ == 0:
                        P.op("pe", lambda e, jl=jl, jj=jj, pX=pX: e.matmul(pX[:, jl * 256:jl * 256 + 128], lhsT=A_sb.ap[:, jj, :], rhs=BP.ap[:, jj, 0:128], start=True, stop=True), r=[A_sb, BP], w=pXk)
                    elif st_ < 5:
                        P.op("pe", lambda e, jl=jl, jj=jj, pX=pX: e.matmul(pX[:, jl * 256:(jl + 1) * 256], lhsT=A_sb.ap[:, jj, :], rhs=BP.ap[:, jj, :], start=True, stop=True), r=[A_sb, BP], w=pXk)
                    else:
                        P.op("pe", lambda e, jl=jl, jj=jj, pX=pX: e.matmul(pX[:, jl * 256 + 128:(jl + 1) * 256], lhsT=A_sb.ap[:, jj, :], rhs=BP.ap[:, jj, 128:256], start=True, stop=True), r=[A_sb, BP], w=pXk)
                    if st_ < 5:
                        P.op("pe", lambda e, jl=jl, jj=jj, pY=pY: e.matmul(pY[:, jl * 128:(jl + 1) * 128], lhsT=BP.ap[:, jj, 0:128], rhs=A_sb.ap[:, jj, :], start=True, stop=True), r=[A_sb, BP], w=[pYk])
                pXv = pX.rearrange("p (j c) -> p j c", j=4)
                if st_ < 5:
                    P.op("act", lambda e, ps=ps, pXv=pXv: e.activation(out=BP.ap[:, ps, 0:128], in_=pXv[:, :, 0:128], func=AF.Copy), r=pXk, w=[BP])
                    P.op("act", lambda e, ps=ps, pY=pY: e.activation(out=A_sb.ap[:, ps, :], in_=pY.rearrange("p (j c) -> p j c", j=4), func=AF.Copy), r=[pYk], w=[A_sb])
                if st_ > 0:
                    P.op("dve", lambda e, ps=ps, pXv=pXv: e.tensor_tensor(out=BP.ap[:, ps, 128:256], in0=BP.ap[:, ps, 128:256], in1=pXv[:, :, 128:256], op=ALU.add), r=pXk + [BP], w=[BP])
        yield
        b4, b4k = bk(4)
        P.op("pe", lambda e: e.matmul(b4[0:64, :], lhsT=ident.ap[:, 64:128], rhs=BP.ap[:, :, 192:256], start=True, stop=True), r=[BP, ident], w=[b4k])
        P.op("act", lambda e: e.activation(out=To.ap, in_=b4[0:64, :].rearrange("p (j c) -> p j c", j=NJ), func=AF.Copy), r=[b4k], w=[To])
        yield
        for jj in range(NJ):
            P.op("pe", lambda e, jj=jj: e.matmul(Q[3][0:64, jj * 128:(jj + 1) * 128], lhsT=A.ap[:, jj, :], rhs=H.ap[:, jj, :], start=True, stop=False, skip_group_check=True), r=[A, H], w=QK[3])
            for par in range(2):
                h = 2 * jj + par
                P.op("pe", lambda e, h=h: e.matmul(Q[3][0:64, h * 64:(h + 1) * 64], lhsT=AakT.ap[:, h, :], rhs=V.ap[:, h * 64:(h + 1) * 64], start=False, stop=True, skip_group_check=True), r=[AakT, V], w=QK[3])
        P.op("act", lambda e: e.activation(out=W.ap, in_=Q[3][0:64, :], func=AF.Copy), r=QK[3], w=[W])
        yield
        for h in range(16):
            P.op("pe", lambda e, h=h: e.matmul(Q[3][0:64, h * 64:(h + 1) * 64], lhsT=(BP.ap[0:64, h // 2, 128:192] if h % 2 == 0 else To.ap[:, h // 2, :]), rhs=W.ap[:, h * 64:(h + 1) * 64], start=True, stop=True), r=[BP, To, W], w=QK[3])
        P.op("dve", lambda e: e.tensor_copy(out=U.ap, in_=Q[3][0:64, :]), r=QK[3], w=[U])
        yield
        for jj in range(NJ):
            P.op("pe", lambda e, jj=jj: e.matmul(Q[3][0:64, jj * 128:(jj + 1) * 128], lhsT=R.ap[:, jj, :], rhs=H.ap[:, jj, :], start=True, stop=False, skip_group_check=True), r=[R, H], w=QK[3])
            for par in range(2):
                h = 2 * jj + par
                P.op("pe", lambda e, h=h: e.matmul(Q[3][0:64, h * 64:(h + 1) * 64], lhsT=ArbT.ap[:, h, :], rhs=U.ap[:, h * 64:(h + 1) * 64], start=False, stop=False, skip_group_check=True), r=[ArbT, U], w=QK[3])
                P.op("pe", lambda e, h=h: e.matmul(Q[3][0:64, h * 64:(h + 1) * 64], lhsT=ArkT.ap[:, h, :], rhs=V.ap[:, h * 64:(h + 1) * 64], start=False, stop=True, skip_group_check=True), r=[ArkT, V], w=QK[3])
        P.op("act", lambda e: e.activation(out=W.ap, in_=Q[3][0:64, :], func=AF.Copy), r=QK[3], w=[W])
        P.dma("pool", TM["Y" + d_][c * C:(c + 1) * C, :], W.ap, r=[W])
        yield
        for jj in range(NJ):
            P.op("pe", lambda e, jj=jj: e.matmul(Q[2][:, jj * 128:(jj + 1) * 128], lhsT=BH.ap[:, jj * 128:(jj + 1) * 128], rhs=U.ap[:, jj * 128:(jj + 1) * 128], start=True, stop=False), r=[BH, U], w=QK[2])
            P.op("pe", lambda e, jj=jj: e.matmul(Q[2][:, jj * 128:(jj + 1) * 128], lhsT=KH.ap[:, jj * 128:(jj + 1) * 128], rhs=V.ap[:, jj * 128:(jj + 1) * 128], start=False, stop=True), r=[KH, V], w=QK[2])
        q2v = Q[2].rearrange("p (j c) -> p j c", j=NJ)
        for (p0, c0_) in ((0, 0), (64, 64)):
            hb = H.ap[p0:p0 + 64, :, c0_:c0_ + 64]
            P.op("pool", lambda e, hb=hb, p0=p0: e.tensor_tensor(out=hb, in0=hb, in1=gc.ap[p0:p0 + 64, :, c:c + 1].to_broadcast([64, NJ, 64]), op=ALU.mult), r=[H, gc], w=[H])
            P.op("dve", lambda e, hb=hb, p0=p0, c0_=c0_: e.tensor_tensor(out=hb, in0=hb, in1=q2v[p0:p0 + 64, :, c0_:c0_ + 64], op=ALU.add), r=[H] + QK[2], w=[H])

    from itertools import zip_longest
    load("f", 0, 0)
    load("b", NCK - 1, 0)
    for i in range(NCK):
        q = i % 2
        if i + 1 < NCK:
            for nm in "ABKR":
                for d_, cn in (("f", i + 1), ("b", NCK - 2 - i)):
                    t_ = S[d_]["fm"][1 - q][nm]
                    P.dma("sp", t_.ap.rearrange("p j t -> p (j t)"), FM[nm + d_][cn], w=[t_])
        for _ in zip_longest(scan("f", i, q), scan("b", NCK - 1 - i, q)):
            pass
        if i + 1 < NCK:
            for d_, cn in (("f", i + 1), ("b", NCK - 2 - i)):
                for nm, src_ in (("V", "V"), ("BH", "BH" + d_), ("KH", "KH" + d_)):
                    P.dma("sp", S[d_][nm].ap, TM[src_][cn * C:(cn + 1) * C, :], w=[S[d_][nm]])


def phase3(nc, P, ar, env):
    NT = env["NT"]
    TM, GATE, P2, xpad, out_d, rows_d = env["TM"], env["GATE"], env["P2"], env["xpad"], env["out_d"], env["rows_d"]
    colt, ident, col, bk = env["colt"], env["ident"], env["col"], env["bk"]
    w_pb_d, w_rb_d, w_out_d, w_ff1_d, w_ff2_d = env["w_pb_d"], env["w_rb_d"], env["w_out_d"], env["w_ff1_d"], env["w_ff2_d"]
    rowt = [ar.alloc("row%d" % i, [128, D]) for i in range(3)]
    for i in range(3):
        P.dma("sp", rowt[i].ap, rows_d[i:i + 1, :].partition_broadcast(128), w=[rowt[i]])
    lnw, lnb, gfin = rowt
    L = [ar.alloc("L%d" % i, [128, D]) for i in range(4)]
    W1 = ar.alloc("W1", [128, D])
    W2 = ar.alloc("W2", [128, D])
    rw = [ar.alloc("rw%d" % i, [128, D]) for i in range(4)]
    rwT = ar.alloc("rwT", [128, NJ, TT])
    mT = ar.alloc("mT", [128, NJ, TT])
    gtr = [ar.alloc("gtr%d" % i, [128, TT]) for i in range(2)]
    gtp = [ar.alloc("gtp%d" % i, [128, TT]) for i in range(2)]
    x1 = [ar.alloc("x1%d" % i, [128, D]) for i in range(4)]
    xr = [ar.alloc("xr%d" % i, [128, D]) for i in range(2)]
    h2 = [ar.alloc("h2%d" % i, [128, 256]) for i in range(3)]
    tmpf = [ar.alloc("tmpf%d" % i, [128, 256]) for i in range(2)]
    wrb = [ar.alloc("wrb%d" % i, [128, NJ, 128]) for i in range(2)]
    wpb = [ar.alloc("wpb%d" % i, [128, 4, 128]) for i in range(2)]
    wo = [ar.alloc("wo%d" % i, [128, D]) for i in range(2)]
    wf1 = [ar.alloc("wf1%d" % i, [128, NJ, 128]) for i in range(3)]
    wf2 = [ar.alloc("wf2%d" % i, [128, D]) for i in range(3)]
    stt = ar.alloc("stt", [128, 64])
    print("phase3 arena cols", ar.off)
    h3 = lambda ap: ap.rearrange("p (h c) -> p h c", h=16)
    bc3 = lambda ap: ap.rearrange("p (h o) -> p h o", o=1).to_broadcast([128, 16, 64])

    def do_block(ti, b):
        t0 = ti * TT + b * 128
        for i, nm in enumerate(("Yf", "Yb", "BV", "G")):
            P.dma("sp", L[i].ap, TM[nm][t0:t0 + 128, :], w=[L[i]])
        P.op("dve", lambda e: e.tensor_tensor(out=L[0].ap, in0=L[0].ap, in1=L[1].ap, op=ALU.add), r=[L[0], L[1]], w=[L[0]])
        P.op("dve", lambda e: e.tensor_reduce(out=stt.ap[:, 0:16], in_=h3(L[0].ap), axis=AX.X, op=ALU.add), r=[L[0]], w=[stt])
        P.op("dve", lambda e: e.tensor_scalar(out=stt.ap[:, 16:32], in0=stt.ap[:, 0:16], scalar1=1.0 / 64, scalar2=None, op0=ALU.mult), r=[stt], w=[stt])
        P.op("dve", lambda e: e.tensor_tensor(out=h3(W1.ap), in0=h3(L[0].ap), in1=bc3(stt.ap[:, 16:32]), op=ALU.subtract), r=[L[0], stt], w=[W1])
        P.op("pool", lambda e: e.tensor_tensor(out=W2.ap, in0=W1.ap, in1=W1.ap, op=ALU.mult), r=[W1], w=[W2])
        P.op("dve", lambda e: e.tensor_reduce(out=stt.ap[:, 32:48], in_=h3(W2.ap), axis=AX.X, op=ALU.add), r=[W2], w=[stt])
        P.op("act", lambda e: e.activation(out=stt.ap[:, 48:64], in_=stt.ap[:, 32:48], func=AF.Sqrt, scale=1.0 / 64, bias=gn_eps_ap(env)), r=[stt], w=[stt])
        P.op("dve", lambda e: e.reciprocal(out=stt.ap[:, 48:64], in_=stt.ap[:, 48:64]), r=[stt], w=[stt])
        P.op("dve", lambda e: e.tensor_tensor(out=h3(W1.ap), in0=h3(W1.ap), in1=bc3(stt.ap[:, 48:64]), op=ALU.mult), r=[W1, stt], w=[W1])
        P.op("pool", lambda e: e.tensor_tensor(out=W1.ap, in0=W1.ap, in1=lnw.ap, op=ALU.mult), r=[W1, lnw], w=[W1])
        P.op("pool", lambda e: e.tensor_tensor(out=W1.ap, in0=W1.ap, in1=lnb.ap, op=ALU.add), r=[W1, lnb], w=[W1])
        P.op("dve", lambda e: e.tensor_tensor(out=W1.ap, in0=W1.ap, in1=L[2].ap, op=ALU.add), r=[W1, L[2]], w=[W1])
        P.op("dve", lambda e: e.tensor_tensor(out=rw[b].ap, in0=W1.ap, in1=L[3].ap, op=ALU.mult), r=[W1, L[3]], w=[rw[b]])

    def rstd_of(src, colidx, junk):
        P.op("act", lambda e: e.activation(out=junk.ap, in_=src.ap, func=AF.Square, accum_out=stt.ap[:, colidx:colidx + 1]), r=[src], w=[junk, stt])
        P.op("act", lambda e: e.activation(out=stt.ap[:, colidx:colidx + 1], in_=stt.ap[:, colidx:colidx + 1], func=AF.Sqrt, scale=1.0 / D, bias=RMS_EPS), r=[stt], w=[stt])
        P.op("dve", lambda e: e.reciprocal(out=stt.ap[:, colidx:colidx + 1], in_=stt.ap[:, colidx:colidx + 1]), r=[stt], w=[stt])

    def do_tile(ti):
        t0 = ti * TT
        for b in range(4):
            do_block(ti, b)
        for j in range(NJ):
            bap, bkey = bk(j % 2)
            for b in range(4):
                P.op("pe", lambda e, b=b, j=j, bap=bap: e.transpose(out=bap[:, b * 128:(b + 1) * 128], in_=rw[b].ap[:, j * 128:(j + 1) * 128], identity=ident.ap), r=[rw[b], ident], w=[bkey])
            P.op("act", lambda e, j=j, bap=bap: e.activation(out=rwT.ap[:, j, :], in_=bap, func=AF.Copy), r=[bkey], w=[rwT])
        for g in range(4):
            P.dma("sp", L[g].ap[:, 0:TT], P2[g, :, t0:t0 + TT], w=[L[g]])
        for ec in range(NJ):
            wr, wp, gr, gp = wrb[ec % 2], wpb[ec % 2], gtr[ec % 2], gtp[ec % 2]
            P.dma("sp", wr.ap, w_rb_d[:, :, ec * 128:(ec + 1) * 128], w=[wr])
            P.dma("sp", wp.ap, w_pb_d[:, :, ec * 128:(ec + 1) * 128], w=[wp])
            P.dma("sp", gr.ap, GATE[8 + ec, :, t0:t0 + TT], w=[gr])
            P.dma("sp", gp.ap, GATE[ec, :, t0:t0 + TT], w=[gp])
            bA, kA = bk(ec % 2)
            bB, kB = bk(2 + ec % 2)
            for j in range(NJ):
                P.op("pe", lambda e, j=j, wr=wr, bA=bA: e.matmul(bA, lhsT=wr.ap[:, j, :], rhs=rwT.ap[:, j, :], start=(j == 0), stop=(j == NJ - 1)), r=[wr, rwT], w=[kA])
            for g in range(4):
                P.op("pe", lambda e, g=g, wp=wp, bB=bB: e.matmul(bB, lhsT=wp.ap[:, g, :], rhs=L[g].ap[:, 0:TT], start=(g == 0), stop=(g == 3)), r=[wp, L[g]], w=[kB])
            P.op("dve", lambda e, ec=ec, bA=bA, gr=gr: e.tensor_tensor(out=mT.ap[:, ec, :], in0=bA, in1=gr.ap, op=ALU.mult), r=[kA, gr], w=[(mT, ec)])
            P.op("dve", lambda e, bB=bB, gp=gp: e.tensor_tensor(out=W2.ap[:, 0:TT], in0=bB, in1=gp.ap, op=ALU.mult), r=[kB, gp], w=[W2])
            P.op("pool", lambda e, ec=ec: e.tensor_tensor(out=mT.ap[:, ec, :], in0=mT.ap[:, ec, :], in1=W2.ap[:, 0:TT], op=ALU.add), r=[(mT, ec), W2], w=[(mT, ec)])
        for ec in range(NJ):
            wt = wo[ec % 2]
            P.dma("sp", wt.ap, w_out_d[ec], w=[wt])
            for b in range(4):
                for hh in range(2):
                    bap, bkey = bk(b * 2 + hh)
                    P.op("pe", lambda e, ec=ec, b=b, hh=hh, wt=wt, bap=bap: e.matmul(bap[:, :], lhsT=mT.ap[:, ec, b * 128:(b + 1) * 128], rhs=wt.ap[:, hh * 512:(hh + 1) * 512], start=(ec == 0), stop=(ec == NJ - 1)), r=[wt, (mT, ec)], w=[bkey])
        for b in range(4):
            xt_ = xr[b % 2]
            P.dma("sp", xt_.ap, xpad[8 + t0 + b * 128:8 + t0 + (b + 1) * 128, :], w=[xt_])
            for hh in range(2):
                bap, bkey = bk(b * 2 + hh)
                P.op("dve", lambda e, b=b, hh=hh, bap=bap, xt_=xt_: e.tensor_tensor(out=x1[b].ap[:, hh * 512:(hh + 1) * 512], in0=bap, in1=xt_.ap[:, hh * 512:(hh + 1) * 512], op=ALU.add), r=[bkey, xt_], w=[x1[b]])
        hsb = [W1, W2]
        for sb in range(2):
            for bl in range(2):
                b = 2 * sb + bl
                rstd_of(x1[b], bl, L[0])
                P.op("dve", lambda e, b=b, bl=bl: e.tensor_scalar(out=hsb[bl].ap, in0=x1[b].ap, scalar1=stt.ap[:, bl:bl + 1], scalar2=None, op0=ALU.mult), r=[x1[b], stt], w=[hsb[bl]])
            for j in range(NJ):
                bap, bkey = bk(4 + j % 2)
                for bl in range(2):
                    P.op("pe", lambda e, j=j, bl=bl, bap=bap: e.transpose(out=bap[:, bl * 128:(bl + 1) * 128], in_=hsb[bl].ap[:, j * 128:(j + 1) * 128], identity=ident.ap), r=[hsb[bl], ident], w=[bkey])
                P.op("act", lambda e, j=j, bap=bap: e.activation(out=rwT.ap[:, j, 0:256], in_=bap[:, 0:256], func=AF.Copy, scale=col("g_ffn", j)), r=[bkey, colt], w=[rwT])
            for fc in range(32):
                w1_, w2_ = wf1[fc % 3], wf2[fc % 3]
                P.dma("sp", w1_.ap, w_ff1_d[fc], w=[w1_])
                P.dma("sp", w2_.ap, w_ff2_d[fc], w=[w2_])
                bap, bkey = bk(4 + fc % 2)
                for j in range(NJ):
                    P.op("pe", lambda e, j=j, w1_=w1_, bap=bap: e.matmul(bap[:, 0:256], lhsT=w1_.ap[:, j, :], rhs=rwT.ap[:, j, 0:256], start=(j == 0), stop=(j == NJ - 1)), r=[w1_, rwT], w=[bkey])
                tf, hh_ = tmpf[fc % 2], h2[fc % 3]
                P.op("act", lambda e, bap=bap, tf=tf: e.activation(out=tf.ap, in_=bap[:, 0:256], func=AF.Copy), r=[bkey], w=[tf])
                P.op("dve", lambda e, tf=tf, hh_=hh_: e.scalar_tensor_tensor(out=hh_.ap, in0=tf.ap, scalar=0.0, in1=tf.ap, op0=ALU.max, op1=ALU.mult), r=[tf], w=[hh_])
                for bl in range(2):
                    for hh in range(2):
                        oap, okey = bk(bl * 2 + hh)
                        P.op("pe", lambda e, fc=fc, bl=bl, hh=hh, hh_=hh_, w2_=w2_, oap=oap: e.matmul(oap, lhsT=hh_.ap[:, bl * 128:(bl + 1) * 128], rhs=w2_.ap[:, hh * 512:(hh + 1) * 512], start=(fc == 0), stop=(fc == 31)), r=[hh_, w2_], w=[okey])
            for bl in range(2):
                b = 2 * sb + bl
                for hh in range(2):
                    oap, okey = bk(bl * 2 + hh)
                    P.op("dve", lambda e, b=b, hh=hh, oap=oap: e.tensor_tensor(out=x1[b].ap[:, hh * 512:(hh + 1) * 512], in0=oap, in1=x1[b].ap[:, hh * 512:(hh + 1) * 512], op=ALU.add), r=[okey, x1[b]], w=[x1[b]])
                rstd_of(x1[b], 2 + bl, L[0])
                P.op("dve", lambda e, b=b, bl=bl: e.scalar_tensor_tensor(out=x1[b].ap, in0=x1[b].ap, scalar=stt.ap[:, 2 + bl:3 + bl], in1=gfin.ap, op0=ALU.mult, op1=ALU.mult), r=[x1[b], stt, gfin], w=[x1[b]])
                P.dma("pool", out_d[t0 + b * 128:t0 + (b + 1) * 128, :], x1[b].ap, r=[x1[b]])

    for ti in range(NT):
        do_tile(ti)


def gn_eps_ap(env):
    return GN_EPS


def _colpack(v, n):
    return np.ascontiguousarray(np.asarray(v, np.float32).reshape(n, 128).T)


def prep_shared(inp):
    f = lambda a: np.asarray(a, np.float32)
    cols = np.zeros((128, NCOLS), np.float32)
    src = {"g_mix": inp["g_mix"][0], "b_gate": inp["b_gate"][0], "mu_prev": inp["mu_prev"][0], "mu_next": inp["mu_next"][0],
           "pool_scale": inp["pool_scale"][0], "k_k": inp["k_k"][0], "k_a": inp["k_a"][0], "r_k": f(inp["r_k"][0]).reshape(-1),
           "w0_f": inp["w0_f"][0], "a0_f": inp["a0_f"][0], "w0_b": inp["w0_b"][0], "a0_b": inp["a0_b"][0], "g_ffn": inp["g_ffn"][0]}
    for n, c in COLSPEC:
        if n in src:
            cols[:, COLOFF[n]:COLOFF[n] + c] = _colpack(src[n], c)
    consts = np.zeros((128, 3840), np.float32)
    s = np.arange(64)[:, None]
    t = np.arange(64)[None, :]
    masks = [(s < t), (s <= t), (s > t), (s >= t)]
    for i, m in enumerate(masks):
        consts[0:64, i * 512:(i + 1) * 512] = np.tile(m.astype(np.float32), (1, 8))
    for i, m in enumerate(masks):
        mf = m.astype(np.float32)
        consts[0:64, 3328 + i * 128:3328 + i * 128 + 64] = mf
        consts[64:128, 3328 + i * 128 + 64:3328 + (i + 1) * 128] = mf
    rm = np.ones(512, np.float32)
    rm[::64] = 0.0
    consts[:, 2048:2560] = rm[None, :]
    consts[:, 2560:2688] = np.eye(128, dtype=np.float32)
    b1 = np.zeros((128, 128), np.float32)
    b1[0:64, 0:64] = 1.0
    b1[64:128, 64:128] = 1.0
    consts[:, 2688:2816] = b1
    consts[0:64, 2816:3328] = np.tile(np.eye(64, dtype=np.float32), (1, 8))
    rows = np.stack([f(inp["ln_w"][0]), f(inp["ln_b"][0]), f(inp["g_final"])], 0)
    lora = np.zeros((128, 2, D), np.float32)
    lora[0:64, 0] = inp["w_up_f"][0]
    lora[64:128, 0] = inp["a_up_f"][0]
    lora[0:64, 1] = inp["w_up_b"][0]
    lora[64:128, 1] = inp["a_up_b"][0]
    sh = {
        "cols": cols, "consts": consts, "rows": rows,
        "w_in": np.ascontiguousarray(f(inp["w_in"][0]).reshape(NJ, 128, NCC, 128).transpose(2, 1, 0, 3)),
        "pool_w": np.ascontiguousarray(f(inp["pool_w"][0]).transpose(1, 0, 2)),
        "lora": lora, "g_up": np.ascontiguousarray(f(inp["g_up"][0])),
        "w_pb": np.ascontiguousarray(f(inp["w_pool_br"][0]).reshape(4, 128, D).transpose(1, 0, 2)),
        "w_rb": np.ascontiguousarray(f(inp["w_rwkv_br"][0]).reshape(NJ, 128, D).transpose(1, 0, 2)),
        "w_out": np.ascontiguousarray(f(inp["w_out"][0]).reshape(NJ, 128, D)),
        "w_ff1": np.ascontiguousarray(f(inp["w_ff1"][0]).reshape(NJ, 128, 32, 128).transpose(2, 1, 0, 3)),
        "w_ff2": np.ascontiguousarray(f(inp["w_ff2"][0]).reshape(32, 128, D)),
    }
    return sh


def prep_core(segs, NS, SL):
    N = NS * SL
    NT = N // TT
    xpad = np.zeros((N + 16, D), np.float32)
    segid = np.full(NS, -1, np.int64)
    rcnt = np.ones((4, N), np.float32)
    cover = np.zeros(NS, bool)
    allsegs = list(segs)
    for s0, ns, arr in segs:
        cover[s0:s0 + ns] = True
    for s in range(NS):
        if not cover[s]:
            allsegs.append((s, 1, None))
    for i, (s0, ns, arr) in enumerate(allsegs):
        segid[s0:s0 + ns] = i
        S = ns * SL
        if arr is not None:
            xpad[8 + s0 * SL: 8 + s0 * SL + S] = arr
        pos = np.arange(S)
        for g, w in enumerate((2, 4, 8, 16)):
            lo = np.maximum(pos - w // 2, 0)
            hi = np.minimum(pos + w // 2 - 1, S - 1)
            rcnt[g, s0 * SL:s0 * SL + S] = 1.0 / (hi - lo + 1).astype(np.float32)
    hm = np.ones((16, NT), np.float32)
    for ti in range(NT):
        t0 = ti * TT
        sl = t0 // SL
        if t0 % SL == 0 and (sl == 0 or segid[sl - 1] != segid[sl]):
            hm[0:8, ti] = 0.0
        t1 = t0 + TT
        sl1 = (t1 - 1) // SL
        if t1 % SL == 0 and (sl1 == NS - 1 or segid[sl1 + 1] != segid[sl1]):
            hm[8:16, ti] = 0.0
    cm = np.zeros((128, 2 * NS), np.float32)
    for s in range(NS):
        if s > 0 and segid[s - 1] == segid[s]:
            cm[:, s] = 1.0
        if s < NS - 1 and segid[s + 1] == segid[s]:
            cm[:, NS + s] = 1.0
    return {"xpad": xpad, "hm": hm, "rcnt": rcnt, "cm": cm}


_NC_CACHE = {}


def kernel(**inputs):
    NS, SL = 8, 2048
    xp = np.asarray(inputs["x_prompt"], np.float32)
    xs = np.asarray(inputs["x_sample"], np.float32)
    sh = prep_shared(inputs)
    plan = []
    plan.append([(0, 8, xs[0])])
    plan.append([(0, 8, xs[1])])
    counts = [6, 6, 5, 5, 5, 5]
    nxt = 0
    owners = []
    for c in counts:
        segs = []
        own = []
        for i in range(c):
            segs.append((i, 1, xp[nxt]))
            own.append(nxt)
            nxt += 1
        plan.append(segs)
        owners.append(own)
    in_maps = []
    for segs in plan:
        m = dict(sh)
        m.update(prep_core(segs, NS, SL))
        in_maps.append(m)
    key = (NS, SL)
    if key not in _NC_CACHE:
        _NC_CACHE[key] = build(NS, SL)
    nc = _NC_CACHE[key]
    res = run_bass_kernel_spmd(nc, in_maps, core_ids=list(range(8)))
    y_prompt = np.empty_like(xp)
    y_sample = np.empty_like(xs)
    for c in range(2):
        y_sample[c] = np.asarray(res.results[c]["out"]).reshape(NS * SL, D)
    for ci, own in enumerate(owners):
        o = np.asarray(res.results[2 + ci]["out"]).reshape(NS, SL, D)
        for i, b in enumerate(own):
            y_prompt[b] = o[i]
    return (y_prompt, y_sample)
```

```python
from contextlib import ExitStack
import numpy as np
import concourse.bass as bass
import concourse.mybir as mybir
from concourse.bass_utils import run_bass_kernel_spmd

F32 = mybir.dt.float32
AF = mybir.ActivationFunctionType
ALU = mybir.AluOpType
AX = mybir.AxisListType

D = 1024
NJ = 8
TT = 512
C = 64
IN_COLS = 5888
NCC = IN_COLS // 128
RMS_EPS = 1e-6
GN_EPS = 64e-5
CENGS = ("pe", "act", "dve", "pool")
ENGS = ("pe", "act", "dve", "pool", "sp")
NSLOT = 8
PHASES = 3
INLINE_WAIT = True
EPOCH = 30000


class T:
    def __init__(self, name, ap):
        self.name = name
        self.ap = ap

    def __getitem__(self, k):
        return self.ap[k]


class Prog:
    def __init__(self):
        self.streams = {e: [] for e in ENGS}
        self.cnt = {e: 0 for e in CENGS}
        self.seen = {e: {} for e in ENGS}
        self.lastw = {}
        self.readers = {}
        self.semkeys = {}
        self.dma_slot = {e: 0 for e in ENGS}
        self.dma_val = {}
        self.nops = 0

    @staticmethod
    def _k(x):
        if isinstance(x, T):
            return x.name
        if isinstance(x, tuple) and isinstance(x[0], T):
            return (x[0].name,) + tuple(x[1:])
        return x

    def _need(self, eng, ev):
        if ev is None:
            return
        k, v = ev
        if k[0] == eng and eng == "pe":
            return
        if self.seen[eng].get(k, 0) >= v:
            return
        self.seen[eng][k] = v
        self.streams[eng].append(("wait", k, v))

    def _deps(self, eng, r, w):
        for key in r:
            self._need(eng, self.lastw.get(key))
        for key in w:
            self._need(eng, self.lastw.get(key))
            for ev in list(self.readers.get(key, {}).items()):
                self._need(eng, ev)

    def _commit(self, ev, r, w):
        for key in r:
            d = self.readers.setdefault(key, {})
            d[ev[0]] = max(d.get(ev[0], 0), ev[1])
        for key in w:
            self.lastw[key] = ev
            self.readers[key] = {}

    def op(self, eng, fn, r=(), w=()):
        r = [self._k(x) for x in r]
        w = [self._k(x) for x in w]
        self._deps(eng, r, w)
        self.cnt[eng] += 1
        n = self.cnt[eng]
        ep = (n - 1) // EPOCH
        k = (eng, ep)
        self.semkeys[k] = 1
        self.streams[eng].append(("op", fn, k, 1))
        self._commit((k, n - ep * EPOCH), r, w)
        self.nops += 1

    def dma(self, q, out, in_, r=(), w=()):
        r = [self._k(x) for x in r]
        w = [self._k(x) for x in w]
        self._deps(q, r, w)
        slot = self.dma_slot[q]
        self.dma_slot[q] = (slot + 1) % NSLOT
        k = ("dma", q, slot)
        self.semkeys[k] = 1
        prev = self.dma_val.get(k, 0)
        if prev:
            self._need(q, (k, prev))
        v = prev + 16
        self.dma_val[k] = v
        self.streams[q].append(("op", lambda e, o=out, i=in_: e.dma_start(out=o, in_=i), k, 16))
        self._commit((k, v), r, w)
        self.nops += 1

    def barrier(self):
        evs = []
        for e in CENGS:
            n = self.cnt[e]
            if n:
                ep = (n - 1) // EPOCH
                evs.append(((e, ep), n - ep * EPOCH))
        for k, v in self.dma_val.items():
            evs.append((k, v))
        for e in ENGS:
            for ev in evs:
                self._need(e, ev)

    def emit(self, nc):
        blockname = {"pe": "tensor", "act": "scalar", "dve": "vector", "pool": "gpsimd", "sp": "sync"}
        waited = {}
        for eng in ENGS:
            for it in self.streams[eng]:
                if it[0] == "wait" and it[1][0] != "dma":
                    waited.setdefault(it[1], set()).add(it[2])
        rank = {k: {v: i + 1 for i, v in enumerate(sorted(vs))} for k, vs in waited.items()}
        with ExitStack() as st:
            sems = {}
            for i, k in enumerate(self.semkeys):
                sems[k] = st.enter_context(nc.semaphore("s%d" % i))

            def do_wait(e, w_, ins=None):
                k, v = w_[1], w_[2]
                if k[0] != "dma":
                    v = rank[k][v]
                if ins is None:
                    e.wait_ge(sems[k], v)
                else:
                    ins._wait_ge(sems[k], v)

            with nc.Block() as block:
                for eng in ENGS:
                    items = self.streams[eng]

                    def body(e, items=items):
                        pend = []
                        cnt = {}
                        for it in items:
                            if it[0] == "wait":
                                pend.append(it)
                            else:
                                if INLINE_WAIT and pend:
                                    for w_ in pend[:-1]:
                                        do_wait(e, w_)
                                    ins = it[1](e)
                                    do_wait(e, pend[-1], ins)
                                else:
                                    for w_ in pend:
                                        do_wait(e, w_)
                                    ins = it[1](e)
                                pend = []
                                k = it[2]
                                if k[0] == "dma":
                                    ins.then_inc(sems[k], it[3])
                                else:
                                    n = cnt.get(k, 0) + 1
                                    cnt[k] = n
                                    if n in waited.get(k, ()):
                                        ins.then_inc(sems[k], 1)
                        for w_ in pend:
                            do_wait(e, w_)

                    getattr(block, blockname[eng])(body)


class Arena:
    def __init__(self, nc, st, name, ncols):
        self.t = st.enter_context(nc.sbuf_tensor(name, [128, ncols], F32))
        self.ncols = ncols
        self.off = 0
        self.uid = 0
        self.name = name

    def reset(self):
        self.off = 0

    def alloc(self, name, shape):
        n = int(np.prod(shape[1:]))
        assert self.off + n <= self.ncols, ("SBUF arena overflow", name, self.off + n, self.ncols)
        ap = self.t[0:shape[0], self.off:self.off + n]
        self.off += n
        if len(shape) == 3:
            ap = ap.rearrange("p (a b) -> p a b", a=shape[1])
        elif len(shape) == 4:
            ap = ap.rearrange("p (a b c) -> p a b c", a=shape[1], b=shape[2])
        self.uid += 1
        return T("%s.%s.%d" % (self.name, name, self.uid), ap)


def build(NS, SL, dbg=False):
    N = NS * SL
    NT = N // TT
    NCK = N // C
    CPS = SL // C
    nc = bass.Bass("TRN2", target_bir_lowering=False)
    P = Prog()

    def din(name, shape):
        return nc.dram_tensor(name, list(shape), F32, kind="ExternalInput").ap()

    def dscr(name, shape):
        kind = "ExternalOutput" if dbg else "Internal"
        return nc.dram_tensor(name, list(shape), F32, kind=kind).ap()

    xpad = din("xpad", [N + 16, D])
    hm_d = din("hm", [16, NT])
    rcnt_d = din("rcnt", [4, N])
    cm_d = din("cm", [128, 2 * NS])
    cols_d = din("cols", [128, NCOLS])
    consts_d = din("consts", [128, 3840])
    rows_d = din("rows", [3, D])
    w_in_d = din("w_in", [NCC, 128, NJ, 128])
    pool_w_d = din("pool_w", [128, 4, 128])
    lora_d = din("lora", [128, 2, D])
    g_up_d = din("g_up", [128, D])
    w_pb_d = din("w_pb", [128, 4, D])
    w_rb_d = din("w_rb", [128, NJ, D])
    w_out_d = din("w_out", [NJ, 128, D])
    w_ff1_d = din("w_ff1", [32, 128, NJ, 128])
    w_ff2_d = din("w_ff2", [32, 128, D])
    out_d = nc.dram_tensor("out", [N, D], F32, kind="ExternalOutput").ap()

    FM = {}
    for d_ in "fb":
        for nm in ("A", "B", "K", "R"):
            FM[nm + d_] = dscr("FM_%s%s" % (nm, d_), [NCK, 128, NJ * C])
    TMn = ["V", "BV", "BHf", "KHf", "BHb", "KHb", "G", "Yf", "Yb"]
    TM = {nm: dscr("TM_" + nm, [N, D]) for nm in TMn}
    GATE = dscr("GATE", [16, 128, N])
    P2 = dscr("P2", [4, 128, N])

    with ExitStack() as st:
        ar = Arena(nc, st, "ar", 43000)
        cst = Arena(nc, st, "cst", 2 * 512 + 128 + 128 + NCOLS + 2 * NJ * NCK + 2 * NS + NT + 64)
        banks = [T("bank%d" % i, st.enter_context(nc.psum_tensor("bank%d" % i, [128, 1024], F32))) for i in range(4)]

        def bk(i):
            return banks[i // 2].ap[:, (i % 2) * 512:(i % 2 + 1) * 512], ("ps", i)

        colt = cst.alloc("cols", [128, NCOLS])
        ident = cst.alloc("ident", [128, 128])
        blk1 = cst.alloc("blk1", [128, 128])
        rmask = cst.alloc("rmask", [128, 512])
        gcs = {"f": cst.alloc("gcf", [128, NJ, NCK]), "b": cst.alloc("gcb", [128, NJ, NCK])}
        cmt = cst.alloc("cm", [128, 2 * NS])
        hmt = cst.alloc("hm", [16, NT])
        P.dma("sp", colt.ap, cols_d, w=[colt])
        P.dma("sp", ident.ap, consts_d[:, 2560:2688], w=[ident])
        P.dma("sp", blk1.ap, consts_d[:, 2688:2816], w=[blk1])
        P.dma("sp", rmask.ap, consts_d[:, 2048:2560], w=[rmask])
        P.dma("sp", cmt.ap, cm_d, w=[cmt])
        P.dma("sp", hmt.ap, hm_d, w=[hmt])

        def col(name, j):
            o = COLOFF[name] + j
            return colt.ap[:, o:o + 1]

        for i in range(26):
            P.op("dve", lambda e, i=i: e.tensor_tensor(out=col("c0", i), in0=col("mu_prev", i), in1=col("mu_next", i), op=ALU.add), r=[colt], w=[colt])
        P.op("dve", lambda e: e.tensor_scalar(out=colt.ap[:, COLOFF["c0"]:COLOFF["c0"] + 26], in0=colt.ap[:, COLOFF["c0"]:COLOFF["c0"] + 26], scalar1=-1.0, scalar2=1.0, op0=ALU.mult, op1=ALU.add), r=[colt], w=[colt])
        P.op("dve", lambda e: e.tensor_scalar(out=colt.ap[:, COLOFF["omka"]:COLOFF["omka"] + 8], in0=colt.ap[:, COLOFF["k_a"]:COLOFF["k_a"] + 8], scalar1=-1.0, scalar2=1.0, op0=ALU.mult, op1=ALU.add), r=[colt], w=[colt])

        phase1(nc, P, ar, locals())
        P.barrier()
        ar.reset()
        if PHASES >= 2:
            phase2(nc, P, ar, locals())
            P.barrier()
            ar.reset()
        if PHASES >= 3:
            phase3(nc, P, ar, locals())
            P.barrier()
        P.emit(nc)
    return nc


COLSPEC = [("g_mix", 8), ("b_gate", 16), ("mu_prev", 26), ("mu_next", 26), ("pool_scale", 4), ("k_k", 8), ("k_a", 8),
           ("r_k", 8), ("w0_f", 8), ("a0_f", 8), ("w0_b", 8), ("a0_b", 8), ("g_ffn", 8), ("c0", 26), ("omka", 8)]
COLOFF = {}
_o = 0
for _n, _c in COLSPEC:
    COLOFF[_n] = _o
    _o += _c
NCOLS = _o


def phase1(nc, P, ar, env):
    NT, N, NCK = env["NT"], env["N"], env["NCK"]
    xpad, rcnt_d, w_in_d = env["xpad"], env["rcnt_d"], env["w_in_d"]
    FM, TM, GATE, P2 = env["FM"], env["TM"], env["GATE"], env["P2"]
    colt, ident, blk1, rmask, gcs, hmt = env["colt"], env["ident"], env["blk1"], env["rmask"], env["gcs"], env["hmt"]
    col, bk = env["col"], env["bk"]

    poolw = ar.alloc("poolw", [128, 4, 128])
    lora = ar.alloc("lora", [128, 2, D])
    gup = ar.alloc("gup", [128, D])
    P.dma("sp", poolw.ap, env["pool_w_d"], w=[poolw])
    P.dma("sp", lora.ap, env["lora_d"], w=[lora])
    P.dma("sp", gup.ap, env["g_up_d"], w=[gup])

    xt = [ar.alloc("xt%d" % b, [128, D]) for b in range(4)]
    xh = ar.alloc("xh", [16, D])
    ss = ar.alloc("ss", [128, 8])
    P.op("pool", lambda e: e.memset(ss.ap, 1.0), w=[ss])
    xnT = ar.alloc("xnT", [128, NJ, TT])
    xnTh = ar.alloc("xnTh", [128, NJ, 16])
    wring = [ar.alloc("w%d" % i, [128, NJ, 128]) for i in range(3)]
    zext = [ar.alloc("zext%d" % i, [128, 528]) for i in range(2)]
    zr = ar.alloc("zr", [128, NJ, TT])
    zk = ar.alloc("zk", [128, NJ, TT])
    zv = ar.alloc("zv", [128, NJ, TT])
    twza = ar.alloc("twza", [128, TT])
    sg = ar.alloc("sg", [128, TT])
    rc = ar.alloc("rc", [128, TT])
    pt = [ar.alloc("pt%d" % i, [128, 528]) for i in range(2)]
    p2T = ar.alloc("p2T", [128, TT])
    gst = [ar.alloc("gst%d" % i, [128, TT]) for i in range(2)]
    tmst = [ar.alloc("tmst%d" % i, [128, 4, 128]) for i in range(3)]
    gtm = [ar.alloc("gtm%d" % i, [128, D]) for i in range(1)]
    junk = gtm[0]
    ded = [ar.alloc("ded%d" % i, [128, TT]) for i in range(13)]
    xsl = [T((xnT.name, j), xnT.ap[:, j, :]) for j in range(NJ)]
    xth = [T((xt[b].name, h), xt[b].ap[:, h * 512:(h + 1) * 512]) for b in range(4) for h in range(2)]
    xtk = lambda b: [(xt[b], 0), (xt[b], 1)]
    print("phase1 arena cols", ar.off)
    PT = dict(kk=ded[0], kkn=ded[1], ksum=ded[2], sq=ded[3], rn=ded[4], bv=ded[5])
    DT1 = dict(sgm=xsl[0], a_=xsl[1], lw=xsl[2], PI=xsl[3], X1=xsl[4], X2=xsl[5], X3=xsl[6], Ea=xsl[7], Eb=xth[0], kd=xth[1], b_=xth[2])
    outs = [xth[3], xth[4], xth[5], xth[6], xth[7]] + ded[6:13]
    DT2 = [dict(At=outs[0], Bt=outs[1], Kt=outs[2], Rt=outs[3], BH=outs[4], KH=outs[5]),
           dict(At=outs[6], Bt=outs[7], Kt=outs[8], Rt=outs[9], BH=outs[10], KH=outs[11])]
    ptmp = [zext[0], zext[1]]
    tctr = [0]
    mctr = [0]
    wctr = [0]
    dctr = [0]

    def transposes_to_tm(src_fn, src_keys, name, j, t0):
        bi = 4 + (mctr[0] % 2)
        bap, bkey = bk(bi)
        for b in range(4):
            P.op("pe", lambda e, b=b: e.transpose(out=bap[:, b * 128:(b + 1) * 128], in_=src_fn(b), identity=ident.ap), r=list(src_keys) + [ident], w=[bkey])
        stg = tmst[mctr[0] % 3]
        mctr[0] += 1
        P.op("act", lambda e: e.activation(out=stg.ap, in_=bap.rearrange("p (b c) -> p b c", b=4), func=AF.Copy), r=[bkey], w=[stg])
        dst = TM[name][t0:t0 + TT, j * 128:(j + 1) * 128].rearrange("(b t) c -> t b c", b=4)
        P.dma("pool", dst, stg.ap, r=[stg])

    def do_tile(ti):
        t0 = ti * TT
        for b in range(4):
            P.dma("sp", xt[b].ap, xpad[8 + t0 + b * 128: 8 + t0 + (b + 1) * 128, :], w=xtk(b))
        P.dma("sp", xh.ap[0:8, :], xpad[t0:t0 + 8, :], w=[xh])
        P.dma("sp", xh.ap[8:16, :], xpad[t0 + 520:t0 + 528, :], w=[xh])
        for b in range(4):
            P.op("act", lambda e, b=b: e.activation(out=junk.ap, in_=xt[b].ap, func=AF.Square, accum_out=ss.ap[:, b:b + 1]), r=xtk(b), w=[junk, ss])
        P.op("act", lambda e: e.activation(out=junk.ap[0:16, :], in_=xh.ap, func=AF.Square, accum_out=ss.ap[0:16, 4:5]), r=[xh], w=[junk, ss])
        P.op("act", lambda e: e.activation(out=ss.ap[:, 0:5], in_=ss.ap[:, 0:5], func=AF.Sqrt, scale=1.0 / D, bias=col_eps(env)), r=[ss], w=[ss])
        P.op("dve", lambda e: e.reciprocal(out=ss.ap[:, 0:5], in_=ss.ap[:, 0:5]), r=[ss], w=[ss])
        P.op("dve", lambda e, ti=ti: e.tensor_tensor(out=ss.ap[0:16, 4:5], in0=ss.ap[0:16, 4:5], in1=hmt.ap[0:16, ti:ti + 1], op=ALU.mult), r=[ss, hmt], w=[ss])
        for b in range(4):
            P.op("dve", lambda e, b=b: e.tensor_scalar(out=xt[b].ap, in0=xt[b].ap, scalar1=ss.ap[:, b:b + 1], scalar2=None, op0=ALU.mult), r=xtk(b) + [ss], w=xtk(b))
        P.op("dve", lambda e: e.tensor_scalar(out=xh.ap, in0=xh.ap, scalar1=ss.ap[0:16, 4:5], scalar2=None, op0=ALU.mult), r=[xh, ss], w=[xh])
        for j in range(NJ):
            bap, bkey = bk(j % 2)
            for b in range(4):
                P.op("pe", lambda e, b=b, j=j, bap=bap: e.transpose(out=bap[:, b * 128:(b + 1) * 128], in_=xt[b].ap[:, j * 128:(j + 1) * 128], identity=ident.ap), r=xtk(b) + [ident], w=[bkey])
            P.op("act", lambda e, j=j, bap=bap: e.activation(out=xnT.ap[:, j, :], in_=bap, func=AF.Copy, scale=col("g_mix", j)), r=[bkey, colt], w=[(xnT, j)])
            hap, hkey = bk(2 + j % 2)
            P.op("pe", lambda e, j=j, hap=hap: e.transpose(out=hap[:, 0:16], in_=xh.ap[0:16, j * 128:(j + 1) * 128], identity=ident.ap[0:16, 0:16]), r=[xh, ident], w=[hkey])
            P.op("act", lambda e, j=j, hap=hap: e.activation(out=xnTh.ap[:, j, :], in_=hap[:, 0:16], func=AF.Copy, scale=col("g_mix", j)), r=[hkey, colt], w=[xnTh])
        xkeys = [(xnT, j) for j in range(NJ)]

        def zchunk(cc, halo):
            wt = wring[wctr[0] % 3]
            wctr[0] += 1
            P.dma("sp", wt.ap, w_in_d[cc], w=[wt])
            bi = wctr[0] % 2
            bap, bkey = bk(bi)
            for j in range(NJ):
                P.op("pe", lambda e, j=j: e.matmul(bap, lhsT=wt.ap[:, j, :], rhs=xnT.ap[:, j, :], start=(j == 0), stop=(j == NJ - 1)), r=[wt, (xnT, j)], w=[bkey])
            hap = hkey = None
            if halo:
                hap, hkey = bk(2 + bi)
                for j in range(NJ):
                    P.op("pe", lambda e, j=j: e.matmul(hap[:, 0:16], lhsT=wt.ap[:, j, :], rhs=xnTh.ap[:, j, :], start=(j == 0), stop=(j == NJ - 1)), r=[wt, xnTh], w=[hkey])
            return bap, bkey, hap, hkey

        def to_zext(cc):
            bap, bkey, hap, hkey = zchunk(cc, True)
            ze = zext[cc % 2]
            P.op("act", lambda e: e.activation(out=ze.ap[:, 8:520], in_=bap, func=AF.Copy), r=[bkey], w=[ze])
            P.op("dve", lambda e: e.tensor_copy(out=ze.ap[:, 0:8], in_=hap[:, 0:8]), r=[hkey], w=[ze])
            P.op("dve", lambda e: e.tensor_copy(out=ze.ap[:, 520:528], in_=hap[:, 8:16]), r=[hkey], w=[ze])
            return ze

        for g in range(4):
            ze = to_zext(g)
            P.dma("sp", rc.ap, rcnt_d[g:g + 1, t0:t0 + TT].partition_broadcast(128), w=[rc])
            cur, L = ze, 528
            sh = 1
            for lev in range(g + 1):
                nxt = pt[lev % 2]
                L2 = L - sh
                P.op("pool", lambda e, cur=cur, nxt=nxt, L2=L2, sh=sh: e.tensor_tensor(out=nxt.ap[:, 0:L2], in0=cur.ap[:, 0:L2], in1=cur.ap[:, sh:sh + L2], op=ALU.add), r=[cur], w=[nxt])
                cur, L, sh = nxt, L2, sh * 2
            w2 = 1 << g
            pl = ded[g % 2]
            P.op("pool", lambda e, cur=cur, w2=w2, pl=pl: e.tensor_tensor(out=pl.ap, in0=cur.ap[:, 8 - w2:8 - w2 + TT], in1=rc.ap, op=ALU.mult), r=[cur, rc], w=[pl])
            P.op("pool", lambda e, pl=pl, ze=ze: e.tensor_tensor(out=pl.ap, in0=pl.ap, in1=ze.ap[:, 8:520], op=ALU.subtract), r=[pl, ze], w=[pl])
            bap, bkey = bk(4)
            P.op("pe", lambda e, g=g, pl=pl: e.matmul(bap, lhsT=poolw.ap[:, g, :], rhs=pl.ap, start=True, stop=True), r=[poolw, pl], w=[bkey])
            P.op("act", lambda e, g=g: e.activation(out=p2T.ap, in_=bap, func=AF.Copy, scale=col("pool_scale", g)), r=[bkey, colt], w=[p2T])
            P.dma("pool", P2[g, :, t0:t0 + TT], p2T.ap, r=[p2T])

        def mix(ze, i, dst_ap, dkeys):
            t1 = ded[2 + (i % 2)]
            P.op("dve", lambda e: e.tensor_scalar(out=t1.ap, in0=ze.ap[:, 8:520], scalar1=col("c0", i), scalar2=None, op0=ALU.mult), r=[ze, colt], w=[t1])
            P.op("dve", lambda e: e.scalar_tensor_tensor(out=t1.ap, in0=ze.ap[:, 7:519], scalar=col("mu_prev", i), in1=t1.ap, op0=ALU.mult, op1=ALU.add), r=[ze, colt, t1], w=[t1])
            P.op("dve", lambda e: e.scalar_tensor_tensor(out=dst_ap, in0=ze.ap[:, 9:521], scalar=col("mu_next", i), in1=t1.ap, op0=ALU.mult, op1=ALU.add), r=[ze, colt, t1], w=dkeys)

        for i in range(26):
            ze = to_zext(4 + i)
            if i < 8:
                mix(ze, i, zr.ap[:, i, :], [(zr, i)])
            elif i < 16:
                mix(ze, i, zk.ap[:, i - 8, :], [(zk, i - 8)])
            elif i < 24:
                mix(ze, i, zv.ap[:, i - 16, :], [(zv, i - 16)])
            elif i == 24:
                mix(ze, i, twza.ap, [twza])
                P.op("act", lambda e: e.activation(out=twza.ap[0:64, :], in_=twza.ap[0:64, :], func=AF.Tanh), r=[twza], w=[twza])
            else:
                mix(ze, i, sg.ap, [sg])
                P.op("act", lambda e: e.activation(out=sg.ap, in_=sg.ap, func=AF.Sigmoid), r=[sg], w=[sg])

        for i in range(16):
            bap, bkey, _, _ = zchunk(30 + i, False)
            gt = gst[i % 2]
            P.op("act", lambda e, i=i, bap=bap, gt=gt: e.activation(out=gt.ap, in_=bap, func=AF.Sigmoid, bias=col("b_gate", i)), r=[bkey, colt], w=[gt])
            P.dma("pool", GATE[i, :, t0:t0 + TT], gt.ap, r=[gt])

        for b in range(4):
            gt_ = gtm[0]
            for hh in range(2):
                bap, bkey = bk(6 + hh)
                P.op("pe", lambda e, b=b, hh=hh, bap=bap: e.matmul(bap, lhsT=sg.ap[:, b * 128:(b + 1) * 128], rhs=gup.ap[:, hh * 512:(hh + 1) * 512], start=True, stop=True), r=[sg, gup], w=[bkey])
                P.op("act", lambda e, hh=hh, bap=bap, gt_=gt_: e.activation(out=gt_.ap[:, hh * 512:(hh + 1) * 512], in_=bap, func=AF.Copy), r=[bkey], w=[gt_])
            P.dma("pool", TM["G"][t0 + b * 128:t0 + (b + 1) * 128, :], gt_.ap, r=[gt_])

        ck0 = ti * (TT // C)
        for j in range(NJ):
            do_pair(j, t0, ck0)

    def do_pair(j, t0, ck0):
        kk, sq, rn, kkn, ksum, bv = PT["kk"], PT["sq"], PT["rn"], PT["kkn"], PT["ksum"], PT["bv"]
        P.op("dve", lambda e: e.tensor_scalar(out=kk.ap, in0=zk.ap[:, j, :], scalar1=col("k_k", j), scalar2=None, op0=ALU.mult), r=[(zk, j), colt], w=[kk])
        P.op("pool", lambda e: e.tensor_tensor(out=sq.ap, in0=kk.ap, in1=kk.ap, op=ALU.mult), r=[kk], w=[sq])
        bap, bkey = bk(6)
        P.op("pe", lambda e: e.matmul(bap, lhsT=blk1.ap, rhs=sq.ap, start=True, stop=True), r=[blk1, sq], w=[bkey])
        P.op("act", lambda e: e.activation(out=rn.ap, in_=bap, func=AF.Sqrt), r=[bkey], w=[rn])
        P.op("dve", lambda e: e.tensor_scalar(out=rn.ap, in0=rn.ap, scalar1=1e-12, scalar2=None, op0=ALU.max), r=[rn], w=[rn])
        P.op("dve", lambda e: e.reciprocal(out=rn.ap, in_=rn.ap), r=[rn], w=[rn])
        P.op("dve", lambda e: e.tensor_tensor(out=kkn.ap, in0=kk.ap, in1=rn.ap, op=ALU.mult), r=[kk, rn], w=[kkn])
        for di, d_ in enumerate("fb"):
            do_dir(j, t0, ck0, di, d_, kkn, ksum)
        t1 = sq
        P.op("dve", lambda e: e.scalar_tensor_tensor(out=t1.ap, in0=ksum.ap, scalar=col("r_k", j), in1=zr.ap[:, j, :], op0=ALU.mult, op1=ALU.mult), r=[ksum, colt, (zr, j)], w=[t1])
        b7ap, b7key = bk(7)
        P.op("pe", lambda e: e.matmul(b7ap, lhsT=blk1.ap, rhs=t1.ap, start=True, stop=True), r=[blk1, t1], w=[b7key])
        P.op("dve", lambda e: e.tensor_tensor(out=bv.ap, in0=b7ap, in1=zv.ap[:, j, :], op=ALU.mult), r=[b7key, (zv, j)], w=[bv])
        transposes_to_tm(lambda b: bv.ap[:, b * 128:(b + 1) * 128], [bv], "BV", j, t0)
        transposes_to_tm(lambda b: zv.ap[:, j, b * 128:(b + 1) * 128], [(zv, j)], "V", j, t0)

    def do_dir(j, t0, ck0, di, d_, kkn, ksum):
        g_ = DT1
        sgm, a_, lw, PI, X1, X2, X3, Ea, Eb, kd, b_ = (g_[k] for k in ("sgm", "a_", "lw", "PI", "X1", "X2", "X3", "Ea", "Eb", "kd", "b_"))
        o_ = DT2[dctr[0] % 2]
        dctr[0] += 1
        At, Bt, Kt, Rt, BH, KH = (o_[k] for k in ("At", "Bt", "Kt", "Rt", "BH", "KH"))
        b1ap, b1key = bk(7)
        P.op("pe", lambda e: e.matmul(b1ap, lhsT=lora.ap[0:64, di, j * 128:(j + 1) * 128], rhs=twza.ap[0:64, :], start=True, stop=True), r=[lora, twza], w=[b1key])
        P.op("act", lambda e: e.activation(out=sgm.ap, in_=b1ap, func=AF.Sigmoid, bias=col("w0_" + d_, j)), r=[b1key, colt], w=[sgm])
        b2ap, b2key = bk(6)
        P.op("pe", lambda e: e.matmul(b2ap, lhsT=lora.ap[64:128, di, j * 128:(j + 1) * 128], rhs=twza.ap[64:128, :], start=True, stop=True), r=[lora, twza], w=[b2key])
        P.op("act", lambda e: e.activation(out=a_.ap, in_=b2ap, func=AF.Sigmoid, bias=col("a0_" + d_, j)), r=[b2key, colt], w=[a_])
        lw = sgm
        K0 = -0.6065306597126334
        P.op("dve", lambda e: e.tensor_tensor_scan(out=PI.ap, data0=rmask.ap, data1=lw.ap, initial=0.0, op0=ALU.mult, op1=ALU.add), r=[rmask, lw], w=[PI])
        P.op("pool", lambda e: e.tensor_tensor(out=X1.ap, in0=PI.ap, in1=lw.ap, op=ALU.subtract), r=[PI, lw], w=[X1])
        PI3 = PI.ap.rearrange("p (c t) -> p c t", c=8)
        P.op("dve", lambda e: e.tensor_tensor(out=X2.ap.rearrange("p (c t) -> p c t", c=8), in0=PI3[:, :, 63:64].to_broadcast([128, 8, 64]), in1=PI3, op=ALU.subtract), r=[PI], w=[X2])
        if d_ == "f":
            srcs = [(X1, K0), (PI, -K0), (PI, K0), (X2, K0)]
        else:
            P.op("pool", lambda e: e.tensor_tensor(out=X3.ap, in0=X2.ap, in1=lw.ap, op=ALU.add), r=[X2, lw], w=[X3])
            srcs = [(X2, K0), (X3, -K0), (X3, K0), (X1, K0)]
        P.op("dve", lambda e: e.tensor_scalar(out=kd.ap, in0=a_.ap, scalar1=col("k_a", j), scalar2=col("omka", j), op0=ALU.mult, op1=ALU.add), r=[a_, colt], w=[kd])
        P.op("dve", lambda e: e.tensor_tensor(out=kd.ap, in0=kd.ap, in1=zk.ap[:, j, :], op=ALU.mult), r=[kd, (zk, j)], w=[kd])
        P.op("pool", lambda e: e.tensor_tensor(out=b_.ap, in0=kkn.ap, in1=a_.ap, op=ALU.mult), r=[kkn, a_], w=[b_])
        if di == 0:
            P.op("pool", lambda e: e.tensor_copy(out=ksum.ap, in_=kd.ap), r=[kd], w=[ksum])
        else:
            P.op("pool", lambda e: e.tensor_tensor(out=ksum.ap, in0=ksum.ap, in1=kd.ap, op=ALU.add), r=[kd, ksum], w=[ksum])

        def ex(Et, k):
            src_, sc = srcs[k]
            P.op("act", lambda e: e.activation(out=Et.ap, in_=src_.ap, func=AF.Exp, scale=sc), r=[src_], w=[Et])

        ex(Ea, 0)
        P.op("dve", lambda e: e.scalar_tensor_tensor(out=At.ap, in0=kkn.ap, scalar=-1.0, in1=Ea.ap, op0=ALU.mult, op1=ALU.mult), r=[kkn, Ea], w=[At])
        ex(Eb, 1)
        P.op("dve", lambda e: e.tensor_tensor(out=Bt.ap, in0=b_.ap, in1=Eb.ap, op=ALU.mult), r=[b_, Eb], w=[Bt])
        P.op("dve", lambda e: e.tensor_tensor(out=Kt.ap, in0=kd.ap, in1=Eb.ap, op=ALU.mult), r=[kd, Eb], w=[Kt])
        ex(Ea, 2)
        P.op("dve", lambda e: e.tensor_tensor(out=Rt.ap, in0=zr.ap[:, j, :], in1=Ea.ap, op=ALU.mult), r=[(zr, j), Ea], w=[Rt])
        E33 = Ea.ap.rearrange("p (c t) -> p c t", c=8)
        cc_ = 63 if d_ == "f" else 0
        P.op("pool", lambda e: e.tensor_copy(out=gcs[d_].ap[:, j, ck0:ck0 + 8], in_=E33[:, :, cc_]), r=[Ea], w=[gcs[d_]])
        ex(Eb, 3)
        P.op("pool", lambda e: e.tensor_tensor(out=BH.ap, in0=b_.ap, in1=Eb.ap, op=ALU.mult), r=[b_, Eb], w=[BH])
        P.op("pool", lambda e: e.tensor_tensor(out=KH.ap, in0=kd.ap, in1=Eb.ap, op=ALU.mult), r=[kd, Eb], w=[KH])
        for nm, tl in (("A", At), ("B", Bt), ("K", Kt), ("R", Rt)):
            dst = FM[nm + d_][ck0:ck0 + 8].rearrange("c p (j t) -> p c j t", j=NJ)[:, :, j, :]
            P.dma("sp", dst, tl.ap.rearrange("p (c t) -> p c t", c=8), r=[tl])
        transposes_to_tm(lambda b: BH.ap[:, b * 128:(b + 1) * 128], [BH], "BH" + d_, j, t0)
        transposes_to_tm(lambda b: KH.ap[:, b * 128:(b + 1) * 128], [KH], "KH" + d_, j, t0)

    for ti in range(NT):
        do_tile(ti)


def col_eps(env):
    return RMS_EPS


def phase2(nc, P, ar, env):
    NCK, NS, CPS = env["NCK"], env["NS"], env["CPS"]
    FM, TM = env["FM"], env["TM"]
    gcs, cmt, bk, banks, consts_d = env["gcs"], env["cmt"], env["bk"], env["banks"], env["consts_d"]
    mk = ar.alloc("masks", [64, 4, 512])
    P.dma("sp", mk.ap, consts_d[0:64, 0:2048].rearrange("p (m c) -> p m c", m=4), w=[mk])
    mkbd = ar.alloc("mkbd", [128, 4, 128])
    P.dma("sp", mkbd.ap, consts_d[:, 3328:3840].rearrange("p (m c) -> p m c", m=4), w=[mkbd])
    ident = env["ident"]
    bc4 = lambda ap: ap.rearrange("p (o c) -> p o c", o=1).to_broadcast([128, 4, 128])
    S = {}
    for d_ in "fb":
        s = {}
        s["fm"] = [{nm: ar.alloc("fm" + nm + d_ + str(q), [128, NJ, C]) for nm in "ABKR"} for q in range(2)]
        for nm in ("V", "BH", "KH"):
            s[nm] = ar.alloc(nm + d_, [64, D])
        s["ARbd"] = ar.alloc("ARbd" + d_, [128, NJ, 256])
        s["Bbd"] = ar.alloc("Bbd" + d_, [128, NJ, 128])
        s["A_sb"] = ar.alloc("A_bd" + d_, [128, NJ, 128])
        s["BP"] = ar.alloc("BPbd" + d_, [128, NJ, 256])
        s["To"] = ar.alloc("To" + d_, [64, NJ, 64])
        s["ArbT"] = ar.alloc("ArbT" + d_, [64, 16, 64])
        s["AakT"] = ar.alloc("AakT" + d_, [64, 16, 64])
        s["ArkT"] = ar.alloc("ArkT" + d_, [64, 16, 64])
        s["W"] = ar.alloc("W" + d_, [64, D])
        s["U"] = ar.alloc("U" + d_, [64, D])
        s["H"] = ar.alloc("H" + d_, [128, NJ, 128])
        for nm in ("ARbd", "Bbd", "H"):
            t_ = s[nm]
            P.op("pool", lambda e, t_=t_: e.memset(t_.ap, 0.0), w=[t_] + [(t_, o0, pp) for o0 in (0, 128) for pp in (0, 1)])
        S[d_] = s
    print("phase2 arena cols", ar.off)
    Q = [banks[i].ap for i in range(4)]
    QK = [[("ps", 2 * i), ("ps", 2 * i + 1)] for i in range(4)]
    MIDX = {"f": dict(A=2, B=0, R=1), "b": dict(A=0, B=2, R=3)}

    def load(d_, c, q):
        s = S[d_]
        for nm in "ABKR":
            t_ = s["fm"][q][nm]
            P.dma("sp", t_.ap.rearrange("p j t -> p (j t)"), FM[nm + d_][c], w=[t_])
        for nm, src_ in (("V", "V"), ("BH", "BH" + d_), ("KH", "KH" + d_)):
            P.dma("sp", s[nm].ap, TM[src_][c * C:(c + 1) * C, :], w=[s[nm]])

    def scan(d_, c, q):
        s = S[d_]
        fm = s["fm"][q]
        A, B, K, R = fm["A"], fm["B"], fm["K"], fm["R"]
        V, BH, KH, ARbd, Bbd, A_sb, BP, ArbT, AakT, ArkT, W, U, H, To = (s[k] for k in ("V", "BH", "KH", "ARbd", "Bbd", "A_sb", "BP", "ArbT", "AakT", "ArkT", "W", "U", "H", "To"))
        mi = MIDX[d_]
        gc = gcs[d_]
        if d_ == "f" and c % CPS == 0 and c > 0:
            idx = c // CPS
            P.op("pool", lambda e: e.tensor_scalar(out=H.ap, in0=H.ap, scalar1=cmt.ap[:, idx:idx + 1], scalar2=None, op0=ALU.mult), r=[H, cmt], w=[H])
        if d_ == "b" and c % CPS == CPS - 1 and c < NCK - 1:
            idx = NS + c // CPS
            P.op("pool", lambda e: e.tensor_scalar(out=H.ap, in0=H.ap, scalar1=cmt.ap[:, idx:idx + 1], scalar2=None, op0=ALU.mult), r=[H, cmt], w=[H])
        for (dst, src_, o0, eng_) in ((ARbd, A, 0, "pool"), (Bbd, B, 0, "pool"), (ARbd, R, 128, "act")):
            if eng_ == "pool":
                P.op("pool", lambda e, dst=dst, src_=src_, o0=o0: e.tensor_copy(out=dst.ap[0:64, :, o0:o0 + 64], in_=src_.ap[0:64, :, :]), r=[src_], w=[(dst, o0, 0)])
                P.op("pool", lambda e, dst=dst, src_=src_, o0=o0: e.tensor_copy(out=dst.ap[64:128, :, o0 + 64:o0 + 128], in_=src_.ap[64:128, :, :]), r=[src_], w=[(dst, o0, 1)])
            else:
                P.op("act", lambda e, dst=dst, src_=src_, o0=o0: e.activation(out=dst.ap[0:64, :, o0:o0 + 64], in_=src_.ap[0:64, :, :], func=AF.Copy), r=[src_], w=[(dst, o0, 0)])
                P.op("act", lambda e, dst=dst, src_=src_, o0=o0: e.activation(out=dst.ap[64:128, :, o0 + 64:o0 + 128], in_=src_.ap[64:128, :, :], func=AF.Copy), r=[src_], w=[(dst, o0, 1)])
        yield
        for hf in range(2):
            yield
            pA, pAk = bk(4)
            b0, b0k = bk(0)
            b1, b1k = bk(1)
            ARk = [(ARbd, 0, 0), (ARbd, 0, 1)]
            RRk = [(ARbd, 128, 0), (ARbd, 128, 1)]
            Bk = [(Bbd, 0, 0), (Bbd, 0, 1)]
            for jl in range(4):
                jj = 4 * hf + jl
                P.op("pe", lambda e, jl=jl, jj=jj: e.matmul(pA[:, jl * 128:(jl + 1) * 128], lhsT=ARbd.ap[:, jj, 0:128], rhs=Bbd.ap[:, jj, :], start=True, stop=True), r=ARk + Bk, w=[pAk])
                P.op("pe", lambda e, jl=jl, jj=jj: e.matmul(b0[:, jl * 128:(jl + 1) * 128], lhsT=Bbd.ap[:, jj, :], rhs=ARbd.ap[:, jj, 0:128], start=True, stop=True), r=ARk + Bk, w=[b0k])
                P.op("pe", lambda e, jl=jl, jj=jj: e.matmul(b1[0:64, jl * 128:(jl + 1) * 128], lhsT=B.ap[:, jj, :], rhs=ARbd.ap[:, jj, 128:256], start=True, stop=True), r=[B] + RRk, w=[b1k])
                P.op("pe", lambda e, jl=jl, jj=jj: e.matmul(Q[1][0:64, jl * 256:(jl + 1) * 256], lhsT=K.ap[:, jj, :], rhs=ARbd.ap[:, jj, :], start=True, stop=True), r=[K] + ARk + RRk, w=QK[1])
            hs = slice(8 * hf, 8 * hf + 8)
            ps = slice(4 * hf, 4 * hf + 4)
            P.op("dve", lambda e, ps=ps: e.tensor_tensor(out=A_sb.ap[:, ps, :], in0=pA.rearrange("p (j c) -> p j c", j=4), in1=bc4(mkbd.ap[:, mi["A"], :]), op=ALU.mult), r=[pAk, mkbd], w=[A_sb])
            P.op("dve", lambda e, ps=ps: e.tensor_tensor(out=BP.ap[:, ps, 0:128], in0=b0.rearrange("p (j c) -> p j c", j=4), in1=bc4(mkbd.ap[:, mi["B"], :]), op=ALU.mult), r=[b0k, mkbd], w=[BP])
            P.op("pool", lambda e, ps=ps: e.tensor_tensor(out=BP.ap[:, ps, 128:256], in0=BP.ap[:, ps, 0:128], in1=bc4(ident.ap), op=ALU.add), r=[BP, ident], w=[BP])
            P.op("dve", lambda e, hs=hs: e.tensor_tensor(out=ArbT.ap[:, hs, :], in0=b1[0:64, :].rearrange("p (h s) -> p h s", h=8), in1=mk.ap[:, mi["R"], :].rearrange("p (h s) -> p h s", h=8), op=ALU.mult), r=[b1k, mk], w=[ArbT])
            q1v = Q[1][0:64, :].rearrange("p (j q s) -> p j q s", j=4, q=4)
            mB = mk.ap[:, mi["B"], :].rearrange("p (j q s) -> p j q s", j=4, q=2)
            mR = mk.ap[:, mi["R"], :].rearrange("p (j q s) -> p j q s", j=4, q=2)
            v4 = lambda ap: ap.rearrange("p (j q) s -> p j q s", j=4)
            P.op("dve", lambda e, hs=hs, q1v=q1v, mB=mB: e.tensor_tensor(out=v4(AakT.ap[:, hs, :]), in0=q1v[:, :, 0:2, :], in1=mB, op=ALU.mult), r=QK[1] + [mk], w=[AakT])
            P.op("dve", lambda e, hs=hs, q1v=q1v, mR=mR: e.tensor_tensor(out=v4(ArkT.ap[:, hs, :]), in0=q1v[:, :, 2:4, :], in1=mR, op=ALU.mult), r=QK[1] + [mk], w=[ArkT])
        pset = [(Q[0], QK[0], bk(4)), (Q[1], QK[1], bk(5))]
        for st_ in range(6):
            yield
            for hf in range(2):
                pX, pXk, (pY, pYk) = pset[hf]
                ps = slice(4 * hf, 4 * hf + 4)
                for jl in range(4):
                    jj = 4 * hf + jl
                    if st_ == 0:
                        P.op("pe", lambda e, jl=jl, jj=jj, pX=pX: e.matmul(pX[:, jl * 256:jl * 256 + 128], lhsT=A_sb.ap[:, jj, :], rhs=BP.ap[:, jj, 0:128], start=True, stop=True), r=[A_sb, BP], w=pXk)
                    elif st_ < 5:
                        P.op("pe", lambda e, jl=jl, jj=jj, pX=pX: e.matmul(pX[:, jl * 256:(jl + 1) * 256], lhsT=A_sb.ap[:, jj, :], rhs=BP.ap[:, jj, :], start=True, stop=True), r=[A_sb, BP], w=pXk)
                    else:
                        P.op("pe", lambda e, jl=jl, jj=jj, pX=pX: e.matmul(pX[:, jl * 256 + 128:(jl + 1) * 256], lhsT=A_sb.ap[:, jj, :], rhs=BP.ap[:, jj, 128:256], start=True, stop=True), r=[A_sb, BP], w=pXk)
                    if st_ < 5:
                        P.op("pe", lambda e, jl=jl, jj=jj, pY=pY: e.matmul(pY[:, jl * 128:(jl + 1) * 128], lhsT=BP.ap[:, jj, 0:128], rhs=A_sb.ap[:, jj, :], start=True, stop=True), r=[A_sb, BP], w=[pYk])
                pXv = pX.rearrange("p (j c) -> p j c", j=4)
                if st_ < 5:
                    P.op("act", lambda e, ps=ps, pXv=pXv: e.activation(out=BP.ap[:, ps, 0:128], in_=pXv[:, :, 0:128], func=AF.Copy), r=pXk, w=[BP])
                    P.op("act", lambda e, ps=ps, pY=pY: e.activation(out=A_sb.ap[:, ps, :], in_=pY.rearrange("p (j c) -> p j c", j=4), func=AF.Copy), r=[pYk], w=[A_sb])
                if st_ > 0:
                    P.op("dve", lambda e, ps=ps, pXv=pXv: e.tensor_tensor(out=BP.ap[:, ps, 128:256], in0=BP.ap[:, ps, 128:256], in1=pXv[:, :, 128:256], op=ALU.add), r=pXk + [BP], w=[BP])
        yield
        b4, b4k = bk(4)
        P.op("pe", lambda e: e.matmul(b4[0:64, :], lhsT=ident.ap[:, 64:128], rhs=BP.ap[:, :, 192:256], start=True, stop=True), r=[BP, ident], w=[b4k])
        P.op("act", lambda e: e.activation(out=To.ap, in_=b4[0:64, :].rearrange("p (j c) -> p j c", j=NJ), func=AF.Copy), r=[b4k], w=[To])
        yield
        for jj in range(NJ):
            P.op("pe", lambda e, jj=jj: e.matmul(Q[3][0:64, jj * 128:(jj + 1) * 128], lhsT=A.ap[:, jj, :], rhs=H.ap[:, jj, :], start=True, stop=False, skip_group_check=True), r=[A, H], w=QK[3])
            for par in range(2):
                h = 2 * jj + par
                P.op("pe", lambda e, h=h: e.matmul(Q[3][0:64, h * 64:(h + 1) * 64], lhsT=AakT.ap[:, h, :], rhs=V.ap[:, h * 64:(h + 1) * 64], start=False, stop=True, skip_group_check=True), r=[AakT, V], w=QK[3])
        P.op("act", lambda e: e.activation(out=W.ap, in_=Q[3][0:64, :], func=AF.Copy), r=QK[3], w=[W])
        yield
        for h in range(16):
            P.op("pe", lambda e, h=h: e.matmul(Q[3][0:64, h * 64:(h + 1) * 64], lhsT=(BP.ap[0:64, h // 2, 128:192] if h % 2 == 0 else To.ap[:, h // 2, :]), rhs=W.ap[:, h * 64:(h + 1) * 64], start=True, stop=True), r=[BP, To, W], w=QK[3])
        P.op("dve", lambda e: e.tensor_copy(out=U.ap, in_=Q[3][0:64, :]), r=QK[3], w=[U])
        yield
        for jj in range(NJ):
            P.op("pe", lambda e, jj=jj: e.matmul(Q[3][0:64, jj * 128:(jj + 1) * 128], lhsT=R.ap[:, jj, :], rhs=H.ap[:, jj, :], start=True, stop=False, skip_group_check=True), r=[R, H], w=QK[3])
            for par in range(2):
                h = 2 * jj + par
                P.op("pe", lambda e, h=h: e.matmul(Q[3][0:64, h * 64:(h + 1) * 64], lhsT=ArbT.ap[:, h, :], rhs=U.ap[:, h * 64:(h + 1) * 64], start=False, stop=False, skip_group_check=True), r=[ArbT, U], w=QK[3])
                P.op("pe", lambda e, h=h: e.matmul(Q[3][0:64, h * 64:(h + 1) * 64], lhsT=ArkT.ap[:, h, :], rhs=V.ap[:, h * 64:(h + 1) * 64], start=False, stop=True, skip_group_check=True), r=[ArkT, V], w=QK[3])
        P.op("act", lambda e: e.activation(out=W.ap, in_=Q[3][0:64, :], func=AF.Copy), r=QK[3], w=[W])
        P.dma("pool", TM["Y" + d_][c * C:(c + 1) * C, :], W.ap, r=[W])
        yield
        for jj in range(NJ):
            P.op("pe", lambda e, jj=jj: e.matmul(Q[2][:, jj * 128:(jj + 1) * 128], lhsT=BH.ap[:, jj * 128:(jj + 1) * 128], rhs=U.ap[:, jj * 128:(jj + 1) * 128], start=True, stop=False), r=[BH, U], w=QK[2])
            P.op("pe", lambda e, jj=jj: e.matmul(Q[2][:, jj * 128:(jj + 1) * 128], lhsT=KH.ap[:, jj * 128:(jj + 1) * 128], rhs=V.ap[:, jj * 128:(jj + 1) * 128], start=False, stop=True), r=[KH, V], w=QK[2])
        q2v = Q[2].rearrange("p (j c) -> p j c", j=NJ)
        for (p0, c0_) in ((0, 0), (64, 64)):
            hb = H.ap[p0:p0 + 64, :, c0_:c0_ + 64]
            P.op("pool", lambda e, hb=hb, p0=p0: e.tensor_tensor(out=hb, in0=hb, in1=gc.ap[p0:p0 + 64, :, c:c + 1].to_broadcast([64, NJ, 64]), op=ALU.mult), r=[H, gc], w=[H])
            P.op("dve", lambda e, hb=hb, p0=p0, c0_=c0_: e.tensor_tensor(out=hb, in0=hb, in1=q2v[p0:p0 + 64, :, c0_:c0_ + 64], op=ALU.add), r=[H] + QK[2], w=[H])

    from itertools import zip_longest
    load("f", 0, 0)
    load("b", NCK - 1, 0)
    for i in range(NCK):
        q = i % 2
        if i + 1 < NCK:
            for nm in "ABKR":
                for d_, cn in (("f", i + 1), ("b", NCK - 2 - i)):
                    t_ = S[d_]["fm"][1 - q][nm]
                    P.dma("sp", t_.ap.rearrange("p j t -> p (j t)"), FM[nm + d_][cn], w=[t_])
        for _ in zip_longest(scan("f", i, q), scan("b", NCK - 1 - i, q)):
            pass
        if i + 1 < NCK:
            for d_, cn in (("f", i + 1), ("b", NCK - 2 - i)):
                for nm, src_ in (("V", "V"), ("BH", "BH" + d_), ("KH", "KH" + d_)):
                    P.dma("sp", S[d_][nm].ap, TM[src_][cn * C:(cn + 1) * C, :], w=[S[d_][nm]])


def phase3(nc, P, ar, env):
    NT = env["NT"]
    TM, GATE, P2, xpad, out_d, rows_d = env["TM"], env["GATE"], env["P2"], env["xpad"], env["out_d"], env["rows_d"]
    colt, ident, col, bk = env["colt"], env["ident"], env["col"], env["bk"]
    w_pb_d, w_rb_d, w_out_d, w_ff1_d, w_ff2_d = env["w_pb_d"], env["w_rb_d"], env["w_out_d"], env["w_ff1_d"], env["w_ff2_d"]
    rowt = [ar.alloc("row%d" % i, [128, D]) for i in range(3)]
    for i in range(3):
        P.dma("sp", rowt[i].ap, rows_d[i:i + 1, :].partition_broadcast(128), w=[rowt[i]])
    lnw, lnb, gfin = rowt
    L = [ar.alloc("L%d" % i, [128, D]) for i in range(4)]
    W1 = ar.alloc("W1", [128, D])
    W2 = ar.alloc("W2", [128, D])
    rw = [ar.alloc("rw%d" % i, [128, D]) for i in range(4)]
    rwT = ar.alloc("rwT", [128, NJ, TT])
    mT = ar.alloc("mT", [128, NJ, TT])
    gtr = [ar.alloc("gtr%d" % i, [128, TT]) for i in range(2)]
    gtp = [ar.alloc("gtp%d" % i, [128, TT]) for i in range(2)]
    x1 = [ar.alloc("x1%d" % i, [128, D]) for i in range(4)]
    xr = [ar.alloc("xr%d" % i, [128, D]) for i in range(2)]
    h2 = [ar.alloc("h2%d" % i, [128, 256]) for i in range(3)]
    tmpf = [ar.alloc("tmpf%d" % i, [128, 256]) for i in range(2)]
    wrb = [ar.alloc("wrb%d" % i, [128, NJ, 128]) for i in range(2)]
    wpb = [ar.alloc("wpb%d" % i, [128, 4, 128]) for i in range(2)]
    wo = [ar.alloc("wo%d" % i, [128, D]) for i in range(2)]
    wf1 = [ar.alloc("wf1%d" % i, [128, NJ, 128]) for i in range(3)]
    wf2 = [ar.alloc("wf2%d" % i, [128, D]) for i in range(3)]
    stt = ar.alloc("stt", [128, 64])
    print("phase3 arena cols", ar.off)
    h3 = lambda ap: ap.rearrange("p (h c) -> p h c", h=16)
    bc3 = lambda ap: ap.rearrange("p (h o) -> p h o", o=1).to_broadcast([128, 16, 64])

    def do_block(ti, b):
        t0 = ti * TT + b * 128
        for i, nm in enumerate(("Yf", "Yb", "BV", "G")):
            P.dma("sp", L[i].ap, TM[nm][t0:t0 + 128, :], w=[L[i]])
        P.op("dve", lambda e: e.tensor_tensor(out=L[0].ap, in0=L[0].ap, in1=L[1].ap, op=ALU.add), r=[L[0], L[1]], w=[L[0]])
        P.op("dve", lambda e: e.tensor_reduce(out=stt.ap[:, 0:16], in_=h3(L[0].ap), axis=AX.X, op=ALU.add), r=[L[0]], w=[stt])
        P.op("dve", lambda e: e.tensor_scalar(out=stt.ap[:, 16:32], in0=stt.ap[:, 0:16], scalar1=1.0 / 64, scalar2=None, op0=ALU.mult), r=[stt], w=[stt])
        P.op("dve", lambda e: e.tensor_tensor(out=h3(W1.ap), in0=h3(L[0].ap), in1=bc3(stt.ap[:, 16:32]), op=ALU.subtract), r=[L[0], stt], w=[W1])
        P.op("pool", lambda e: e.tensor_tensor(out=W2.ap, in0=W1.ap, in1=W1.ap, op=ALU.mult), r=[W1], w=[W2])
        P.op("dve", lambda e: e.tensor_reduce(out=stt.ap[:, 32:48], in_=h3(W2.ap), axis=AX.X, op=ALU.add), r=[W2], w=[stt])
        P.op("act", lambda e: e.activation(out=stt.ap[:, 48:64], in_=stt.ap[:, 32:48], func=AF.Sqrt, scale=1.0 / 64, bias=gn_eps_ap(env)), r=[stt], w=[stt])
        P.op("dve", lambda e: e.reciprocal(out=stt.ap[:, 48:64], in_=stt.ap[:, 48:64]), r=[stt], w=[stt])
        P.op("dve", lambda e: e.tensor_tensor(out=h3(W1.ap), in0=h3(W1.ap), in1=bc3(stt.ap[:, 48:64]), op=ALU.mult), r=[W1, stt], w=[W1])
        P.op("pool", lambda e: e.tensor_tensor(out=W1.ap, in0=W1.ap, in1=lnw.ap, op=ALU.mult), r=[W1, lnw], w=[W1])
        P.op("pool", lambda e: e.tensor_tensor(out=W1.ap, in0=W1.ap, in1=lnb.ap, op=ALU.add), r=[W1, lnb], w=[W1])
        P.op("dve", lambda e: e.tensor_tensor(out=W1.ap, in0=W1.ap, in1=L[2].ap, op=ALU.add), r=[W1, L[2]], w=[W1])
        P.op("dve", lambda e: e.tensor_tensor(out=rw[b].ap, in0=W1.ap, in1=L[3].ap, op=ALU.mult), r=[W1, L[3]], w=[rw[b]])

    def rstd_of(src, colidx, junk):
        P.op("act", lambda e: e.activation(out=junk.ap, in_=src.ap, func=AF.Square, accum_out=stt.ap[:, colidx:colidx + 1]), r=[src], w=[junk, stt])
        P.op("act", lambda e: e.activation(out=stt.ap[:, colidx:colidx + 1], in_=stt.ap[:, colidx:colidx + 1], func=AF.Sqrt, scale=1.0 / D, bias=RMS_EPS), r=[stt], w=[stt])
        P.op("dve", lambda e: e.reciprocal(out=stt.ap[:, colidx:colidx + 1], in_=stt.ap[:, colidx:colidx + 1]), r=[stt], w=[stt])

    def do_tile(ti):
        t0 = ti * TT
        for b in range(4):
            do_block(ti, b)
        for j in range(NJ):
            bap, bkey = bk(j % 2)
            for b in range(4):
                P.op("pe", lambda e, b=b, j=j, bap=bap: e.transpose(out=bap[:, b * 128:(b + 1) * 128], in_=rw[b].ap[:, j * 128:(j + 1) * 128], identity=ident.ap), r=[rw[b], ident], w=[bkey])
            P.op("act", lambda e, j=j, bap=bap: e.activation(out=rwT.ap[:, j, :], in_=bap, func=AF.Copy), r=[bkey], w=[rwT])
        for g in range(4):
            P.dma("sp", L[g].ap[:, 0:TT], P2[g, :, t0:t0 + TT], w=[L[g]])
        for ec in range(NJ):
            wr, wp, gr, gp = wrb[ec % 2], wpb[ec % 2], gtr[ec % 2], gtp[ec % 2]
            P.dma("sp", wr.ap, w_rb_d[:, :, ec * 128:(ec + 1) * 128], w=[wr])
            P.dma("sp", wp.ap, w_pb_d[:, :, ec * 128:(ec + 1) * 128], w=[wp])
            P.dma("sp", gr.ap, GATE[8 + ec, :, t0:t0 + TT], w=[gr])
            P.dma("sp", gp.ap, GATE[ec, :, t0:t0 + TT], w=[gp])
            bA, kA = bk(ec % 2)
            bB, kB = bk(2 + ec % 2)
            for j in range(NJ):
                P.op("pe", lambda e, j=j, wr=wr, bA=bA: e.matmul(bA, lhsT=wr.ap[:, j, :], rhs=rwT.ap[:, j, :], start=(j == 0), stop=(j == NJ - 1)), r=[wr, rwT], w=[kA])
            for g in range(4):
                P.op("pe", lambda e, g=g, wp=wp, bB=bB: e.matmul(bB, lhsT=wp.ap[:, g, :], rhs=L[g].ap[:, 0:TT], start=(g == 0), stop=(g == 3)), r=[wp, L[g]], w=[kB])
            P.op("dve", lambda e, ec=ec, bA=bA, gr=gr: e.tensor_tensor(out=mT.ap[:, ec, :], in0=bA, in1=gr.ap, op=ALU.mult), r=[kA, gr], w=[(mT, ec)])
            P.op("dve", lambda e, bB=bB, gp=gp: e.tensor_tensor(out=W2.ap[:, 0:TT], in0=bB, in1=gp.ap, op=ALU.mult), r=[kB, gp], w=[W2])
            P.op("pool", lambda e, ec=ec: e.tensor_tensor(out=mT.ap[:, ec, :], in0=mT.ap[:, ec, :], in1=W2.ap[:, 0:TT], op=ALU.add), r=[(mT, ec), W2], w=[(mT, ec)])
        for ec in range(NJ):
            wt = wo[ec % 2]
            P.dma("sp", wt.ap, w_out_d[ec], w=[wt])
            for b in range(4):
                for hh in range(2):
                    bap, bkey = bk(b * 2 + hh)
                    P.op("pe", lambda e, ec=ec, b=b, hh=hh, wt=wt, bap=bap: e.matmul(bap[:, :], lhsT=mT.ap[:, ec, b * 128:(b + 1) * 128], rhs=wt.ap[:, hh * 512:(hh + 1) * 512], start=(ec == 0), stop=(ec == NJ - 1)), r=[wt, (mT, ec)], w=[bkey])
        for b in range(4):
            xt_ = xr[b % 2]
            P.dma("sp", xt_.ap, xpad[8 + t0 + b * 128:8 + t0 + (b + 1) * 128, :], w=[xt_])
            for hh in range(2):
                bap, bkey = bk(b * 2 + hh)
                P.op("dve", lambda e, b=b, hh=hh, bap=bap, xt_=xt_: e.tensor_tensor(out=x1[b].ap[:, hh * 512:(hh + 1) * 512], in0=bap, in1=xt_.ap[:, hh * 512:(hh + 1) * 512], op=ALU.add), r=[bkey, xt_], w=[x1[b]])
        hsb = [W1, W2]
        for sb in range(2):
            for bl in range(2):
                b = 2 * sb + bl
                rstd_of(x1[b], bl, L[0])
                P.op("dve", lambda e, b=b, bl=bl: e.tensor_scalar(out=hsb[bl].ap, in0=x1[b].ap, scalar1=stt.ap[:, bl:bl + 1], scalar2=None, op0=ALU.mult), r=[x1[b], stt], w=[hsb[bl]])
            for j in range(NJ):
                bap, bkey = bk(4 + j % 2)
                for bl in range(2):
                    P.op("pe", lambda e, j=j, bl=bl, bap=bap: e.transpose(out=bap[:, bl * 128:(bl + 1) * 128], in_=hsb[bl].ap[:, j * 128:(j + 1) * 128], identity=ident.ap), r=[hsb[bl], ident], w=[bkey])
                P.op("act", lambda e, j=j, bap=bap: e.activation(out=rwT.ap[:, j, 0:256], in_=bap[:, 0:256], func=AF.Copy, scale=col("g_ffn", j)), r=[bkey, colt], w=[rwT])
            def ff1(fc):
                w1_, w2_ = wf1[fc % 3], wf2[fc % 3]
                P.dma("sp", w1_.ap, w_ff1_d[fc], w=[w1_])
                P.dma("sp", w2_.ap, w_ff2_d[fc], w=[w2_])
                bap, bkey = bk(4 + fc % 2)
                for j in range(NJ):
                    P.op("pe", lambda e, j=j, w1_=w1_, bap=bap: e.matmul(bap[:, 0:256], lhsT=w1_.ap[:, j, :], rhs=rwT.ap[:, j, 0:256], start=(j == 0), stop=(j == NJ - 1)), r=[w1_, rwT], w=[bkey])
                tf, hh_ = tmpf[fc % 2], h2[fc % 3]
                P.op("act", lambda e, bap=bap, tf=tf: e.activation(out=tf.ap, in_=bap[:, 0:256], func=AF.Copy), r=[bkey], w=[tf])
                P.op("dve", lambda e, tf=tf, hh_=hh_: e.scalar_tensor_tensor(out=hh_.ap, in0=tf.ap, scalar=0.0, in1=tf.ap, op0=ALU.max, op1=ALU.mult), r=[tf], w=[hh_])

            def ff2(fc):
                w2_, hh_ = wf2[fc % 3], h2[fc % 3]
                for bl in range(2):
                    for hh in range(2):
                        oap, okey = bk(bl * 2 + hh)
                        P.op("pe", lambda e, fc=fc, bl=bl, hh=hh, hh_=hh_, w2_=w2_, oap=oap: e.matmul(oap, lhsT=hh_.ap[:, bl * 128:(bl + 1) * 128], rhs=w2_.ap[:, hh * 512:(hh + 1) * 512], start=(fc == 0), stop=(fc == 31)), r=[hh_, w2_], w=[okey])

            ff1(0)
            for fc in range(32):
                if fc + 1 < 32:
                    ff1(fc + 1)
                ff2(fc)
            for bl in range(2):
                b = 2 * sb + bl
                for hh in range(2):
                    oap, okey = bk(bl * 2 + hh)
                    P.op("dve", lambda e, b=b, hh=hh, oap=oap: e.tensor_tensor(out=x1[b].ap[:, hh * 512:(hh + 1) * 512], in0=oap, in1=x1[b].ap[:, hh * 512:(hh + 1) * 512], op=ALU.add), r=[okey, x1[b]], w=[x1[b]])
                rstd_of(x1[b], 2 + bl, L[0])
                P.op("dve", lambda e, b=b, bl=bl: e.scalar_tensor_tensor(out=x1[b].ap, in0=x1[b].ap, scalar=stt.ap[:, 2 + bl:3 + bl], in1=gfin.ap, op0=ALU.mult, op1=ALU.mult), r=[x1[b], stt, gfin], w=[x1[b]])
                P.dma("pool", out_d[t0 + b * 128:t0 + (b + 1) * 128, :], x1[b].ap, r=[x1[b]])

    for ti in range(NT):
        do_tile(ti)


def gn_eps_ap(env):
    return GN_EPS


def _colpack(v, n):
    return np.ascontiguousarray(np.asarray(v, np.float32).reshape(n, 128).T)


def prep_shared(inp):
    f = lambda a: np.asarray(a, np.float32)
    cols = np.zeros((128, NCOLS), np.float32)
    src = {"g_mix": inp["g_mix"][0], "b_gate": inp["b_gate"][0], "mu_prev": inp["mu_prev"][0], "mu_next": inp["mu_next"][0],
           "pool_scale": inp["pool_scale"][0], "k_k": inp["k_k"][0], "k_a": inp["k_a"][0], "r_k": f(inp["r_k"][0]).reshape(-1),
           "w0_f": inp["w0_f"][0], "a0_f": inp["a0_f"][0], "w0_b": inp["w0_b"][0], "a0_b": inp["a0_b"][0], "g_ffn": inp["g_ffn"][0]}
    for n, c in COLSPEC:
        if n in src:
            cols[:, COLOFF[n]:COLOFF[n] + c] = _colpack(src[n], c)
    consts = np.zeros((128, 3840), np.float32)
    s = np.arange(64)[:, None]
    t = np.arange(64)[None, :]
    masks = [(s < t), (s <= t), (s > t), (s >= t)]
    for i, m in enumerate(masks):
        consts[0:64, i * 512:(i + 1) * 512] = np.tile(m.astype(np.float32), (1, 8))
    for i, m in enumerate(masks):
        mf = m.astype(np.float32)
        consts[0:64, 3328 + i * 128:3328 + i * 128 + 64] = mf
        consts[64:128, 3328 + i * 128 + 64:3328 + (i + 1) * 128] = mf
    rm = np.ones(512, np.float32)
    rm[::64] = 0.0
    consts[:, 2048:2560] = rm[None, :]
    consts[:, 2560:2688] = np.eye(128, dtype=np.float32)
    b1 = np.zeros((128, 128), np.float32)
    b1[0:64, 0:64] = 1.0
    b1[64:128, 64:128] = 1.0
    consts[:, 2688:2816] = b1
    consts[0:64, 2816:3328] = np.tile(np.eye(64, dtype=np.float32), (1, 8))
    rows = np.stack([f(inp["ln_w"][0]), f(inp["ln_b"][0]), f(inp["g_final"])], 0)
    lora = np.zeros((128, 2, D), np.float32)
    lora[0:64, 0] = inp["w_up_f"][0]
    lora[64:128, 0] = inp["a_up_f"][0]
    lora[0:64, 1] = inp["w_up_b"][0]
    lora[64:128, 1] = inp["a_up_b"][0]
    sh = {
        "cols": cols, "consts": consts, "rows": rows,
        "w_in": np.ascontiguousarray(f(inp["w_in"][0]).reshape(NJ, 128, NCC, 128).transpose(2, 1, 0, 3)),
        "pool_w": np.ascontiguousarray(f(inp["pool_w"][0]).transpose(1, 0, 2)),
        "lora": lora, "g_up": np.ascontiguousarray(f(inp["g_up"][0])),
        "w_pb": np.ascontiguousarray(f(inp["w_pool_br"][0]).reshape(4, 128, D).transpose(1, 0, 2)),
        "w_rb": np.ascontiguousarray(f(inp["w_rwkv_br"][0]).reshape(NJ, 128, D).transpose(1, 0, 2)),
        "w_out": np.ascontiguousarray(f(inp["w_out"][0]).reshape(NJ, 128, D)),
        "w_ff1": np.ascontiguousarray(f(inp["w_ff1"][0]).reshape(NJ, 128, 32, 128).transpose(2, 1, 0, 3)),
        "w_ff2": np.ascontiguousarray(f(inp["w_ff2"][0]).reshape(32, 128, D)),
    }
    return sh


def prep_core(segs, NS, SL):
    N = NS * SL
    NT = N // TT
    xpad = np.zeros((N + 16, D), np.float32)
    segid = np.full(NS, -1, np.int64)
    rcnt = np.ones((4, N), np.float32)
    cover = np.zeros(NS, bool)
    allsegs = list(segs)
    for s0, ns, arr in segs:
        cover[s0:s0 + ns] = True
    for s in range(NS):
        if not cover[s]:
            allsegs.append((s, 1, None))
    for i, (s0, ns, arr) in enumerate(allsegs):
        segid[s0:s0 + ns] = i
        S = ns * SL
        if arr is not None:
            xpad[8 + s0 * SL: 8 + s0 * SL + S] = arr
        pos = np.arange(S)
        for g, w in enumerate((2, 4, 8, 16)):
            lo = np.maximum(pos - w // 2, 0)
            hi = np.minimum(pos + w // 2 - 1, S - 1)
            rcnt[g, s0 * SL:s0 * SL + S] = 1.0 / (hi - lo + 1).astype(np.float32)
    hm = np.ones((16, NT), np.float32)
    for ti in range(NT):
        t0 = ti * TT
        sl = t0 // SL
        if t0 % SL == 0 and (sl == 0 or segid[sl - 1] != segid[sl]):
            hm[0:8, ti] = 0.0
        t1 = t0 + TT
        sl1 = (t1 - 1) // SL
        if t1 % SL == 0 and (sl1 == NS - 1 or segid[sl1 + 1] != segid[sl1]):
            hm[8:16, ti] = 0.0
    cm = np.zeros((128, 2 * NS), np.float32)
    for s in range(NS):
        if s > 0 and segid[s - 1] == segid[s]:
            cm[:, s] = 1.0
        if s < NS - 1 and segid[s + 1] == segid[s]:
            cm[:, NS + s] = 1.0
    return {"xpad": xpad, "hm": hm, "rcnt": rcnt, "cm": cm}


_NC_CACHE = {}


def kernel(**inputs):
    NS, SL = 8, 2048
    xp = np.asarray(inputs["x_prompt"], np.float32)
    xs = np.asarray(inputs["x_sample"], np.float32)
    sh = prep_shared(inputs)
    plan = []
    plan.append([(0, 8, xs[0])])
    plan.append([(0, 8, xs[1])])
    counts = [6, 6, 5, 5, 5, 5]
    nxt = 0
    owners = []
    for c in counts:
        segs = []
        own = []
        for i in range(c):
            segs.append((i, 1, xp[nxt]))
            own.append(nxt)
            nxt += 1
        plan.append(segs)
        owners.append(own)
    in_maps = []
    for segs in plan:
        m = dict(sh)
        m.update(prep_core(segs, NS, SL))
        in_maps.append(m)
    key = (NS, SL)
    if key not in _NC_CACHE:
        _NC_CACHE[key] = build(NS, SL)
    nc = _NC_CACHE[key]
    res = run_bass_kernel_spmd(nc, in_maps, core_ids=list(range(8)))
    y_prompt = np.empty_like(xp)
    y_sample = np.empty_like(xs)
    for c in range(2):
        y_sample[c] = np.asarray(res.results[c]["out"]).reshape(NS * SL, D)
    for ci, own in enumerate(owners):
        o = np.asarray(res.results[2 + ci]["out"]).reshape(NS, SL, D)
        for i, b in enumerate(own):
            y_prompt[b] = o[i]
    return (y_prompt, y_sample)
```
